# Optimizing a Trainium2 kernel written in Bass

```python
import jax, jax.numpy as jnp
from jax import lax
import numpy as np

D_MODEL = 2048
BATCH = 4
SEQ = 2048
DEPTH = 1

N_DN_HEADS = 8
DN_HEAD_DIM = 128
DN_WIDTH = N_DN_HEADS * DN_HEAD_DIM
N_FOURIER_GROUPS = 8
FOURIER_GROUP_DIM = 128
FOURIER_WIDTH = N_FOURIER_GROUPS * FOURIER_GROUP_DIM
CONV_WIDTH = 5
CHUNK = 64
D_FF = 4 * D_MODEL
N_BRANCHES = 2
EPS = 1e-6

SPLIT_SIZES = (DN_WIDTH, DN_WIDTH, DN_WIDTH, DN_WIDTH, FOURIER_WIDTH, 4 * N_DN_HEADS, N_BRANCHES * D_MODEL)
IN_WIDTH = int(sum(SPLIT_SIZES))
SPLIT_POINTS = tuple(int(s) for s in np.cumsum(SPLIT_SIZES)[:-1])

kernel_name = "hybrid_gdn_fnet_gated_encoder_block"


def rms_norm(x, g):
    xf = x.astype(jnp.float32)
    y = xf * lax.rsqrt(jnp.mean(xf * xf, axis=-1, keepdims=True) + EPS)
    return (y * g.astype(jnp.float32)).astype(x.dtype)


def l2_norm(x):
    return x * lax.rsqrt(jnp.sum(x * x, axis=-1, keepdims=True) + EPS)


def centred_short_conv(u, w):
    c = u.shape[-1]
    pad = (CONV_WIDTH - 1) // 2
    return lax.conv_general_dilated(u, w[:, None, :].astype(u.dtype), window_strides=(1,),
                                    padding=[(pad, pad)], dimension_numbers=("NWC", "WIO", "NWC"),
                                    feature_group_count=c)


def gated_delta_rule_chunked(q, k, v, g, beta):
    b, h, t, dk = q.shape
    dv = v.shape[-1]
    n = t // CHUNK
    q = q.reshape(b, h, n, CHUNK, dk)
    k = k.reshape(b, h, n, CHUNK, dk)
    v = v.reshape(b, h, n, CHUNK, dv)
    gc = jnp.cumsum(g.reshape(b, h, n, CHUNK), axis=-1)
    beta = beta.reshape(b, h, n, CHUNK, 1)
    idx = jnp.arange(CHUNK)
    incl = idx[:, None] >= idx[None, :]
    strict = idx[:, None] > idx[None, :]
    diff = gc[..., :, None] - gc[..., None, :]
    decay = jnp.where(incl, jnp.exp(jnp.where(incl, diff, 0.0)), 0.0)
    kb = k * beta
    m = jnp.where(strict, jnp.einsum("bhnid,bhnjd->bhnij", kb, k) * decay, 0.0)
    a_mat = m + jnp.eye(CHUNK, dtype=m.dtype)
    rhs = jnp.concatenate([v * beta, kb * jnp.exp(gc)[..., None]], axis=-1)
    sol = lax.linalg.triangular_solve(a_mat, rhs, left_side=True, lower=True, unit_diagonal=True)
    u, w = sol[..., :dv], sol[..., dv:]
    qk = jnp.where(incl, jnp.einsum("bhnid,bhnjd->bhnij", q, k) * decay, 0.0)
    q_dec = q * jnp.exp(gc)[..., None]
    k_dec = k * jnp.exp(gc[..., -1:] - gc)[..., None]
    g_last = jnp.exp(gc[..., -1])
    xs = tuple(jnp.moveaxis(a, 2, 0) for a in (q_dec, k_dec, u, w, qk, g_last))

    def step(s, inp):
        qd, kd, uc, wc, qkc, gl = inp
        v_new = uc - jnp.einsum("bhck,bhkv->bhcv", wc, s)
        o = jnp.einsum("bhck,bhkv->bhcv", qd, s) + jnp.einsum("bhij,bhjv->bhiv", qkc, v_new)
        s = s * gl[..., None, None] + jnp.einsum("bhck,bhcv->bhkv", kd, v_new)
        return s, o

    s0 = jnp.zeros((b, h, dk, dv), jnp.float32)
    _, o = lax.scan(step, s0, xs)
    return jnp.moveaxis(o, 0, 2).reshape(b, h, t, dv)


def gated_deltanet_branch(q, k, v, z, scal, conv_w, a_log_f, a_log_b, dt_b_f, dt_b_b, g_head, w_up):
    b, t, _ = q.shape
    dtype = q.dtype
    qkv = jax.nn.silu(centred_short_conv(jnp.concatenate([q, k, v], axis=-1), conv_w)).astype(jnp.float32)
    q, k, v = jnp.split(qkv, 3, axis=-1)
    heads = lambda a: a.reshape(b, t, N_DN_HEADS, DN_HEAD_DIM).transpose(0, 2, 1, 3)
    q = l2_norm(heads(q)) * (DN_HEAD_DIM ** -0.5)
    k = l2_norm(heads(k))
    v = heads(v)
    scal = scal.astype(jnp.float32).transpose(0, 2, 1)
    beta_f, beta_b, a_f, a_b = jnp.split(scal, 4, axis=1)
    beta_f, beta_b = jax.nn.sigmoid(beta_f), jax.nn.sigmoid(beta_b)
    g_f = -jnp.exp(a_log_f.astype(jnp.float32))[:, None] * jax.nn.softplus(a_f + dt_b_f.astype(jnp.float32)[:, None])
    g_b = -jnp.exp(a_log_b.astype(jnp.float32))[:, None] * jax.nn.softplus(a_b + dt_b_b.astype(jnp.float32)[:, None])
    o_f = gated_delta_rule_chunked(q, k, v, g_f, beta_f)
    flip = lambda a: jnp.flip(a, axis=2)
    o_b = flip(gated_delta_rule_chunked(flip(q), flip(k), flip(v), flip(g_b), flip(beta_b)))
    o = (o_f + o_b).transpose(0, 2, 1, 3)
    o = o * lax.rsqrt(jnp.mean(o * o, axis=-1, keepdims=True) + EPS) * g_head.astype(jnp.float32)
    o = o * jax.nn.silu(z.astype(jnp.float32).reshape(b, t, N_DN_HEADS, DN_HEAD_DIM))
    return o.reshape(b, t, DN_WIDTH).astype(dtype) @ w_up


def fourier_branch(f, w_lin):
    b, t, _ = f.shape
    fg = f.astype(jnp.float32).reshape(b, t, N_FOURIER_GROUPS, FOURIER_GROUP_DIM)
    fr = jnp.real(jnp.fft.fft2(fg, axes=(1, 3), norm="ortho"))
    return fr.reshape(b, t, FOURIER_WIDTH).astype(f.dtype) @ w_lin


def setup_inputs(seed: int = 0) -> dict:
    key = jax.random.key(seed)
    ks = jax.random.split(key, 20)
    nrm = lambda k, shape, fan_in: jax.random.normal(k, shape, jnp.float32) * (fan_in ** -0.5)
    gain = lambda k, shape: 1.0 + 0.02 * jax.random.normal(k, shape, jnp.float32)
    a_log = lambda k: jnp.log(jax.random.uniform(k, (DEPTH, N_DN_HEADS), jnp.float32, 1.0, 16.0))

    def dt_bias(k):
        dt = jnp.exp(jax.random.uniform(k, (DEPTH, N_DN_HEADS), jnp.float32, np.log(1e-3), np.log(1e-1)))
        return dt + jnp.log(-jnp.expm1(-dt))

    return {
        "x": jax.random.normal(ks[0], (BATCH, SEQ, D_MODEL), jnp.float32),
        "g_mix": gain(ks[1], (DEPTH, D_MODEL)),
        "w_in": nrm(ks[2], (DEPTH, D_MODEL, IN_WIDTH), D_MODEL),
        "conv_w": nrm(ks[3], (DEPTH, CONV_WIDTH, 3 * DN_WIDTH), CONV_WIDTH),
        "a_log_fwd": a_log(ks[4]),
        "a_log_bwd": a_log(ks[5]),
        "dt_bias_fwd": dt_bias(ks[6]),
        "dt_bias_bwd": dt_bias(ks[7]),
        "g_dn_head": gain(ks[8], (DEPTH, DN_HEAD_DIM)),
        "w_dn_up": nrm(ks[9], (DEPTH, DN_WIDTH, D_MODEL), DN_WIDTH),
        "w_fourier": nrm(ks[10], (DEPTH, FOURIER_WIDTH, D_MODEL), FOURIER_WIDTH),
        "w_o": nrm(ks[11], (DEPTH, D_MODEL, D_MODEL), D_MODEL),
        "g_mlp": gain(ks[12], (DEPTH, D_MODEL)),
        "w_mlp_up": nrm(ks[13], (DEPTH, D_MODEL, D_FF), D_MODEL),
        "w_mlp_down": nrm(ks[14], (DEPTH, D_FF, D_MODEL), D_FF),
        "g_final": gain(ks[15], (D_MODEL,)),
    }


def reference(x, g_mix, w_in, conv_w, a_log_fwd, a_log_bwd, dt_bias_fwd, dt_bias_bwd, g_dn_head,
              w_dn_up, w_fourier, w_o, g_mlp, w_mlp_up, w_mlp_down, g_final):
    for l in range(DEPTH):
        h = rms_norm(x, g_mix[l])
        proj = h @ w_in[l]
        q, k, v, z, f, scal, gates = jnp.split(proj, SPLIT_POINTS, axis=-1)
        y_a = gated_deltanet_branch(q, k, v, z, scal, conv_w[l], a_log_fwd[l], a_log_bwd[l],
                                    dt_bias_fwd[l], dt_bias_bwd[l], g_dn_head[l], w_dn_up[l])
        y_f = fourier_branch(f, w_fourier[l])
        gate_a, gate_f = jnp.split(jax.nn.sigmoid(gates), N_BRANCHES, axis=-1)
        x = x + (gate_a * y_a + gate_f * y_f) @ w_o[l]
        h = rms_norm(x, g_mlp[l])
        x = x + jnp.square(jax.nn.relu(h @ w_mlp_up[l])) @ w_mlp_down[l]
    return rms_norm(x, g_final)
```

```python
from contextlib import ExitStack
import numpy as np
import ml_dtypes
import concourse.bass as bass
import concourse.mybir as mybir
from concourse.bass_utils import run_bass_kernel_spmd

F32 = mybir.dt.float32
BF16 = mybir.dt.bfloat16
AF = mybir.ActivationFunctionType
ALU = mybir.AluOpType
AX = mybir.AxisListType
NPBF = ml_dtypes.bfloat16

D = 2048
T = 2048
OWN = 1024
NTILE = 16
H = 4
EPS = 1e-6
DFF = 8192
BIG = 30000.0
SELF_SYNC = True


class Prog:
    ENGS = ("pe", "act", "dve", "pool", "sp")

    def __init__(self, nc, stack, tag=""):
        self.nc = nc
        self.stack = stack
        self.tag = tag
        self.streams = {e: [] for e in self.ENGS}
        self.count = {e: 0 for e in self.ENGS}
        self.sems = {e: stack.enter_context(nc.semaphore(f"pg{tag}_{e}")) for e in self.ENGS}
        self.waited = {e: {} for e in self.ENGS}
        self.last_w = {}
        self.readers = {}
        self.bank_last = {}
        self.dma = {}

    def _deps(self, eng, reads, writes, banks):
        deps = []
        for r in reads:
            lw = self.last_w.get(r)
            if lw is not None:
                deps.append(lw)
        for w in writes:
            lw = self.last_w.get(w)
            if lw is not None:
                deps.append(lw)
            deps.extend(self.readers.get(w, ()))
        need = {}
        for (k, v) in deps:
            if k == eng and (eng == "pe" or not SELF_SYNC):
                continue
            if need.get(k, 0) < v:
                need[k] = v
        for bk in banks:
            for k, v in self.bank_last.get(bk, {}).items():
                if k != eng and need.get(k, 0) < v:
                    need[k] = v
        for k, v in need.items():
            if self.waited[eng].get(k, 0) >= v:
                continue
            self.waited[eng][k] = v
            sem = self.sems[k] if k in self.sems else self.dma[k][0]
            self.streams[eng].append(("wait", sem, v))

    def _commit(self, ev, reads, writes, banks):
        for r in reads:
            self.readers.setdefault(r, []).append(ev)
        for w in writes:
            self.last_w[w] = ev
            self.readers[w] = []
        for bk in banks:
            self.bank_last.setdefault(bk, {})[ev[0]] = ev[1]

    def op(self, eng, fn, reads=(), writes=(), banks=()):
        self._deps(eng, reads, writes, banks)
        self.count[eng] += 1
        ev = (eng, self.count[eng])
        self.streams[eng].append(("op", fn))
        self._commit(ev, reads, writes, banks)
        return ev

    def dma_op(self, queue, key, out, in_, reads=(), writes=()):
        if key not in self.dma:
            self.dma[key] = [self.stack.enter_context(self.nc.semaphore(f"dm{self.tag}_{key}")), 0]
        ent = self.dma[key]
        self._deps(queue, reads, writes, ())
        if ent[1] and self.waited[queue].get(key, 0) < 16 * ent[1]:
            self.waited[queue][key] = 16 * ent[1]
            self.streams[queue].append(("wait", ent[0], 16 * ent[1]))
        ent[1] += 1
        ev = (key, 16 * ent[1])
        self.streams[queue].append(("dma", out, in_, ent[0]))
        self._commit(ev, reads, writes, ())
        return ev

    def cc_allgather(self, src, dst, groups):
        key = "cc"
        self.dma[key] = [self.stack.enter_context(self.nc.semaphore(f"cc{self.tag}")), 0]
        self.streams["pool"].append(("cc", src, dst, groups, self.dma[key][0]))
        self.dma[key][1] = 1.0 / 16.0
        self.last_w["rcv"] = (key, 1)
        self.readers["rcv"] = []

    def barrier(self):
        for e in self.ENGS:
            for k in self.ENGS:
                if k != e and self.count[k] > self.waited[e].get(k, 0):
                    self.waited[e][k] = self.count[k]
                    self.streams[e].append(("wait", self.sems[k], self.count[k]))
            for key, (sem, cnt) in self.dma.items():
                if cnt and self.waited[e].get(key, 0) < 16 * cnt:
                    self.waited[e][key] = int(16 * cnt)
                    self.streams[e].append(("wait", sem, int(16 * cnt)))

    def emit(self):
        nc = self.nc
        handles = {"pe": "tensor", "act": "scalar", "dve": "vector", "pool": "gpsimd", "sp": "sync"}
        with nc.Block() as block:
            for e in self.ENGS:
                stream = self.streams[e]
                sem = self.sems[e]

                def body(eng, stream=stream, sem=sem):
                    for it in stream:
                        if it[0] == "wait":
                            eng.wait_ge(it[1], it[2])
                        elif it[0] == "op":
                            it[1](eng).then_inc(sem, 1)
                        elif it[0] == "cc":
                            eng.collective_compute("AllGather", ALU.bypass, replica_groups=it[3], ins=[it[1]],
                                                   outs=[it[2]]).then_inc(it[4], 1)
                        else:
                            eng.dma_start(out=it[1], in_=it[2]).then_inc(it[3], 16)

                getattr(block, handles[e])(body)


class Ctx:
    pass


def _sb(nc, st, name, shape, dt):
    return st.enter_context(nc.sbuf_tensor("s_" + name, shape, dt))


def _ps(nc, st, name, shape, dt):
    return st.enter_context(nc.psum_tensor("p_" + name, shape, dt))


def rms_tile(P, C, xb, xkey, gbc, gkey, hb, hkey, col):
    xkeys = list(xkey) if isinstance(xkey, (list, tuple)) else [xkey]
    P.op("act", lambda e: e.activation(out=C.junk[:], in_=xb, func=AF.Square, accum_out=C.st[:, col, 0:1]),
         reads=xkeys, writes=["junk", "st"])
    P.op("pool", lambda e: e.tensor_scalar(out=C.st[:, col, 1:2], in0=C.st[:, col, 0:1], scalar1=1.0 / D, scalar2=EPS,
                                           op0=ALU.mult, op1=ALU.add), reads=["st"], writes=["st"])
    P.op("pool", lambda e: e.tensor_tensor(out=C.st[:, col, 2:3], in0=C.st[:, col, 1:2], in1=C.mhalf[:], op=ALU.pow),
         reads=["st", "mhalf"], writes=["st"])
    P.op("dve", lambda e: e.scalar_tensor_tensor(out=hb, in0=xb, scalar=C.st[:, col, 2:3], in1=gbc,
                                                 op0=ALU.mult, op1=ALU.mult),
         reads=xkeys + ["st", gkey], writes=[hkey])


def transpose_tile16(P, C, hb, hkey, dst, dkeys, trk):
    for half in range(2):
        pt = C.tr[half]
        for j in range(8):
            kc = half * 8 + j
            P.op("pe", lambda e, pt=pt, j=j, kc=kc: e.transpose(out=pt[:, j, :], in_=hb[:, kc * 128:(kc + 1) * 128],
                                                               identity=C.identb[:]),
                 reads=[hkey, "const"], banks=[f"{trk}{half}"])
        if half == 0:
            P.op("act", lambda e, pt=pt: e.copy(out=dst(0), in_=pt[:]), writes=[dkeys[0]], banks=[f"{trk}0"])
        else:
            P.op("dve", lambda e, pt=pt: e.tensor_copy(out=dst(1), in_=pt[:]), writes=[dkeys[1]], banks=[f"{trk}1"])


def build_A(dbg=False, upto=9):
    nc = bass.Bass("TRN2", target_bir_lowering=False)
    do = lambda name, shape, dt: nc.dram_tensor(name, shape, dt, kind="ExternalOutput").ap()
    ins = declare_A_inputs(nc)
    outs = (do("own_og", [4, 128, OWN], BF16), do("snd_og", [4, 128, OWN], BF16),
            do("own_fr", [4, 128, OWN], BF16), do("snd_fr", [4, 128, OWN], BF16))
    D_ = {}
    if dbg:
        D_["qkv"] = do("d_qkv", [128, 12 * T], BF16)
        D_["rowsraw"] = do("d_rowsraw", [128, T], F32)
        D_["fT"] = do("d_fT", [128, 4 * T], BF16)
        D_["hT"] = do("d_hT", [128, 16 * T], BF16)
        D_["rows"] = do("d_rows", [128, T], F32)
        D_["cc"] = do("d_cc", [128, 16 * 24], F32)
        D_["glbc"] = do("d_glbc", [128, 256], F32)
        D_["szt"] = do("d_szt", [128, 16 * 512], BF16)
    emit_A(nc, ins, outs, D_, dbg, upto)
    return nc


def declare_A_inputs(nc):
    di = lambda name, shape, dt: nc.dram_tensor(name, shape, dt, kind="ExternalInput").ap()
    return dict(
        x=di("x", [T, D], F32), gmix=di("gmix", [128, D], F32), wmix=di("wmix", [128, 16, 16 * 128], F32),
        wz=di("wz", [128, 16, 512], F32), wscal=di("wscal", [128, 16, 128], F32), cw=di("cw", [128, 60], F32),
        pc=di("pc", [128, 8], F32), ghead=di("ghead", [128, 128], F32), cf=di("cf", [128, 17 * 128], F32),
        cb=di("cb", [128, 10 * 128], BF16), selb=di("selb", [128, 16 * 128], BF16), dft=di("dft", [8, 128, 16, 2, 256], BF16))


def emit_A(nc, ins, outs, D_, dbg=False, upto=9):
    x_d, gmix_d, wmix_d, wz_d, wscal_d, cw_d, pc_d, ghead_d, cf_d, cb_d, dft_d = (
        ins[k] for k in ("x", "gmix", "wmix", "wz", "wscal", "cw", "pc", "ghead", "cf", "cb", "dft"))
    C_selb_d = ins["selb"]
    own_og, snd_og, own_fr, snd_fr = outs
    with ExitStack() as stA:
        C = Ctx()
        C.nc = nc
        cf = _sb(nc, stA, "cf", [128, 17, 128], F32)
        cb = _sb(nc, stA, "cb", [128, 10, 128], BF16)
        C.identf = cf[:, 0, :]
        C.sel = lambda kind, d, h: cf[:, 1 + kind * 8 + d * 4 + h, :]
        C.identb = cb[:, 0, :]
        C.onesb = cb[:, 1, :]
        C.Jb = cb[:, 2, :]
        C.negI = cb[:, 3, :]
        C.mask = lambda d, incl: cb[:, 4 + d * 2 + incl, :]
        C.CS = cb[:, 8:10, :]
        C.epsb = _sb(nc, stA, "epsb", [128, 1], F32)
        C.mhalf = _sb(nc, stA, "mhalf", [128, 1], F32)
        C.st = _sb(nc, stA, "st", [128, 16, 4], F32)
        pc = _sb(nc, stA, "pc", [128, 8], F32)
        cw = _sb(nc, stA, "cw", [128, 12, 5], F32)
        ghead = _sb(nc, stA, "ghead", [128, 128], F32)

        with ExitStack() as stB:
            QKV = _sb(nc, stB, "QKV", [128, 3, H, T], BF16)
            szt = _sb(nc, stB, "szt", [128, NTILE, 512], BF16)
            rows = _sb(nc, stB, "rows", [128, T], F32)
            with ExitStack() as stF:
                fT = _sb(nc, stF, "fT", [128, H, T], BF16)
                with ExitStack() as stH:
                    hT = _sb(nc, stH, "hT", [128, 16, T], BF16)
                    with ExitStack() as s0:
                        P = Prog(nc, s0, "p0")
                        xt = [_sb(nc, s0, f"xt{i}", [128, D], F32) for i in range(2)]
                        gm = _sb(nc, s0, "gm", [128, D], F32)
                        C.junk = _sb(nc, s0, "junk", [128, D], BF16)
                        hb = [_sb(nc, s0, f"hb{i}", [128, D], BF16) for i in range(2)]
                        C.tr = [_ps(nc, s0, f"tr{i}", [128, 8, 128], BF16) for i in range(2)]
                        P.dma_op("sp", "c0", cf[:], cf_d.rearrange("p (a b) -> p a b", b=128), writes=["const"])
                        P.dma_op("sp", "c1", cb[:], cb_d.rearrange("p (a b) -> p a b", b=128), writes=["const"])
                        P.dma_op("sp", "c2", gm[:], gmix_d, writes=["gm"])
                        P.dma_op("sp", "c3", pc[:], pc_d, writes=["pc"])
                        P.dma_op("sp", "c4", cw[:], cw_d.rearrange("p (a b) -> p a b", b=5), writes=["cw"])
                        P.dma_op("sp", "c5", ghead[:], ghead_d, writes=["ghead"])
                        P.op("pool", lambda e: e.memset(C.epsb[:], EPS), writes=["epsb"])
                        P.op("pool", lambda e: e.memset(C.mhalf[:], -0.5), writes=["mhalf"])
                        for t in range(NTILE):
                            xb = xt[t % 2]
                            P.dma_op("sp", f"x{t%2}", xb[:], x_d[t * 128:(t + 1) * 128, :], writes=[f"xt{t%2}"])
                            rms_tile(P, C, xb[:], f"xt{t%2}", gm[:], "gm", hb[t % 2][:], f"hb{t%2}", t)
                            transpose_tile16(P, C, hb[t % 2], f"hb{t%2}",
                                             lambda half, t=t: hT[:, half * 8:(half + 1) * 8, t * 128:(t + 1) * 128],
                                             [f"hT{t}a", f"hT{t}b"], "tr")
                        if dbg:
                            P.barrier()
                            P.dma_op("sp", "dbg0", D_["hT"], hT[:].rearrange("p a b -> p (a b)"))
                        P.barrier()
                        P.emit()
                    if upto < 1:
                        return
                    with ExitStack() as s1:
                        P = Prog(nc, s1, "p1")
                        wb = [_sb(nc, s1, f"wb{i}", [128, 16, 128], BF16) for i in range(2)]
                        wsc = wb[1]
                        pre = _sb(nc, s1, "pre", [128, T + 4], F32)
                        accs = [_sb(nc, s1, f"acc{i}", [128, T], F32) for i in range(2)]
                        sd = _sb(nc, s1, "sd", [128, T], F32)
                        pacc = [_ps(nc, s1, f"pacc{i}", [128, 512], F32) for i in range(4)]
                        pss = [_ps(nc, s1, f"pss{i}", [128, 512], F32) for i in range(4)]
                        hkeys = lambda blk: [f"hT{t}{s}" for t in range(4 * blk, 4 * blk + 4) for s in "ab"]
                        allh = [f"hT{t}{s}" for t in range(16) for s in "ab"]
                        P.op("pool", lambda e: e.memset(pre[:, 0:2], 0.0), writes=["pre"])
                        P.op("pool", lambda e: e.memset(pre[:, T + 2:T + 4], 0.0), writes=["pre"])
                        P.dma_op("pool", "wb1", wsc[:], wscal_d, writes=["wb1"])

                        def load_w(ci):
                            P.dma_op("pool", f"wb{ci%2}", wb[ci % 2][:], wmix_d[:, :, ci * 128:(ci + 1) * 128],
                                     writes=[f"wb{ci%2}"])
                        load_w(0)
                        for blk in range(4):
                            for kc in range(16):
                                P.op("pe", lambda e, blk=blk, kc=kc: e.matmul(
                                    out=pacc[blk][:], lhsT=wsc[:, kc, :], rhs=hT[:, kc, blk * 512:(blk + 1) * 512],
                                    start=(kc == 0), stop=(kc == 15)), reads=["wb1"] + hkeys(blk), banks=[f"pacc{blk}"])
                            P.op("act", lambda e, blk=blk: e.copy(out=rows[:, blk * 512:(blk + 1) * 512], in_=pacc[blk][:]),
                                 writes=["rows"], banks=[f"pacc{blk}"])
                        pending = []
                        for ci in range(16):
                            if ci + 1 < 16:
                                load_w(ci + 1)
                            w = wb[ci % 2]
                            isf = ci >= 12
                            acc = accs[ci % 2]
                            ak = f"acc{ci % 2}"
                            j = (ci - 12) if isf else ci // 3
                            kind = None if isf else ci % 3
                            for blk in range(4):
                                for kc in range(16):
                                    P.op("pe", lambda e, blk=blk, kc=kc, w=w: e.matmul(
                                        out=pacc[blk][:], lhsT=w[:, kc, :], rhs=hT[:, kc, blk * 512:(blk + 1) * 512],
                                        start=(kc == 0), stop=(kc == 15)),
                                        reads=[f"wb{ci%2}"] + hkeys(blk), banks=[f"pacc{blk}"])
                                if isf:
                                    P.op("dve", lambda e, blk=blk, j=j: e.tensor_copy(
                                        out=fT[:, j, blk * 512:(blk + 1) * 512], in_=pacc[blk][:]),
                                        writes=[f"fT{j}"], banks=[f"pacc{blk}"])
                                else:
                                    P.op("act", lambda e, blk=blk: e.copy(out=pre[:, 2 + blk * 512:2 + (blk + 1) * 512],
                                                                          in_=pacc[blk][:]),
                                         writes=["pre"], banks=[f"pacc{blk}"])
                            prev_b = pending
                            pending = []
                            if isf:
                                for fn in prev_b:
                                    fn()
                                continue
                            P.op("dve", lambda e, ci=ci, acc=acc: e.tensor_scalar(out=acc[:], in0=pre[:, 0:T], scalar1=cw[:, ci, 0:1],
                                                                        scalar2=None, op0=ALU.mult),
                                 reads=["pre", "cw"], writes=[ak])
                            for tap in range(1, 5):
                                P.op("dve", lambda e, ci=ci, tap=tap, acc=acc: e.scalar_tensor_tensor(
                                    out=acc[:], in0=pre[:, tap:tap + T], scalar=cw[:, ci, tap:tap + 1], in1=acc[:],
                                    op0=ALU.mult, op1=ALU.add), reads=["pre", "cw", ak], writes=[ak])
                            if kind == 1:
                                P.op("act", lambda e, j=j, acc=acc: e.activation(out=QKV[:, 1, j, :], in_=acc[:], func=AF.Silu),
                                     reads=[ak], writes=[f"qkv1_{j}"])
                                for fn in prev_b:
                                    fn()
                                continue
                            P.op("act", lambda e, acc=acc: e.activation(out=acc[:], in_=acc[:], func=AF.Silu),
                                 reads=[ak], writes=[ak])
                            sqv = QKV[:, kind, j, :]
                            sqk = f"qkv{kind}_{j}"
                            P.op("act", lambda e, acc=acc, sqv=sqv: e.activation(out=sqv, in_=acc[:], func=AF.Square),
                                 reads=[ak], writes=[sqk])
                            def part_b(acc=acc, ak=ak, sqv=sqv, sqk=sqk, kind=kind, j=j):
                                for blk in range(4):
                                    pb = pss[blk]
                                    P.op("pe", lambda e, blk=blk, pb=pb, sqv=sqv: e.matmul(out=pb[:], lhsT=C.onesb,
                                                                                 rhs=sqv[:, blk * 512:(blk + 1) * 512],
                                                                                 start=True, stop=True),
                                         reads=[sqk, "const"], banks=[f"pss{blk}"])
                                    P.op("act", lambda e, blk=blk, pb=pb: e.activation(
                                        out=sd[:, blk * 512:(blk + 1) * 512], in_=pb[:], func=AF.Ln, bias=C.epsb[:], scale=1.0),
                                        reads=["epsb"], writes=["sd"], banks=[f"pss{blk}"])
                                P.op("act", lambda e: e.activation(out=sd[:], in_=sd[:], func=AF.Exp, scale=-0.5),
                                     reads=["sd"], writes=["sd"])
                                scl = (128.0 ** -0.5) if kind == 2 else 1.0
                                P.op("dve", lambda e, kind=kind, j=j, scl=scl, acc=acc: e.scalar_tensor_tensor(
                                    out=QKV[:, kind, j, :], in0=acc[:], scalar=scl, in1=sd[:], op0=ALU.mult, op1=ALU.mult),
                                    reads=[ak, "sd"], writes=[f"qkv{kind}_{j}"])
                            pending.append(part_b)
                            for fn in prev_b:
                                fn()
                        assert not pending
                        if dbg:
                            P.barrier()
                            P.dma_op("sp", "dbg1", D_["qkv"], QKV[:].rearrange("p a b c -> p (a b c)"))
                            P.dma_op("sp", "dbg2", D_["rowsraw"], rows[:])
                            P.dma_op("sp", "dbg3", D_["fT"], fT[:].rearrange("p a b -> p (a b)"))
                        P.barrier()
                        P.emit()
                    if upto < 2:
                        return
                    with ExitStack() as s1:
                        P = Prog(nc, s1, "p1z")
                        wz = _sb(nc, s1, "wz", [128, 16, 512], BF16)
                        pacc = [_ps(nc, s1, f"pzacc{i}", [128, 512], F32) for i in range(4)]
                        P.dma_op("pool", "wz", wz[:], wz_d, writes=["wz"])
                        for t in range(NTILE):
                            pb = pacc[t % 4]
                            for kc in range(16):
                                P.op("pe", lambda e, t=t, kc=kc, pb=pb: e.matmul(
                                    out=pb[:], lhsT=hT[:, kc, t * 128:(t + 1) * 128], rhs=wz[:, kc, :],
                                    start=(kc == 0), stop=(kc == 15)), reads=["wz"], banks=[f"pz{t%4}"])
                            P.op("act", lambda e, t=t, pb=pb: e.activation(out=szt[:, t, :], in_=pb[:], func=AF.Silu),
                                 writes=[f"szt{t}"], banks=[f"pz{t%4}"])
                        if dbg:
                            P.barrier()
                            P.dma_op("sp", "dbg4", D_["szt"], szt[:].rearrange("p a b -> p (a b)"))
                        P.barrier()
                        P.emit()
                if upto < 3:
                    return
                with ExitStack() as s2:
                    P = Prog(nc, s2, "pf")
                    Y = _sb(nc, s2, "Y", [128, NTILE, H, 256], BF16)
                    DT = [_sb(nc, s2, f"DT{i}", [128, 16, 2, 256], BF16) for i in range(2)]
                    stg = [_sb(nc, s2, f"fstg{i}", [128, H, 256], BF16) for i in range(2)]
                    pf = [_ps(nc, s2, f"pf{i}", [128, 512], F32) for i in range(4)]
                    P.dma_op("sp", "dt0", DT[0][:], dft_d[0], writes=["DT0"])
                    P.dma_op("sp", "dt1", DT[1][:], dft_d[1], writes=["DT1"])
                    n = 0
                    for t in range(NTILE):
                        for gp in range(2):
                            pb = pf[n % 4]
                            for gi in range(2):
                                g = gp * 2 + gi
                                P.op("pe", lambda e, t=t, g=g, gi=gi, pb=pb: e.matmul(
                                    out=pb[:, gi * 256:(gi + 1) * 256], lhsT=fT[:, g, t * 128:(t + 1) * 128],
                                    rhs=C.CS.rearrange("p a b -> p (a b)"), start=True, stop=True),
                                    reads=["const"], banks=[f"pf{n%4}"])
                            eng = "act" if n % 2 == 0 else "dve"
                            if eng == "act":
                                P.op("act", lambda e, t=t, gp=gp, pb=pb: e.copy(
                                    out=Y[:, t, gp * 2:gp * 2 + 2, :].rearrange("p a b -> p (a b)"), in_=pb[:]),
                                    writes=[f"Y{t}_{gp}"], banks=[f"pf{n%4}"])
                            else:
                                P.op("dve", lambda e, t=t, gp=gp, pb=pb: e.tensor_copy(
                                    out=Y[:, t, gp * 2:gp * 2 + 2, :].rearrange("p a b -> p (a b)"), in_=pb[:]),
                                    writes=[f"Y{t}_{gp}"], banks=[f"pf{n%4}"])
                            n += 1
                    yk = [f"Y{t}_{gp}" for t in range(NTILE) for gp in range(2)]
                    for blk in range(8):
                        dtb = DT[blk % 2]
                        for gp in range(2):
                            pb = pf[(blk * 2 + gp) % 4]
                            bk = f"pf{(blk * 2 + gp) % 4}"
                            for gi in range(2):
                                g = gp * 2 + gi
                                for t in range(NTILE):
                                    for cs in range(2):
                                        P.op("pe", lambda e, t=t, g=g, gi=gi, cs=cs, pb=pb, dtb=dtb: e.matmul(
                                            out=pb[:, gi * 256:(gi + 1) * 256],
                                            lhsT=Y[:, t, g, cs * 128:(cs + 1) * 128], rhs=dtb[:, t, cs, :],
                                            start=(t == 0 and cs == 0), stop=(t == NTILE - 1 and cs == 1)),
                                            reads=yk + [f"DT{blk%2}"], banks=[bk])
                            sg = stg[blk % 2]
                            if gp == 0:
                                P.op("act", lambda e, pb=pb, sg=sg: e.copy(
                                    out=sg[:, 0:2, :].rearrange("p a b -> p (a b)"), in_=pb[:]),
                                    writes=[f"fstg{blk%2}a"], banks=[bk])
                            else:
                                P.op("dve", lambda e, pb=pb, sg=sg: e.tensor_copy(
                                    out=sg[:, 2:4, :].rearrange("p a b -> p (a b)"), in_=pb[:]),
                                    writes=[f"fstg{blk%2}b"], banks=[bk])
                        dst = own_fr if blk < 4 else snd_fr
                        c0 = (blk % 4) * 256
                        P.dma_op("sp", f"fo{blk%2}", dst[:, :, c0:c0 + 256].rearrange("g p n -> p g n"), stg[blk % 2][:],
                                 reads=[f"fstg{blk%2}a", f"fstg{blk%2}b"])
                        if blk + 2 < 8:
                            P.dma_op("sp", f"dt{blk%2}", DT[blk % 2][:], dft_d[blk + 2], writes=[f"DT{blk%2}"])
                    P.barrier()
                    P.emit()
            if upto < 4:
                return
            with ExitStack() as s3:
                build_gdn(nc, s3, C, QKV, szt, rows, pc, ghead, own_og, snd_og, D_, upto, C_selb_d)
    return


def build_gdn(nc, st, C, QKV, szt, rows, pc, ghead, own_og, snd_og, D_, upto=9, selb_d=None):
    sb = lambda name, shape, dt: _sb(nc, st, name, shape, dt)
    ps = lambda name, shape, dt: _ps(nc, st, name, shape, dt)
    selbt = sb("selbt", [128, 16, 128], BF16)
    C.selb = lambda kind, d, h: selbt[:, kind * 8 + d * 4 + h, :]
    Rhl = sb("Rhl", [128, 2, T], BF16)
    NRhl = sb("NRhl", [128, 2, T], BF16)
    CC = sb("CC", [128, NTILE, 3, 2, H], F32)
    GLbc = sb("GLbc", [128, 8, 32], F32)
    st1 = ExitStack()
    sb1 = lambda name, shape, dt: _sb(nc, st1, name, shape, dt)
    P = Prog(nc, st1, "pgs")
    R = sb1("R", [128, T], F32)
    NR = sb1("NR", [128, T], F32)
    Cs = sb1("Cs", [128, T], F32)
    R2 = sb1("R2", [128, T], F32)
    COLR = sb1("COLR", [128, NTILE, 128], F32)
    COLR2 = sb1("COLR2", [128, NTILE, 128], F32)
    m01 = sb1("m01", [128, T], F32)
    cmul = sb1("cmul", [128, 1], F32)
    GLrow = sb1("GLrow", [128, 32], F32)
    BK = [[ps(f"gB{d}{i}", [128, 4, 128], F32) for i in range(3)] for d in range(2)]
    W = [ps(f"gW{d}", [128, 4, 128], F32) for d in range(2)]
    B1, B2, B3 = BK[0]

    P.op("act", lambda e: e.activation(out=rows[:], in_=rows[:], func=AF.Exp, bias=pc[:, 0:1], scale=pc[:, 1:2]),
         reads=["rows", "pc"], writes=["rows"])
    P.op("act", lambda e: e.activation(out=rows[:], in_=rows[:], func=AF.Ln, bias=1.0, scale=1.0),
         reads=["rows"], writes=["rows"])
    P.op("act", lambda e: e.activation(out=cmul[:], in_=pc[:, 2:3], func=AF.Exp), reads=["pc"], writes=["cmul"])
    P.op("dve", lambda e: e.tensor_scalar(out=cmul[:], in0=cmul[:], scalar1=-1.0, scalar2=None, op0=ALU.mult),
         reads=["cmul"], writes=["cmul"])
    P.op("dve", lambda e: e.tensor_scalar(out=R[:], in0=rows[:], scalar1=cmul[:, 0:1], scalar2=None, op0=ALU.mult),
         reads=["rows", "cmul"], writes=["R"])
    P.op("pool", lambda e: e.memset(m01[:], 1.0), writes=["m01"])
    P.op("pool", lambda e: e.memset(m01[:].rearrange("p (c k) -> p c k", k=64)[:, :, 0:1], 0.0), writes=["m01"])
    v3 = lambda t: t[:].rearrange("p (c k) -> p c k", k=64)
    P.op("dve", lambda e: e.tensor_tensor_scan(out=Cs[:], data0=m01[:], data1=R[:], initial=0.0, op0=ALU.mult, op1=ALU.add),
         reads=["R", "m01"], writes=["Cs"])
    P.op("act", lambda e: e.copy(out=GLrow[:], in_=v3(Cs)[:, :, 63]), reads=["Cs"], writes=["GLrow"])
    P.op("dve", lambda e: e.tensor_tensor(out=v3(R2)[0:32], in0=v3(Cs)[0:32, :, 63:64].to_broadcast([32, 32, 64]),
                                          in1=v3(Cs)[0:32], op=ALU.subtract), reads=["Cs"], writes=["R2"])
    P.op("dve", lambda e: e.tensor_tensor(out=R2[32:64, :], in0=Cs[32:64, :], in1=R[32:64, :], op=ALU.subtract),
         reads=["Cs", "R"], writes=["R2"])
    P.op("pool", lambda e: e.memset(R2[64:128, :], 0.0), writes=["R2"])
    P.op("dve", lambda e: e.tensor_tensor(out=v3(R)[32:64], in0=v3(Cs)[32:64, :, 63:64].to_broadcast([32, 32, 64]),
                                          in1=v3(R2)[32:64], op=ALU.subtract), reads=["Cs", "R2", "R"], writes=["R"])
    P.op("act", lambda e: e.copy(out=R[0:32, :], in_=Cs[0:32, :]), reads=["Cs", "R"], writes=["R"])
    P.op("dve", lambda e: e.tensor_scalar(out=NR[:], in0=R[:], scalar1=-1.0, scalar2=None, op0=ALU.mult),
         reads=["R"], writes=["NR"])
    for t in range(NTILE):
        bk = B1 if t % 2 == 0 else B2
        bkn = "B1" if t % 2 == 0 else "B2"
        P.op("pe", lambda e, t=t, bk=bk: e.matmul(out=bk[:, 0, :], lhsT=R[:, t * 128:(t + 1) * 128], rhs=C.identf,
                                                  start=True, stop=True), reads=["R", "const"], banks=[bkn])
        P.op("pe", lambda e, t=t, bk=bk: e.matmul(out=bk[:, 1, :], lhsT=R2[:, t * 128:(t + 1) * 128], rhs=C.identf,
                                                  start=True, stop=True), reads=["R2", "const"], banks=[bkn])
        P.op("act", lambda e, t=t, bk=bk: e.copy(out=COLR[:, t, :], in_=bk[:, 0, :]), writes=["COLR"], banks=[bkn])
        P.op("dve", lambda e, t=t, bk=bk: e.tensor_copy(out=COLR2[:, t, :], in_=bk[:, 1, :]), writes=["COLR2"], banks=[bkn])
    for d in range(2):
        P.op("act", lambda e, d=d: e.activation(out=CC[:, :, 0, d, :], in_=COLR[:, :, 64 + 32 * d:64 + 32 * d + 4],
                                                func=AF.Exp), reads=["COLR"], writes=["CC"])
        P.op("dve", lambda e, d=d: e.tensor_tensor(out=CC[:, :, 1, d, :], in0=COLR[:, :, 32 * d:32 * d + 4],
                                                   in1=COLR[:, :, 64 + 32 * d:64 + 32 * d + 4], op=ALU.add),
             reads=["COLR"], writes=["CC"])
        P.op("act", lambda e, d=d: e.activation(out=CC[:, :, 1, d, :], in_=CC[:, :, 1, d, :], func=AF.Exp),
             reads=["CC"], writes=["CC"])
        P.op("act", lambda e, d=d: e.activation(out=CC[:, :, 2, d, :], in_=COLR2[:, :, 32 * d:32 * d + 4], func=AF.Exp),
             reads=["COLR2"], writes=["CC"])
    for d in range(2):
        for h in range(H):
            P.op("pe", lambda e, d=d, h=h: e.matmul(
                out=B3[:].rearrange("p a b -> p (a b)")[:, (d * 4 + h) * 32:(d * 4 + h + 1) * 32],
                lhsT=C.sel(0, d, h), rhs=GLrow[:], start=True, stop=True),
                 reads=["GLrow", "const"], banks=["B3"])
    P.op("act", lambda e: e.activation(out=GLbc[:].rearrange("p a b -> p (a b)"),
                                       in_=B3[:].rearrange("p a b -> p (a b)")[:, 0:256], func=AF.Exp),
         writes=["GLbc"], banks=["B3"])
    if D_:
        P.dma_op("sp", "dbg_rows", D_["rows"], R[:], reads=["R"])
        P.dma_op("sp", "dbg_cc", D_["cc"], CC[:].rearrange("p a b c d -> p (a b c d)"), reads=["CC"])
        P.dma_op("sp", "dbg_gl", D_["glbc"], GLbc[:].rearrange("p a b -> p (a b)"), reads=["GLbc"])

    P.dma_op("sp", "selb", selbt[:], selb_d.rearrange("p (a b) -> p a b", b=128), writes=["const"])
    P.op("act", lambda e: e.copy(out=Rhl[:, 0, :], in_=R[:]), reads=["R"], writes=["Rhi"])
    P.op("dve", lambda e: e.tensor_tensor(out=NR[:], in0=R[:], in1=Rhl[:, 0, :], op=ALU.subtract),
         reads=["R", "Rhi", "NR"], writes=["NR"])
    P.op("act", lambda e: e.copy(out=Rhl[:, 1, :], in_=NR[:]), reads=["NR"], writes=["Rlo"])
    P.op("dve", lambda e: e.tensor_scalar(out=NRhl[:].rearrange("p a b -> p (a b)"), in0=Rhl[:].rearrange("p a b -> p (a b)"),
                                          scalar1=-1.0, scalar2=None, op0=ALU.mult), reads=["Rhi", "Rlo"], writes=["NRhl"])
    P.barrier()
    P.emit()
    st1.close()
    if upto < 5:
        return
    P = Prog(nc, st, "pg")
    tmp = {}
    for d in range(2):
        tmp[d] = dict(
            E2=sb(f"E2_{d}", [128, 4, 128], F32), E3=sb(f"E3_{d}", [128, 4, 128], F32),
            NT=sb(f"NTm{d}", [128, 4, 128], BF16), Nm=sb(f"Nm{d}", [128, 4, 128], BF16),
            Pm=[sb(f"Pm{d}{i}", [128, 4, 128], BF16) for i in range(2)],
            PTm=[sb(f"PTm{d}{i}", [128, 4, 128], BF16) for i in range(2)],
            Rm=[sb(f"Rm{d}{i}", [128, 4, 128], BF16) for i in range(2)],
            kbg=sb(f"kbg{d}", [128, 4, 128], BF16), vb=sb(f"vb{d}", [128, 4, 128], BF16),
            egc=sb(f"egc{d}", [128, 4, 128], F32))
    slot = {}
    for d in range(2):
        for s in range(2):
            slot[(d, s)] = dict(
                qkT=sb(f"qkT{d}{s}", [128, 4, 128], BF16), u=sb(f"u{d}{s}", [128, 4, 128], F32),
                wT=sb(f"wT{d}{s}", [128, 4, 128], BF16), kd=sb(f"kd{d}{s}", [128, 4, 128], BF16),
                qd=sb(f"qd{d}{s}", [128, 4, 128], BF16))
    S32 = sb("S32", [128, 2, 4, 128], F32)
    Sbf = sb("Sbf", [128, 2, 4, 128], BF16)
    vnew = sb("vnew", [128, 2, 4, 128], BF16)
    ost = sb("ost", [128, 16, 4, 128], BF16)
    ot = sb("ot", [128, 2, 4, 128], F32)
    osq = sb("osq", [128, 2, 4, 128], F32)
    oss = sb("oss", [128, 2, 4, 2], F32)
    ogt = [sb(f"ogt{d}", [128, 4, 128], BF16) for d in range(2)]
    ostg = [sb(f"ostg{i}", [128, 4, 128], BF16) for i in range(2)]
    P.op("pool", lambda e: e.memset(S32[:].rearrange("p a b c -> p (a b c)"), 0.0), writes=["S32_0", "S32_1"])
    P.op("pool", lambda e: e.memset(Sbf[:].rearrange("p a b c -> p (a b c)"), 0.0), writes=["Sbf0", "Sbf1"])
    P.op("pool", lambda e: e.memset(vnew[:].rearrange("p a b c -> p (a b c)"), 0.0), writes=["vnew0", "vnew1"])

    f4 = lambda ap: ap.rearrange("p a b -> p (a b)")

    def prep(tt, d, s):
        sl = slot[(d, s)]
        tm = tmp[d]
        E2, E3, NT, Nm, Pm, PTm, Rm, kbg, vb, egc = (tm[k_] for k_ in ("E2", "E3", "NT", "Nm", "Pm", "PTm", "Rm", "kbg", "vb", "egc"))
        k = f"{d}{s}"
        B1, B2, B3 = BK[d]
        b1k, b2k, b3k = f"B1_{d}", f"B2_{d}", f"B3_{d}"
        tsl = slice(tt * 128, (tt + 1) * 128)
        bc = lambda kind: CC[:, tt, kind, d, :].unsqueeze(2).to_broadcast([128, 4, 128])
        for (incl, Et, ek) in ((0, E2, f"E2_{d}"), (1, E3, f"E3_{d}")):
            for h in range(H):
                for hl in range(2):
                    P.op("pe", lambda e, h=h, incl=incl, hl=hl: e.matmul(out=B3[:, h, :], lhsT=C.selb(1 - incl, d, h),
                                                                         rhs=Rhl[:, hl, tsl], start=(hl == 0), stop=False),
                         reads=["const"], banks=[b3k])
                for hl in range(2):
                    P.op("pe", lambda e, h=h, hl=hl: e.matmul(out=B3[:, h, :], lhsT=NRhl[:, hl, tsl], rhs=C.selb(0, d, h),
                                                              start=False, stop=False), reads=["const"], banks=[b3k])
                P.op("pe", lambda e, h=h, incl=incl: e.matmul(out=B3[:, h, :], lhsT=C.negI, rhs=C.mask(d, incl),
                                                              start=False, stop=True), reads=["const"], banks=[b3k])
            P.op("act", lambda e, Et=Et: e.activation(out=f4(Et[:]), in_=f4(B3[:]), func=AF.Exp),
                 writes=[ek], banks=[b3k])
            yield
            if incl == 0:
                for h in range(H):
                    P.op("pe", lambda e, h=h: e.matmul(out=B1[:, h, :], lhsT=QKV[:, 0, h, tsl], rhs=QKV[:, 0, h, tsl],
                                                       start=True, stop=True), banks=[b1k])
                P.op("dve", lambda e: e.scalar_tensor_tensor(out=f4(NT[:]), in0=f4(B1[:]), scalar=-1.0, in1=f4(E2[:]),
                                                             op0=ALU.mult, op1=ALU.mult),
                     reads=[f"E2_{d}"], writes=[f"NT{d}"], banks=[b1k])
                yield
                for h in range(H):
                    P.op("pe", lambda e, h=h: e.matmul(out=B2[:, h, :], lhsT=NT[:, h, :], rhs=C.identb, start=True, stop=True),
                         reads=[f"NT{d}", "const"], banks=[b2k])
                P.op("act", lambda e: e.copy(out=f4(Nm[:]), in_=f4(B2[:])), writes=[f"Nm{d}"], banks=[b2k])
                P.op("pool", lambda e: e.tensor_tensor(out=Rm[0][:], in0=NT[:],
                                                       in1=C.identb.unsqueeze(1).to_broadcast([128, 4, 128]), op=ALU.add),
                     reads=[f"NT{d}", "const"], writes=[f"Rm{d}0"])
                yield
        Pc, PTc, pk, ptk = Nm, NT, f"Nm{d}", f"NT{d}"
        ri = 0
        for lvl in range(5):
            Pn, PTn = Pm[lvl % 2], PTm[lvl % 2]
            pnk, ptnk = f"Pm{d}{lvl%2}", f"PTm{d}{lvl%2}"
            for h in range(H):
                P.op("pe", lambda e, h=h, Pc=Pc, PTc=PTc: e.matmul(out=B1[:, h, :], lhsT=PTc[:, h, :], rhs=Pc[:, h, :],
                                                                   start=True, stop=True), reads=[pk, ptk], banks=[b1k])
            P.op("act", lambda e, Pn=Pn: e.copy(out=f4(Pn[:]), in_=f4(B1[:])), writes=[pnk], banks=[b1k])
            if lvl < 4:
                for h in range(H):
                    P.op("pe", lambda e, h=h, Pc=Pc, PTc=PTc: e.matmul(out=B2[:, h, :], lhsT=Pc[:, h, :], rhs=PTc[:, h, :],
                                                                       start=True, stop=True), reads=[pk, ptk], banks=[b2k])
                if lvl % 2 == 0:
                    P.op("dve", lambda e, PTn=PTn: e.tensor_copy(out=f4(PTn[:]), in_=f4(B2[:])), writes=[ptnk], banks=[b2k])
                else:
                    P.op("act", lambda e, PTn=PTn: e.copy(out=f4(PTn[:]), in_=f4(B2[:])), writes=[ptnk], banks=[b2k])
            yield
            for h in range(H):
                P.op("pe", lambda e, h=h, Pn=Pn, ri=ri: e.matmul(out=B3[:, h, :], lhsT=Pn[:, h, :], rhs=Rm[ri][:, h, :],
                                                                 start=True, stop=True),
                     reads=[pnk, f"Rm{d}{ri}"], banks=[b3k])
            P.op("dve", lambda e, ri=ri: e.tensor_tensor(out=f4(Rm[1 - ri][:]), in0=f4(B3[:]), in1=f4(Rm[ri][:]), op=ALU.add),
                 reads=[f"Rm{d}{ri}"], writes=[f"Rm{d}{1-ri}"], banks=[b3k])
            ri = 1 - ri
            Pc, PTc, pk, ptk = Pn, PTn, pnk, ptnk
            yield
            if lvl == 0:
                for h in range(H):
                    P.op("pe", lambda e, h=h: e.matmul(out=B3[:, h, :], lhsT=QKV[:, 0, h, tsl], rhs=C.identb, start=True, stop=True),
                         reads=["const"], banks=[b3k])
                P.op("dve", lambda e: e.tensor_tensor(out=kbg[:], in0=B3[:], in1=bc(1), op=ALU.mult),
                     reads=["CC"], writes=[f"kbg{d}"], banks=[b3k])
                P.op("act", lambda e: e.copy(out=f4(egc[:]), in_=f4(B3[:])), writes=[f"egc{d}"], banks=[b3k])
                P.op("pool", lambda e: e.tensor_tensor(out=sl["kd"][:], in0=egc[:], in1=bc(2), op=ALU.mult),
                     reads=["CC", f"egc{d}"], writes=["kd" + k])
                yield
                for h in range(H):
                    P.op("pe", lambda e, h=h: e.matmul(out=B3[:, h, :], lhsT=QKV[:, 1, h, tsl], rhs=C.identb, start=True, stop=True),
                         reads=["const"], banks=[b3k])
                P.op("dve", lambda e: e.tensor_tensor(out=vb[:], in0=B3[:], in1=bc(0), op=ALU.mult),
                     reads=["CC"], writes=[f"vb{d}"], banks=[b3k])
                yield
            elif lvl == 1:
                for h in range(H):
                    P.op("pe", lambda e, h=h: e.matmul(out=B2[:, h, :], lhsT=QKV[:, 0, h, tsl], rhs=QKV[:, 2, h, tsl],
                                                       start=True, stop=True), banks=[b2k])
                P.op("dve", lambda e: e.tensor_tensor(out=f4(sl["qkT"][:]), in0=f4(B2[:]), in1=f4(E3[:]), op=ALU.mult),
                     reads=[f"E3_{d}"], writes=["qkT" + k], banks=[b2k])
                yield
            elif lvl == 2:
                for h in range(H):
                    for hl in range(2):
                        P.op("pe", lambda e, h=h, hl=hl: e.matmul(out=B3[:, h, :], lhsT=C.selb(0, d, h), rhs=Rhl[:, hl, tsl],
                                                                  start=(hl == 0), stop=(hl == 1)),
                             reads=["const"], banks=[b3k])
                P.op("act", lambda e: e.activation(out=f4(egc[:]), in_=f4(B3[:]), func=AF.Exp),
                     reads=["kd" + k], writes=[f"egc{d}"], banks=[b3k])
                P.op("pool", lambda e: e.tensor_tensor(out=sl["qd"][:], in0=QKV[:, 2, :, tsl], in1=egc[:], op=ALU.mult),
                     reads=[f"egc{d}"], writes=["qd" + k])
                yield
        Rf, rk = Rm[ri], f"Rm{d}{ri}"
        for h in range(H):
            P.op("pe", lambda e, h=h: e.matmul(out=B1[:, h, :], lhsT=Rf[:, h, :], rhs=vb[:, h, :], start=True, stop=True),
                 reads=[rk, f"vb{d}"], banks=[b1k])
        P.op("act", lambda e: e.copy(out=f4(sl["u"][:]), in_=f4(B1[:])), writes=["u" + k], banks=[b1k])
        for h in range(H):
            P.op("pe", lambda e, h=h: e.matmul(out=B2[:, h, :], lhsT=kbg[:, h, :], rhs=Rf[:, h, :], start=True, stop=True),
                 reads=[rk, f"kbg{d}"], banks=[b2k])
        P.op("dve", lambda e: e.tensor_copy(out=f4(sl["wT"][:]), in_=f4(B2[:])), writes=["wT" + k], banks=[b2k])
        yield

    def step(s):
        info = []
        for d in range(2):
            c = s if d == 0 else 31 - s
            tt, half = c // 2, c % 2
            sl = slot[(d, tt % 2)]
            k = f"{d}{tt%2}"
            rs = slice(half * 64, half * 64 + 64)
            info.append((c, tt, half, sl, k, rs))
            for h in range(H):
                P.op("pe", lambda e, h=h, sl=sl, d=d: e.matmul(out=W[d][:, h, :], lhsT=sl["wT"][:, h, :], rhs=Sbf[:, d, h, :],
                                                               start=True, stop=True),
                     reads=["wT" + k, f"Sbf{d}"], banks=[f"W{d}"])
            P.op("dve", lambda e, sl=sl, d=d, rs=rs: e.tensor_tensor(out=vnew[rs, d], in0=sl["u"][rs], in1=W[d][rs],
                                                                      op=ALU.subtract),
                 reads=["u" + k], writes=[f"vnew{d}"], banks=[f"W{d}"])
            P.op("pool", lambda e, d=d, c=c: e.tensor_tensor(
                out=S32[:, d], in0=S32[:, d], in1=GLbc[:, d * 4:d * 4 + 4, c:c + 1].to_broadcast([128, 4, 128]), op=ALU.mult),
                reads=["GLbc", f"S32_{d}"], writes=[f"S32_{d}"])
        yield
        for d in range(2):
            c, tt, half, sl, k, rs = info[d]
            for h in range(H):
                P.op("pe", lambda e, h=h, sl=sl, d=d, rs=rs: e.matmul(out=W[d][:, h, :], lhsT=sl["kd"][rs, h, :],
                                                                      rhs=vnew[rs, d, h, :], start=True, stop=True),
                     reads=["kd" + k, f"vnew{d}"], banks=[f"W{d}"])
            P.op("dve", lambda e, d=d: e.tensor_tensor(out=S32[:, d], in0=S32[:, d], in1=W[d][:], op=ALU.add),
                 reads=[f"S32_{d}"], writes=[f"S32_{d}"], banks=[f"W{d}"])
        yield
        for d in range(2):
            c, tt, half, sl, k, rs = info[d]
            for h in range(H):
                P.op("pe", lambda e, h=h, sl=sl, d=d: e.matmul(out=W[d][:, h, :], lhsT=sl["qd"][:, h, :], rhs=Sbf[:, d, h, :],
                                                               start=True, stop=False),
                     reads=["qd" + k, f"Sbf{d}"], banks=[f"W{d}"])
                P.op("pe", lambda e, h=h, sl=sl, d=d, rs=rs: e.matmul(out=W[d][:, h, :], lhsT=sl["qkT"][rs, h, :],
                                                                      rhs=vnew[rs, d, h, :], start=False, stop=True),
                     reads=["qkT" + k, f"vnew{d}"], banks=[f"W{d}"])
            if s < 16:
                P.op("act", lambda e, d=d, rs=rs, s=s: e.copy(out=ost[rs, s], in_=W[d][rs]),
                     writes=[f"ost{s}_{d}"], banks=[f"W{d}"])
            else:
                s0 = 31 - s
                P.op("dve", lambda e, d=d, rs=rs, s0=s0: e.tensor_tensor(out=ot[rs, d], in0=W[d][rs], in1=ost[rs, s0], op=ALU.add),
                     reads=[f"ost{s0}_{1-d}"], writes=[f"ot{d}"], banks=[f"W{d}"])
            P.op("act", lambda e, d=d: e.copy(out=Sbf[:, d], in_=S32[:, d]), reads=[f"S32_{d}"], writes=[f"Sbf{d}"])
        yield
        if s < 16:
            return
        for d in range(2):
            c, tt, half, sl, k, rs = info[d]
            P.op("pool", lambda e, d=d, rs=rs: e.tensor_tensor(out=osq[rs, d], in0=ot[rs, d], in1=ot[rs, d], op=ALU.mult),
                 reads=[f"ot{d}"], writes=[f"osq{d}"])
            P.op("dve", lambda e, d=d, rs=rs: e.tensor_reduce(out=oss[rs, d, :, 0], in_=osq[rs, d], axis=AX.X, op=ALU.add),
                 reads=[f"osq{d}"], writes=[f"oss{d}"])
            P.op("pool", lambda e, d=d, rs=rs: e.tensor_scalar(out=oss[rs, d, :, 1], in0=oss[rs, d, :, 0], scalar1=1.0 / 128.0,
                                                               scalar2=EPS, op0=ALU.mult, op1=ALU.add),
                 reads=[f"oss{d}"], writes=[f"oss{d}"])
            P.op("pool", lambda e, d=d, rs=rs: e.tensor_tensor(out=oss[rs, d, :, 1], in0=oss[rs, d, :, 1],
                                                               in1=C.mhalf[rs, 0:1].to_broadcast([64, 4]), op=ALU.pow),
                 reads=[f"oss{d}", "mhalf"], writes=[f"oss{d}"])
        yield
        for d in range(2):
            c, tt, half, sl, k, rs = info[d]
            P.op("dve", lambda e, d=d, rs=rs: e.tensor_tensor(out=ot[rs, d], in0=ot[rs, d],
                                                              in1=oss[rs, d, :, 1:2].to_broadcast([64, 4, 128]), op=ALU.mult),
                 reads=[f"oss{d}", f"ot{d}"], writes=[f"ot{d}"])
            P.op("pool", lambda e, d=d, rs=rs: e.tensor_tensor(out=ot[rs, d], in0=ot[rs, d],
                                                               in1=ghead[rs].unsqueeze(1).to_broadcast([64, 4, 128]), op=ALU.mult),
                 reads=["ghead", f"ot{d}"], writes=[f"ot{d}"])
            P.op("pool", lambda e, d=d, rs=rs, tt=tt: e.tensor_tensor(
                out=ogt[d][rs], in0=ot[rs, d], in1=szt[rs, tt, :].rearrange("p (a b) -> p a b", b=128), op=ALU.mult),
                reads=[f"ot{d}"], writes=[f"ogt{d}"])
        yield
        for d in range(2):
            c, tt, half, sl, k, rs = info[d]
            done = (half == 1) if d == 0 else (half == 0)
            if done:
                ij = C.Jb if d == 0 else C.identb
                for h in range(H):
                    P.op("pe", lambda e, h=h, d=d, ij=ij: e.matmul(out=BK[d][2][:, h, :], lhsT=ogt[d][:, h, :], rhs=ij,
                                                                   start=True, stop=True),
                         reads=[f"ogt{d}", "const"], banks=[f"B3_{d}"])
                P.op("act", lambda e, d=d: e.copy(out=f4(ostg[d][:]), in_=f4(BK[d][2][:])), writes=[f"ostg{d}"],
                     banks=[f"B3_{d}"])
                if d == 0:
                    dst, blk = snd_og, 15 - tt
                else:
                    dst, blk = own_og, tt
                P.dma_op("sp", f"og{d}", dst[:, :, blk * 128:(blk + 1) * 128].rearrange("g p n -> p g n"), ostg[d][:],
                         reads=[f"ostg{d}"])
        yield

    def steps2(a):
        yield from step(2 * a)
        yield from step(2 * a + 1)

    def run(gens):
        gens = list(gens)
        while gens:
            for g in list(gens):
                try:
                    next(g)
                except StopIteration:
                    gens.remove(g)

    nround = 16 if upto >= 9 else 2
    run([prep(0, 0, 0), prep(15, 1, 1)])
    for a in range(nround):
        gl = [steps2(a)]
        if a + 1 < 16:
            gl += [prep(a + 1, 0, (a + 1) % 2), prep(15 - (a + 1), 1, (15 - (a + 1)) % 2)]
        run(gl)
    P.barrier()
    P.emit()


def _consts(c):
    cf = np.zeros((128, 17, 128), np.float32)
    cf[:, 0, :] = np.eye(128, dtype=np.float32)
    for d in range(2):
        for h in range(H):
            cf[d * 32 + h, 1 + d * 4 + h, :] = 1.0
            cf[d * 32 + h, 1 + 8 + d * 4 + h, :] = 1.0
            cf[64 + d * 32 + h, 1 + 8 + d * 4 + h, :] = 1.0
    cb = np.zeros((128, 10, 128), np.float32)
    cb[:, 0, :] = np.eye(128)
    cb[:, 1, :] = 1.0
    cb[:, 2, :] = np.eye(128)[::-1]
    cb[:, 3, :] = -BIG * np.eye(128)
    j = np.arange(128)[:, None]
    i = np.arange(128)[None, :]
    same = (j // 64) == (i // 64)
    cb[:, 4, :] = 1.0 - (same & (i > j))
    cb[:, 5, :] = 1.0 - (same & (i >= j))
    cb[:, 6, :] = 1.0 - (same & (i < j))
    cb[:, 7, :] = 1.0 - (same & (i <= j))
    cc = np.arange(128)
    ang = 2 * np.pi * ((cc[:, None] * cc[None, :]) % 128) / 128.0
    cb[:, 8, :] = np.cos(ang)
    cb[:, 9, :] = -np.sin(ang)
    tin = np.arange(T)
    tg_in = tin if c == 0 else (T - 1 - tin)
    n = np.arange(T)
    tl = np.where(n < OWN, n, 3071 - n)
    tg_out = tl if c == 0 else (T - 1 - tl)
    prod = (tg_in[:, None].astype(np.int64) * tg_out[None, :].astype(np.int64)) % T
    ang = 2 * np.pi * prod / T
    norm = 1.0 / np.sqrt(T * 128.0)
    dtc = (np.cos(ang) * norm).astype(np.float32)
    dts = (np.sin(ang) * norm).astype(np.float32)
    dft = np.stack([dtc, dts], 0)
    dft = dft.reshape(2, 16, 128, 8, 256).transpose(3, 2, 1, 0, 4)
    selb = np.ascontiguousarray(cf[:, 1:17, :]).reshape(128, -1).astype(NPBF)
    return (cf.reshape(128, -1), cb.reshape(128, -1).astype(NPBF), np.ascontiguousarray(dft).astype(NPBF), selb)


def _klayout(w):
    K, N = w.shape
    return np.ascontiguousarray(w.reshape(K // 128, 128, N).transpose(1, 0, 2))


def prep_A(inp, b, c, consts):
    flip = c == 1
    x = inp["x"][b]
    x = x[::-1] if flip else x
    w_in = inp["w_in"][0]
    hs = [4 * c + j for j in range(4)]
    cols = []
    for j in hs:
        cols += [np.arange(1024 + j * 128, 1024 + (j + 1) * 128), np.arange(2048 + j * 128, 2048 + (j + 1) * 128),
                 np.arange(j * 128, (j + 1) * 128)]
    for j in hs:
        cols.append(np.arange(4096 + j * 128, 4096 + (j + 1) * 128))
    cols = np.concatenate(cols)
    wmix = _klayout(w_in[:, cols])
    zc = np.concatenate([np.arange(3072 + j * 128, 3072 + (j + 1) * 128) for j in hs])
    wz = _klayout(w_in[:, zc])
    kf, kb = (1, 0) if flip else (0, 1)
    wsc = np.zeros((D, 128), np.float32)
    for jj, j in enumerate(hs):
        wsc[:, 0 + jj] = w_in[:, 5120 + (2 + kf) * 8 + j]
        wsc[:, 32 + jj] = w_in[:, 5120 + (2 + kb) * 8 + j]
        wsc[:, 64 + jj] = w_in[:, 5120 + kf * 8 + j]
        wsc[:, 96 + jj] = w_in[:, 5120 + kb * 8 + j]
    wsc = _klayout(wsc)
    conv = inp["conv_w"][0]
    conv = conv[::-1] if flip else conv
    cw = np.zeros((128, 12, 5), np.float32)
    for jj, j in enumerate(hs):
        for kind, base in ((0, 1024), (1, 2048), (2, 0)):
            cw[:, 3 * jj + kind, :] = conv[:, base + j * 128: base + (j + 1) * 128].T
    alog = (inp["a_log_fwd"][0], inp["a_log_bwd"][0])
    dtb = (inp["dt_bias_fwd"][0], inp["dt_bias_bwd"][0])
    pc = np.zeros((128, 8), np.float32)
    for jj, j in enumerate(hs):
        pc[0 + jj, 0] = dtb[kf][j]
        pc[32 + jj, 0] = dtb[kb][j]
        pc[0 + jj, 2] = alog[kf][j]
        pc[32 + jj, 2] = alog[kb][j]
    pc[0:64, 1] = 1.0
    pc[64:128, 1] = -1.0
    cf, cb, dft, selb = consts[c]
    return {
        "selb": selb,
        "x": np.ascontiguousarray(x), "gmix": np.ascontiguousarray(np.tile(inp["g_mix"][0][None], (128, 1))),
        "wmix": wmix, "wz": wz, "wscal": wsc, "cw": cw.reshape(128, 60), "pc": pc,
        "ghead": np.ascontiguousarray(np.tile(inp["g_dn_head"][0][None], (128, 1))),
        "cf": cf, "cb": cb, "dft": dft,
    }


def build_B(nrcv=4):
    nc = bass.Bass("TRN2", target_bir_lowering=False)
    di = lambda name, shape, dt: nc.dram_tensor(name, shape, dt, kind="ExternalInput").ap()
    do = lambda name, shape, dt: nc.dram_tensor(name, shape, dt, kind="ExternalOutput").ap()
    x_d = di("xo", [OWN, D], F32)
    ogfr = [(di("og_own", [4, 128, OWN], BF16), 4), (di("og_rcv", [nrcv, 128, OWN], BF16), nrcv),
            (di("fr_own", [4, 128, OWN], BF16), 4), (di("fr_rcv", [nrcv, 128, OWN], BF16), nrcv)]
    w = declare_B_weights(nc, nrcv)
    out_d = do("out", [OWN, D], F32)
    build_B_body(nc, x_d, w["g3"], w["idb"], ogfr, nrcv, w["wst"], w["wo"], w["wup"], w["wdn"], out_d)
    return nc


def declare_B_weights(nc, nrcv):
    di = lambda name, shape, dt: nc.dram_tensor(name, shape, dt, kind="ExternalInput").ap()
    NW = 8 + 2 * nrcv + 32
    return dict(g3=di("g3", [3, 128, D], F32), idb=di("idb", [128, 128], BF16),
                wst=di("wst", [16, 128, NW, 128], F32), wo=di("wo", [4, 128, 16, 512], F32),
                wup=di("wup", [64, 128, 16, 128], F32), wdn=di("wdn", [4, 4, 128, 16, 512], F32))


def build_B_body(nc, x_d, g3_d, idb_d, ogfr_d, nrcv, wst_d, wo_d, wup_d, wdn_d, out_d, cc=None):
    NK = 8 + 2 * nrcv
    NW = NK + 32
    NT8 = OWN // 128
    with ExitStack() as stP:
        C = Ctx()
        C.nc = nc
        idb = _sb(nc, stP, "b_idb", [128, 128], BF16)
        C.identb = idb
        C.epsb = _sb(nc, stP, "b_epsb", [128, 1], F32)
        C.mhalf = _sb(nc, stP, "b_mhalf", [128, 1], F32)
        C.st = _sb(nc, stP, "b_st", [128, 8, 4], F32)
        C.junk = _sb(nc, stP, "b_junk", [128, D], BF16)
        hT = _sb(nc, stP, "b_hT", [128, 16, OWN], BF16)
        mU = _sb(nc, stP, "b_mU", [128, 16, OWN], BF16)
        gbuf = _sb(nc, stP, "b_gbuf", [128, D], F32)
        hb = _sb(nc, stP, "b_hb", [128, D], BF16)
        hkeys = lambda blk: [f"hT{t}{s}" for t in range(4 * blk, 4 * blk + 4) for s in "ab"]
        with ExitStack() as s:
            P = Prog(nc, s, "b1")
            xt = [_sb(nc, s, f"b_xt{i}", [128, D], F32) for i in range(2)]
            ogfr = _sb(nc, s, "b_ogfr", [128, NK, OWN], BF16)
            wsb = [_sb(nc, s, f"b_wsb{i}", [128, NW, 128], BF16) for i in range(2)]
            sg = [_sb(nc, s, f"b_sg{i}", [128, 512], F32) for i in range(2)]
            tt_ = [_sb(nc, s, f"b_t{i}", [128, 512], F32) for i in range(2)]
            C.tr = None
            pb = [_ps(nc, s, f"b_pc{i}", [128, 512], F32) for i in range(6)]
            C.tr = [_ps(nc, s, f"b_tr{i}", [128, 8, 128], BF16) for i in range(2)]
            if cc is not None:
                P.cc_allgather(*cc)
            P.dma_op("sp", "c0", idb[:], idb_d, writes=["const"])
            P.dma_op("sp", "c1", gbuf[:], g3_d[0], writes=["gbuf"])
            P.op("pool", lambda e: e.memset(C.epsb[:], EPS), writes=["epsb"])
            P.op("pool", lambda e: e.memset(C.mhalf[:], -0.5), writes=["mhalf"])

            def load_ws(oc):
                P.dma_op("pool", f"ws{oc%2}", wsb[oc % 2][:], wst_d[oc], writes=[f"wsb{oc%2}"])
            load_ws(0)
            for t in range(NT8):
                xb = xt[t % 2]
                P.dma_op("sp", f"x{t%2}", xb[:], x_d[t * 128:(t + 1) * 128, :], writes=[f"xt{t%2}"])
                rms_tile(P, C, xb[:], f"xt{t%2}", gbuf[:], "gbuf", hb[:], "hb", t)
                transpose_tile16(P, C, hb, "hb",
                                 lambda half, t=t: hT[:, half * 8:(half + 1) * 8, t * 128:(t + 1) * 128],
                                 [f"hT{t}a", f"hT{t}b"], "tr")
            o0 = 0
            for (src, n) in ogfr_d:
                P.dma_op("sp", f"ogfr{o0}", ogfr[:, o0:o0 + n, :], src.rearrange("g p n -> p g n"), reads=["rcv"],
                         writes=["ogfr"])
                o0 += n
            nog = 4 + nrcv
            for oc in range(16):
                if oc + 1 < 16:
                    load_ws(oc + 1)
                w = wsb[oc % 2]
                wk = f"wsb{oc%2}"
                for blk in range(2):
                    bs = slice(blk * 512, (blk + 1) * 512)
                    Ya, Yf = pb[0], pb[1]
                    Ga, Gf = pb[2 + 2 * blk], pb[3 + 2 * blk]
                    gak, gfk = f"pc{2+2*blk}", f"pc{3+2*blk}"
                    for kc in range(16):
                        P.op("pe", lambda e, kc=kc, w=w, Ga=Ga, bs=bs: e.matmul(out=Ga[:], lhsT=w[:, NK + kc, :], rhs=hT[:, kc, bs],
                                                                             start=(kc == 0), stop=(kc == 15)),
                             reads=[wk] + hkeys(blk), banks=[gak])
                    P.op("act", lambda e, Ga=Ga: e.activation(out=sg[0][:], in_=Ga[:], func=AF.Sigmoid),
                         writes=["sg0"], banks=[gak])
                    for kc in range(16):
                        P.op("pe", lambda e, kc=kc, w=w, Gf=Gf, bs=bs: e.matmul(out=Gf[:], lhsT=w[:, NK + 16 + kc, :], rhs=hT[:, kc, bs],
                                                                             start=(kc == 0), stop=(kc == 15)),
                             reads=[wk] + hkeys(blk), banks=[gfk])
                    P.op("act", lambda e, Gf=Gf: e.activation(out=sg[1][:], in_=Gf[:], func=AF.Sigmoid),
                         writes=["sg1"], banks=[gfk])
                    for kc in range(nog):
                        P.op("pe", lambda e, kc=kc, w=w, bs=bs: e.matmul(out=Ya[:], lhsT=w[:, kc, :], rhs=ogfr[:, kc, bs],
                                                                      start=(kc == 0), stop=(kc == nog - 1)),
                             reads=[wk, "ogfr"], banks=["pc0"])
                    P.op("dve", lambda e: e.tensor_tensor(out=tt_[0][:], in0=Ya[:], in1=sg[0][:], op=ALU.mult),
                         reads=["sg0"], writes=["t0"], banks=["pc0"])
                    for kc in range(nog):
                        P.op("pe", lambda e, kc=kc, w=w, bs=bs: e.matmul(out=Yf[:], lhsT=w[:, nog + kc, :], rhs=ogfr[:, nog + kc, bs],
                                                                      start=(kc == 0), stop=(kc == nog - 1)),
                             reads=[wk, "ogfr"], banks=["pc1"])
                    P.op("dve", lambda e: e.tensor_tensor(out=tt_[1][:], in0=Yf[:], in1=sg[1][:], op=ALU.mult),
                         reads=["sg1"], writes=["t1"], banks=["pc1"])
                    P.op("pool", lambda e, oc=oc, bs=bs: e.tensor_tensor(out=mU[:, oc, bs], in0=tt_[0][:], in1=tt_[1][:], op=ALU.add),
                         reads=["t0", "t1"], writes=[f"mU{oc}"])
            P.barrier()
            P.emit()
        with ExitStack() as s:
            x1 = _sb(nc, s, "b_x1", [128, NT8, D], F32)
            mkeys = [f"mU{oc}" for oc in range(16)]
            with ExitStack() as s2:
                P = Prog(nc, s2, "b2")
                wob = [_sb(nc, s2, f"b_wob{i}", [128, 16, 512], BF16) for i in range(2)]
                hbs = [hb] + [_sb(nc, s2, f"b_hb{i}", [128, D], BF16) for i in range(2)]
                pb = [_ps(nc, s2, f"b_pd{i}", [128, 512], F32) for i in range(4)]
                C.tr = [_ps(nc, s2, f"b_tr2{i}", [128, 8, 128], BF16) for i in range(2)]
                P.dma_op("sp", "gl", gbuf[:], g3_d[1], writes=["gbuf"])
                for t in range(NT8):
                    P.dma_op("sp", f"x1_{t%4}", x1[:, t, :], x_d[t * 128:(t + 1) * 128, :], writes=[f"x1_{t}_{c}" for c in range(4)])
                P.dma_op("pool", "wo0", wob[0][:], wo_d[0], writes=["wob0"])
                n = 0
                for cb in range(4):
                    if cb + 1 < 4:
                        P.dma_op("pool", f"wo{(cb+1)%2}", wob[(cb + 1) % 2][:], wo_d[cb + 1], writes=[f"wob{(cb+1)%2}"])
                    w = wob[cb % 2]
                    for t in range(NT8):
                        bk = pb[n % 4]
                        for kc in range(16):
                            P.op("pe", lambda e, kc=kc, t=t, w=w, bk=bk: e.matmul(out=bk[:], lhsT=mU[:, kc, t * 128:(t + 1) * 128],
                                                                               rhs=w[:, kc, :], start=(kc == 0), stop=(kc == 15)),
                                 reads=[f"wob{cb%2}"], banks=[f"pd{n%4}"])
                        P.op("dve", lambda e, t=t, cb=cb, bk=bk: e.tensor_tensor(out=x1[:, t, cb * 512:(cb + 1) * 512],
                                                                                  in0=bk[:], in1=x1[:, t, cb * 512:(cb + 1) * 512], op=ALU.add),
                             reads=[f"x1_{t}_{cb}"], writes=[f"x1_{t}_{cb}"], banks=[f"pd{n%4}"])
                        n += 1
                        if cb == 3:
                            rms_tile(P, C, x1[:, t, :], [f"x1_{t}_{c}" for c in range(4)], gbuf[:], "gbuf",
                                     hbs[t % 3][:], f"hbs{t%3}", t)
                            if t >= 2:
                                tq = t - 2
                                transpose_tile16(P, C, hbs[tq % 3], f"hbs{tq%3}",
                                                 lambda half, tq=tq: hT[:, half * 8:(half + 1) * 8, tq * 128:(tq + 1) * 128],
                                                 [f"hT{tq}a", f"hT{tq}b"], "tr")
                for tq in (NT8 - 2, NT8 - 1):
                    transpose_tile16(P, C, hbs[tq % 3], f"hbs{tq%3}",
                                     lambda half, tq=tq: hT[:, half * 8:(half + 1) * 8, tq * 128:(tq + 1) * 128],
                                     [f"hT{tq}a", f"hT{tq}b"], "tr")
                P.barrier()
                P.emit()
            with ExitStack() as s2:
                P = Prog(nc, s2, "b3")
                wub = [_sb(nc, s2, f"b_wub{i}", [128, 16, 128], BF16) for i in range(2)]
                wdb = [_sb(nc, s2, f"b_wdb{i}", [128, 16, 512], BF16) for i in range(2)]
                rl = [_sb(nc, s2, f"b_rl{i}", [128, 512], F32) for i in range(2)]
                ob = [_sb(nc, s2, f"b_ob{i}", [128, D], F32) for i in range(2)]
                P.dma_op("sp", "gl", gbuf[:], g3_d[2], writes=["gbuf"])
                pu = [_ps(nc, s2, f"b_pu{i}", [128, 512], F32) for i in range(4)]
                pd = [_ps(nc, s2, f"b_pdn{i}", [128, 512], F32) for i in range(4)]
                nu = 0
                nd = 0
                nwd = 0

                def load_up(i):
                    P.dma_op("pool", f"wu{i%2}", wub[i % 2][:], wup_d[i], writes=[f"wub{i%2}"])

                def load_dn(i):
                    P.dma_op("pool", f"wd{i%2}", wdb[i % 2][:], wdn_d[i // 4, i % 4], writes=[f"wdb{i%2}"])
                load_up(0)
                load_dn(0)
                for q in range(4):
                    for j in range(16):
                        fc = q * 16 + j
                        if fc + 1 < 64:
                            load_up(fc + 1)
                        w = wub[fc % 2]
                        for blk in range(2):
                            bs = slice(blk * 512, (blk + 1) * 512)
                            bk = pu[nu % 4]
                            for kc in range(16):
                                P.op("pe", lambda e, kc=kc, w=w, bk=bk, bs=bs: e.matmul(out=bk[:], lhsT=w[:, kc, :], rhs=hT[:, kc, bs],
                                                                                     start=(kc == 0), stop=(kc == 15)),
                                     reads=[f"wub{fc%2}"], banks=[f"pu{nu%4}"])
                            r = rl[nu % 2]
                            P.op("act", lambda e, bk=bk, r=r: e.activation(out=r[:], in_=bk[:], func=AF.Relu),
                                 writes=[f"rl{nu%2}"], banks=[f"pu{nu%4}"])
                            eng = "dve" if nu % 2 == 0 else "pool"
                            P.op(eng, lambda e, r=r, j=j, bs=bs: e.tensor_tensor(out=mU[:, j, bs], in0=r[:], in1=r[:], op=ALU.mult),
                                 reads=[f"rl{nu%2}"], writes=[f"up{j}"])
                            nu += 1
                    upk = [f"up{j}" for j in range(16)]
                    for cb in range(4):
                        if nwd + 1 < 16:
                            load_dn(nwd + 1)
                        w = wdb[nwd % 2]
                        for t in range(NT8):
                            bk = pd[nd % 4]
                            for kc in range(16):
                                P.op("pe", lambda e, kc=kc, t=t, w=w, bk=bk: e.matmul(out=bk[:], lhsT=mU[:, kc, t * 128:(t + 1) * 128],
                                                                                   rhs=w[:, kc, :], start=(kc == 0), stop=(kc == 15)),
                                     reads=[f"wdb{nwd%2}"] + upk, banks=[f"pdn{nd%4}"])
                            P.op("dve", lambda e, t=t, cb=cb, bk=bk: e.tensor_tensor(out=x1[:, t, cb * 512:(cb + 1) * 512],
                                                                                      in0=bk[:], in1=x1[:, t, cb * 512:(cb + 1) * 512], op=ALU.add),
                                 reads=[f"x1_{t}_{cb}"], writes=[f"x1_{t}_{cb}"], banks=[f"pdn{nd%4}"])
                            nd += 1
                            if q == 3 and cb == 3:
                                rms_tile(P, C, x1[:, t, :], [f"x1_{t}_{c}" for c in range(4)], gbuf[:], "gbuf",
                                         ob[t % 2][:], f"ob{t%2}", t)
                                P.dma_op("sp", f"o{t%2}", out_d[t * 128:(t + 1) * 128, :], ob[t % 2][:], reads=[f"ob{t%2}"])
                        nwd += 1
                P.barrier()
                P.emit()


def prep_B(inp, b, c, rcv, idb):
    flip = c == 1
    x = inp["x"][b]
    x = x[::-1] if flip else x
    own = [4 * c + j for j in range(4)]
    oth = [4 * (1 - c) + j for j in range(4)]
    w_in = inp["w_in"][0]
    wdu = inp["w_dn_up"][0]
    wfo = inp["w_fourier"][0]
    rows = np.concatenate([np.arange(h * 128, (h + 1) * 128) for h in own + oth])
    ga = w_in[:, 5152:5152 + 2048]
    gf = w_in[:, 5152 + 2048:5152 + 4096]
    wst = np.concatenate([wdu[rows], wfo[rows], ga, gf], axis=0)
    nk = wst.shape[0] // 128
    wst = wst.reshape(nk, 128, 16, 128).transpose(2, 1, 0, 3)
    wo = inp["w_o"][0].reshape(16, 128, 4, 512).transpose(2, 1, 0, 3)
    wup = inp["w_mlp_up"][0].reshape(16, 128, 64, 128).transpose(2, 1, 0, 3)
    wdn = inp["w_mlp_down"][0].reshape(4, 16, 128, 4, 512).transpose(0, 3, 2, 1, 4)
    g3 = np.stack([np.tile(inp[k].reshape(-1)[None], (128, 1)) for k in ("g_mix", "g_mlp", "g_final")], 0)
    d = {
        "xo": np.ascontiguousarray(x[:OWN]), "g3": np.ascontiguousarray(g3), "idb": idb,
        "wst": np.ascontiguousarray(wst), "wo": np.ascontiguousarray(wo), "wup": np.ascontiguousarray(wup),
        "wdn": np.ascontiguousarray(wdn),
    }
    d.update(rcv)
    return d


def build_F():
    nc = bass.Bass("TRN2", target_bir_lowering=False)
    ins = declare_A_inputs(nc)
    w = declare_B_weights(nc, 8)
    out_d = nc.dram_tensor("out", [OWN, D], F32, kind="ExternalOutput").ap()
    own_og = nc.dram_tensor("i_own_og", [4, 128, OWN], BF16, kind="Internal").ap()
    own_fr = nc.dram_tensor("i_own_fr", [4, 128, OWN], BF16, kind="Internal").ap()
    snd = nc.dram_tensor("i_snd", [8 * 128, OWN], BF16, kind="Internal").ap()
    rcv = nc.dram_tensor("i_rcv", [2 * 8 * 128, OWN], BF16, kind="Internal").ap()
    snd3 = snd.rearrange("(g p) n -> g p n", p=128)
    rcv4 = rcv.rearrange("(r g p) n -> r g p n", r=2, p=128)
    emit_A(nc, ins, (own_og, snd3[0:4], own_fr, snd3[4:8]), {}, False, 9)
    ogfr = [(own_og, 4), (rcv4[0, 0:4], 4), (rcv4[1, 0:4], 4), (own_fr, 4), (rcv4[0, 4:8], 4), (rcv4[1, 4:8], 4)]
    build_B_body(nc, ins["x"][0:OWN, :], w["g3"], w["idb"], ogfr, 8, w["wst"], w["wo"], w["wup"], w["wdn"], out_d,
                 cc=(snd, rcv, [[0, 1], [2, 3], [4, 5], [6, 7]]))
    return nc


def prep_F(inp, b, c, consts, idb, shared):
    d = prep_A(inp, b, c, consts)
    wdu = inp["w_dn_up"][0]
    wfo = inp["w_fourier"][0]
    w_in = inp["w_in"][0]
    own = np.concatenate([np.arange(h * 128, (h + 1) * 128) for h in range(4 * c, 4 * c + 4)])
    z512 = np.zeros((512, D), np.float32)
    parts = []
    for wmat in (wdu, wfo):
        parts.append(wmat[own])
        parts.append(z512 if c == 0 else wmat[0:512])
        parts.append(z512 if c == 1 else wmat[512:1024])
    parts.append(w_in[:, 5152:5152 + 2048])
    parts.append(w_in[:, 5152 + 2048:5152 + 4096])
    wst = np.concatenate(parts, axis=0)
    nk = wst.shape[0] // 128
    wst = wst.reshape(nk, 128, 16, 128).transpose(2, 1, 0, 3)
    d["wst"] = np.ascontiguousarray(wst)
    d["idb"] = idb
    d.update(shared)
    return d


def _shared_B(inp):
    wo = inp["w_o"][0].reshape(16, 128, 4, 512).transpose(2, 1, 0, 3)
    wup = inp["w_mlp_up"][0].reshape(16, 128, 64, 128).transpose(2, 1, 0, 3)
    wdn = inp["w_mlp_down"][0].reshape(4, 16, 128, 4, 512).transpose(0, 3, 2, 1, 4)
    g3 = np.stack([np.tile(inp[k].reshape(-1)[None], (128, 1)) for k in ("g_mix", "g_mlp", "g_final")], 0)
    return {"wo": np.ascontiguousarray(wo), "wup": np.ascontiguousarray(wup), "wdn": np.ascontiguousarray(wdn),
            "g3": np.ascontiguousarray(g3)}


_CACHE = {}


def kernel(**inputs):
    inp = {k: np.asarray(v) for k, v in inputs.items()}
    if "F" not in _CACHE:
        _CACHE["F"] = build_F()
        _CACHE["consts"] = [_consts(0), _consts(1)]
    consts = _CACHE["consts"]
    idb = np.eye(128, dtype=np.float32).astype(NPBF)
    shared = _shared_B(inp)
    maps = [prep_F(inp, i // 2, i % 2, consts, idb, shared) for i in range(8)]
    res = run_bass_kernel_spmd(_CACHE["F"], maps, core_ids=list(range(8))).results
    out = np.zeros((4, T, D), np.float32)
    for i in range(8):
        b, c = i // 2, i % 2
        o = np.asarray(res[i]["out"])
        if c == 0:
            out[b, :OWN] = o
        else:
            out[b, OWN:] = o[::-1]
    return out
```

```python
from contextlib import ExitStack
import numpy as np
import ml_dtypes
import concourse.bass as bass
import concourse.mybir as mybir
from concourse.bass_utils import run_bass_kernel_spmd

F32 = mybir.dt.float32
BF16 = mybir.dt.bfloat16
AF = mybir.ActivationFunctionType
ALU = mybir.AluOpType
AX = mybir.AxisListType
NPBF = ml_dtypes.bfloat16

D = 2048
T = 2048
OWN = 1024
NTILE = 16
H = 4
EPS = 1e-6
DFF = 8192
BIG = 30000.0
SELF_SYNC = True


class Prog:
    ENGS = ("pe", "act", "dve", "pool", "sp")

    def __init__(self, nc, stack, tag=""):
        self.nc = nc
        self.stack = stack
        self.tag = tag
        self.streams = {e: [] for e in self.ENGS}
        self.count = {e: 0 for e in self.ENGS}
        self.sems = {e: stack.enter_context(nc.semaphore(f"pg{tag}_{e}")) for e in self.ENGS}
        self.waited = {e: {} for e in self.ENGS}
        self.last_w = {}
        self.readers = {}
        self.bank_last = {}
        self.dma = {}

    def _deps(self, eng, reads, writes, banks):
        deps = []
        for r in reads:
            lw = self.last_w.get(r)
            if lw is not None:
                deps.append(lw)
        for w in writes:
            lw = self.last_w.get(w)
            if lw is not None:
                deps.append(lw)
            deps.extend(self.readers.get(w, ()))
        need = {}
        for (k, v) in deps:
            if k == eng and (eng == "pe" or not SELF_SYNC):
                continue
            if need.get(k, 0) < v:
                need[k] = v
        for bk in banks:
            for k, v in self.bank_last.get(bk, {}).items():
                if k != eng and need.get(k, 0) < v:
                    need[k] = v
        for k, v in need.items():
            if self.waited[eng].get(k, 0) >= v:
                continue
            self.waited[eng][k] = v
            sem = self.sems[k] if k in self.sems else self.dma[k][0]
            self.streams[eng].append(("wait", sem, v))

    def _commit(self, ev, reads, writes, banks):
        for r in reads:
            self.readers.setdefault(r, []).append(ev)
        for w in writes:
            self.last_w[w] = ev
            self.readers[w] = []
        for bk in banks:
            self.bank_last.setdefault(bk, {})[ev[0]] = ev[1]

    def op(self, eng, fn, reads=(), writes=(), banks=()):
        self._deps(eng, reads, writes, banks)
        self.count[eng] += 1
        ev = (eng, self.count[eng])
        self.streams[eng].append(("op", fn))
        self._commit(ev, reads, writes, banks)
        return ev

    def dma_op(self, queue, key, out, in_, reads=(), writes=()):
        if key not in self.dma:
            self.dma[key] = [self.stack.enter_context(self.nc.semaphore(f"dm{self.tag}_{key}")), 0]
        ent = self.dma[key]
        self._deps(queue, reads, writes, ())
        if ent[1] and self.waited[queue].get(key, 0) < 16 * ent[1]:
            self.waited[queue][key] = 16 * ent[1]
            self.streams[queue].append(("wait", ent[0], 16 * ent[1]))
        ent[1] += 1
        ev = (key, 16 * ent[1])
        self.streams[queue].append(("dma", out, in_, ent[0]))
        self._commit(ev, reads, writes, ())
        return ev

    def cc_allgather(self, src, dst, groups):
        key = "cc"
        self.dma[key] = [self.stack.enter_context(self.nc.semaphore(f"cc{self.tag}")), 0]
        self.streams["pool"].append(("cc", src, dst, groups, self.dma[key][0]))
        self.dma[key][1] = 1.0 / 16.0
        self.last_w["rcv"] = (key, 1)
        self.readers["rcv"] = []

    def barrier(self):
        for e in self.ENGS:
            for k in self.ENGS:
                if k != e and self.count[k] > self.waited[e].get(k, 0):
                    self.waited[e][k] = self.count[k]
                    self.streams[e].append(("wait", self.sems[k], self.count[k]))
            for key, (sem, cnt) in self.dma.items():
                if cnt and self.waited[e].get(key, 0) < 16 * cnt:
                    self.waited[e][key] = int(16 * cnt)
                    self.streams[e].append(("wait", sem, int(16 * cnt)))

    def emit(self):
        nc = self.nc
        handles = {"pe": "tensor", "act": "scalar", "dve": "vector", "pool": "gpsimd", "sp": "sync"}
        with nc.Block() as block:
            for e in self.ENGS:
                stream = self.streams[e]
                sem = self.sems[e]

                def body(eng, stream=stream, sem=sem):
                    for it in stream:
                        if it[0] == "wait":
                            eng.wait_ge(it[1], it[2])
                        elif it[0] == "op":
                            it[1](eng).then_inc(sem, 1)
                        elif it[0] == "cc":
                            eng.collective_compute("AllGather", ALU.bypass, replica_groups=it[3], ins=[it[1]],
                                                   outs=[it[2]]).then_inc(it[4], 1)
                        else:
                            eng.dma_start(out=it[1], in_=it[2]).then_inc(it[3], 16)

                getattr(block, handles[e])(body)


class Ctx:
    pass


def _sb(nc, st, name, shape, dt):
    return st.enter_context(nc.sbuf_tensor("s_" + name, shape, dt))


def _ps(nc, st, name, shape, dt):
    return st.enter_context(nc.psum_tensor("p_" + name, shape, dt))


def rms_tile(P, C, xb, xkey, gbc, gkey, hb, hkey, col):
    xkeys = list(xkey) if isinstance(xkey, (list, tuple)) else [xkey]
    P.op("act", lambda e: e.activation(out=C.junk[:], in_=xb, func=AF.Square, accum_out=C.st[:, col, 0:1]),
         reads=xkeys, writes=["junk", "st"])
    P.op("pool", lambda e: e.tensor_scalar(out=C.st[:, col, 1:2], in0=C.st[:, col, 0:1], scalar1=1.0 / D, scalar2=EPS,
                                           op0=ALU.mult, op1=ALU.add), reads=["st"], writes=["st"])
    P.op("pool", lambda e: e.tensor_tensor(out=C.st[:, col, 2:3], in0=C.st[:, col, 1:2], in1=C.mhalf[:], op=ALU.pow),
         reads=["st", "mhalf"], writes=["st"])
    P.op("dve", lambda e: e.scalar_tensor_tensor(out=hb, in0=xb, scalar=C.st[:, col, 2:3], in1=gbc,
                                                 op0=ALU.mult, op1=ALU.mult),
         reads=xkeys + ["st", gkey], writes=[hkey])


def transpose_tile16(P, C, hb, hkey, dst, dkeys, trk):
    for half in range(2):
        pt = C.tr[half]
        for j in range(8):
            kc = half * 8 + j
            P.op("pe", lambda e, pt=pt, j=j, kc=kc: e.transpose(out=pt[:, j, :], in_=hb[:, kc * 128:(kc + 1) * 128],
                                                               identity=C.identb[:]),
                 reads=[hkey, "const"], banks=[f"{trk}{half}"])
        if half == 0:
            P.op("act", lambda e, pt=pt: e.copy(out=dst(0), in_=pt[:]), writes=[dkeys[0]], banks=[f"{trk}0"])
        else:
            P.op("dve", lambda e, pt=pt: e.tensor_copy(out=dst(1), in_=pt[:]), writes=[dkeys[1]], banks=[f"{trk}1"])


def build_A(dbg=False, upto=9):
    nc = bass.Bass("TRN2", target_bir_lowering=False)
    do = lambda name, shape, dt: nc.dram_tensor(name, shape, dt, kind="ExternalOutput").ap()
    ins = declare_A_inputs(nc)
    outs = (do("own_og", [4, 128, OWN], BF16), do("snd_og", [4, 128, OWN], BF16),
            do("own_fr", [4, 128, OWN], BF16), do("snd_fr", [4, 128, OWN], BF16))
    D_ = {}
    if dbg:
        D_["qkv"] = do("d_qkv", [128, 12 * T], BF16)
        D_["rowsraw"] = do("d_rowsraw", [128, T], F32)
        D_["fT"] = do("d_fT", [128, 4 * T], BF16)
        D_["hT"] = do("d_hT", [128, 16 * T], BF16)
        D_["rows"] = do("d_rows", [128, T], F32)
        D_["cc"] = do("d_cc", [128, 16 * 24], F32)
        D_["glbc"] = do("d_glbc", [128, 256], F32)
        D_["szt"] = do("d_szt", [128, 16 * 512], BF16)
    emit_A(nc, ins, outs, D_, dbg, upto)
    return nc


def declare_A_inputs(nc):
    di = lambda name, shape, dt: nc.dram_tensor(name, shape, dt, kind="ExternalInput").ap()
    return dict(
        x=di("x", [T, D], F32), gmix=di("gmix", [128, D], F32), wmix=di("wmix", [128, 16, 16 * 128], F32),
        wz=di("wz", [128, 16, 512], F32), wscal=di("wscal", [128, 16, 128], F32), cw=di("cw", [128, 60], F32),
        pc=di("pc", [128, 8], F32), ghead=di("ghead", [128, 128], F32), cf=di("cf", [128, 17 * 128], F32),
        cb=di("cb", [128, 10 * 128], BF16), selb=di("selb", [128, 16 * 128], BF16), dft=di("dft", [4, 128, 16, 2, 512], BF16))


def emit_A(nc, ins, outs, D_, dbg=False, upto=9):
    x_d, gmix_d, wmix_d, wz_d, wscal_d, cw_d, pc_d, ghead_d, cf_d, cb_d, dft_d = (
        ins[k] for k in ("x", "gmix", "wmix", "wz", "wscal", "cw", "pc", "ghead", "cf", "cb", "dft"))
    C_selb_d = ins["selb"]
    own_og, snd_og, own_fr, snd_fr = outs
    with ExitStack() as stA:
        C = Ctx()
        C.nc = nc
        cf = _sb(nc, stA, "cf", [128, 17, 128], F32)
        cb = _sb(nc, stA, "cb", [128, 10, 128], BF16)
        C.identf = cf[:, 0, :]
        C.sel = lambda kind, d, h: cf[:, 1 + kind * 8 + d * 4 + h, :]
        C.identb = cb[:, 0, :]
        C.onesb = cb[:, 1, :]
        C.Jb = cb[:, 2, :]
        C.negI = cb[:, 3, :]
        C.mask = lambda d, incl: cb[:, 4 + d * 2 + incl, :]
        C.CS = cb[:, 8:10, :]
        C.epsb = _sb(nc, stA, "epsb", [128, 1], F32)
        C.mhalf = _sb(nc, stA, "mhalf", [128, 1], F32)
        C.st = _sb(nc, stA, "st", [128, 16, 4], F32)
        pc = _sb(nc, stA, "pc", [128, 8], F32)
        cw = _sb(nc, stA, "cw", [128, 12, 5], F32)
        ghead = _sb(nc, stA, "ghead", [128, 128], F32)

        with ExitStack() as stB:
            QKV = _sb(nc, stB, "QKV", [128, 3, H, T], BF16)
            szt = _sb(nc, stB, "szt", [128, NTILE, 512], BF16)
            rows = _sb(nc, stB, "rows", [128, T], F32)
            with ExitStack() as stF:
                fT = _sb(nc, stF, "fT", [128, H, T], BF16)
                with ExitStack() as stH:
                    hT = _sb(nc, stH, "hT", [128, 16, T], BF16)
                    with ExitStack() as s0:
                        P = Prog(nc, s0, "p0")
                        xt = [_sb(nc, s0, f"xt{i}", [128, D], F32) for i in range(2)]
                        gm = _sb(nc, s0, "gm", [128, D], F32)
                        C.junk = _sb(nc, s0, "junk", [128, D], BF16)
                        hb = [_sb(nc, s0, f"hb{i}", [128, D], BF16) for i in range(3)]
                        C.tr = [_ps(nc, s0, f"tr{i}", [128, 8, 128], BF16) for i in range(2)]
                        P.dma_op("sp", "c0", cf[:], cf_d.rearrange("p (a b) -> p a b", b=128), writes=["const"])
                        P.dma_op("sp", "c1", cb[:], cb_d.rearrange("p (a b) -> p a b", b=128), writes=["const"])
                        P.dma_op("sp", "c2", gm[:], gmix_d, writes=["gm"])
                        P.dma_op("sp", "c3", pc[:], pc_d, writes=["pc"])
                        P.dma_op("sp", "c4", cw[:], cw_d.rearrange("p (a b) -> p a b", b=5), writes=["cw"])
                        P.dma_op("sp", "c5", ghead[:], ghead_d, writes=["ghead"])
                        P.op("pool", lambda e: e.memset(C.epsb[:], EPS), writes=["epsb"])
                        P.op("pool", lambda e: e.memset(C.mhalf[:], -0.5), writes=["mhalf"])

                        def tr0(t):
                            transpose_tile16(P, C, hb[t % 3], f"hb{t%3}",
                                             lambda half, t=t: hT[:, half * 8:(half + 1) * 8, t * 128:(t + 1) * 128],
                                             [f"hT{t}a", f"hT{t}b"], "tr")
                        for t in range(NTILE):
                            xb = xt[t % 2]
                            P.dma_op("sp", f"x{t%2}", xb[:], x_d[t * 128:(t + 1) * 128, :], writes=[f"xt{t%2}"])
                            rms_tile(P, C, xb[:], f"xt{t%2}", gm[:], "gm", hb[t % 3][:], f"hb{t%3}", t)
                            if t >= 1:
                                tr0(t - 1)
                        tr0(NTILE - 1)
                        if dbg:
                            P.barrier()
                            P.dma_op("sp", "dbg0", D_["hT"], hT[:].rearrange("p a b -> p (a b)"))
                        P.barrier()
                        P.emit()
                    if upto < 1:
                        return
                    with ExitStack() as s1:
                        P = Prog(nc, s1, "p1")
                        wb = [_sb(nc, s1, f"wb{i}", [128, 16, 128], BF16) for i in range(2)]
                        wsc = wb[1]
                        pre = _sb(nc, s1, "pre", [128, T + 4], F32)
                        accs = [_sb(nc, s1, f"acc{i}", [128, T], F32) for i in range(2)]
                        sd = _sb(nc, s1, "sd", [128, T], F32)
                        pacc = [_ps(nc, s1, f"pacc{i}", [128, 512], F32) for i in range(4)]
                        pss = [_ps(nc, s1, f"pss{i}", [128, 512], F32) for i in range(4)]
                        hkeys = lambda blk: [f"hT{t}{s}" for t in range(4 * blk, 4 * blk + 4) for s in "ab"]
                        allh = [f"hT{t}{s}" for t in range(16) for s in "ab"]
                        P.op("pool", lambda e: e.memset(pre[:, 0:2], 0.0), writes=["pre"])
                        P.op("pool", lambda e: e.memset(pre[:, T + 2:T + 4], 0.0), writes=["pre"])
                        P.dma_op("pool", "wb1", wsc[:], wscal_d, writes=["wb1"])

                        def load_w(ci):
                            P.dma_op("pool", f"wb{ci%2}", wb[ci % 2][:], wmix_d[:, :, ci * 128:(ci + 1) * 128],
                                     writes=[f"wb{ci%2}"])
                        load_w(0)
                        for blk in range(4):
                            for kc in range(16):
                                P.op("pe", lambda e, blk=blk, kc=kc: e.matmul(
                                    out=pacc[blk][:], lhsT=wsc[:, kc, :], rhs=hT[:, kc, blk * 512:(blk + 1) * 512],
                                    start=(kc == 0), stop=(kc == 15)), reads=["wb1"] + hkeys(blk), banks=[f"pacc{blk}"])
                            P.op("act", lambda e, blk=blk: e.copy(out=rows[:, blk * 512:(blk + 1) * 512], in_=pacc[blk][:]),
                                 writes=["rows"], banks=[f"pacc{blk}"])
                        pending = []
                        for ci in range(16):
                            if ci + 1 < 16:
                                load_w(ci + 1)
                            w = wb[ci % 2]
                            isf = ci >= 12
                            acc = accs[ci % 2]
                            ak = f"acc{ci % 2}"
                            j = (ci - 12) if isf else ci // 3
                            kind = None if isf else ci % 3
                            for blk in range(4):
                                for kc in range(16):
                                    P.op("pe", lambda e, blk=blk, kc=kc, w=w: e.matmul(
                                        out=pacc[blk][:], lhsT=w[:, kc, :], rhs=hT[:, kc, blk * 512:(blk + 1) * 512],
                                        start=(kc == 0), stop=(kc == 15)),
                                        reads=[f"wb{ci%2}"] + hkeys(blk), banks=[f"pacc{blk}"])
                                if isf:
                                    P.op("dve", lambda e, blk=blk, j=j: e.tensor_copy(
                                        out=fT[:, j, blk * 512:(blk + 1) * 512], in_=pacc[blk][:]),
                                        writes=[f"fT{j}"], banks=[f"pacc{blk}"])
                                else:
                                    P.op("act", lambda e, blk=blk: e.copy(out=pre[:, 2 + blk * 512:2 + (blk + 1) * 512],
                                                                          in_=pacc[blk][:]),
                                         writes=["pre"], banks=[f"pacc{blk}"])
                            prev_b = pending
                            pending = []
                            if isf:
                                for fn in prev_b:
                                    fn()
                                continue
                            P.op("dve", lambda e, ci=ci, acc=acc: e.tensor_scalar(out=acc[:], in0=pre[:, 0:T], scalar1=cw[:, ci, 0:1],
                                                                        scalar2=None, op0=ALU.mult),
                                 reads=["pre", "cw"], writes=[ak])
                            for tap in range(1, 5):
                                P.op("dve", lambda e, ci=ci, tap=tap, acc=acc: e.scalar_tensor_tensor(
                                    out=acc[:], in0=pre[:, tap:tap + T], scalar=cw[:, ci, tap:tap + 1], in1=acc[:],
                                    op0=ALU.mult, op1=ALU.add), reads=["pre", "cw", ak], writes=[ak])
                            if kind == 1:
                                P.op("act", lambda e, j=j, acc=acc: e.activation(out=QKV[:, 1, j, :], in_=acc[:], func=AF.Silu),
                                     reads=[ak], writes=[f"qkv1_{j}"])
                                for fn in prev_b:
                                    fn()
                                continue
                            P.op("act", lambda e, acc=acc: e.activation(out=acc[:], in_=acc[:], func=AF.Silu),
                                 reads=[ak], writes=[ak])
                            sqv = QKV[:, kind, j, :]
                            sqk = f"qkv{kind}_{j}"
                            P.op("act", lambda e, acc=acc, sqv=sqv: e.activation(out=sqv, in_=acc[:], func=AF.Square),
                                 reads=[ak], writes=[sqk])
                            def part_b(acc=acc, ak=ak, sqv=sqv, sqk=sqk, kind=kind, j=j):
                                for blk in range(4):
                                    pb = pss[blk]
                                    P.op("pe", lambda e, blk=blk, pb=pb, sqv=sqv: e.matmul(out=pb[:], lhsT=C.onesb,
                                                                                 rhs=sqv[:, blk * 512:(blk + 1) * 512],
                                                                                 start=True, stop=True),
                                         reads=[sqk, "const"], banks=[f"pss{blk}"])
                                    P.op("act", lambda e, blk=blk, pb=pb: e.activation(
                                        out=sd[:, blk * 512:(blk + 1) * 512], in_=pb[:], func=AF.Ln, bias=C.epsb[:], scale=1.0),
                                        reads=["epsb"], writes=["sd"], banks=[f"pss{blk}"])
                                P.op("act", lambda e: e.activation(out=sd[:], in_=sd[:], func=AF.Exp, scale=-0.5),
                                     reads=["sd"], writes=["sd"])
                                scl = (128.0 ** -0.5) if kind == 2 else 1.0
                                P.op("dve", lambda e, kind=kind, j=j, scl=scl, acc=acc: e.scalar_tensor_tensor(
                                    out=QKV[:, kind, j, :], in0=acc[:], scalar=scl, in1=sd[:], op0=ALU.mult, op1=ALU.mult),
                                    reads=[ak, "sd"], writes=[f"qkv{kind}_{j}"])
                            pending.append(part_b)
                            for fn in prev_b:
                                fn()
                        assert not pending
                        if dbg:
                            P.barrier()
                            P.dma_op("sp", "dbg1", D_["qkv"], QKV[:].rearrange("p a b c -> p (a b c)"))
                            P.dma_op("sp", "dbg2", D_["rowsraw"], rows[:])
                            P.dma_op("sp", "dbg3", D_["fT"], fT[:].rearrange("p a b -> p (a b)"))
                        P.barrier()
                        P.emit()
                    if upto < 2:
                        return
                    with ExitStack() as s1:
                        P = Prog(nc, s1, "p1z")
                        wz = _sb(nc, s1, "wz", [128, 16, 512], BF16)
                        pacc = [_ps(nc, s1, f"pzacc{i}", [128, 512], F32) for i in range(4)]
                        P.dma_op("pool", "wz", wz[:], wz_d, writes=["wz"])
                        for t in range(NTILE):
                            pb = pacc[t % 4]
                            for kc in range(16):
                                P.op("pe", lambda e, t=t, kc=kc, pb=pb: e.matmul(
                                    out=pb[:], lhsT=hT[:, kc, t * 128:(t + 1) * 128], rhs=wz[:, kc, :],
                                    start=(kc == 0), stop=(kc == 15)), reads=["wz"], banks=[f"pz{t%4}"])
                            P.op("act", lambda e, t=t, pb=pb: e.activation(out=szt[:, t, :], in_=pb[:], func=AF.Silu),
                                 writes=[f"szt{t}"], banks=[f"pz{t%4}"])
                        if dbg:
                            P.barrier()
                            P.dma_op("sp", "dbg4", D_["szt"], szt[:].rearrange("p a b -> p (a b)"))
                        P.barrier()
                        P.emit()
                if upto < 3:
                    return
                with ExitStack() as s2:
                    P = Prog(nc, s2, "pf")
                    Y = _sb(nc, s2, "Y", [128, NTILE, H, 256], BF16)
                    DT = [_sb(nc, s2, f"DT{i}", [128, 16, 2, 512], BF16) for i in range(2)]
                    stg = [_sb(nc, s2, f"fstg{i}", [128, H, 512], BF16) for i in range(2)]
                    pf = [_ps(nc, s2, f"pf{i}", [128, 512], F32) for i in range(8)]
                    P.dma_op("sp", "dt0", DT[0][:], dft_d[0], writes=["DT0"])
                    P.dma_op("sp", "dt1", DT[1][:], dft_d[1], writes=["DT1"])
                    n = 0
                    for t in range(NTILE):
                        for gp in range(2):
                            pb = pf[n % 4]
                            for gi in range(2):
                                g = gp * 2 + gi
                                P.op("pe", lambda e, t=t, g=g, gi=gi, pb=pb: e.matmul(
                                    out=pb[:, gi * 256:(gi + 1) * 256], lhsT=fT[:, g, t * 128:(t + 1) * 128],
                                    rhs=C.CS.rearrange("p a b -> p (a b)"), start=True, stop=True),
                                    reads=["const"], banks=[f"pf{n%4}"])
                            if n % 2 == 0:
                                P.op("act", lambda e, t=t, gp=gp, pb=pb: e.copy(
                                    out=Y[:, t, gp * 2:gp * 2 + 2, :].rearrange("p a b -> p (a b)"), in_=pb[:]),
                                    writes=[f"Y{t}_{gp}"], banks=[f"pf{n%4}"])
                            else:
                                P.op("dve", lambda e, t=t, gp=gp, pb=pb: e.tensor_copy(
                                    out=Y[:, t, gp * 2:gp * 2 + 2, :].rearrange("p a b -> p (a b)"), in_=pb[:]),
                                    writes=[f"Y{t}_{gp}"], banks=[f"pf{n%4}"])
                            n += 1
                    yk = [f"Y{t}_{gp}" for t in range(NTILE) for gp in range(2)]
                    for blk in range(4):
                        dtb = DT[blk % 2]
                        sg = stg[blk % 2]
                        for g in range(H):
                            bi = (blk % 2) * 4 + g
                            pb = pf[bi]
                            for t in range(NTILE):
                                for cs in range(2):
                                    P.op("pe", lambda e, t=t, g=g, cs=cs, pb=pb, dtb=dtb: e.matmul(
                                        out=pb[:], lhsT=Y[:, t, g, cs * 128:(cs + 1) * 128], rhs=dtb[:, t, cs, :],
                                        start=(t == 0 and cs == 0), stop=(t == NTILE - 1 and cs == 1)),
                                        reads=yk + [f"DT{blk%2}"], banks=[f"pf{bi}"])
                            if g % 2 == 0:
                                P.op("act", lambda e, pb=pb, sg=sg, g=g: e.copy(out=sg[:, g, :], in_=pb[:]),
                                     writes=[f"fstg{blk%2}_{g}"], banks=[f"pf{bi}"])
                            else:
                                P.op("dve", lambda e, pb=pb, sg=sg, g=g: e.tensor_copy(out=sg[:, g, :], in_=pb[:]),
                                     writes=[f"fstg{blk%2}_{g}"], banks=[f"pf{bi}"])
                        dst = own_fr if blk < 2 else snd_fr
                        c0 = (blk % 2) * 512
                        P.dma_op("sp", f"fo{blk%2}", dst[:, :, c0:c0 + 512].rearrange("g p n -> p g n"), sg[:],
                                 reads=[f"fstg{blk%2}_{g}" for g in range(H)])
                        if blk + 2 < 4:
                            P.dma_op("sp", f"dt{blk%2}", DT[blk % 2][:], dft_d[blk + 2], writes=[f"DT{blk%2}"])
                    P.barrier()
                    P.emit()
            if upto < 4:
                return
            with ExitStack() as s3:
                build_gdn(nc, s3, C, QKV, szt, rows, pc, ghead, own_og, snd_og, D_, upto, C_selb_d)
    return


def build_gdn(nc, st, C, QKV, szt, rows, pc, ghead, own_og, snd_og, D_, upto=9, selb_d=None):
    sb = lambda name, shape, dt: _sb(nc, st, name, shape, dt)
    ps = lambda name, shape, dt: _ps(nc, st, name, shape, dt)
    selbt = sb("selbt", [128, 16, 128], BF16)
    C.selb = lambda kind, d, h: selbt[:, kind * 8 + d * 4 + h, :]
    Rhl = sb("Rhl", [128, 2, T], BF16)
    NRhl = sb("NRhl", [128, 2, T], BF16)
    CC = sb("CC", [128, NTILE, 3, 2, H], F32)
    GLbc = sb("GLbc", [128, 8, 32], F32)
    st1 = ExitStack()
    sb1 = lambda name, shape, dt: _sb(nc, st1, name, shape, dt)
    P = Prog(nc, st1, "pgs")
    R = sb1("R", [128, T], F32)
    NR = sb1("NR", [128, T], F32)
    Cs = sb1("Cs", [128, T], F32)
    R2 = sb1("R2", [128, T], F32)
    COLR = sb1("COLR", [128, NTILE, 128], F32)
    COLR2 = sb1("COLR2", [128, NTILE, 128], F32)
    m01 = sb1("m01", [128, T], F32)
    cmul = sb1("cmul", [128, 1], F32)
    GLrow = sb1("GLrow", [128, 32], F32)
    BK = [[ps(f"gB{d}{i}", [128, 4, 128], F32) for i in range(3)] for d in range(2)]
    W = [ps(f"gW{d}", [128, 4, 128], F32) for d in range(2)]
    B1, B2, B3 = BK[0]

    P.op("act", lambda e: e.activation(out=rows[:], in_=rows[:], func=AF.Exp, bias=pc[:, 0:1], scale=pc[:, 1:2]),
         reads=["rows", "pc"], writes=["rows"])
    P.op("act", lambda e: e.activation(out=rows[:], in_=rows[:], func=AF.Ln, bias=1.0, scale=1.0),
         reads=["rows"], writes=["rows"])
    P.op("act", lambda e: e.activation(out=cmul[:], in_=pc[:, 2:3], func=AF.Exp), reads=["pc"], writes=["cmul"])
    P.op("dve", lambda e: e.tensor_scalar(out=cmul[:], in0=cmul[:], scalar1=-1.0, scalar2=None, op0=ALU.mult),
         reads=["cmul"], writes=["cmul"])
    P.op("dve", lambda e: e.tensor_scalar(out=R[:], in0=rows[:], scalar1=cmul[:, 0:1], scalar2=None, op0=ALU.mult),
         reads=["rows", "cmul"], writes=["R"])
    P.op("pool", lambda e: e.memset(m01[:], 1.0), writes=["m01"])
    P.op("pool", lambda e: e.memset(m01[:].rearrange("p (c k) -> p c k", k=64)[:, :, 0:1], 0.0), writes=["m01"])
    v3 = lambda t: t[:].rearrange("p (c k) -> p c k", k=64)
    P.op("dve", lambda e: e.tensor_tensor_scan(out=Cs[:], data0=m01[:], data1=R[:], initial=0.0, op0=ALU.mult, op1=ALU.add),
         reads=["R", "m01"], writes=["Cs"])
    P.op("act", lambda e: e.copy(out=GLrow[:], in_=v3(Cs)[:, :, 63]), reads=["Cs"], writes=["GLrow"])
    P.op("dve", lambda e: e.tensor_tensor(out=v3(R2)[0:32], in0=v3(Cs)[0:32, :, 63:64].to_broadcast([32, 32, 64]),
                                          in1=v3(Cs)[0:32], op=ALU.subtract), reads=["Cs"], writes=["R2"])
    P.op("dve", lambda e: e.tensor_tensor(out=R2[32:64, :], in0=Cs[32:64, :], in1=R[32:64, :], op=ALU.subtract),
         reads=["Cs", "R"], writes=["R2"])
    P.op("pool", lambda e: e.memset(R2[64:128, :], 0.0), writes=["R2"])
    P.op("dve", lambda e: e.tensor_tensor(out=v3(R)[32:64], in0=v3(Cs)[32:64, :, 63:64].to_broadcast([32, 32, 64]),
                                          in1=v3(R2)[32:64], op=ALU.subtract), reads=["Cs", "R2", "R"], writes=["R"])
    P.op("act", lambda e: e.copy(out=R[0:32, :], in_=Cs[0:32, :]), reads=["Cs", "R"], writes=["R"])
    P.op("dve", lambda e: e.tensor_scalar(out=NR[:], in0=R[:], scalar1=-1.0, scalar2=None, op0=ALU.mult),
         reads=["R"], writes=["NR"])
    for t in range(NTILE):
        bk = B1 if t % 2 == 0 else B2
        bkn = "B1" if t % 2 == 0 else "B2"
        P.op("pe", lambda e, t=t, bk=bk: e.matmul(out=bk[:, 0, :], lhsT=R[:, t * 128:(t + 1) * 128], rhs=C.identf,
                                                  start=True, stop=True), reads=["R", "const"], banks=[bkn])
        P.op("pe", lambda e, t=t, bk=bk: e.matmul(out=bk[:, 1, :], lhsT=R2[:, t * 128:(t + 1) * 128], rhs=C.identf,
                                                  start=True, stop=True), reads=["R2", "const"], banks=[bkn])
        P.op("act", lambda e, t=t, bk=bk: e.copy(out=COLR[:, t, :], in_=bk[:, 0, :]), writes=["COLR"], banks=[bkn])
        P.op("dve", lambda e, t=t, bk=bk: e.tensor_copy(out=COLR2[:, t, :], in_=bk[:, 1, :]), writes=["COLR2"], banks=[bkn])
    for d in range(2):
        P.op("act", lambda e, d=d: e.activation(out=CC[:, :, 0, d, :], in_=COLR[:, :, 64 + 32 * d:64 + 32 * d + 4],
                                                func=AF.Exp), reads=["COLR"], writes=["CC"])
        P.op("dve", lambda e, d=d: e.tensor_tensor(out=CC[:, :, 1, d, :], in0=COLR[:, :, 32 * d:32 * d + 4],
                                                   in1=COLR[:, :, 64 + 32 * d:64 + 32 * d + 4], op=ALU.add),
             reads=["COLR"], writes=["CC"])
        P.op("act", lambda e, d=d: e.activation(out=CC[:, :, 1, d, :], in_=CC[:, :, 1, d, :], func=AF.Exp),
             reads=["CC"], writes=["CC"])
        P.op("act", lambda e, d=d: e.activation(out=CC[:, :, 2, d, :], in_=COLR2[:, :, 32 * d:32 * d + 4], func=AF.Exp),
             reads=["COLR2"], writes=["CC"])
    for d in range(2):
        for h in range(H):
            P.op("pe", lambda e, d=d, h=h: e.matmul(
                out=B3[:].rearrange("p a b -> p (a b)")[:, (d * 4 + h) * 32:(d * 4 + h + 1) * 32],
                lhsT=C.sel(0, d, h), rhs=GLrow[:], start=True, stop=True),
                 reads=["GLrow", "const"], banks=["B3"])
    P.op("act", lambda e: e.activation(out=GLbc[:].rearrange("p a b -> p (a b)"),
                                       in_=B3[:].rearrange("p a b -> p (a b)")[:, 0:256], func=AF.Exp),
         writes=["GLbc"], banks=["B3"])
    if D_:
        P.dma_op("sp", "dbg_rows", D_["rows"], R[:], reads=["R"])
        P.dma_op("sp", "dbg_cc", D_["cc"], CC[:].rearrange("p a b c d -> p (a b c d)"), reads=["CC"])
        P.dma_op("sp", "dbg_gl", D_["glbc"], GLbc[:].rearrange("p a b -> p (a b)"), reads=["GLbc"])

    P.dma_op("sp", "selb", selbt[:], selb_d.rearrange("p (a b) -> p a b", b=128), writes=["const"])
    P.op("act", lambda e: e.copy(out=Rhl[:, 0, :], in_=R[:]), reads=["R"], writes=["Rhi"])
    P.op("dve", lambda e: e.tensor_tensor(out=NR[:], in0=R[:], in1=Rhl[:, 0, :], op=ALU.subtract),
         reads=["R", "Rhi", "NR"], writes=["NR"])
    P.op("act", lambda e: e.copy(out=Rhl[:, 1, :], in_=NR[:]), reads=["NR"], writes=["Rlo"])
    P.op("dve", lambda e: e.tensor_scalar(out=NRhl[:].rearrange("p a b -> p (a b)"), in0=Rhl[:].rearrange("p a b -> p (a b)"),
                                          scalar1=-1.0, scalar2=None, op0=ALU.mult), reads=["Rhi", "Rlo"], writes=["NRhl"])
    P.barrier()
    P.emit()
    st1.close()
    if upto < 5:
        return
    P = Prog(nc, st, "pg")
    tmp = {}
    for d in range(2):
        tmp[d] = dict(
            E2=sb(f"E2_{d}", [128, 4, 128], F32), E3=sb(f"E3_{d}", [128, 4, 128], F32),
            NT=sb(f"NTm{d}", [128, 4, 128], BF16), Nm=sb(f"Nm{d}", [128, 4, 128], BF16),
            Pm=[sb(f"Pm{d}{i}", [128, 4, 128], BF16) for i in range(2)],
            PTm=[sb(f"PTm{d}{i}", [128, 4, 128], BF16) for i in range(2)],
            Rm=[sb(f"Rm{d}{i}", [128, 4, 128], BF16) for i in range(2)],
            kbg=sb(f"kbg{d}", [128, 4, 128], BF16), vb=sb(f"vb{d}", [128, 4, 128], BF16),
            egc=sb(f"egc{d}", [128, 4, 128], F32))
    slot = {}
    for d in range(2):
        for s in range(2):
            slot[(d, s)] = dict(
                qkT=sb(f"qkT{d}{s}", [128, 4, 128], BF16), u=sb(f"u{d}{s}", [128, 4, 128], F32),
                wT=sb(f"wT{d}{s}", [128, 4, 128], BF16), kd=sb(f"kd{d}{s}", [128, 4, 128], BF16),
                qd=sb(f"qd{d}{s}", [128, 4, 128], BF16))
    S32 = sb("S32", [128, 2, 4, 128], F32)
    Sbf = sb("Sbf", [128, 2, 4, 128], BF16)
    vnew = sb("vnew", [128, 2, 4, 128], BF16)
    ost = sb("ost", [128, 16, 4, 128], BF16)
    ot = sb("ot", [128, 2, 4, 128], F32)
    osq = sb("osq", [128, 2, 4, 128], F32)
    oss = sb("oss", [128, 2, 4, 2], F32)
    ogt = [sb(f"ogt{d}", [128, 4, 128], BF16) for d in range(2)]
    ostg = [sb(f"ostg{i}", [128, 4, 128], BF16) for i in range(2)]
    P.op("pool", lambda e: e.memset(S32[:].rearrange("p a b c -> p (a b c)"), 0.0), writes=["S32_0", "S32_1"])
    P.op("pool", lambda e: e.memset(Sbf[:].rearrange("p a b c -> p (a b c)"), 0.0), writes=["Sbf0", "Sbf1"])
    P.op("pool", lambda e: e.memset(vnew[:].rearrange("p a b c -> p (a b c)"), 0.0), writes=["vnew0", "vnew1"])

    f4 = lambda ap: ap.rearrange("p a b -> p (a b)")

    def prep(tt, d, s):
        sl = slot[(d, s)]
        tm = tmp[d]
        E2, E3, NT, Nm, Pm, PTm, Rm, kbg, vb, egc = (tm[k_] for k_ in ("E2", "E3", "NT", "Nm", "Pm", "PTm", "Rm", "kbg", "vb", "egc"))
        k = f"{d}{s}"
        B1, B2, B3 = BK[d]
        b1k, b2k, b3k = f"B1_{d}", f"B2_{d}", f"B3_{d}"
        tsl = slice(tt * 128, (tt + 1) * 128)
        bc = lambda kind: CC[:, tt, kind, d, :].unsqueeze(2).to_broadcast([128, 4, 128])
        for (incl, Et, ek) in ((0, E2, f"E2_{d}"), (1, E3, f"E3_{d}")):
            for h in range(H):
                for hl in range(2):
                    P.op("pe", lambda e, h=h, incl=incl, hl=hl: e.matmul(out=B3[:, h, :], lhsT=C.selb(1 - incl, d, h),
                                                                         rhs=Rhl[:, hl, tsl], start=(hl == 0), stop=False),
                         reads=["const"], banks=[b3k])
                for hl in range(2):
                    P.op("pe", lambda e, h=h, hl=hl: e.matmul(out=B3[:, h, :], lhsT=NRhl[:, hl, tsl], rhs=C.selb(0, d, h),
                                                              start=False, stop=False), reads=["const"], banks=[b3k])
                P.op("pe", lambda e, h=h, incl=incl: e.matmul(out=B3[:, h, :], lhsT=C.negI, rhs=C.mask(d, incl),
                                                              start=False, stop=True), reads=["const"], banks=[b3k])
            P.op("act", lambda e, Et=Et: e.activation(out=f4(Et[:]), in_=f4(B3[:]), func=AF.Exp),
                 writes=[ek], banks=[b3k])
            yield
            if incl == 0:
                for h in range(H):
                    P.op("pe", lambda e, h=h: e.matmul(out=B1[:, h, :], lhsT=QKV[:, 0, h, tsl], rhs=QKV[:, 0, h, tsl],
                                                       start=True, stop=True), banks=[b1k])
                P.op("dve", lambda e: e.scalar_tensor_tensor(out=f4(NT[:]), in0=f4(B1[:]), scalar=-1.0, in1=f4(E2[:]),
                                                             op0=ALU.mult, op1=ALU.mult),
                     reads=[f"E2_{d}"], writes=[f"NT{d}"], banks=[b1k])
                yield
                for h in range(H):
                    P.op("pe", lambda e, h=h: e.matmul(out=B2[:, h, :], lhsT=NT[:, h, :], rhs=C.identb, start=True, stop=True),
                         reads=[f"NT{d}", "const"], banks=[b2k])
                P.op("act", lambda e: e.copy(out=f4(Nm[:]), in_=f4(B2[:])), writes=[f"Nm{d}"], banks=[b2k])
                P.op("pool", lambda e: e.tensor_tensor(out=Rm[0][:], in0=NT[:],
                                                       in1=C.identb.unsqueeze(1).to_broadcast([128, 4, 128]), op=ALU.add),
                     reads=[f"NT{d}", "const"], writes=[f"Rm{d}0"])
                yield
        Pc, PTc, pk, ptk = Nm, NT, f"Nm{d}", f"NT{d}"
        ri = 0
        for lvl in range(5):
            Pn, PTn = Pm[lvl % 2], PTm[lvl % 2]
            pnk, ptnk = f"Pm{d}{lvl%2}", f"PTm{d}{lvl%2}"
            for h in range(H):
                P.op("pe", lambda e, h=h, Pc=Pc, PTc=PTc: e.matmul(out=B1[:, h, :], lhsT=PTc[:, h, :], rhs=Pc[:, h, :],
                                                                   start=True, stop=True), reads=[pk, ptk], banks=[b1k])
            P.op("act", lambda e, Pn=Pn: e.copy(out=f4(Pn[:]), in_=f4(B1[:])), writes=[pnk], banks=[b1k])
            if lvl < 4:
                for h in range(H):
                    P.op("pe", lambda e, h=h, Pc=Pc, PTc=PTc: e.matmul(out=B2[:, h, :], lhsT=Pc[:, h, :], rhs=PTc[:, h, :],
                                                                       start=True, stop=True), reads=[pk, ptk], banks=[b2k])
                if lvl % 2 == 0:
                    P.op("dve", lambda e, PTn=PTn: e.tensor_copy(out=f4(PTn[:]), in_=f4(B2[:])), writes=[ptnk], banks=[b2k])
                else:
                    P.op("act", lambda e, PTn=PTn: e.copy(out=f4(PTn[:]), in_=f4(B2[:])), writes=[ptnk], banks=[b2k])
            yield
            for h in range(H):
                P.op("pe", lambda e, h=h, Pn=Pn, ri=ri: e.matmul(out=B3[:, h, :], lhsT=Pn[:, h, :], rhs=Rm[ri][:, h, :],
                                                                 start=True, stop=True),
                     reads=[pnk, f"Rm{d}{ri}"], banks=[b3k])
            P.op("dve", lambda e, ri=ri: e.tensor_tensor(out=f4(Rm[1 - ri][:]), in0=f4(B3[:]), in1=f4(Rm[ri][:]), op=ALU.add),
                 reads=[f"Rm{d}{ri}"], writes=[f"Rm{d}{1-ri}"], banks=[b3k])
            ri = 1 - ri
            Pc, PTc, pk, ptk = Pn, PTn, pnk, ptnk
            yield
            if lvl == 0:
                for h in range(H):
                    P.op("pe", lambda e, h=h: e.matmul(out=B3[:, h, :], lhsT=QKV[:, 0, h, tsl], rhs=C.identb, start=True, stop=True),
                         reads=["const"], banks=[b3k])
                P.op("dve", lambda e: e.tensor_tensor(out=kbg[:], in0=B3[:], in1=bc(1), op=ALU.mult),
                     reads=["CC"], writes=[f"kbg{d}"], banks=[b3k])
                P.op("act", lambda e: e.copy(out=f4(egc[:]), in_=f4(B3[:])), writes=[f"egc{d}"], banks=[b3k])
                P.op("pool", lambda e: e.tensor_tensor(out=sl["kd"][:], in0=egc[:], in1=bc(2), op=ALU.mult),
                     reads=["CC", f"egc{d}"], writes=["kd" + k])
                yield
                for h in range(H):
                    P.op("pe", lambda e, h=h: e.matmul(out=B3[:, h, :], lhsT=QKV[:, 1, h, tsl], rhs=C.identb, start=True, stop=True),
                         reads=["const"], banks=[b3k])
                P.op("dve", lambda e: e.tensor_tensor(out=vb[:], in0=B3[:], in1=bc(0), op=ALU.mult),
                     reads=["CC"], writes=[f"vb{d}"], banks=[b3k])
                yield
            elif lvl == 1:
                for h in range(H):
                    P.op("pe", lambda e, h=h: e.matmul(out=B2[:, h, :], lhsT=QKV[:, 0, h, tsl], rhs=QKV[:, 2, h, tsl],
                                                       start=True, stop=True), banks=[b2k])
                P.op("dve", lambda e: e.tensor_tensor(out=f4(sl["qkT"][:]), in0=f4(B2[:]), in1=f4(E3[:]), op=ALU.mult),
                     reads=[f"E3_{d}"], writes=["qkT" + k], banks=[b2k])
                yield
            elif lvl == 2:
                for h in range(H):
                    for hl in range(2):
                        P.op("pe", lambda e, h=h, hl=hl: e.matmul(out=B3[:, h, :], lhsT=C.selb(0, d, h), rhs=Rhl[:, hl, tsl],
                                                                  start=(hl == 0), stop=(hl == 1)),
                             reads=["const"], banks=[b3k])
                P.op("act", lambda e: e.activation(out=f4(egc[:]), in_=f4(B3[:]), func=AF.Exp),
                     reads=["kd" + k], writes=[f"egc{d}"], banks=[b3k])
                P.op("pool", lambda e: e.tensor_tensor(out=sl["qd"][:], in0=QKV[:, 2, :, tsl], in1=egc[:], op=ALU.mult),
                     reads=[f"egc{d}"], writes=["qd" + k])
                yield
        Rf, rk = Rm[ri], f"Rm{d}{ri}"
        for h in range(H):
            P.op("pe", lambda e, h=h: e.matmul(out=B1[:, h, :], lhsT=Rf[:, h, :], rhs=vb[:, h, :], start=True, stop=True),
                 reads=[rk, f"vb{d}"], banks=[b1k])
        P.op("act", lambda e: e.copy(out=f4(sl["u"][:]), in_=f4(B1[:])), writes=["u" + k], banks=[b1k])
        for h in range(H):
            P.op("pe", lambda e, h=h: e.matmul(out=B2[:, h, :], lhsT=kbg[:, h, :], rhs=Rf[:, h, :], start=True, stop=True),
                 reads=[rk, f"kbg{d}"], banks=[b2k])
        P.op("dve", lambda e: e.tensor_copy(out=f4(sl["wT"][:]), in_=f4(B2[:])), writes=["wT" + k], banks=[b2k])
        yield

    def step(s):
        info = []
        for d in range(2):
            c = s if d == 0 else 31 - s
            tt, half = c // 2, c % 2
            sl = slot[(d, tt % 2)]
            k = f"{d}{tt%2}"
            rs = slice(half * 64, half * 64 + 64)
            info.append((c, tt, half, sl, k, rs))
            for h in range(H):
                P.op("pe", lambda e, h=h, sl=sl, d=d: e.matmul(out=W[d][:, h, :], lhsT=sl["wT"][:, h, :], rhs=Sbf[:, d, h, :],
                                                               start=True, stop=True),
                     reads=["wT" + k, f"Sbf{d}"], banks=[f"W{d}"])
            P.op("dve", lambda e, sl=sl, d=d, rs=rs: e.tensor_tensor(out=vnew[rs, d], in0=sl["u"][rs], in1=W[d][rs],
                                                                      op=ALU.subtract),
                 reads=["u" + k], writes=[f"vnew{d}"], banks=[f"W{d}"])
            P.op("pool", lambda e, d=d, c=c: e.tensor_tensor(
                out=S32[:, d], in0=S32[:, d], in1=GLbc[:, d * 4:d * 4 + 4, c:c + 1].to_broadcast([128, 4, 128]), op=ALU.mult),
                reads=["GLbc", f"S32_{d}"], writes=[f"S32_{d}"])
        yield
        for d in range(2):
            c, tt, half, sl, k, rs = info[d]
            for h in range(H):
                P.op("pe", lambda e, h=h, sl=sl, d=d, rs=rs: e.matmul(out=W[d][:, h, :], lhsT=sl["kd"][rs, h, :],
                                                                      rhs=vnew[rs, d, h, :], start=True, stop=True),
                     reads=["kd" + k, f"vnew{d}"], banks=[f"W{d}"])
            P.op("dve", lambda e, d=d: e.tensor_tensor(out=S32[:, d], in0=S32[:, d], in1=W[d][:], op=ALU.add),
                 reads=[f"S32_{d}"], writes=[f"S32_{d}"], banks=[f"W{d}"])
        yield
        for d in range(2):
            c, tt, half, sl, k, rs = info[d]
            for h in range(H):
                P.op("pe", lambda e, h=h, sl=sl, d=d: e.matmul(out=W[d][:, h, :], lhsT=sl["qd"][:, h, :], rhs=Sbf[:, d, h, :],
                                                               start=True, stop=False),
                     reads=["qd" + k, f"Sbf{d}"], banks=[f"W{d}"])
                P.op("pe", lambda e, h=h, sl=sl, d=d, rs=rs: e.matmul(out=W[d][:, h, :], lhsT=sl["qkT"][rs, h, :],
                                                                      rhs=vnew[rs, d, h, :], start=False, stop=True),
                     reads=["qkT" + k, f"vnew{d}"], banks=[f"W{d}"])
            if s < 16:
                P.op("act", lambda e, d=d, rs=rs, s=s: e.copy(out=ost[rs, s], in_=W[d][rs]),
                     writes=[f"ost{s}_{d}"], banks=[f"W{d}"])
            else:
                s0 = 31 - s
                P.op("dve", lambda e, d=d, rs=rs, s0=s0: e.tensor_tensor(out=ot[rs, d], in0=W[d][rs], in1=ost[rs, s0], op=ALU.add),
                     reads=[f"ost{s0}_{1-d}"], writes=[f"ot{d}"], banks=[f"W{d}"])
            P.op("act", lambda e, d=d: e.copy(out=Sbf[:, d], in_=S32[:, d]), reads=[f"S32_{d}"], writes=[f"Sbf{d}"])
        yield
        if s < 16:
            return
        for d in range(2):
            c, tt, half, sl, k, rs = info[d]
            P.op("pool", lambda e, d=d, rs=rs: e.tensor_tensor(out=osq[rs, d], in0=ot[rs, d], in1=ot[rs, d], op=ALU.mult),
                 reads=[f"ot{d}"], writes=[f"osq{d}"])
            P.op("dve", lambda e, d=d, rs=rs: e.tensor_reduce(out=oss[rs, d, :, 0], in_=osq[rs, d], axis=AX.X, op=ALU.add),
                 reads=[f"osq{d}"], writes=[f"oss{d}"])
            P.op("pool", lambda e, d=d, rs=rs: e.tensor_scalar(out=oss[rs, d, :, 1], in0=oss[rs, d, :, 0], scalar1=1.0 / 128.0,
                                                               scalar2=EPS, op0=ALU.mult, op1=ALU.add),
                 reads=[f"oss{d}"], writes=[f"oss{d}"])
            P.op("pool", lambda e, d=d, rs=rs: e.tensor_tensor(out=oss[rs, d, :, 1], in0=oss[rs, d, :, 1],
                                                               in1=C.mhalf[rs, 0:1].to_broadcast([64, 4]), op=ALU.pow),
                 reads=[f"oss{d}", "mhalf"], writes=[f"oss{d}"])
        yield
        for d in range(2):
            c, tt, half, sl, k, rs = info[d]
            P.op("dve", lambda e, d=d, rs=rs: e.tensor_tensor(out=ot[rs, d], in0=ot[rs, d],
                                                              in1=oss[rs, d, :, 1:2].to_broadcast([64, 4, 128]), op=ALU.mult),
                 reads=[f"oss{d}", f"ot{d}"], writes=[f"ot{d}"])
            P.op("pool", lambda e, d=d, rs=rs: e.tensor_tensor(out=ot[rs, d], in0=ot[rs, d],
                                                               in1=ghead[rs].unsqueeze(1).to_broadcast([64, 4, 128]), op=ALU.mult),
                 reads=["ghead", f"ot{d}"], writes=[f"ot{d}"])
            P.op("pool", lambda e, d=d, rs=rs, tt=tt: e.tensor_tensor(
                out=ogt[d][rs], in0=ot[rs, d], in1=szt[rs, tt, :].rearrange("p (a b) -> p a b", b=128), op=ALU.mult),
                reads=[f"ot{d}"], writes=[f"ogt{d}"])
        yield
        for d in range(2):
            c, tt, half, sl, k, rs = info[d]
            done = (half == 1) if d == 0 else (half == 0)
            if done:
                ij = C.Jb if d == 0 else C.identb
                for h in range(H):
                    P.op("pe", lambda e, h=h, d=d, ij=ij: e.matmul(out=BK[d][2][:, h, :], lhsT=ogt[d][:, h, :], rhs=ij,
                                                                   start=True, stop=True),
                         reads=[f"ogt{d}", "const"], banks=[f"B3_{d}"])
                P.op("act", lambda e, d=d: e.copy(out=f4(ostg[d][:]), in_=f4(BK[d][2][:])), writes=[f"ostg{d}"],
                     banks=[f"B3_{d}"])
                if d == 0:
                    dst, blk = snd_og, 15 - tt
                else:
                    dst, blk = own_og, tt
                P.dma_op("sp", f"og{d}", dst[:, :, blk * 128:(blk + 1) * 128].rearrange("g p n -> p g n"), ostg[d][:],
                         reads=[f"ostg{d}"])
        yield

    def steps2(a):
        yield from step(2 * a)
        yield from step(2 * a + 1)

    def run(gens):
        gens = list(gens)
        while gens:
            for g in list(gens):
                try:
                    next(g)
                except StopIteration:
                    gens.remove(g)

    nround = 16 if upto >= 9 else 2
    run([prep(0, 0, 0), prep(15, 1, 1)])
    for a in range(nround):
        gl = [steps2(a)]
        if a + 1 < 16:
            gl += [prep(a + 1, 0, (a + 1) % 2), prep(15 - (a + 1), 1, (15 - (a + 1)) % 2)]
        run(gl)
    P.barrier()
    P.emit()


def _consts(c):
    cf = np.zeros((128, 17, 128), np.float32)
    cf[:, 0, :] = np.eye(128, dtype=np.float32)
    for d in range(2):
        for h in range(H):
            cf[d * 32 + h, 1 + d * 4 + h, :] = 1.0
            cf[d * 32 + h, 1 + 8 + d * 4 + h, :] = 1.0
            cf[64 + d * 32 + h, 1 + 8 + d * 4 + h, :] = 1.0
    cb = np.zeros((128, 10, 128), np.float32)
    cb[:, 0, :] = np.eye(128)
    cb[:, 1, :] = 1.0
    cb[:, 2, :] = np.eye(128)[::-1]
    cb[:, 3, :] = -BIG * np.eye(128)
    j = np.arange(128)[:, None]
    i = np.arange(128)[None, :]
    same = (j // 64) == (i // 64)
    cb[:, 4, :] = 1.0 - (same & (i > j))
    cb[:, 5, :] = 1.0 - (same & (i >= j))
    cb[:, 6, :] = 1.0 - (same & (i < j))
    cb[:, 7, :] = 1.0 - (same & (i <= j))
    cc = np.arange(128)
    ang = 2 * np.pi * ((cc[:, None] * cc[None, :]) % 128) / 128.0
    cb[:, 8, :] = np.cos(ang)
    cb[:, 9, :] = -np.sin(ang)
    tin = np.arange(T)
    tg_in = tin if c == 0 else (T - 1 - tin)
    n = np.arange(T)
    tl = np.where(n < OWN, n, 3071 - n)
    tg_out = tl if c == 0 else (T - 1 - tl)
    prod = (tg_in[:, None].astype(np.int64) * tg_out[None, :].astype(np.int64)) % T
    ang = 2 * np.pi * prod / T
    norm = 1.0 / np.sqrt(T * 128.0)
    dtc = (np.cos(ang) * norm).astype(np.float32)
    dts = (np.sin(ang) * norm).astype(np.float32)
    dft = np.stack([dtc, dts], 0)
    dft = dft.reshape(2, 16, 128, 4, 512).transpose(3, 2, 1, 0, 4)
    selb = np.ascontiguousarray(cf[:, 1:17, :]).reshape(128, -1).astype(NPBF)
    return (cf.reshape(128, -1), cb.reshape(128, -1).astype(NPBF), np.ascontiguousarray(dft).astype(NPBF), selb)


def _klayout(w):
    K, N = w.shape
    return np.ascontiguousarray(w.reshape(K // 128, 128, N).transpose(1, 0, 2))


def prep_A(inp, b, c, consts):
    flip = c == 1
    x = inp["x"][b]
    x = x[::-1] if flip else x
    w_in = inp["w_in"][0]
    hs = [4 * c + j for j in range(4)]
    cols = []
    for j in hs:
        cols += [np.arange(1024 + j * 128, 1024 + (j + 1) * 128), np.arange(2048 + j * 128, 2048 + (j + 1) * 128),
                 np.arange(j * 128, (j + 1) * 128)]
    for j in hs:
        cols.append(np.arange(4096 + j * 128, 4096 + (j + 1) * 128))
    cols = np.concatenate(cols)
    wmix = _klayout(w_in[:, cols])
    zc = np.concatenate([np.arange(3072 + j * 128, 3072 + (j + 1) * 128) for j in hs])
    wz = _klayout(w_in[:, zc])
    kf, kb = (1, 0) if flip else (0, 1)
    wsc = np.zeros((D, 128), np.float32)
    for jj, j in enumerate(hs):
        wsc[:, 0 + jj] = w_in[:, 5120 + (2 + kf) * 8 + j]
        wsc[:, 32 + jj] = w_in[:, 5120 + (2 + kb) * 8 + j]
        wsc[:, 64 + jj] = w_in[:, 5120 + kf * 8 + j]
        wsc[:, 96 + jj] = w_in[:, 5120 + kb * 8 + j]
    wsc = _klayout(wsc)
    conv = inp["conv_w"][0]
    conv = conv[::-1] if flip else conv
    cw = np.zeros((128, 12, 5), np.float32)
    for jj, j in enumerate(hs):
        for kind, base in ((0, 1024), (1, 2048), (2, 0)):
            cw[:, 3 * jj + kind, :] = conv[:, base + j * 128: base + (j + 1) * 128].T
    alog = (inp["a_log_fwd"][0], inp["a_log_bwd"][0])
    dtb = (inp["dt_bias_fwd"][0], inp["dt_bias_bwd"][0])
    pc = np.zeros((128, 8), np.float32)
    for jj, j in enumerate(hs):
        pc[0 + jj, 0] = dtb[kf][j]
        pc[32 + jj, 0] = dtb[kb][j]
        pc[0 + jj, 2] = alog[kf][j]
        pc[32 + jj, 2] = alog[kb][j]
    pc[0:64, 1] = 1.0
    pc[64:128, 1] = -1.0
    cf, cb, dft, selb = consts[c]
    return {
        "selb": selb,
        "x": np.ascontiguousarray(x), "gmix": np.ascontiguousarray(np.tile(inp["g_mix"][0][None], (128, 1))),
        "wmix": wmix, "wz": wz, "wscal": wsc, "cw": cw.reshape(128, 60), "pc": pc,
        "ghead": np.ascontiguousarray(np.tile(inp["g_dn_head"][0][None], (128, 1))),
        "cf": cf, "cb": cb, "dft": dft,
    }


def build_B(nrcv=4):
    nc = bass.Bass("TRN2", target_bir_lowering=False)
    di = lambda name, shape, dt: nc.dram_tensor(name, shape, dt, kind="ExternalInput").ap()
    do = lambda name, shape, dt: nc.dram_tensor(name, shape, dt, kind="ExternalOutput").ap()
    x_d = di("xo", [OWN, D], F32)
    ogfr = [(di("og_own", [4, 128, OWN], BF16), 4), (di("og_rcv", [nrcv, 128, OWN], BF16), nrcv),
            (di("fr_own", [4, 128, OWN], BF16), 4), (di("fr_rcv", [nrcv, 128, OWN], BF16), nrcv)]
    w = declare_B_weights(nc, nrcv)
    out_d = do("out", [OWN, D], F32)
    build_B_body(nc, x_d, w["g3"], w["idb"], ogfr, nrcv, w["wst"], w["wo"], w["wup"], w["wdn"], out_d)
    return nc


def declare_B_weights(nc, nrcv):
    di = lambda name, shape, dt: nc.dram_tensor(name, shape, dt, kind="ExternalInput").ap()
    NW = 8 + 2 * nrcv + 32
    return dict(g3=di("g3", [3, 128, D], F32), idb=di("idb", [128, 128], BF16),
                wst=di("wst", [16, 128, NW, 128], F32), wo=di("wo", [4, 128, 16, 512], F32),
                wup=di("wup", [64, 128, 16, 128], F32), wdn=di("wdn", [4, 4, 128, 16, 512], F32))


def build_B_body(nc, x_d, g3_d, idb_d, ogfr_d, nrcv, wst_d, wo_d, wup_d, wdn_d, out_d, cc=None):
    NK = 8 + 2 * nrcv
    NW = NK + 32
    NT8 = OWN // 128
    with ExitStack() as stP:
        C = Ctx()
        C.nc = nc
        idb = _sb(nc, stP, "b_idb", [128, 128], BF16)
        C.identb = idb
        C.epsb = _sb(nc, stP, "b_epsb", [128, 1], F32)
        C.mhalf = _sb(nc, stP, "b_mhalf", [128, 1], F32)
        C.st = _sb(nc, stP, "b_st", [128, 8, 4], F32)
        C.junk = _sb(nc, stP, "b_junk", [128, D], BF16)
        hT = _sb(nc, stP, "b_hT", [128, 16, OWN], BF16)
        mU = _sb(nc, stP, "b_mU", [128, 16, OWN], BF16)
        gbuf = _sb(nc, stP, "b_gbuf", [128, D], F32)
        hb = _sb(nc, stP, "b_hb", [128, D], BF16)
        hkeys = lambda blk: [f"hT{t}{s}" for t in range(4 * blk, 4 * blk + 4) for s in "ab"]
        with ExitStack() as s:
            P = Prog(nc, s, "b1")
            xt = [_sb(nc, s, f"b_xt{i}", [128, D], F32) for i in range(2)]
            ogfr = _sb(nc, s, "b_ogfr", [128, NK, OWN], BF16)
            wsb = [_sb(nc, s, f"b_wsb{i}", [128, NW, 128], BF16) for i in range(2)]
            sg = [_sb(nc, s, f"b_sg{i}", [128, 512], F32) for i in range(2)]
            tt_ = [_sb(nc, s, f"b_t{i}", [128, 512], F32) for i in range(2)]
            C.tr = None
            pb = [_ps(nc, s, f"b_pc{i}", [128, 512], F32) for i in range(6)]
            C.tr = [_ps(nc, s, f"b_tr{i}", [128, 8, 128], BF16) for i in range(2)]
            P.dma_op("sp", "c0", idb[:], idb_d, writes=["const"])
            P.dma_op("sp", "c1", gbuf[:], g3_d[0], writes=["gbuf"])
            P.op("pool", lambda e: e.memset(C.epsb[:], EPS), writes=["epsb"])
            P.op("pool", lambda e: e.memset(C.mhalf[:], -0.5), writes=["mhalf"])
            if cc is not None:
                P.cc_allgather(*cc)

            def load_ws(oc):
                P.dma_op("pool", f"ws{oc%2}", wsb[oc % 2][:], wst_d[oc], writes=[f"wsb{oc%2}"])
            load_ws(0)
            hba = [hb, _sb(nc, s, "b_hba1", [128, D], BF16), _sb(nc, s, "b_hba2", [128, D], BF16)]

            def tra(t):
                transpose_tile16(P, C, hba[t % 3], f"hba{t%3}",
                                 lambda half, t=t: hT[:, half * 8:(half + 1) * 8, t * 128:(t + 1) * 128],
                                 [f"hT{t}a", f"hT{t}b"], "tr")
            for t in range(NT8):
                xb = xt[t % 2]
                P.dma_op("sp", f"x{t%2}", xb[:], x_d[t * 128:(t + 1) * 128, :], writes=[f"xt{t%2}"])
                rms_tile(P, C, xb[:], f"xt{t%2}", gbuf[:], "gbuf", hba[t % 3][:], f"hba{t%3}", t)
                if t >= 1:
                    tra(t - 1)
            tra(NT8 - 1)
            o0 = 0
            for (src, n) in ogfr_d:
                P.dma_op("sp", f"ogfr{o0}", ogfr[:, o0:o0 + n, :], src.rearrange("g p n -> p g n"), reads=["rcv"],
                         writes=["ogfr"])
                o0 += n
            nog = 4 + nrcv
            for oc in range(16):
                if oc + 1 < 16:
                    load_ws(oc + 1)
                w = wsb[oc % 2]
                wk = f"wsb{oc%2}"
                for blk in range(2):
                    bs = slice(blk * 512, (blk + 1) * 512)
                    Ya, Yf = pb[0], pb[1]
                    Ga, Gf = pb[2 + 2 * blk], pb[3 + 2 * blk]
                    gak, gfk = f"pc{2+2*blk}", f"pc{3+2*blk}"
                    for kc in range(16):
                        P.op("pe", lambda e, kc=kc, w=w, Ga=Ga, bs=bs: e.matmul(out=Ga[:], lhsT=w[:, NK + kc, :], rhs=hT[:, kc, bs],
                                                                             start=(kc == 0), stop=(kc == 15)),
                             reads=[wk] + hkeys(blk), banks=[gak])
                    P.op("act", lambda e, Ga=Ga: e.activation(out=sg[0][:], in_=Ga[:], func=AF.Sigmoid),
                         writes=["sg0"], banks=[gak])
                    for kc in range(16):
                        P.op("pe", lambda e, kc=kc, w=w, Gf=Gf, bs=bs: e.matmul(out=Gf[:], lhsT=w[:, NK + 16 + kc, :], rhs=hT[:, kc, bs],
                                                                             start=(kc == 0), stop=(kc == 15)),
                             reads=[wk] + hkeys(blk), banks=[gfk])
                    P.op("act", lambda e, Gf=Gf: e.activation(out=sg[1][:], in_=Gf[:], func=AF.Sigmoid),
                         writes=["sg1"], banks=[gfk])
                    for kc in range(nog):
                        P.op("pe", lambda e, kc=kc, w=w, bs=bs: e.matmul(out=Ya[:], lhsT=w[:, kc, :], rhs=ogfr[:, kc, bs],
                                                                      start=(kc == 0), stop=(kc == nog - 1)),
                             reads=[wk, "ogfr"], banks=["pc0"])
                    P.op("dve", lambda e: e.tensor_tensor(out=tt_[0][:], in0=Ya[:], in1=sg[0][:], op=ALU.mult),
                         reads=["sg0"], writes=["t0"], banks=["pc0"])
                    for kc in range(nog):
                        P.op("pe", lambda e, kc=kc, w=w, bs=bs: e.matmul(out=Yf[:], lhsT=w[:, nog + kc, :], rhs=ogfr[:, nog + kc, bs],
                                                                      start=(kc == 0), stop=(kc == nog - 1)),
                             reads=[wk, "ogfr"], banks=["pc1"])
                    P.op("dve", lambda e: e.tensor_tensor(out=tt_[1][:], in0=Yf[:], in1=sg[1][:], op=ALU.mult),
                         reads=["sg1"], writes=["t1"], banks=["pc1"])
                    P.op("pool", lambda e, oc=oc, bs=bs: e.tensor_tensor(out=mU[:, oc, bs], in0=tt_[0][:], in1=tt_[1][:], op=ALU.add),
                         reads=["t0", "t1"], writes=[f"mU{oc}"])
            P.barrier()
            P.emit()
        with ExitStack() as s:
            x1 = _sb(nc, s, "b_x1", [128, NT8, D], F32)
            mkeys = [f"mU{oc}" for oc in range(16)]
            with ExitStack() as s2:
                P = Prog(nc, s2, "b2")
                wob = [_sb(nc, s2, f"b_wob{i}", [128, 16, 512], BF16) for i in range(2)]
                hbs = [hb] + [_sb(nc, s2, f"b_hb{i}", [128, D], BF16) for i in range(2)]
                pb = [_ps(nc, s2, f"b_pd{i}", [128, 512], F32) for i in range(4)]
                C.tr = [_ps(nc, s2, f"b_tr2{i}", [128, 8, 128], BF16) for i in range(2)]
                P.dma_op("sp", "gl", gbuf[:], g3_d[1], writes=["gbuf"])
                for t in range(NT8):
                    P.dma_op("sp", f"x1_{t%4}", x1[:, t, :], x_d[t * 128:(t + 1) * 128, :], writes=[f"x1_{t}_{c}" for c in range(4)])
                P.dma_op("pool", "wo0", wob[0][:], wo_d[0], writes=["wob0"])
                n = 0
                for cb in range(4):
                    if cb + 1 < 4:
                        P.dma_op("pool", f"wo{(cb+1)%2}", wob[(cb + 1) % 2][:], wo_d[cb + 1], writes=[f"wob{(cb+1)%2}"])
                    w = wob[cb % 2]
                    for t in range(NT8):
                        bk = pb[n % 4]
                        for kc in range(16):
                            P.op("pe", lambda e, kc=kc, t=t, w=w, bk=bk: e.matmul(out=bk[:], lhsT=mU[:, kc, t * 128:(t + 1) * 128],
                                                                               rhs=w[:, kc, :], start=(kc == 0), stop=(kc == 15)),
                                 reads=[f"wob{cb%2}"], banks=[f"pd{n%4}"])
                        P.op("dve", lambda e, t=t, cb=cb, bk=bk: e.tensor_tensor(out=x1[:, t, cb * 512:(cb + 1) * 512],
                                                                                  in0=bk[:], in1=x1[:, t, cb * 512:(cb + 1) * 512], op=ALU.add),
                             reads=[f"x1_{t}_{cb}"], writes=[f"x1_{t}_{cb}"], banks=[f"pd{n%4}"])
                        n += 1
                        if cb == 3:
                            rms_tile(P, C, x1[:, t, :], [f"x1_{t}_{c}" for c in range(4)], gbuf[:], "gbuf",
                                     hbs[t % 3][:], f"hbs{t%3}", t)
                            if t >= 2:
                                tq = t - 2
                                transpose_tile16(P, C, hbs[tq % 3], f"hbs{tq%3}",
                                                 lambda half, tq=tq: hT[:, half * 8:(half + 1) * 8, tq * 128:(tq + 1) * 128],
                                                 [f"hT{tq}a", f"hT{tq}b"], "tr")
                for tq in (NT8 - 2, NT8 - 1):
                    transpose_tile16(P, C, hbs[tq % 3], f"hbs{tq%3}",
                                     lambda half, tq=tq: hT[:, half * 8:(half + 1) * 8, tq * 128:(tq + 1) * 128],
                                     [f"hT{tq}a", f"hT{tq}b"], "tr")
                P.barrier()
                P.emit()
            with ExitStack() as s2:
                P = Prog(nc, s2, "b3")
                wub = [_sb(nc, s2, f"b_wub{i}", [128, 16, 128], BF16) for i in range(2)]
                wdb = [_sb(nc, s2, f"b_wdb{i}", [128, 16, 512], BF16) for i in range(2)]
                rl = [_sb(nc, s2, f"b_rl{i}", [128, 512], F32) for i in range(2)]
                ob = [_sb(nc, s2, f"b_ob{i}", [128, D], F32) for i in range(2)]
                P.dma_op("sp", "gl", gbuf[:], g3_d[2], writes=["gbuf"])
                pu = [_ps(nc, s2, f"b_pu{i}", [128, 512], F32) for i in range(4)]
                pd = [_ps(nc, s2, f"b_pdn{i}", [128, 512], F32) for i in range(4)]
                nu = 0
                nd = 0
                nwd = 0

                def load_up(i):
                    P.dma_op("pool", f"wu{i%2}", wub[i % 2][:], wup_d[i], writes=[f"wub{i%2}"])

                def load_dn(i):
                    P.dma_op("pool", f"wd{i%2}", wdb[i % 2][:], wdn_d[i // 4, i % 4], writes=[f"wdb{i%2}"])
                load_up(0)
                load_dn(0)
                for q in range(4):
                    for j in range(16):
                        fc = q * 16 + j
                        if fc + 1 < 64:
                            load_up(fc + 1)
                        w = wub[fc % 2]
                        for blk in range(2):
                            bs = slice(blk * 512, (blk + 1) * 512)
                            bk = pu[nu % 4]
                            for kc in range(16):
                                P.op("pe", lambda e, kc=kc, w=w, bk=bk, bs=bs: e.matmul(out=bk[:], lhsT=w[:, kc, :], rhs=hT[:, kc, bs],
                                                                                     start=(kc == 0), stop=(kc == 15)),
                                     reads=[f"wub{fc%2}"], banks=[f"pu{nu%4}"])
                            r = rl[nu % 2]
                            P.op("act", lambda e, bk=bk, r=r: e.activation(out=r[:], in_=bk[:], func=AF.Relu),
                                 writes=[f"rl{nu%2}"], banks=[f"pu{nu%4}"])
                            eng = "dve" if nu % 2 == 0 else "pool"
                            P.op(eng, lambda e, r=r, j=j, bs=bs: e.tensor_tensor(out=mU[:, j, bs], in0=r[:], in1=r[:], op=ALU.mult),
                                 reads=[f"rl{nu%2}"], writes=[f"up{j}"])
                            nu += 1
                    upk = [f"up{j}" for j in range(16)]
                    for cb in range(4):
                        if nwd + 1 < 16:
                            load_dn(nwd + 1)
                        w = wdb[nwd % 2]
                        for t in range(NT8):
                            bk = pd[nd % 4]
                            for kc in range(16):
                                P.op("pe", lambda e, kc=kc, t=t, w=w, bk=bk: e.matmul(out=bk[:], lhsT=mU[:, kc, t * 128:(t + 1) * 128],
                                                                                   rhs=w[:, kc, :], start=(kc == 0), stop=(kc == 15)),
                                     reads=[f"wdb{nwd%2}"] + upk, banks=[f"pdn{nd%4}"])
                            P.op("dve", lambda e, t=t, cb=cb, bk=bk: e.tensor_tensor(out=x1[:, t, cb * 512:(cb + 1) * 512],
                                                                                      in0=bk[:], in1=x1[:, t, cb * 512:(cb + 1) * 512], op=ALU.add),
                                 reads=[f"x1_{t}_{cb}"], writes=[f"x1_{t}_{cb}"], banks=[f"pdn{nd%4}"])
                            nd += 1
                            if q == 3 and cb == 3:
                                rms_tile(P, C, x1[:, t, :], [f"x1_{t}_{c}" for c in range(4)], gbuf[:], "gbuf",
                                         ob[t % 2][:], f"ob{t%2}", t)
                                P.dma_op("sp", f"o{t%2}", out_d[t * 128:(t + 1) * 128, :], ob[t % 2][:], reads=[f"ob{t%2}"])
                        nwd += 1
                P.barrier()
                P.emit()


def prep_B(inp, b, c, rcv, idb):
    flip = c == 1
    x = inp["x"][b]
    x = x[::-1] if flip else x
    own = [4 * c + j for j in range(4)]
    oth = [4 * (1 - c) + j for j in range(4)]
    w_in = inp["w_in"][0]
    wdu = inp["w_dn_up"][0]
    wfo = inp["w_fourier"][0]
    rows = np.concatenate([np.arange(h * 128, (h + 1) * 128) for h in own + oth])
    ga = w_in[:, 5152:5152 + 2048]
    gf = w_in[:, 5152 + 2048:5152 + 4096]
    wst = np.concatenate([wdu[rows], wfo[rows], ga, gf], axis=0)
    nk = wst.shape[0] // 128
    wst = wst.reshape(nk, 128, 16, 128).transpose(2, 1, 0, 3)
    wo = inp["w_o"][0].reshape(16, 128, 4, 512).transpose(2, 1, 0, 3)
    wup = inp["w_mlp_up"][0].reshape(16, 128, 64, 128).transpose(2, 1, 0, 3)
    wdn = inp["w_mlp_down"][0].reshape(4, 16, 128, 4, 512).transpose(0, 3, 2, 1, 4)
    g3 = np.stack([np.tile(inp[k].reshape(-1)[None], (128, 1)) for k in ("g_mix", "g_mlp", "g_final")], 0)
    d = {
        "xo": np.ascontiguousarray(x[:OWN]), "g3": np.ascontiguousarray(g3), "idb": idb,
        "wst": np.ascontiguousarray(wst), "wo": np.ascontiguousarray(wo), "wup": np.ascontiguousarray(wup),
        "wdn": np.ascontiguousarray(wdn),
    }
    d.update(rcv)
    return d


def build_F():
    nc = bass.Bass("TRN2", target_bir_lowering=False)
    ins = declare_A_inputs(nc)
    w = declare_B_weights(nc, 8)
    out_d = nc.dram_tensor("out", [OWN, D], F32, kind="ExternalOutput").ap()
    own_og = nc.dram_tensor("i_own_og", [4, 128, OWN], BF16, kind="Internal").ap()
    own_fr = nc.dram_tensor("i_own_fr", [4, 128, OWN], BF16, kind="Internal").ap()
    snd = nc.dram_tensor("i_snd", [8 * 128, OWN], BF16, kind="Internal").ap()
    rcv = nc.dram_tensor("i_rcv", [2 * 8 * 128, OWN], BF16, kind="Internal").ap()
    snd3 = snd.rearrange("(g p) n -> g p n", p=128)
    rcv4 = rcv.rearrange("(r g p) n -> r g p n", r=2, p=128)
    emit_A(nc, ins, (own_og, snd3[0:4], own_fr, snd3[4:8]), {}, False, 9)
    ogfr = [(own_og, 4), (rcv4[0, 0:4], 4), (rcv4[1, 0:4], 4), (own_fr, 4), (rcv4[0, 4:8], 4), (rcv4[1, 4:8], 4)]
    build_B_body(nc, ins["x"][0:OWN, :], w["g3"], w["idb"], ogfr, 8, w["wst"], w["wo"], w["wup"], w["wdn"], out_d,
                 cc=(snd, rcv, [[0, 1], [2, 3], [4, 5], [6, 7]]))
    return nc


def prep_F(inp, b, c, consts, idb, shared):
    d = prep_A(inp, b, c, consts)
    wdu = inp["w_dn_up"][0]
    wfo = inp["w_fourier"][0]
    w_in = inp["w_in"][0]
    own = np.concatenate([np.arange(h * 128, (h + 1) * 128) for h in range(4 * c, 4 * c + 4)])
    z512 = np.zeros((512, D), np.float32)
    parts = []
    for wmat in (wdu, wfo):
        parts.append(wmat[own])
        parts.append(z512 if c == 0 else wmat[0:512])
        parts.append(z512 if c == 1 else wmat[512:1024])
    parts.append(w_in[:, 5152:5152 + 2048])
    parts.append(w_in[:, 5152 + 2048:5152 + 4096])
    wst = np.concatenate(parts, axis=0)
    nk = wst.shape[0] // 128
    wst = wst.reshape(nk, 128, 16, 128).transpose(2, 1, 0, 3)
    d["wst"] = np.ascontiguousarray(wst)
    d["idb"] = idb
    d.update(shared)
    return d


def _shared_B(inp):
    wo = inp["w_o"][0].reshape(16, 128, 4, 512).transpose(2, 1, 0, 3)
    wup = inp["w_mlp_up"][0].reshape(16, 128, 64, 128).transpose(2, 1, 0, 3)
    wdn = inp["w_mlp_down"][0].reshape(4, 16, 128, 4, 512).transpose(0, 3, 2, 1, 4)
    g3 = np.stack([np.tile(inp[k].reshape(-1)[None], (128, 1)) for k in ("g_mix", "g_mlp", "g_final")], 0)
    return {"wo": np.ascontiguousarray(wo), "wup": np.ascontiguousarray(wup), "wdn": np.ascontiguousarray(wdn),
            "g3": np.ascontiguousarray(g3)}


_CACHE = {}


def kernel(**inputs):
    inp = {k: np.asarray(v) for k, v in inputs.items()}
    if "F" not in _CACHE:
        _CACHE["F"] = build_F()
        _CACHE["consts"] = [_consts(0), _consts(1)]
    consts = _CACHE["consts"]
    idb = np.eye(128, dtype=np.float32).astype(NPBF)
    shared = _shared_B(inp)
    maps = [prep_F(inp, i // 2, i % 2, consts, idb, shared) for i in range(8)]
    res = run_bass_kernel_spmd(_CACHE["F"], maps, core_ids=list(range(8))).results
    out = np.zeros((4, T, D), np.float32)
    for i in range(8):
        b, c = i // 2, i % 2
        o = np.asarray(res[i]["out"])
        if c == 0:
            out[b, :OWN] = o
        else:
            out[b, OWN:] = o[::-1]
    return out
```

```python
from contextlib import ExitStack
import numpy as np
import ml_dtypes
import concourse.bass as bass
import concourse.mybir as mybir
from concourse.bass_utils import run_bass_kernel_spmd

F32 = mybir.dt.float32
BF16 = mybir.dt.bfloat16
AF = mybir.ActivationFunctionType
ALU = mybir.AluOpType
AX = mybir.AxisListType
NPBF = ml_dtypes.bfloat16

D = 2048
T = 2048
OWN = 1024
NTILE = 16
H = 4
EPS = 1e-6
DFF = 8192
BIG = 30000.0
SELF_SYNC = True


class Prog:
    ENGS = ("pe", "act", "dve", "pool", "sp")

    def __init__(self, nc, stack, tag=""):
        self.nc = nc
        self.stack = stack
        self.tag = tag
        self.streams = {e: [] for e in self.ENGS}
        self.count = {e: 0 for e in self.ENGS}
        self.sems = {e: stack.enter_context(nc.semaphore(f"pg{tag}_{e}")) for e in self.ENGS}
        self.waited = {e: {} for e in self.ENGS}
        self.last_w = {}
        self.readers = {}
        self.bank_last = {}
        self.dma = {}

    def _deps(self, eng, reads, writes, banks):
        deps = []
        for r in reads:
            lw = self.last_w.get(r)
            if lw is not None:
                deps.append(lw)
        for w in writes:
            lw = self.last_w.get(w)
            if lw is not None:
                deps.append(lw)
            deps.extend(self.readers.get(w, ()))
        need = {}
        for (k, v) in deps:
            if k == eng and (eng == "pe" or not SELF_SYNC):
                continue
            if need.get(k, 0) < v:
                need[k] = v
        for bk in banks:
            for k, v in self.bank_last.get(bk, {}).items():
                if k != eng and need.get(k, 0) < v:
                    need[k] = v
        for k, v in need.items():
            if self.waited[eng].get(k, 0) >= v:
                continue
            self.waited[eng][k] = v
            sem = self.sems[k] if k in self.sems else self.dma[k][0]
            self.streams[eng].append(("wait", sem, v))

    def _commit(self, ev, reads, writes, banks):
        for r in reads:
            self.readers.setdefault(r, []).append(ev)
        for w in writes:
            self.last_w[w] = ev
            self.readers[w] = []
        for bk in banks:
            self.bank_last.setdefault(bk, {})[ev[0]] = ev[1]

    def op(self, eng, fn, reads=(), writes=(), banks=()):
        self._deps(eng, reads, writes, banks)
        self.count[eng] += 1
        ev = (eng, self.count[eng])
        self.streams[eng].append(("op", fn))
        self._commit(ev, reads, writes, banks)
        return ev

    def dma_op(self, queue, key, out, in_, reads=(), writes=()):
        if key not in self.dma:
            self.dma[key] = [self.stack.enter_context(self.nc.semaphore(f"dm{self.tag}_{key}")), 0]
        ent = self.dma[key]
        self._deps(queue, reads, writes, ())
        if ent[1] and self.waited[queue].get(key, 0) < 16 * ent[1]:
            self.waited[queue][key] = 16 * ent[1]
            self.streams[queue].append(("wait", ent[0], 16 * ent[1]))
        ent[1] += 1
        ev = (key, 16 * ent[1])
        self.streams[queue].append(("dma", out, in_, ent[0]))
        self._commit(ev, reads, writes, ())
        return ev

    def cc_allgather(self, src, dst, groups):
        key = "cc"
        self.dma[key] = [self.stack.enter_context(self.nc.semaphore(f"cc{self.tag}")), 0]
        self.streams["pool"].append(("cc", src, dst, groups, self.dma[key][0]))
        self.dma[key][1] = 1.0 / 16.0
        self.last_w["rcv"] = (key, 1)
        self.readers["rcv"] = []

    def barrier(self):
        for e in self.ENGS:
            for k in self.ENGS:
                if k != e and self.count[k] > self.waited[e].get(k, 0):
                    self.waited[e][k] = self.count[k]
                    self.streams[e].append(("wait", self.sems[k], self.count[k]))
            for key, (sem, cnt) in self.dma.items():
                if cnt and self.waited[e].get(key, 0) < 16 * cnt:
                    self.waited[e][key] = int(16 * cnt)
                    self.streams[e].append(("wait", sem, int(16 * cnt)))

    def emit(self):
        nc = self.nc
        handles = {"pe": "tensor", "act": "scalar", "dve": "vector", "pool": "gpsimd", "sp": "sync"}
        with nc.Block() as block:
            for e in self.ENGS:
                stream = self.streams[e]
                sem = self.sems[e]

                def body(eng, stream=stream, sem=sem):
                    for it in stream:
                        if it[0] == "wait":
                            eng.wait_ge(it[1], it[2])
                        elif it[0] == "op":
                            it[1](eng).then_inc(sem, 1)
                        elif it[0] == "cc":
                            eng.collective_compute("AllGather", ALU.bypass, replica_groups=it[3], ins=[it[1]],
                                                   outs=[it[2]]).then_inc(it[4], 1)
                        else:
                            eng.dma_start(out=it[1], in_=it[2]).then_inc(it[3], 16)

                getattr(block, handles[e])(body)


class Ctx:
    pass


def _sb(nc, st, name, shape, dt):
    return st.enter_context(nc.sbuf_tensor("s_" + name, shape, dt))


def _ps(nc, st, name, shape, dt):
    return st.enter_context(nc.psum_tensor("p_" + name, shape, dt))


def rms_tile(P, C, xb, xkey, gbc, gkey, hb, hkey, col):
    xkeys = list(xkey) if isinstance(xkey, (list, tuple)) else [xkey]
    P.op("act", lambda e: e.activation(out=C.junk[:], in_=xb, func=AF.Square, accum_out=C.st[:, col, 0:1]),
         reads=xkeys, writes=["junk", "st"])
    P.op("pool", lambda e: e.tensor_scalar(out=C.st[:, col, 1:2], in0=C.st[:, col, 0:1], scalar1=1.0 / D, scalar2=EPS,
                                           op0=ALU.mult, op1=ALU.add), reads=["st"], writes=["st"])
    P.op("pool", lambda e: e.tensor_tensor(out=C.st[:, col, 2:3], in0=C.st[:, col, 1:2], in1=C.mhalf[:], op=ALU.pow),
         reads=["st", "mhalf"], writes=["st"])
    P.op("dve", lambda e: e.scalar_tensor_tensor(out=hb, in0=xb, scalar=C.st[:, col, 2:3], in1=gbc,
                                                 op0=ALU.mult, op1=ALU.mult),
         reads=xkeys + ["st", gkey], writes=[hkey])


def transpose_tile16(P, C, hb, hkey, dst, dkeys, trk):
    for half in range(2):
        pt = C.tr[half]
        for j in range(8):
            kc = half * 8 + j
            P.op("pe", lambda e, pt=pt, j=j, kc=kc: e.transpose(out=pt[:, j, :], in_=hb[:, kc * 128:(kc + 1) * 128],
                                                               identity=C.identb[:]),
                 reads=[hkey, "const"], banks=[f"{trk}{half}"])
        if half == 0:
            P.op("act", lambda e, pt=pt: e.copy(out=dst(0), in_=pt[:]), writes=[dkeys[0]], banks=[f"{trk}0"])
        else:
            P.op("dve", lambda e, pt=pt: e.tensor_copy(out=dst(1), in_=pt[:]), writes=[dkeys[1]], banks=[f"{trk}1"])


def build_A(dbg=False, upto=9):
    nc = bass.Bass("TRN2", target_bir_lowering=False)
    do = lambda name, shape, dt: nc.dram_tensor(name, shape, dt, kind="ExternalOutput").ap()
    ins = declare_A_inputs(nc)
    outs = (do("own_og", [4, 128, OWN], BF16), do("snd_og", [4, 128, OWN], BF16),
            do("own_fr", [4, 128, OWN], BF16), do("snd_fr", [4, 128, OWN], BF16))
    D_ = {}
    if dbg:
        D_["qkv"] = do("d_qkv", [128, 12 * T], BF16)
        D_["rowsraw"] = do("d_rowsraw", [128, T], F32)
        D_["fT"] = do("d_fT", [128, 4 * T], BF16)
        D_["hT"] = do("d_hT", [128, 16 * T], BF16)
        D_["rows"] = do("d_rows", [128, T], F32)
        D_["cc"] = do("d_cc", [128, 16 * 24], F32)
        D_["glbc"] = do("d_glbc", [128, 256], F32)
        D_["szt"] = do("d_szt", [128, 16 * 512], BF16)
    emit_A(nc, ins, outs, D_, dbg, upto)
    return nc


def declare_A_inputs(nc):
    di = lambda name, shape, dt: nc.dram_tensor(name, shape, dt, kind="ExternalInput").ap()
    return dict(
        x=di("x", [T, D], F32), gmix=di("gmix", [128, D], F32), wmix=di("wmix", [128, 16, 16 * 128], F32),
        wz=di("wz", [128, 16, 512], F32), wscal=di("wscal", [128, 16, 128], F32), cw=di("cw", [128, 60], F32),
        pc=di("pc", [128, 8], F32), ghead=di("ghead", [128, 128], F32), cf=di("cf", [128, 17 * 128], F32),
        cb=di("cb", [128, 10 * 128], BF16), selb=di("selb", [128, 16 * 128], BF16), dft=di("dft", [4, 128, 16, 2, 512], BF16))


def emit_A(nc, ins, outs, D_, dbg=False, upto=9):
    x_d, gmix_d, wmix_d, wz_d, wscal_d, cw_d, pc_d, ghead_d, cf_d, cb_d, dft_d = (
        ins[k] for k in ("x", "gmix", "wmix", "wz", "wscal", "cw", "pc", "ghead", "cf", "cb", "dft"))
    C_selb_d = ins["selb"]
    own_og, snd_og, own_fr, snd_fr = outs
    with ExitStack() as stA:
        C = Ctx()
        C.nc = nc
        cf = _sb(nc, stA, "cf", [128, 17, 128], F32)
        cb = _sb(nc, stA, "cb", [128, 10, 128], BF16)
        C.identf = cf[:, 0, :]
        C.sel = lambda kind, d, h: cf[:, 1 + kind * 8 + d * 4 + h, :]
        C.identb = cb[:, 0, :]
        C.onesb = cb[:, 1, :]
        C.Jb = cb[:, 2, :]
        C.negI = cb[:, 3, :]
        C.mask = lambda d, incl: cb[:, 4 + d * 2 + incl, :]
        C.CS = cb[:, 8:10, :]
        C.epsb = _sb(nc, stA, "epsb", [128, 1], F32)
        C.mhalf = _sb(nc, stA, "mhalf", [128, 1], F32)
        C.st = _sb(nc, stA, "st", [128, 16, 4], F32)
        pc = _sb(nc, stA, "pc", [128, 8], F32)
        cw = _sb(nc, stA, "cw", [128, 12, 5], F32)
        ghead = _sb(nc, stA, "ghead", [128, 128], F32)

        with ExitStack() as stB:
            QKV = _sb(nc, stB, "QKV", [128, 3, H, T], BF16)
            szt = _sb(nc, stB, "szt", [128, NTILE, 512], BF16)
            rows = _sb(nc, stB, "rows", [128, T], F32)
            with ExitStack() as stF:
                fT = _sb(nc, stF, "fT", [128, H, T], BF16)
                with ExitStack() as stH:
                    hT = _sb(nc, stH, "hT", [128, 16, T], BF16)
                    with ExitStack() as s0:
                        P = Prog(nc, s0, "p0")
                        xt = [_sb(nc, s0, f"xt{i}", [128, D], F32) for i in range(2)]
                        gm = _sb(nc, s0, "gm", [128, D], F32)
                        C.junk = _sb(nc, s0, "junk", [128, D], BF16)
                        hb = [_sb(nc, s0, f"hb{i}", [128, D], BF16) for i in range(3)]
                        C.tr = [_ps(nc, s0, f"tr{i}", [128, 8, 128], BF16) for i in range(2)]
                        P.dma_op("sp", "c0", cf[:], cf_d.rearrange("p (a b) -> p a b", b=128), writes=["const"])
                        P.dma_op("sp", "c1", cb[:], cb_d.rearrange("p (a b) -> p a b", b=128), writes=["const"])
                        P.dma_op("sp", "c2", gm[:], gmix_d, writes=["gm"])
                        P.dma_op("sp", "c3", pc[:], pc_d, writes=["pc"])
                        P.dma_op("sp", "c4", cw[:], cw_d.rearrange("p (a b) -> p a b", b=5), writes=["cw"])
                        P.dma_op("sp", "c5", ghead[:], ghead_d, writes=["ghead"])
                        P.op("pool", lambda e: e.memset(C.epsb[:], EPS), writes=["epsb"])
                        P.op("pool", lambda e: e.memset(C.mhalf[:], -0.5), writes=["mhalf"])

                        def tr0(t):
                            transpose_tile16(P, C, hb[t % 3], f"hb{t%3}",
                                             lambda half, t=t: hT[:, half * 8:(half + 1) * 8, t * 128:(t + 1) * 128],
                                             [f"hT{t}a", f"hT{t}b"], "tr")
                        for t in range(NTILE):
                            xb = xt[t % 2]
                            P.dma_op("sp", f"x{t%2}", xb[:], x_d[t * 128:(t + 1) * 128, :], writes=[f"xt{t%2}"])
                            rms_tile(P, C, xb[:], f"xt{t%2}", gm[:], "gm", hb[t % 3][:], f"hb{t%3}", t)
                            if t >= 1:
                                tr0(t - 1)
                        tr0(NTILE - 1)
                        if dbg:
                            P.barrier()
                            P.dma_op("sp", "dbg0", D_["hT"], hT[:].rearrange("p a b -> p (a b)"))
                        P.barrier()
                        P.emit()
                    if upto < 1:
                        return
                    with ExitStack() as s1:
                        P = Prog(nc, s1, "p1")
                        wb = [_sb(nc, s1, f"wb{i}", [128, 16, 128], BF16) for i in range(2)]
                        wsc = wb[1]
                        pre = _sb(nc, s1, "pre", [128, T + 4], F32)
                        accs = [_sb(nc, s1, f"acc{i}", [128, T], F32) for i in range(2)]
                        sd = _sb(nc, s1, "sd", [128, T], F32)
                        pacc = [_ps(nc, s1, f"pacc{i}", [128, 512], F32) for i in range(4)]
                        pss = [_ps(nc, s1, f"pss{i}", [128, 512], F32) for i in range(4)]
                        hkeys = lambda blk: [f"hT{t}{s}" for t in range(4 * blk, 4 * blk + 4) for s in "ab"]
                        allh = [f"hT{t}{s}" for t in range(16) for s in "ab"]
                        P.op("pool", lambda e: e.memset(pre[:, 0:2], 0.0), writes=["pre"])
                        P.op("pool", lambda e: e.memset(pre[:, T + 2:T + 4], 0.0), writes=["pre"])
                        P.dma_op("pool", "wb1", wsc[:], wscal_d, writes=["wb1"])

                        def load_w(ci):
                            P.dma_op("pool", f"wb{ci%2}", wb[ci % 2][:], wmix_d[:, :, ci * 128:(ci + 1) * 128],
                                     writes=[f"wb{ci%2}"])
                        load_w(0)
                        for blk in range(4):
                            for kc in range(16):
                                P.op("pe", lambda e, blk=blk, kc=kc: e.matmul(
                                    out=pacc[blk][:], lhsT=wsc[:, kc, :], rhs=hT[:, kc, blk * 512:(blk + 1) * 512],
                                    start=(kc == 0), stop=(kc == 15)), reads=["wb1"] + hkeys(blk), banks=[f"pacc{blk}"])
                            P.op("act", lambda e, blk=blk: e.copy(out=rows[:, blk * 512:(blk + 1) * 512], in_=pacc[blk][:]),
                                 writes=["rows"], banks=[f"pacc{blk}"])
                        pending = []
                        for ci in range(16):
                            if ci + 1 < 16:
                                load_w(ci + 1)
                            w = wb[ci % 2]
                            isf = ci >= 12
                            acc = accs[ci % 2]
                            ak = f"acc{ci % 2}"
                            j = (ci - 12) if isf else ci // 3
                            kind = None if isf else ci % 3
                            for blk in range(4):
                                for kc in range(16):
                                    P.op("pe", lambda e, blk=blk, kc=kc, w=w: e.matmul(
                                        out=pacc[blk][:], lhsT=w[:, kc, :], rhs=hT[:, kc, blk * 512:(blk + 1) * 512],
                                        start=(kc == 0), stop=(kc == 15)),
                                        reads=[f"wb{ci%2}"] + hkeys(blk), banks=[f"pacc{blk}"])
                                if isf:
                                    P.op("dve", lambda e, blk=blk, j=j: e.tensor_copy(
                                        out=fT[:, j, blk * 512:(blk + 1) * 512], in_=pacc[blk][:]),
                                        writes=[f"fT{j}"], banks=[f"pacc{blk}"])
                                else:
                                    P.op("act", lambda e, blk=blk: e.copy(out=pre[:, 2 + blk * 512:2 + (blk + 1) * 512],
                                                                          in_=pacc[blk][:]),
                                         writes=["pre"], banks=[f"pacc{blk}"])
                            prev_b = pending
                            pending = []
                            if isf:
                                for fn in prev_b:
                                    fn()
                                continue
                            P.op("dve", lambda e, ci=ci, acc=acc: e.tensor_scalar(out=acc[:], in0=pre[:, 0:T], scalar1=cw[:, ci, 0:1],
                                                                        scalar2=None, op0=ALU.mult),
                                 reads=["pre", "cw"], writes=[ak])
                            for tap in range(1, 5):
                                P.op("dve", lambda e, ci=ci, tap=tap, acc=acc: e.scalar_tensor_tensor(
                                    out=acc[:], in0=pre[:, tap:tap + T], scalar=cw[:, ci, tap:tap + 1], in1=acc[:],
                                    op0=ALU.mult, op1=ALU.add), reads=["pre", "cw", ak], writes=[ak])
                            if kind == 1:
                                P.op("act", lambda e, j=j, acc=acc: e.activation(out=QKV[:, 1, j, :], in_=acc[:], func=AF.Silu),
                                     reads=[ak], writes=[f"qkv1_{j}"])
                                for fn in prev_b:
                                    fn()
                                continue
                            P.op("act", lambda e, acc=acc: e.activation(out=acc[:], in_=acc[:], func=AF.Silu),
                                 reads=[ak], writes=[ak])
                            sqv = QKV[:, kind, j, :]
                            sqk = f"qkv{kind}_{j}"
                            P.op("act", lambda e, acc=acc, sqv=sqv: e.activation(out=sqv, in_=acc[:], func=AF.Square),
                                 reads=[ak], writes=[sqk])
                            def part_b(acc=acc, ak=ak, sqv=sqv, sqk=sqk, kind=kind, j=j):
                                for blk in range(4):
                                    pb = pss[blk]
                                    P.op("pe", lambda e, blk=blk, pb=pb, sqv=sqv: e.matmul(out=pb[:], lhsT=C.onesb,
                                                                                 rhs=sqv[:, blk * 512:(blk + 1) * 512],
                                                                                 start=True, stop=True),
                                         reads=[sqk, "const"], banks=[f"pss{blk}"])
                                    P.op("act", lambda e, blk=blk, pb=pb: e.activation(
                                        out=sd[:, blk * 512:(blk + 1) * 512], in_=pb[:], func=AF.Ln, bias=C.epsb[:], scale=1.0),
                                        reads=["epsb"], writes=["sd"], banks=[f"pss{blk}"])
                                P.op("act", lambda e: e.activation(out=sd[:], in_=sd[:], func=AF.Exp, scale=-0.5),
                                     reads=["sd"], writes=["sd"])
                                scl = (128.0 ** -0.5) if kind == 2 else 1.0
                                P.op("dve", lambda e, kind=kind, j=j, scl=scl, acc=acc: e.scalar_tensor_tensor(
                                    out=QKV[:, kind, j, :], in0=acc[:], scalar=scl, in1=sd[:], op0=ALU.mult, op1=ALU.mult),
                                    reads=[ak, "sd"], writes=[f"qkv{kind}_{j}"])
                            pending.append(part_b)
                            for fn in prev_b:
                                fn()
                        assert not pending
                        if dbg:
                            P.barrier()
                            P.dma_op("sp", "dbg1", D_["qkv"], QKV[:].rearrange("p a b c -> p (a b c)"))
                            P.dma_op("sp", "dbg2", D_["rowsraw"], rows[:])
                            P.dma_op("sp", "dbg3", D_["fT"], fT[:].rearrange("p a b -> p (a b)"))
                        P.barrier()
                        P.emit()
                    if upto < 2:
                        return
                    with ExitStack() as s1:
                        P = Prog(nc, s1, "p1z")
                        wz = _sb(nc, s1, "wz", [128, 16, 512], BF16)
                        pacc = [_ps(nc, s1, f"pzacc{i}", [128, 512], F32) for i in range(4)]
                        P.dma_op("pool", "wz", wz[:], wz_d, writes=["wz"])
                        for t in range(NTILE):
                            pb = pacc[t % 4]
                            for kc in range(16):
                                P.op("pe", lambda e, t=t, kc=kc, pb=pb: e.matmul(
                                    out=pb[:], lhsT=hT[:, kc, t * 128:(t + 1) * 128], rhs=wz[:, kc, :],
                                    start=(kc == 0), stop=(kc == 15)), reads=["wz"], banks=[f"pz{t%4}"])
                            P.op("act", lambda e, t=t, pb=pb: e.activation(out=szt[:, t, :], in_=pb[:], func=AF.Silu),
                                 writes=[f"szt{t}"], banks=[f"pz{t%4}"])
                        if dbg:
                            P.barrier()
                            P.dma_op("sp", "dbg4", D_["szt"], szt[:].rearrange("p a b -> p (a b)"))
                        P.barrier()
                        P.emit()
                if upto < 3:
                    return
                with ExitStack() as s2:
                    P = Prog(nc, s2, "pf")
                    Y = _sb(nc, s2, "Y", [128, NTILE, H, 256], BF16)
                    DT = [_sb(nc, s2, f"DT{i}", [128, 16, 2, 512], BF16) for i in range(2)]
                    stg = [_sb(nc, s2, f"fstg{i}", [128, H, 512], BF16) for i in range(2)]
                    pf = [_ps(nc, s2, f"pf{i}", [128, 512], F32) for i in range(8)]
                    P.dma_op("sp", "dt0", DT[0][:], dft_d[0], writes=["DT0"])
                    P.dma_op("sp", "dt1", DT[1][:], dft_d[1], writes=["DT1"])
                    n = 0
                    for t in range(NTILE):
                        for gp in range(2):
                            pb = pf[n % 4]
                            for gi in range(2):
                                g = gp * 2 + gi
                                P.op("pe", lambda e, t=t, g=g, gi=gi, pb=pb: e.matmul(
                                    out=pb[:, gi * 256:(gi + 1) * 256], lhsT=fT[:, g, t * 128:(t + 1) * 128],
                                    rhs=C.CS.rearrange("p a b -> p (a b)"), start=True, stop=True),
                                    reads=["const"], banks=[f"pf{n%4}"])
                            if n % 2 == 0:
                                P.op("act", lambda e, t=t, gp=gp, pb=pb: e.copy(
                                    out=Y[:, t, gp * 2:gp * 2 + 2, :].rearrange("p a b -> p (a b)"), in_=pb[:]),
                                    writes=[f"Y{t}_{gp}"], banks=[f"pf{n%4}"])
                            else:
                                P.op("dve", lambda e, t=t, gp=gp, pb=pb: e.tensor_copy(
                                    out=Y[:, t, gp * 2:gp * 2 + 2, :].rearrange("p a b -> p (a b)"), in_=pb[:]),
                                    writes=[f"Y{t}_{gp}"], banks=[f"pf{n%4}"])
                            n += 1
                    yk = [f"Y{t}_{gp}" for t in range(NTILE) for gp in range(2)]
                    for blk in range(4):
                        dtb = DT[blk % 2]
                        sg = stg[blk % 2]
                        for g in range(H):
                            bi = (blk % 2) * 4 + g
                            pb = pf[bi]
                            for t in range(NTILE):
                                for cs in range(2):
                                    P.op("pe", lambda e, t=t, g=g, cs=cs, pb=pb, dtb=dtb: e.matmul(
                                        out=pb[:], lhsT=Y[:, t, g, cs * 128:(cs + 1) * 128], rhs=dtb[:, t, cs, :],
                                        start=(t == 0 and cs == 0), stop=(t == NTILE - 1 and cs == 1)),
                                        reads=yk + [f"DT{blk%2}"], banks=[f"pf{bi}"])
                            if g % 2 == 0:
                                P.op("act", lambda e, pb=pb, sg=sg, g=g: e.copy(out=sg[:, g, :], in_=pb[:]),
                                     writes=[f"fstg{blk%2}_{g}"], banks=[f"pf{bi}"])
                            else:
                                P.op("dve", lambda e, pb=pb, sg=sg, g=g: e.tensor_copy(out=sg[:, g, :], in_=pb[:]),
                                     writes=[f"fstg{blk%2}_{g}"], banks=[f"pf{bi}"])
                        dst = own_fr if blk < 2 else snd_fr
                        c0 = (blk % 2) * 512
                        P.dma_op("sp", f"fo{blk%2}", dst[:, :, c0:c0 + 512].rearrange("g p n -> p g n"), sg[:],
                                 reads=[f"fstg{blk%2}_{g}" for g in range(H)])
                        if blk + 2 < 4:
                            P.dma_op("sp", f"dt{blk%2}", DT[blk % 2][:], dft_d[blk + 2], writes=[f"DT{blk%2}"])
                    P.barrier()
                    P.emit()
            if upto < 4:
                return
            with ExitStack() as s3:
                build_gdn(nc, s3, C, QKV, szt, rows, pc, ghead, own_og, snd_og, D_, upto, C_selb_d)
    return


def build_gdn(nc, st, C, QKV, szt, rows, pc, ghead, own_og, snd_og, D_, upto=9, selb_d=None):
    sb = lambda name, shape, dt: _sb(nc, st, name, shape, dt)
    ps = lambda name, shape, dt: _ps(nc, st, name, shape, dt)
    selbt = sb("selbt", [128, 16, 128], BF16)
    C.selb = lambda kind, d, h: selbt[:, kind * 8 + d * 4 + h, :]
    Rhl = sb("Rhl", [128, 2, T], BF16)
    NRhl = sb("NRhl", [128, 2, T], BF16)
    CC = sb("CC", [128, NTILE, 3, 2, H], F32)
    GLbc = sb("GLbc", [128, 8, 32], F32)
    st1 = ExitStack()
    sb1 = lambda name, shape, dt: _sb(nc, st1, name, shape, dt)
    P = Prog(nc, st1, "pgs")
    R = sb1("R", [128, T], F32)
    NR = sb1("NR", [128, T], F32)
    Cs = sb1("Cs", [128, T], F32)
    R2 = sb1("R2", [128, T], F32)
    COLR = sb1("COLR", [128, NTILE, 128], F32)
    COLR2 = sb1("COLR2", [128, NTILE, 128], F32)
    m01 = sb1("m01", [128, T], F32)
    cmul = sb1("cmul", [128, 1], F32)
    GLrow = sb1("GLrow", [128, 32], F32)
    BK = [[ps(f"gB{d}{i}", [128, 4, 128], F32) for i in range(3)] for d in range(2)]
    W = [ps(f"gW{d}", [128, 4, 128], F32) for d in range(2)]
    B1, B2, B3 = BK[0]

    P.op("act", lambda e: e.activation(out=rows[:], in_=rows[:], func=AF.Exp, bias=pc[:, 0:1], scale=pc[:, 1:2]),
         reads=["rows", "pc"], writes=["rows"])
    P.op("act", lambda e: e.activation(out=rows[:], in_=rows[:], func=AF.Ln, bias=1.0, scale=1.0),
         reads=["rows"], writes=["rows"])
    P.op("act", lambda e: e.activation(out=cmul[:], in_=pc[:, 2:3], func=AF.Exp), reads=["pc"], writes=["cmul"])
    P.op("dve", lambda e: e.tensor_scalar(out=cmul[:], in0=cmul[:], scalar1=-1.0, scalar2=None, op0=ALU.mult),
         reads=["cmul"], writes=["cmul"])
    P.op("dve", lambda e: e.tensor_scalar(out=R[:], in0=rows[:], scalar1=cmul[:, 0:1], scalar2=None, op0=ALU.mult),
         reads=["rows", "cmul"], writes=["R"])
    P.op("pool", lambda e: e.memset(m01[:], 1.0), writes=["m01"])
    P.op("pool", lambda e: e.memset(m01[:].rearrange("p (c k) -> p c k", k=64)[:, :, 0:1], 0.0), writes=["m01"])
    v3 = lambda t: t[:].rearrange("p (c k) -> p c k", k=64)
    P.op("dve", lambda e: e.tensor_tensor_scan(out=Cs[:], data0=m01[:], data1=R[:], initial=0.0, op0=ALU.mult, op1=ALU.add),
         reads=["R", "m01"], writes=["Cs"])
    P.op("act", lambda e: e.copy(out=GLrow[:], in_=v3(Cs)[:, :, 63]), reads=["Cs"], writes=["GLrow"])
    P.op("dve", lambda e: e.tensor_tensor(out=v3(R2)[0:32], in0=v3(Cs)[0:32, :, 63:64].to_broadcast([32, 32, 64]),
                                          in1=v3(Cs)[0:32], op=ALU.subtract), reads=["Cs"], writes=["R2"])
    P.op("dve", lambda e: e.tensor_tensor(out=R2[32:64, :], in0=Cs[32:64, :], in1=R[32:64, :], op=ALU.subtract),
         reads=["Cs", "R"], writes=["R2"])
    P.op("pool", lambda e: e.memset(R2[64:128, :], 0.0), writes=["R2"])
    P.op("dve", lambda e: e.tensor_tensor(out=v3(R)[32:64], in0=v3(Cs)[32:64, :, 63:64].to_broadcast([32, 32, 64]),
                                          in1=v3(R2)[32:64], op=ALU.subtract), reads=["Cs", "R2", "R"], writes=["R"])
    P.op("act", lambda e: e.copy(out=R[0:32, :], in_=Cs[0:32, :]), reads=["Cs", "R"], writes=["R"])
    P.op("dve", lambda e: e.tensor_scalar(out=NR[:], in0=R[:], scalar1=-1.0, scalar2=None, op0=ALU.mult),
         reads=["R"], writes=["NR"])
    for t in range(NTILE):
        bk = B1 if t % 2 == 0 else B2
        bkn = "B1" if t % 2 == 0 else "B2"
        P.op("pe", lambda e, t=t, bk=bk: e.matmul(out=bk[:, 0, :], lhsT=R[:, t * 128:(t + 1) * 128], rhs=C.identf,
                                                  start=True, stop=True), reads=["R", "const"], banks=[bkn])
        P.op("pe", lambda e, t=t, bk=bk: e.matmul(out=bk[:, 1, :], lhsT=R2[:, t * 128:(t + 1) * 128], rhs=C.identf,
                                                  start=True, stop=True), reads=["R2", "const"], banks=[bkn])
        P.op("act", lambda e, t=t, bk=bk: e.copy(out=COLR[:, t, :], in_=bk[:, 0, :]), writes=["COLR"], banks=[bkn])
        P.op("dve", lambda e, t=t, bk=bk: e.tensor_copy(out=COLR2[:, t, :], in_=bk[:, 1, :]), writes=["COLR2"], banks=[bkn])
    for d in range(2):
        P.op("act", lambda e, d=d: e.activation(out=CC[:, :, 0, d, :], in_=COLR[:, :, 64 + 32 * d:64 + 32 * d + 4],
                                                func=AF.Exp), reads=["COLR"], writes=["CC"])
        P.op("dve", lambda e, d=d: e.tensor_tensor(out=CC[:, :, 1, d, :], in0=COLR[:, :, 32 * d:32 * d + 4],
                                                   in1=COLR[:, :, 64 + 32 * d:64 + 32 * d + 4], op=ALU.add),
             reads=["COLR"], writes=["CC"])
        P.op("act", lambda e, d=d: e.activation(out=CC[:, :, 1, d, :], in_=CC[:, :, 1, d, :], func=AF.Exp),
             reads=["CC"], writes=["CC"])
        P.op("act", lambda e, d=d: e.activation(out=CC[:, :, 2, d, :], in_=COLR2[:, :, 32 * d:32 * d + 4], func=AF.Exp),
             reads=["COLR2"], writes=["CC"])
    for d in range(2):
        for h in range(H):
            P.op("pe", lambda e, d=d, h=h: e.matmul(
                out=B3[:].rearrange("p a b -> p (a b)")[:, (d * 4 + h) * 32:(d * 4 + h + 1) * 32],
                lhsT=C.sel(0, d, h), rhs=GLrow[:], start=True, stop=True),
                 reads=["GLrow", "const"], banks=["B3"])
    P.op("act", lambda e: e.activation(out=GLbc[:].rearrange("p a b -> p (a b)"),
                                       in_=B3[:].rearrange("p a b -> p (a b)")[:, 0:256], func=AF.Exp),
         writes=["GLbc"], banks=["B3"])
    if D_:
        P.dma_op("sp", "dbg_rows", D_["rows"], R[:], reads=["R"])
        P.dma_op("sp", "dbg_cc", D_["cc"], CC[:].rearrange("p a b c d -> p (a b c d)"), reads=["CC"])
        P.dma_op("sp", "dbg_gl", D_["glbc"], GLbc[:].rearrange("p a b -> p (a b)"), reads=["GLbc"])

    P.dma_op("sp", "selb", selbt[:], selb_d.rearrange("p (a b) -> p a b", b=128), writes=["const"])
    for t in range(NTILE):
        P.op("pool", lambda e, t=t: e.tensor_tensor(out=szt[:, t, :].rearrange("p (a b) -> p a b", b=128),
                                                    in0=szt[:, t, :].rearrange("p (a b) -> p a b", b=128),
                                                    in1=ghead[:].unsqueeze(1).to_broadcast([128, 4, 128]), op=ALU.mult),
             reads=["ghead"], writes=[f"gz{t}"])
    P.op("act", lambda e: e.copy(out=Rhl[:, 0, :], in_=R[:]), reads=["R"], writes=["Rhi"])
    P.op("dve", lambda e: e.tensor_tensor(out=NR[:], in0=R[:], in1=Rhl[:, 0, :], op=ALU.subtract),
         reads=["R", "Rhi", "NR"], writes=["NR"])
    P.op("act", lambda e: e.copy(out=Rhl[:, 1, :], in_=NR[:]), reads=["NR"], writes=["Rlo"])
    P.op("dve", lambda e: e.tensor_scalar(out=NRhl[:].rearrange("p a b -> p (a b)"), in0=Rhl[:].rearrange("p a b -> p (a b)"),
                                          scalar1=-1.0, scalar2=None, op0=ALU.mult), reads=["Rhi", "Rlo"], writes=["NRhl"])
    P.barrier()
    P.emit()
    st1.close()
    if upto < 5:
        return
    P = Prog(nc, st, "pg")
    tmp = {}
    for d in range(2):
        tmp[d] = dict(
            E2=sb(f"E2_{d}", [128, 4, 128], F32), E3=sb(f"E3_{d}", [128, 4, 128], F32),
            NT=sb(f"NTm{d}", [128, 4, 128], BF16), Nm=sb(f"Nm{d}", [128, 4, 128], BF16),
            Pm=[sb(f"Pm{d}{i}", [128, 4, 128], BF16) for i in range(2)],
            PTm=[sb(f"PTm{d}{i}", [128, 4, 128], BF16) for i in range(2)],
            Rm=[sb(f"Rm{d}{i}", [128, 4, 128], BF16) for i in range(2)],
            kbg=sb(f"kbg{d}", [128, 4, 128], BF16), vb=sb(f"vb{d}", [128, 4, 128], BF16),
            egc=sb(f"egc{d}", [128, 4, 128], F32))
    slot = {}
    for d in range(2):
        for s in range(2):
            slot[(d, s)] = dict(
                qkT=sb(f"qkT{d}{s}", [128, 4, 128], BF16), u=sb(f"u{d}{s}", [128, 4, 128], F32),
                wT=sb(f"wT{d}{s}", [128, 4, 128], BF16), kd=sb(f"kd{d}{s}", [128, 4, 128], BF16),
                qd=sb(f"qd{d}{s}", [128, 4, 128], BF16))
    S32 = sb("S32", [128, 2, 4, 128], F32)
    Sbf = sb("Sbf", [128, 2, 4, 128], BF16)
    vnew = sb("vnew", [128, 2, 4, 128], BF16)
    ost = sb("ost", [128, 16, 4, 128], BF16)
    ot = sb("ot", [128, 2, 4, 128], F32)
    osq = sb("osq", [128, 2, 4, 128], F32)
    oss = sb("oss", [128, 2, 4, 2], F32)
    ogt = [sb(f"ogt{d}", [128, 4, 128], BF16) for d in range(2)]
    ostg = [sb(f"ostg{i}", [128, 4, 128], BF16) for i in range(2)]
    P.op("pool", lambda e: e.memset(S32[:].rearrange("p a b c -> p (a b c)"), 0.0), writes=["S32_0", "S32_1"])
    P.op("pool", lambda e: e.memset(Sbf[:].rearrange("p a b c -> p (a b c)"), 0.0), writes=["Sbf0", "Sbf1"])
    P.op("pool", lambda e: e.memset(vnew[:].rearrange("p a b c -> p (a b c)"), 0.0), writes=["vnew0", "vnew1"])

    f4 = lambda ap: ap.rearrange("p a b -> p (a b)")

    def prep(tt, d, s):
        sl = slot[(d, s)]
        tm = tmp[d]
        E2, E3, NT, Nm, Pm, PTm, Rm, kbg, vb, egc = (tm[k_] for k_ in ("E2", "E3", "NT", "Nm", "Pm", "PTm", "Rm", "kbg", "vb", "egc"))
        k = f"{d}{s}"
        B1, B2, B3 = BK[d]
        b1k, b2k, b3k = f"B1_{d}", f"B2_{d}", f"B3_{d}"
        tsl = slice(tt * 128, (tt + 1) * 128)
        bc = lambda kind: CC[:, tt, kind, d, :].unsqueeze(2).to_broadcast([128, 4, 128])
        for (incl, Et, ek) in ((0, E2, f"E2_{d}"), (1, E3, f"E3_{d}")):
            for h in range(H):
                for hl in range(2):
                    P.op("pe", lambda e, h=h, incl=incl, hl=hl: e.matmul(out=B3[:, h, :], lhsT=C.selb(1 - incl, d, h),
                                                                         rhs=Rhl[:, hl, tsl], start=(hl == 0), stop=False),
                         reads=["const"], banks=[b3k])
                for hl in range(2):
                    P.op("pe", lambda e, h=h, hl=hl: e.matmul(out=B3[:, h, :], lhsT=NRhl[:, hl, tsl], rhs=C.selb(0, d, h),
                                                              start=False, stop=False), reads=["const"], banks=[b3k])
                P.op("pe", lambda e, h=h, incl=incl: e.matmul(out=B3[:, h, :], lhsT=C.negI, rhs=C.mask(d, incl),
                                                              start=False, stop=True), reads=["const"], banks=[b3k])
            P.op("act", lambda e, Et=Et: e.activation(out=f4(Et[:]), in_=f4(B3[:]), func=AF.Exp),
                 writes=[ek], banks=[b3k])
            yield
            if incl == 0:
                for h in range(H):
                    P.op("pe", lambda e, h=h: e.matmul(out=B1[:, h, :], lhsT=QKV[:, 0, h, tsl], rhs=QKV[:, 0, h, tsl],
                                                       start=True, stop=True), banks=[b1k])
                P.op("dve", lambda e: e.scalar_tensor_tensor(out=f4(NT[:]), in0=f4(B1[:]), scalar=-1.0, in1=f4(E2[:]),
                                                             op0=ALU.mult, op1=ALU.mult),
                     reads=[f"E2_{d}"], writes=[f"NT{d}"], banks=[b1k])
                yield
                for h in range(H):
                    P.op("pe", lambda e, h=h: e.matmul(out=B2[:, h, :], lhsT=NT[:, h, :], rhs=C.identb, start=True, stop=True),
                         reads=[f"NT{d}", "const"], banks=[b2k])
                P.op("act", lambda e: e.copy(out=f4(Nm[:]), in_=f4(B2[:])), writes=[f"Nm{d}"], banks=[b2k])
                P.op("pool", lambda e: e.tensor_tensor(out=Rm[0][:], in0=NT[:],
                                                       in1=C.identb.unsqueeze(1).to_broadcast([128, 4, 128]), op=ALU.add),
                     reads=[f"NT{d}", "const"], writes=[f"Rm{d}0"])
                yield
        Pc, PTc, pk, ptk = Nm, NT, f"Nm{d}", f"NT{d}"
        ri = 0
        for lvl in range(5):
            Pn, PTn = Pm[lvl % 2], PTm[lvl % 2]
            pnk, ptnk = f"Pm{d}{lvl%2}", f"PTm{d}{lvl%2}"
            for h in range(H):
                P.op("pe", lambda e, h=h, Pc=Pc, PTc=PTc: e.matmul(out=B1[:, h, :], lhsT=PTc[:, h, :], rhs=Pc[:, h, :],
                                                                   start=True, stop=True), reads=[pk, ptk], banks=[b1k])
            P.op("act", lambda e, Pn=Pn: e.copy(out=f4(Pn[:]), in_=f4(B1[:])), writes=[pnk], banks=[b1k])
            if lvl < 4:
                for h in range(H):
                    P.op("pe", lambda e, h=h, Pc=Pc, PTc=PTc: e.matmul(out=B2[:, h, :], lhsT=Pc[:, h, :], rhs=PTc[:, h, :],
                                                                       start=True, stop=True), reads=[pk, ptk], banks=[b2k])
                if lvl % 2 == 0:
                    P.op("dve", lambda e, PTn=PTn: e.tensor_copy(out=f4(PTn[:]), in_=f4(B2[:])), writes=[ptnk], banks=[b2k])
                else:
                    P.op("act", lambda e, PTn=PTn: e.copy(out=f4(PTn[:]), in_=f4(B2[:])), writes=[ptnk], banks=[b2k])
            yield
            for h in range(H):
                P.op("pe", lambda e, h=h, Pn=Pn, ri=ri: e.matmul(out=B3[:, h, :], lhsT=Pn[:, h, :], rhs=Rm[ri][:, h, :],
                                                                 start=True, stop=True),
                     reads=[pnk, f"Rm{d}{ri}"], banks=[b3k])
            P.op("dve", lambda e, ri=ri: e.tensor_tensor(out=f4(Rm[1 - ri][:]), in0=f4(B3[:]), in1=f4(Rm[ri][:]), op=ALU.add),
                 reads=[f"Rm{d}{ri}"], writes=[f"Rm{d}{1-ri}"], banks=[b3k])
            ri = 1 - ri
            Pc, PTc, pk, ptk = Pn, PTn, pnk, ptnk
            yield
            if lvl == 0:
                for h in range(H):
                    P.op("pe", lambda e, h=h: e.matmul(out=B3[:, h, :], lhsT=QKV[:, 0, h, tsl], rhs=C.identb, start=True, stop=True),
                         reads=["const"], banks=[b3k])
                P.op("dve", lambda e: e.tensor_tensor(out=kbg[:], in0=B3[:], in1=bc(1), op=ALU.mult),
                     reads=["CC"], writes=[f"kbg{d}"], banks=[b3k])
                P.op("act", lambda e: e.copy(out=f4(egc[:]), in_=f4(B3[:])), writes=[f"egc{d}"], banks=[b3k])
                P.op("pool", lambda e: e.tensor_tensor(out=sl["kd"][:], in0=egc[:], in1=bc(2), op=ALU.mult),
                     reads=["CC", f"egc{d}"], writes=["kd" + k])
                yield
                for h in range(H):
                    P.op("pe", lambda e, h=h: e.matmul(out=B3[:, h, :], lhsT=QKV[:, 1, h, tsl], rhs=C.identb, start=True, stop=True),
                         reads=["const"], banks=[b3k])
                P.op("dve", lambda e: e.tensor_tensor(out=vb[:], in0=B3[:], in1=bc(0), op=ALU.mult),
                     reads=["CC"], writes=[f"vb{d}"], banks=[b3k])
                yield
            elif lvl == 1:
                for h in range(H):
                    P.op("pe", lambda e, h=h: e.matmul(out=B2[:, h, :], lhsT=QKV[:, 0, h, tsl], rhs=QKV[:, 2, h, tsl],
                                                       start=True, stop=True), banks=[b2k])
                P.op("dve", lambda e: e.tensor_tensor(out=f4(sl["qkT"][:]), in0=f4(B2[:]), in1=f4(E3[:]), op=ALU.mult),
                     reads=[f"E3_{d}"], writes=["qkT" + k], banks=[b2k])
                yield
            elif lvl == 2:
                for h in range(H):
                    for hl in range(2):
                        P.op("pe", lambda e, h=h, hl=hl: e.matmul(out=B3[:, h, :], lhsT=C.selb(0, d, h), rhs=Rhl[:, hl, tsl],
                                                                  start=(hl == 0), stop=(hl == 1)),
                             reads=["const"], banks=[b3k])
                P.op("act", lambda e: e.activation(out=f4(egc[:]), in_=f4(B3[:]), func=AF.Exp),
                     reads=["kd" + k], writes=[f"egc{d}"], banks=[b3k])
                P.op("pool", lambda e: e.tensor_tensor(out=sl["qd"][:], in0=QKV[:, 2, :, tsl], in1=egc[:], op=ALU.mult),
                     reads=[f"egc{d}"], writes=["qd" + k])
                yield
        Rf, rk = Rm[ri], f"Rm{d}{ri}"
        for h in range(H):
            P.op("pe", lambda e, h=h: e.matmul(out=B1[:, h, :], lhsT=Rf[:, h, :], rhs=vb[:, h, :], start=True, stop=True),
                 reads=[rk, f"vb{d}"], banks=[b1k])
        P.op("act", lambda e: e.copy(out=f4(sl["u"][:]), in_=f4(B1[:])), writes=["u" + k], banks=[b1k])
        for h in range(H):
            P.op("pe", lambda e, h=h: e.matmul(out=B2[:, h, :], lhsT=kbg[:, h, :], rhs=Rf[:, h, :], start=True, stop=True),
                 reads=[rk, f"kbg{d}"], banks=[b2k])
        P.op("dve", lambda e: e.tensor_copy(out=f4(sl["wT"][:]), in_=f4(B2[:])), writes=["wT" + k], banks=[b2k])
        yield

    def step(s):
        info = []
        for d in range(2):
            c = s if d == 0 else 31 - s
            tt, half = c // 2, c % 2
            sl = slot[(d, tt % 2)]
            k = f"{d}{tt%2}"
            rs = slice(half * 64, half * 64 + 64)
            info.append((c, tt, half, sl, k, rs))
            for h in range(H):
                P.op("pe", lambda e, h=h, sl=sl, d=d: e.matmul(out=W[d][:, h, :], lhsT=sl["wT"][:, h, :], rhs=Sbf[:, d, h, :],
                                                               start=True, stop=True),
                     reads=["wT" + k, f"Sbf{d}"], banks=[f"W{d}"])
            P.op("dve", lambda e, sl=sl, d=d, rs=rs: e.tensor_tensor(out=vnew[rs, d], in0=sl["u"][rs], in1=W[d][rs],
                                                                      op=ALU.subtract),
                 reads=["u" + k], writes=[f"vnew{d}"], banks=[f"W{d}"])
            P.op("pool", lambda e, d=d, c=c: e.tensor_tensor(
                out=S32[:, d], in0=S32[:, d], in1=GLbc[:, d * 4:d * 4 + 4, c:c + 1].to_broadcast([128, 4, 128]), op=ALU.mult),
                reads=["GLbc", f"S32_{d}"], writes=[f"S32_{d}"])
        yield
        for d in range(2):
            c, tt, half, sl, k, rs = info[d]
            for h in range(H):
                P.op("pe", lambda e, h=h, sl=sl, d=d, rs=rs: e.matmul(out=W[d][:, h, :], lhsT=sl["kd"][rs, h, :],
                                                                      rhs=vnew[rs, d, h, :], start=True, stop=True),
                     reads=["kd" + k, f"vnew{d}"], banks=[f"W{d}"])
            P.op("dve", lambda e, d=d: e.tensor_tensor(out=S32[:, d], in0=S32[:, d], in1=W[d][:], op=ALU.add),
                 reads=[f"S32_{d}"], writes=[f"S32_{d}"], banks=[f"W{d}"])
        yield
        for d in range(2):
            c, tt, half, sl, k, rs = info[d]
            for h in range(H):
                P.op("pe", lambda e, h=h, sl=sl, d=d: e.matmul(out=W[d][:, h, :], lhsT=sl["qd"][:, h, :], rhs=Sbf[:, d, h, :],
                                                               start=True, stop=False),
                     reads=["qd" + k, f"Sbf{d}"], banks=[f"W{d}"])
                P.op("pe", lambda e, h=h, sl=sl, d=d, rs=rs: e.matmul(out=W[d][:, h, :], lhsT=sl["qkT"][rs, h, :],
                                                                      rhs=vnew[rs, d, h, :], start=False, stop=True),
                     reads=["qkT" + k, f"vnew{d}"], banks=[f"W{d}"])
            if s < 16:
                P.op("act", lambda e, d=d, rs=rs, s=s: e.copy(out=ost[rs, s], in_=W[d][rs]),
                     writes=[f"ost{s}_{d}"], banks=[f"W{d}"])
            else:
                s0 = 31 - s
                P.op("dve", lambda e, d=d, rs=rs, s0=s0: e.tensor_tensor(out=ot[rs, d], in0=W[d][rs], in1=ost[rs, s0], op=ALU.add),
                     reads=[f"ost{s0}_{1-d}"], writes=[f"ot{d}"], banks=[f"W{d}"])
            P.op("act", lambda e, d=d: e.copy(out=Sbf[:, d], in_=S32[:, d]), reads=[f"S32_{d}"], writes=[f"Sbf{d}"])
        yield
        if s < 16:
            return
        for d in range(2):
            c, tt, half, sl, k, rs = info[d]
            for h in range(H):
                P.op("act", lambda e, d=d, rs=rs, h=h: e.activation(out=osq[rs, d, h, :], in_=ot[rs, d, h, :], func=AF.Square,
                                                                    accum_out=oss[rs, d, h, 0:1]),
                     reads=[f"ot{d}"], writes=[f"osq{d}", f"oss{d}"])
            P.op("pool", lambda e, d=d, rs=rs: e.tensor_scalar(out=oss[rs, d, :, 1], in0=oss[rs, d, :, 0], scalar1=1.0 / 128.0,
                                                               scalar2=EPS, op0=ALU.mult, op1=ALU.add),
                 reads=[f"oss{d}"], writes=[f"oss{d}"])
            P.op("pool", lambda e, d=d, rs=rs: e.tensor_tensor(out=oss[rs, d, :, 1], in0=oss[rs, d, :, 1],
                                                               in1=C.mhalf[rs, 0:1].to_broadcast([64, 4]), op=ALU.pow),
                 reads=[f"oss{d}", "mhalf"], writes=[f"oss{d}"])
        yield
        for d in range(2):
            c, tt, half, sl, k, rs = info[d]
            P.op("dve", lambda e, d=d, rs=rs: e.tensor_tensor(out=ot[rs, d], in0=ot[rs, d],
                                                              in1=oss[rs, d, :, 1:2].to_broadcast([64, 4, 128]), op=ALU.mult),
                 reads=[f"oss{d}", f"ot{d}"], writes=[f"ot{d}"])
            P.op("dve", lambda e, d=d, rs=rs, tt=tt: e.tensor_tensor(
                out=ogt[d][rs], in0=ot[rs, d], in1=szt[rs, tt, :].rearrange("p (a b) -> p a b", b=128), op=ALU.mult),
                reads=[f"ot{d}"], writes=[f"ogt{d}"])
        yield
        for d in range(2):
            c, tt, half, sl, k, rs = info[d]
            done = (half == 1) if d == 0 else (half == 0)
            if done:
                ij = C.Jb if d == 0 else C.identb
                for h in range(H):
                    P.op("pe", lambda e, h=h, d=d, ij=ij: e.matmul(out=BK[d][2][:, h, :], lhsT=ogt[d][:, h, :], rhs=ij,
                                                                   start=True, stop=True),
                         reads=[f"ogt{d}", "const"], banks=[f"B3_{d}"])
                P.op("act", lambda e, d=d: e.copy(out=f4(ostg[d][:]), in_=f4(BK[d][2][:])), writes=[f"ostg{d}"],
                     banks=[f"B3_{d}"])
                if d == 0:
                    dst, blk = snd_og, 15 - tt
                else:
                    dst, blk = own_og, tt
                P.dma_op("sp", f"og{d}", dst[:, :, blk * 128:(blk + 1) * 128].rearrange("g p n -> p g n"), ostg[d][:],
                         reads=[f"ostg{d}"])
        yield

    def steps2(a):
        yield from step(2 * a)
        yield from step(2 * a + 1)

    def run(gens):
        gens = list(gens)
        while gens:
            for g in list(gens):
                try:
                    next(g)
                except StopIteration:
                    gens.remove(g)

    nround = 16 if upto >= 9 else 2
    run([prep(0, 0, 0), prep(15, 1, 1)])
    for a in range(nround):
        gl = [steps2(a)]
        if a + 1 < 16:
            gl += [prep(a + 1, 0, (a + 1) % 2), prep(15 - (a + 1), 1, (15 - (a + 1)) % 2)]
        run(gl)
    P.barrier()
    P.emit()


def _consts(c):
    cf = np.zeros((128, 17, 128), np.float32)
    cf[:, 0, :] = np.eye(128, dtype=np.float32)
    for d in range(2):
        for h in range(H):
            cf[d * 32 + h, 1 + d * 4 + h, :] = 1.0
            cf[d * 32 + h, 1 + 8 + d * 4 + h, :] = 1.0
            cf[64 + d * 32 + h, 1 + 8 + d * 4 + h, :] = 1.0
    cb = np.zeros((128, 10, 128), np.float32)
    cb[:, 0, :] = np.eye(128)
    cb[:, 1, :] = 1.0
    cb[:, 2, :] = np.eye(128)[::-1]
    cb[:, 3, :] = -BIG * np.eye(128)
    j = np.arange(128)[:, None]
    i = np.arange(128)[None, :]
    same = (j // 64) == (i // 64)
    cb[:, 4, :] = 1.0 - (same & (i > j))
    cb[:, 5, :] = 1.0 - (same & (i >= j))
    cb[:, 6, :] = 1.0 - (same & (i < j))
    cb[:, 7, :] = 1.0 - (same & (i <= j))
    cc = np.arange(128)
    ang = 2 * np.pi * ((cc[:, None] * cc[None, :]) % 128) / 128.0
    cb[:, 8, :] = np.cos(ang)
    cb[:, 9, :] = -np.sin(ang)
    tin = np.arange(T)
    tg_in = tin if c == 0 else (T - 1 - tin)
    n = np.arange(T)
    tl = np.where(n < OWN, n, 3071 - n)
    tg_out = tl if c == 0 else (T - 1 - tl)
    prod = (tg_in[:, None].astype(np.int64) * tg_out[None, :].astype(np.int64)) % T
    ang = 2 * np.pi * prod / T
    norm = 1.0 / np.sqrt(T * 128.0)
    dtc = (np.cos(ang) * norm).astype(np.float32)
    dts = (np.sin(ang) * norm).astype(np.float32)
    dft = np.stack([dtc, dts], 0)
    dft = dft.reshape(2, 16, 128, 4, 512).transpose(3, 2, 1, 0, 4)
    selb = np.ascontiguousarray(cf[:, 1:17, :]).reshape(128, -1).astype(NPBF)
    return (cf.reshape(128, -1), cb.reshape(128, -1).astype(NPBF), np.ascontiguousarray(dft).astype(NPBF), selb)


def _klayout(w):
    K, N = w.shape
    return np.ascontiguousarray(w.reshape(K // 128, 128, N).transpose(1, 0, 2))


def prep_A(inp, b, c, consts):
    flip = c == 1
    x = inp["x"][b]
    x = x[::-1] if flip else x
    w_in = inp["w_in"][0]
    hs = [4 * c + j for j in range(4)]
    cols = []
    for j in hs:
        cols += [np.arange(1024 + j * 128, 1024 + (j + 1) * 128), np.arange(2048 + j * 128, 2048 + (j + 1) * 128),
                 np.arange(j * 128, (j + 1) * 128)]
    for j in hs:
        cols.append(np.arange(4096 + j * 128, 4096 + (j + 1) * 128))
    cols = np.concatenate(cols)
    wmix = _klayout(w_in[:, cols])
    zc = np.concatenate([np.arange(3072 + j * 128, 3072 + (j + 1) * 128) for j in hs])
    wz = _klayout(w_in[:, zc])
    kf, kb = (1, 0) if flip else (0, 1)
    wsc = np.zeros((D, 128), np.float32)
    for jj, j in enumerate(hs):
        wsc[:, 0 + jj] = w_in[:, 5120 + (2 + kf) * 8 + j]
        wsc[:, 32 + jj] = w_in[:, 5120 + (2 + kb) * 8 + j]
        wsc[:, 64 + jj] = w_in[:, 5120 + kf * 8 + j]
        wsc[:, 96 + jj] = w_in[:, 5120 + kb * 8 + j]
    wsc = _klayout(wsc)
    conv = inp["conv_w"][0]
    conv = conv[::-1] if flip else conv
    cw = np.zeros((128, 12, 5), np.float32)
    for jj, j in enumerate(hs):
        for kind, base in ((0, 1024), (1, 2048), (2, 0)):
            cw[:, 3 * jj + kind, :] = conv[:, base + j * 128: base + (j + 1) * 128].T
    alog = (inp["a_log_fwd"][0], inp["a_log_bwd"][0])
    dtb = (inp["dt_bias_fwd"][0], inp["dt_bias_bwd"][0])
    pc = np.zeros((128, 8), np.float32)
    for jj, j in enumerate(hs):
        pc[0 + jj, 0] = dtb[kf][j]
        pc[32 + jj, 0] = dtb[kb][j]
        pc[0 + jj, 2] = alog[kf][j]
        pc[32 + jj, 2] = alog[kb][j]
    pc[0:64, 1] = 1.0
    pc[64:128, 1] = -1.0
    cf, cb, dft, selb = consts[c]
    return {
        "selb": selb,
        "x": np.ascontiguousarray(x), "gmix": np.ascontiguousarray(np.tile(inp["g_mix"][0][None], (128, 1))),
        "wmix": wmix, "wz": wz, "wscal": wsc, "cw": cw.reshape(128, 60), "pc": pc,
        "ghead": np.ascontiguousarray(np.tile(inp["g_dn_head"][0][None], (128, 1))),
        "cf": cf, "cb": cb, "dft": dft,
    }


def build_B(nrcv=4):
    nc = bass.Bass("TRN2", target_bir_lowering=False)
    di = lambda name, shape, dt: nc.dram_tensor(name, shape, dt, kind="ExternalInput").ap()
    do = lambda name, shape, dt: nc.dram_tensor(name, shape, dt, kind="ExternalOutput").ap()
    x_d = di("xo", [OWN, D], F32)
    ogfr = [(di("og_own", [4, 128, OWN], BF16), 4), (di("og_rcv", [nrcv, 128, OWN], BF16), nrcv),
            (di("fr_own", [4, 128, OWN], BF16), 4), (di("fr_rcv", [nrcv, 128, OWN], BF16), nrcv)]
    w = declare_B_weights(nc, nrcv)
    out_d = do("out", [OWN, D], F32)
    build_B_body(nc, x_d, w["g3"], w["idb"], ogfr, nrcv, w["wst"], w["wo"], w["wup"], w["wdn"], out_d)
    return nc


def declare_B_weights(nc, nrcv):
    di = lambda name, shape, dt: nc.dram_tensor(name, shape, dt, kind="ExternalInput").ap()
    NW = 8 + 2 * nrcv + 32
    return dict(g3=di("g3", [3, 128, D], F32), idb=di("idb", [128, 128], BF16),
                wst=di("wst", [16, 128, NW, 128], F32), wo=di("wo", [4, 128, 16, 512], F32),
                wup=di("wup", [64, 128, 16, 128], F32), wdn=di("wdn", [4, 4, 128, 16, 512], F32))


def build_B_body(nc, x_d, g3_d, idb_d, ogfr_d, nrcv, wst_d, wo_d, wup_d, wdn_d, out_d, cc=None):
    NK = 8 + 2 * nrcv
    NW = NK + 32
    NT8 = OWN // 128
    with ExitStack() as stP:
        C = Ctx()
        C.nc = nc
        idb = _sb(nc, stP, "b_idb", [128, 128], BF16)
        C.identb = idb
        C.epsb = _sb(nc, stP, "b_epsb", [128, 1], F32)
        C.mhalf = _sb(nc, stP, "b_mhalf", [128, 1], F32)
        C.st = _sb(nc, stP, "b_st", [128, 8, 4], F32)
        C.junk = _sb(nc, stP, "b_junk", [128, D], BF16)
        hT = _sb(nc, stP, "b_hT", [128, 16, OWN], BF16)
        mU = _sb(nc, stP, "b_mU", [128, 16, OWN], BF16)
        gbuf = _sb(nc, stP, "b_gbuf", [128, D], F32)
        hb = _sb(nc, stP, "b_hb", [128, D], BF16)
        hkeys = lambda blk: [f"hT{t}{s}" for t in range(4 * blk, 4 * blk + 4) for s in "ab"]
        with ExitStack() as s:
            P = Prog(nc, s, "b1")
            xt = [_sb(nc, s, f"b_xt{i}", [128, D], F32) for i in range(2)]
            ogfr = _sb(nc, s, "b_ogfr", [128, NK, OWN], BF16)
            wsb = [_sb(nc, s, f"b_wsb{i}", [128, NW, 128], BF16) for i in range(2)]
            sg = [_sb(nc, s, f"b_sg{i}", [128, 512], F32) for i in range(2)]
            tt_ = [_sb(nc, s, f"b_t{i}", [128, 512], F32) for i in range(2)]
            C.tr = None
            pb = [_ps(nc, s, f"b_pc{i}", [128, 512], F32) for i in range(6)]
            C.tr = [_ps(nc, s, f"b_tr{i}", [128, 8, 128], BF16) for i in range(2)]
            P.dma_op("sp", "c0", idb[:], idb_d, writes=["const"])
            P.dma_op("sp", "c1", gbuf[:], g3_d[0], writes=["gbuf"])
            P.op("pool", lambda e: e.memset(C.epsb[:], EPS), writes=["epsb"])
            P.op("pool", lambda e: e.memset(C.mhalf[:], -0.5), writes=["mhalf"])
            if cc is not None:
                P.cc_allgather(*cc)

            def load_ws(oc):
                P.dma_op("pool", f"ws{oc%2}", wsb[oc % 2][:], wst_d[oc], writes=[f"wsb{oc%2}"])
            load_ws(0)
            hba = [hb, _sb(nc, s, "b_hba1", [128, D], BF16), _sb(nc, s, "b_hba2", [128, D], BF16)]

            def tra(t):
                transpose_tile16(P, C, hba[t % 3], f"hba{t%3}",
                                 lambda half, t=t: hT[:, half * 8:(half + 1) * 8, t * 128:(t + 1) * 128],
                                 [f"hT{t}a", f"hT{t}b"], "tr")
            for t in range(NT8):
                xb = xt[t % 2]
                P.dma_op("sp", f"x{t%2}", xb[:], x_d[t * 128:(t + 1) * 128, :], writes=[f"xt{t%2}"])
                rms_tile(P, C, xb[:], f"xt{t%2}", gbuf[:], "gbuf", hba[t % 3][:], f"hba{t%3}", t)
                if t >= 1:
                    tra(t - 1)
            tra(NT8 - 1)
            o0 = 0
            for (src, n) in ogfr_d:
                P.dma_op("sp", f"ogfr{o0}", ogfr[:, o0:o0 + n, :], src.rearrange("g p n -> p g n"), reads=["rcv"],
                         writes=["ogfr"])
                o0 += n
            nog = 4 + nrcv
            for oc in range(16):
                if oc + 1 < 16:
                    load_ws(oc + 1)
                w = wsb[oc % 2]
                wk = f"wsb{oc%2}"
                for blk in range(2):
                    bs = slice(blk * 512, (blk + 1) * 512)
                    Ya, Yf = pb[0], pb[1]
                    Ga, Gf = pb[2 + 2 * blk], pb[3 + 2 * blk]
                    gak, gfk = f"pc{2+2*blk}", f"pc{3+2*blk}"
                    for kc in range(16):
                        P.op("pe", lambda e, kc=kc, w=w, Ga=Ga, bs=bs: e.matmul(out=Ga[:], lhsT=w[:, NK + kc, :], rhs=hT[:, kc, bs],
                                                                             start=(kc == 0), stop=(kc == 15)),
                             reads=[wk] + hkeys(blk), banks=[gak])
                    P.op("act", lambda e, Ga=Ga: e.activation(out=sg[0][:], in_=Ga[:], func=AF.Sigmoid),
                         writes=["sg0"], banks=[gak])
                    for kc in range(16):
                        P.op("pe", lambda e, kc=kc, w=w, Gf=Gf, bs=bs: e.matmul(out=Gf[:], lhsT=w[:, NK + 16 + kc, :], rhs=hT[:, kc, bs],
                                                                             start=(kc == 0), stop=(kc == 15)),
                             reads=[wk] + hkeys(blk), banks=[gfk])
                    P.op("act", lambda e, Gf=Gf: e.activation(out=sg[1][:], in_=Gf[:], func=AF.Sigmoid),
                         writes=["sg1"], banks=[gfk])
                    for kc in range(nog):
                        P.op("pe", lambda e, kc=kc, w=w, bs=bs: e.matmul(out=Ya[:], lhsT=w[:, kc, :], rhs=ogfr[:, kc, bs],
                                                                      start=(kc == 0), stop=(kc == nog - 1)),
                             reads=[wk, "ogfr"], banks=["pc0"])
                    P.op("dve", lambda e: e.tensor_tensor(out=tt_[0][:], in0=Ya[:], in1=sg[0][:], op=ALU.mult),
                         reads=["sg0"], writes=["t0"], banks=["pc0"])
                    for kc in range(nog):
                        P.op("pe", lambda e, kc=kc, w=w, bs=bs: e.matmul(out=Yf[:], lhsT=w[:, nog + kc, :], rhs=ogfr[:, nog + kc, bs],
                                                                      start=(kc == 0), stop=(kc == nog - 1)),
                             reads=[wk, "ogfr"], banks=["pc1"])
                    P.op("dve", lambda e: e.tensor_tensor(out=tt_[1][:], in0=Yf[:], in1=sg[1][:], op=ALU.mult),
                         reads=["sg1"], writes=["t1"], banks=["pc1"])
                    P.op("pool", lambda e, oc=oc, bs=bs: e.tensor_tensor(out=mU[:, oc, bs], in0=tt_[0][:], in1=tt_[1][:], op=ALU.add),
                         reads=["t0", "t1"], writes=[f"mU{oc}"])
            P.barrier()
            P.emit()
        with ExitStack() as s:
            x1 = _sb(nc, s, "b_x1", [128, NT8, D], F32)
            mkeys = [f"mU{oc}" for oc in range(16)]
            with ExitStack() as s2:
                P = Prog(nc, s2, "b2")
                wob = [_sb(nc, s2, f"b_wob{i}", [128, 16, 512], BF16) for i in range(2)]
                hbs = [hb] + [_sb(nc, s2, f"b_hb{i}", [128, D], BF16) for i in range(2)]
                pb = [_ps(nc, s2, f"b_pd{i}", [128, 512], F32) for i in range(4)]
                C.tr = [_ps(nc, s2, f"b_tr2{i}", [128, 8, 128], BF16) for i in range(2)]
                P.dma_op("sp", "gl", gbuf[:], g3_d[1], writes=["gbuf"])
                for t in range(NT8):
                    P.dma_op("sp", f"x1_{t%4}", x1[:, t, :], x_d[t * 128:(t + 1) * 128, :], writes=[f"x1_{t}_{c}" for c in range(4)])
                P.dma_op("pool", "wo0", wob[0][:], wo_d[0], writes=["wob0"])
                n = 0
                for cb in range(4):
                    if cb + 1 < 4:
                        P.dma_op("pool", f"wo{(cb+1)%2}", wob[(cb + 1) % 2][:], wo_d[cb + 1], writes=[f"wob{(cb+1)%2}"])
                    w = wob[cb % 2]
                    for t in range(NT8):
                        bk = pb[n % 4]
                        for kc in range(16):
                            P.op("pe", lambda e, kc=kc, t=t, w=w, bk=bk: e.matmul(out=bk[:], lhsT=mU[:, kc, t * 128:(t + 1) * 128],
                                                                               rhs=w[:, kc, :], start=(kc == 0), stop=(kc == 15)),
                                 reads=[f"wob{cb%2}"], banks=[f"pd{n%4}"])
                        P.op("dve", lambda e, t=t, cb=cb, bk=bk: e.tensor_tensor(out=x1[:, t, cb * 512:(cb + 1) * 512],
                                                                                  in0=bk[:], in1=x1[:, t, cb * 512:(cb + 1) * 512], op=ALU.add),
                             reads=[f"x1_{t}_{cb}"], writes=[f"x1_{t}_{cb}"], banks=[f"pd{n%4}"])
                        n += 1
                        if cb == 3:
                            rms_tile(P, C, x1[:, t, :], [f"x1_{t}_{c}" for c in range(4)], gbuf[:], "gbuf",
                                     hbs[t % 3][:], f"hbs{t%3}", t)
                            if t >= 2:
                                tq = t - 2
                                transpose_tile16(P, C, hbs[tq % 3], f"hbs{tq%3}",
                                                 lambda half, tq=tq: hT[:, half * 8:(half + 1) * 8, tq * 128:(tq + 1) * 128],
                                                 [f"hT{tq}a", f"hT{tq}b"], "tr")
                for tq in (NT8 - 2, NT8 - 1):
                    transpose_tile16(P, C, hbs[tq % 3], f"hbs{tq%3}",
                                     lambda half, tq=tq: hT[:, half * 8:(half + 1) * 8, tq * 128:(tq + 1) * 128],
                                     [f"hT{tq}a", f"hT{tq}b"], "tr")
                P.barrier()
                P.emit()
            with ExitStack() as s2:
                P = Prog(nc, s2, "b3")
                wub = [_sb(nc, s2, f"b_wub{i}", [128, 16, 128], BF16) for i in range(2)]
                wdb = [_sb(nc, s2, f"b_wdb{i}", [128, 16, 512], BF16) for i in range(2)]
                rl = [_sb(nc, s2, f"b_rl{i}", [128, 512], F32) for i in range(2)]
                ob = [_sb(nc, s2, f"b_ob{i}", [128, D], F32) for i in range(2)]
                P.dma_op("sp", "gl", gbuf[:], g3_d[2], writes=["gbuf"])
                pu = [_ps(nc, s2, f"b_pu{i}", [128, 512], F32) for i in range(4)]
                pd = [_ps(nc, s2, f"b_pdn{i}", [128, 512], F32) for i in range(4)]
                nu = 0
                nd = 0
                nwd = 0

                def load_up(i):
                    P.dma_op("pool", f"wu{i%2}", wub[i % 2][:], wup_d[i], writes=[f"wub{i%2}"])

                def load_dn(i):
                    P.dma_op("pool", f"wd{i%2}", wdb[i % 2][:], wdn_d[i // 4, i % 4], writes=[f"wdb{i%2}"])
                load_up(0)
                load_dn(0)
                for q in range(4):
                    for j in range(16):
                        fc = q * 16 + j
                        if fc + 1 < 64:
                            load_up(fc + 1)
                        w = wub[fc % 2]
                        for blk in range(2):
                            bs = slice(blk * 512, (blk + 1) * 512)
                            bk = pu[nu % 4]
                            for kc in range(16):
                                P.op("pe", lambda e, kc=kc, w=w, bk=bk, bs=bs: e.matmul(out=bk[:], lhsT=w[:, kc, :], rhs=hT[:, kc, bs],
                                                                                     start=(kc == 0), stop=(kc == 15)),
                                     reads=[f"wub{fc%2}"], banks=[f"pu{nu%4}"])
                            r = rl[nu % 2]
                            P.op("act", lambda e, bk=bk, r=r: e.activation(out=r[:], in_=bk[:], func=AF.Relu),
                                 writes=[f"rl{nu%2}"], banks=[f"pu{nu%4}"])
                            eng = "dve" if nu % 2 == 0 else "pool"
                            P.op(eng, lambda e, r=r, j=j, bs=bs: e.tensor_tensor(out=mU[:, j, bs], in0=r[:], in1=r[:], op=ALU.mult),
                                 reads=[f"rl{nu%2}"], writes=[f"up{j}"])
                            nu += 1
                    upk = [f"up{j}" for j in range(16)]
                    for cb in range(4):
                        if nwd + 1 < 16:
                            load_dn(nwd + 1)
                        w = wdb[nwd % 2]
                        for t in range(NT8):
                            bk = pd[nd % 4]
                            for kc in range(16):
                                P.op("pe", lambda e, kc=kc, t=t, w=w, bk=bk: e.matmul(out=bk[:], lhsT=mU[:, kc, t * 128:(t + 1) * 128],
                                                                                   rhs=w[:, kc, :], start=(kc == 0), stop=(kc == 15)),
                                     reads=[f"wdb{nwd%2}"] + upk, banks=[f"pdn{nd%4}"])
                            P.op("dve", lambda e, t=t, cb=cb, bk=bk: e.tensor_tensor(out=x1[:, t, cb * 512:(cb + 1) * 512],
                                                                                      in0=bk[:], in1=x1[:, t, cb * 512:(cb + 1) * 512], op=ALU.add),
                                 reads=[f"x1_{t}_{cb}"], writes=[f"x1_{t}_{cb}"], banks=[f"pdn{nd%4}"])
                            nd += 1
                            if q == 3 and cb == 3:
                                rms_tile(P, C, x1[:, t, :], [f"x1_{t}_{c}" for c in range(4)], gbuf[:], "gbuf",
                                         ob[t % 2][:], f"ob{t%2}", t)
                                P.dma_op("sp", f"o{t%2}", out_d[t * 128:(t + 1) * 128, :], ob[t % 2][:], reads=[f"ob{t%2}"])
                        nwd += 1
                P.barrier()
                P.emit()


def prep_B(inp, b, c, rcv, idb):
    flip = c == 1
    x = inp["x"][b]
    x = x[::-1] if flip else x
    own = [4 * c + j for j in range(4)]
    oth = [4 * (1 - c) + j for j in range(4)]
    w_in = inp["w_in"][0]
    wdu = inp["w_dn_up"][0]
    wfo = inp["w_fourier"][0]
    rows = np.concatenate([np.arange(h * 128, (h + 1) * 128) for h in own + oth])
    ga = w_in[:, 5152:5152 + 2048]
    gf = w_in[:, 5152 + 2048:5152 + 4096]
    wst = np.concatenate([wdu[rows], wfo[rows], ga, gf], axis=0)
    nk = wst.shape[0] // 128
    wst = wst.reshape(nk, 128, 16, 128).transpose(2, 1, 0, 3)
    wo = inp["w_o"][0].reshape(16, 128, 4, 512).transpose(2, 1, 0, 3)
    wup = inp["w_mlp_up"][0].reshape(16, 128, 64, 128).transpose(2, 1, 0, 3)
    wdn = inp["w_mlp_down"][0].reshape(4, 16, 128, 4, 512).transpose(0, 3, 2, 1, 4)
    g3 = np.stack([np.tile(inp[k].reshape(-1)[None], (128, 1)) for k in ("g_mix", "g_mlp", "g_final")], 0)
    d = {
        "xo": np.ascontiguousarray(x[:OWN]), "g3": np.ascontiguousarray(g3), "idb": idb,
        "wst": np.ascontiguousarray(wst), "wo": np.ascontiguousarray(wo), "wup": np.ascontiguousarray(wup),
        "wdn": np.ascontiguousarray(wdn),
    }
    d.update(rcv)
    return d


def build_F():
    nc = bass.Bass("TRN2", target_bir_lowering=False)
    ins = declare_A_inputs(nc)
    w = declare_B_weights(nc, 8)
    out_d = nc.dram_tensor("out", [OWN, D], F32, kind="ExternalOutput").ap()
    own_og = nc.dram_tensor("i_own_og", [4, 128, OWN], BF16, kind="Internal").ap()
    own_fr = nc.dram_tensor("i_own_fr", [4, 128, OWN], BF16, kind="Internal").ap()
    snd = nc.dram_tensor("i_snd", [8 * 128, OWN], BF16, kind="Internal").ap()
    rcv = nc.dram_tensor("i_rcv", [2 * 8 * 128, OWN], BF16, kind="Internal").ap()
    snd3 = snd.rearrange("(g p) n -> g p n", p=128)
    rcv4 = rcv.rearrange("(r g p) n -> r g p n", r=2, p=128)
    emit_A(nc, ins, (own_og, snd3[0:4], own_fr, snd3[4:8]), {}, False, 9)
    ogfr = [(own_og, 4), (rcv4[0, 0:4], 4), (rcv4[1, 0:4], 4), (own_fr, 4), (rcv4[0, 4:8], 4), (rcv4[1, 4:8], 4)]
    build_B_body(nc, ins["x"][0:OWN, :], w["g3"], w["idb"], ogfr, 8, w["wst"], w["wo"], w["wup"], w["wdn"], out_d,
                 cc=(snd, rcv, [[0, 1], [2, 3], [4, 5], [6, 7]]))
    return nc


def prep_F(inp, b, c, consts, idb, shared):
    d = prep_A(inp, b, c, consts)
    wdu = inp["w_dn_up"][0]
    wfo = inp["w_fourier"][0]
    w_in = inp["w_in"][0]
    own = np.concatenate([np.arange(h * 128, (h + 1) * 128) for h in range(4 * c, 4 * c + 4)])
    z512 = np.zeros((512, D), np.float32)
    parts = []
    for wmat in (wdu, wfo):
        parts.append(wmat[own])
        parts.append(z512 if c == 0 else wmat[0:512])
        parts.append(z512 if c == 1 else wmat[512:1024])
    parts.append(w_in[:, 5152:5152 + 2048])
    parts.append(w_in[:, 5152 + 2048:5152 + 4096])
    wst = np.concatenate(parts, axis=0)
    nk = wst.shape[0] // 128
    wst = wst.reshape(nk, 128, 16, 128).transpose(2, 1, 0, 3)
    d["wst"] = np.ascontiguousarray(wst)
    d["idb"] = idb
    d.update(shared)
    return d


def _shared_B(inp):
    wo = inp["w_o"][0].reshape(16, 128, 4, 512).transpose(2, 1, 0, 3)
    wup = inp["w_mlp_up"][0].reshape(16, 128, 64, 128).transpose(2, 1, 0, 3)
    wdn = inp["w_mlp_down"][0].reshape(4, 16, 128, 4, 512).transpose(0, 3, 2, 1, 4)
    g3 = np.stack([np.tile(inp[k].reshape(-1)[None], (128, 1)) for k in ("g_mix", "g_mlp", "g_final")], 0)
    return {"wo": np.ascontiguousarray(wo), "wup": np.ascontiguousarray(wup), "wdn": np.ascontiguousarray(wdn),
            "g3": np.ascontiguousarray(g3)}


_CACHE = {}


def kernel(**inputs):
    inp = {k: np.asarray(v) for k, v in inputs.items()}
    if "F" not in _CACHE:
        _CACHE["F"] = build_F()
        _CACHE["consts"] = [_consts(0), _consts(1)]
    consts = _CACHE["consts"]
    idb = np.eye(128, dtype=np.float32).astype(NPBF)
    shared = _shared_B(inp)
    maps = [prep_F(inp, i // 2, i % 2, consts, idb, shared) for i in range(8)]
    res = run_bass_kernel_spmd(_CACHE["F"], maps, core_ids=list(range(8))).results
    out = np.zeros((4, T, D), np.float32)
    for i in range(8):
        b, c = i // 2, i % 2
        o = np.asarray(res[i]["out"])
        if c == 0:
            out[b, :OWN] = o
        else:
            out[b, OWN:] = o[::-1]
    return out
```

```python
from contextlib import ExitStack
import numpy as np
import ml_dtypes
import concourse.bass as bass
import concourse.mybir as mybir
from concourse.bass_utils import run_bass_kernel_spmd

F32 = mybir.dt.float32
BF16 = mybir.dt.bfloat16
AF = mybir.ActivationFunctionType
ALU = mybir.AluOpType
AX = mybir.AxisListType
NPBF = ml_dtypes.bfloat16

D = 2048
T = 2048
OWN = 1024
NTILE = 16
H = 4
EPS = 1e-6
DFF = 8192
BIG = 30000.0
SELF_SYNC = True


class Prog:
    ENGS = ("pe", "act", "dve", "pool", "sp")

    def __init__(self, nc, stack, tag=""):
        self.nc = nc
        self.stack = stack
        self.tag = tag
        self.streams = {e: [] for e in self.ENGS}
        self.count = {e: 0 for e in self.ENGS}
        self.sems = {e: stack.enter_context(nc.semaphore(f"pg{tag}_{e}")) for e in self.ENGS}
        self.waited = {e: {} for e in self.ENGS}
        self.last_w = {}
        self.readers = {}
        self.bank_last = {}
        self.dma = {}

    def _deps(self, eng, reads, writes, banks):
        deps = []
        for r in reads:
            lw = self.last_w.get(r)
            if lw is not None:
                deps.append(lw)
        for w in writes:
            lw = self.last_w.get(w)
            if lw is not None:
                deps.append(lw)
            deps.extend(self.readers.get(w, ()))
        need = {}
        for (k, v) in deps:
            if k == eng and (eng == "pe" or not SELF_SYNC):
                continue
            if need.get(k, 0) < v:
                need[k] = v
        for bk in banks:
            for k, v in self.bank_last.get(bk, {}).items():
                if k != eng and need.get(k, 0) < v:
                    need[k] = v
        for k, v in need.items():
            if self.waited[eng].get(k, 0) >= v:
                continue
            self.waited[eng][k] = v
            sem = self.sems[k] if k in self.sems else self.dma[k][0]
            self.streams[eng].append(("wait", sem, v))

    def _commit(self, ev, reads, writes, banks):
        for r in reads:
            self.readers.setdefault(r, []).append(ev)
        for w in writes:
            self.last_w[w] = ev
            self.readers[w] = []
        for bk in banks:
            self.bank_last.setdefault(bk, {})[ev[0]] = ev[1]

    def op(self, eng, fn, reads=(), writes=(), banks=()):
        self._deps(eng, reads, writes, banks)
        self.count[eng] += 1
        ev = (eng, self.count[eng])
        self.streams[eng].append(("op", fn))
        self._commit(ev, reads, writes, banks)
        return ev

    def dma_op(self, queue, key, out, in_, reads=(), writes=()):
        if key not in self.dma:
            self.dma[key] = [self.stack.enter_context(self.nc.semaphore(f"dm{self.tag}_{key}")), 0]
        ent = self.dma[key]
        self._deps(queue, reads, writes, ())
        if ent[1] and self.waited[queue].get(key, 0) < 16 * ent[1]:
            self.waited[queue][key] = 16 * ent[1]
            self.streams[queue].append(("wait", ent[0], 16 * ent[1]))
        ent[1] += 1
        ev = (key, 16 * ent[1])
        self.streams[queue].append(("dma", out, in_, ent[0]))
        self._commit(ev, reads, writes, ())
        return ev

    def cc_allgather(self, src, dst, groups):
        key = "cc"
        self.dma[key] = [self.stack.enter_context(self.nc.semaphore(f"cc{self.tag}")), 0]
        self.streams["pool"].append(("cc", src, dst, groups, self.dma[key][0]))
        self.dma[key][1] = 1.0 / 16.0
        self.last_w["rcv"] = (key, 1)
        self.readers["rcv"] = []

    def barrier(self):
        for e in self.ENGS:
            for k in self.ENGS:
                if k != e and self.count[k] > self.waited[e].get(k, 0):
                    self.waited[e][k] = self.count[k]
                    self.streams[e].append(("wait", self.sems[k], self.count[k]))
            for key, (sem, cnt) in self.dma.items():
                if cnt and self.waited[e].get(key, 0) < 16 * cnt:
                    self.waited[e][key] = int(16 * cnt)
                    self.streams[e].append(("wait", sem, int(16 * cnt)))

    def emit(self):
        nc = self.nc
        handles = {"pe": "tensor", "act": "scalar", "dve": "vector", "pool": "gpsimd", "sp": "sync"}
        with nc.Block() as block:
            for e in self.ENGS:
                stream = self.streams[e]
                sem = self.sems[e]

                def body(eng, stream=stream, sem=sem):
                    for it in stream:
                        if it[0] == "wait":
                            eng.wait_ge(it[1], it[2])
                        elif it[0] == "op":
                            it[1](eng).then_inc(sem, 1)
                        elif it[0] == "cc":
                            eng.collective_compute("AllGather", ALU.bypass, replica_groups=it[3], ins=[it[1]],
                                                   outs=[it[2]]).then_inc(it[4], 1)
                        else:
                            eng.dma_start(out=it[1], in_=it[2]).then_inc(it[3], 16)

                getattr(block, handles[e])(body)


class Ctx:
    pass


def _sb(nc, st, name, shape, dt):
    return st.enter_context(nc.sbuf_tensor("s_" + name, shape, dt))


def _ps(nc, st, name, shape, dt):
    return st.enter_context(nc.psum_tensor("p_" + name, shape, dt))


def rms_tile(P, C, xb, xkey, gbc, gkey, hb, hkey, col):
    xkeys = list(xkey) if isinstance(xkey, (list, tuple)) else [xkey]
    P.op("act", lambda e: e.activation(out=C.junk[:], in_=xb, func=AF.Square, accum_out=C.st[:, col, 0:1]),
         reads=xkeys, writes=["junk", "st"])
    P.op("pool", lambda e: e.tensor_scalar(out=C.st[:, col, 1:2], in0=C.st[:, col, 0:1], scalar1=1.0 / D, scalar2=EPS,
                                           op0=ALU.mult, op1=ALU.add), reads=["st"], writes=["st"])
    P.op("pool", lambda e: e.tensor_tensor(out=C.st[:, col, 2:3], in0=C.st[:, col, 1:2], in1=C.mhalf[:], op=ALU.pow),
         reads=["st", "mhalf"], writes=["st"])
    P.op("dve", lambda e: e.scalar_tensor_tensor(out=hb, in0=xb, scalar=C.st[:, col, 2:3], in1=gbc,
                                                 op0=ALU.mult, op1=ALU.mult),
         reads=xkeys + ["st", gkey], writes=[hkey])


def transpose_tile16(P, C, hb, hkey, dst, dkeys, trk):
    for half in range(2):
        pt = C.tr[half]
        for j in range(8):
            kc = half * 8 + j
            P.op("pe", lambda e, pt=pt, j=j, kc=kc: e.transpose(out=pt[:, j, :], in_=hb[:, kc * 128:(kc + 1) * 128],
                                                               identity=C.identb[:]),
                 reads=[hkey, "const"], banks=[f"{trk}{half}"])
        if half == 0:
            P.op("act", lambda e, pt=pt: e.copy(out=dst(0), in_=pt[:]), writes=[dkeys[0]], banks=[f"{trk}0"])
        else:
            P.op("dve", lambda e, pt=pt: e.tensor_copy(out=dst(1), in_=pt[:]), writes=[dkeys[1]], banks=[f"{trk}1"])


def build_A(dbg=False, upto=9):
    nc = bass.Bass("TRN2", target_bir_lowering=False)
    do = lambda name, shape, dt: nc.dram_tensor(name, shape, dt, kind="ExternalOutput").ap()
    ins = declare_A_inputs(nc)
    outs = (do("own_og", [4, 128, OWN], BF16), do("snd_og", [4, 128, OWN], BF16),
            do("own_fr", [4, 128, OWN], BF16), do("snd_fr", [4, 128, OWN], BF16))
    D_ = {}
    if dbg:
        D_["qkv"] = do("d_qkv", [128, 12 * T], BF16)
        D_["rowsraw"] = do("d_rowsraw", [128, T], F32)
        D_["fT"] = do("d_fT", [128, 4 * T], BF16)
        D_["hT"] = do("d_hT", [128, 16 * T], BF16)
        D_["rows"] = do("d_rows", [128, T], F32)
        D_["cc"] = do("d_cc", [128, 16 * 24], F32)
        D_["glbc"] = do("d_glbc", [128, 256], F32)
        D_["szt"] = do("d_szt", [128, 16 * 512], BF16)
    emit_A(nc, ins, outs, D_, dbg, upto)
    return nc


def declare_A_inputs(nc):
    di = lambda name, shape, dt: nc.dram_tensor(name, shape, dt, kind="ExternalInput").ap()
    return dict(
        x=di("x", [T, D], F32), gmix=di("gmix", [128, D], F32), wmix=di("wmix", [128, 16, 16 * 128], F32),
        wz=di("wz", [128, 16, 512], F32), wscal=di("wscal", [128, 16, 128], F32), cw=di("cw", [128, 60], F32),
        pc=di("pc", [128, 8], F32), ghead=di("ghead", [128, 128], F32), cf=di("cf", [128, 17 * 128], F32),
        cb=di("cb", [128, 10 * 128], BF16), selb=di("selb", [128, 16 * 128], BF16), dft=di("dft", [4, 128, 16, 2, 512], BF16))


def emit_A(nc, ins, outs, D_, dbg=False, upto=9):
    x_d, gmix_d, wmix_d, wz_d, wscal_d, cw_d, pc_d, ghead_d, cf_d, cb_d, dft_d = (
        ins[k] for k in ("x", "gmix", "wmix", "wz", "wscal", "cw", "pc", "ghead", "cf", "cb", "dft"))
    C_selb_d = ins["selb"]
    own_og, snd_og, own_fr, snd_fr = outs
    with ExitStack() as stA:
        C = Ctx()
        C.nc = nc
        cf = _sb(nc, stA, "cf", [128, 17, 128], F32)
        cb = _sb(nc, stA, "cb", [128, 10, 128], BF16)
        C.identf = cf[:, 0, :]
        C.sel = lambda kind, d, h: cf[:, 1 + kind * 8 + d * 4 + h, :]
        C.identb = cb[:, 0, :]
        C.onesb = cb[:, 1, :]
        C.Jb = cb[:, 2, :]
        C.negI = cb[:, 3, :]
        C.mask = lambda d, incl: cb[:, 4 + d * 2 + incl, :]
        C.CS = cb[:, 8:10, :]
        C.epsb = _sb(nc, stA, "epsb", [128, 1], F32)
        C.mhalf = _sb(nc, stA, "mhalf", [128, 1], F32)
        C.st = _sb(nc, stA, "st", [128, 16, 4], F32)
        pc = _sb(nc, stA, "pc", [128, 8], F32)
        cw = _sb(nc, stA, "cw", [128, 12, 5], F32)
        ghead = _sb(nc, stA, "ghead", [128, 128], F32)

        with ExitStack() as stB:
            QKV = _sb(nc, stB, "QKV", [128, 3, H, T], BF16)
            szt = _sb(nc, stB, "szt", [128, NTILE, 512], BF16)
            rows = _sb(nc, stB, "rows", [128, T], F32)
            with ExitStack() as stF:
                fT = _sb(nc, stF, "fT", [128, H, T], BF16)
                with ExitStack() as stH:
                    hT = _sb(nc, stH, "hT", [128, 16, T], BF16)
                    with ExitStack() as s0:
                        P = Prog(nc, s0, "p0")
                        xt = [_sb(nc, s0, f"xt{i}", [128, D], F32) for i in range(2)]
                        gm = _sb(nc, s0, "gm", [128, D], F32)
                        C.junk = _sb(nc, s0, "junk", [128, D], BF16)
                        hb = [_sb(nc, s0, f"hb{i}", [128, D], BF16) for i in range(3)]
                        C.tr = [_ps(nc, s0, f"tr{i}", [128, 8, 128], BF16) for i in range(2)]
                        P.dma_op("sp", "c0", cf[:], cf_d.rearrange("p (a b) -> p a b", b=128), writes=["const"])
                        P.dma_op("sp", "c1", cb[:], cb_d.rearrange("p (a b) -> p a b", b=128), writes=["const"])
                        P.dma_op("sp", "c2", gm[:], gmix_d, writes=["gm"])
                        P.dma_op("sp", "c3", pc[:], pc_d, writes=["pc"])
                        P.dma_op("sp", "c4", cw[:], cw_d.rearrange("p (a b) -> p a b", b=5), writes=["cw"])
                        P.dma_op("sp", "c5", ghead[:], ghead_d, writes=["ghead"])
                        P.op("pool", lambda e: e.memset(C.epsb[:], EPS), writes=["epsb"])
                        P.op("pool", lambda e: e.memset(C.mhalf[:], -0.5), writes=["mhalf"])

                        def tr0(t):
                            transpose_tile16(P, C, hb[t % 3], f"hb{t%3}",
                                             lambda half, t=t: hT[:, half * 8:(half + 1) * 8, t * 128:(t + 1) * 128],
                                             [f"hT{t}a", f"hT{t}b"], "tr")
                        for t in range(NTILE):
                            xb = xt[t % 2]
                            P.dma_op("sp", f"x{t%2}", xb[:], x_d[t * 128:(t + 1) * 128, :], writes=[f"xt{t%2}"])
                            rms_tile(P, C, xb[:], f"xt{t%2}", gm[:], "gm", hb[t % 3][:], f"hb{t%3}", t)
                            if t >= 1:
                                tr0(t - 1)
                        tr0(NTILE - 1)
                        if dbg:
                            P.barrier()
                            P.dma_op("sp", "dbg0", D_["hT"], hT[:].rearrange("p a b -> p (a b)"))
                        P.barrier()
                        P.emit()
                    if upto < 1:
                        return
                    with ExitStack() as s1:
                        P = Prog(nc, s1, "p1")
                        wb = [_sb(nc, s1, f"wb{i}", [128, 16, 128], BF16) for i in range(2)]
                        wsc = wb[1]
                        pre = _sb(nc, s1, "pre", [128, T + 4], F32)
                        accs = [_sb(nc, s1, f"acc{i}", [128, T], F32) for i in range(2)]
                        sd = _sb(nc, s1, "sd", [128, T], F32)
                        pacc = [_ps(nc, s1, f"pacc{i}", [128, 512], F32) for i in range(4)]
                        pss = [_ps(nc, s1, f"pss{i}", [128, 512], F32) for i in range(4)]
                        hkeys = lambda blk: [f"hT{t}{s}" for t in range(4 * blk, 4 * blk + 4) for s in "ab"]
                        allh = [f"hT{t}{s}" for t in range(16) for s in "ab"]
                        P.op("pool", lambda e: e.memset(pre[:, 0:2], 0.0), writes=["pre"])
                        P.op("pool", lambda e: e.memset(pre[:, T + 2:T + 4], 0.0), writes=["pre"])
                        P.dma_op("pool", "wb1", wsc[:], wscal_d, writes=["wb1"])

                        def load_w(ci):
                            P.dma_op("pool", f"wb{ci%2}", wb[ci % 2][:], wmix_d[:, :, ci * 128:(ci + 1) * 128],
                                     writes=[f"wb{ci%2}"])
                        load_w(0)
                        for blk in range(4):
                            for kc in range(16):
                                P.op("pe", lambda e, blk=blk, kc=kc: e.matmul(
                                    out=pacc[blk][:], lhsT=wsc[:, kc, :], rhs=hT[:, kc, blk * 512:(blk + 1) * 512],
                                    start=(kc == 0), stop=(kc == 15)), reads=["wb1"] + hkeys(blk), banks=[f"pacc{blk}"])
                            P.op("act", lambda e, blk=blk: e.copy(out=rows[:, blk * 512:(blk + 1) * 512], in_=pacc[blk][:]),
                                 writes=["rows"], banks=[f"pacc{blk}"])
                        pending = []
                        for ci in range(16):
                            if ci + 1 < 16:
                                load_w(ci + 1)
                            w = wb[ci % 2]
                            isf = ci >= 12
                            acc = accs[ci % 2]
                            ak = f"acc{ci % 2}"
                            j = (ci - 12) if isf else ci // 3
                            kind = None if isf else ci % 3
                            for blk in range(4):
                                for kc in range(16):
                                    P.op("pe", lambda e, blk=blk, kc=kc, w=w: e.matmul(
                                        out=pacc[blk][:], lhsT=w[:, kc, :], rhs=hT[:, kc, blk * 512:(blk + 1) * 512],
                                        start=(kc == 0), stop=(kc == 15)),
                                        reads=[f"wb{ci%2}"] + hkeys(blk), banks=[f"pacc{blk}"])
                                if isf:
                                    P.op("dve", lambda e, blk=blk, j=j: e.tensor_copy(
                                        out=fT[:, j, blk * 512:(blk + 1) * 512], in_=pacc[blk][:]),
                                        writes=[f"fT{j}"], banks=[f"pacc{blk}"])
                                else:
                                    P.op("act", lambda e, blk=blk: e.copy(out=pre[:, 2 + blk * 512:2 + (blk + 1) * 512],
                                                                          in_=pacc[blk][:]),
                                         writes=["pre"], banks=[f"pacc{blk}"])
                            prev_b = pending
                            pending = []
                            if isf:
                                for fn in prev_b:
                                    fn()
                                continue
                            P.op("dve", lambda e, ci=ci, acc=acc: e.tensor_scalar(out=acc[:], in0=pre[:, 0:T], scalar1=cw[:, ci, 0:1],
                                                                        scalar2=None, op0=ALU.mult),
                                 reads=["pre", "cw"], writes=[ak])
                            for tap in range(1, 5):
                                P.op("dve", lambda e, ci=ci, tap=tap, acc=acc: e.scalar_tensor_tensor(
                                    out=acc[:], in0=pre[:, tap:tap + T], scalar=cw[:, ci, tap:tap + 1], in1=acc[:],
                                    op0=ALU.mult, op1=ALU.add), reads=["pre", "cw", ak], writes=[ak])
                            if kind == 1:
                                P.op("act", lambda e, j=j, acc=acc: e.activation(out=QKV[:, 1, j, :], in_=acc[:], func=AF.Silu),
                                     reads=[ak], writes=[f"qkv1_{j}"])
                                for fn in prev_b:
                                    fn()
                                continue
                            P.op("act", lambda e, acc=acc: e.activation(out=acc[:], in_=acc[:], func=AF.Silu),
                                 reads=[ak], writes=[ak])
                            sqv = QKV[:, kind, j, :]
                            sqk = f"qkv{kind}_{j}"
                            P.op("act", lambda e, acc=acc, sqv=sqv: e.activation(out=sqv, in_=acc[:], func=AF.Square),
                                 reads=[ak], writes=[sqk])
                            def part_b(acc=acc, ak=ak, sqv=sqv, sqk=sqk, kind=kind, j=j):
                                for blk in range(4):
                                    pb = pss[blk]
                                    P.op("pe", lambda e, blk=blk, pb=pb, sqv=sqv: e.matmul(out=pb[:], lhsT=C.onesb,
                                                                                 rhs=sqv[:, blk * 512:(blk + 1) * 512],
                                                                                 start=True, stop=True),
                                         reads=[sqk, "const"], banks=[f"pss{blk}"])
                                    P.op("act", lambda e, blk=blk, pb=pb: e.activation(
                                        out=sd[:, blk * 512:(blk + 1) * 512], in_=pb[:], func=AF.Ln, bias=C.epsb[:], scale=1.0),
                                        reads=["epsb"], writes=["sd"], banks=[f"pss{blk}"])
                                P.op("act", lambda e: e.activation(out=sd[:], in_=sd[:], func=AF.Exp, scale=-0.5),
                                     reads=["sd"], writes=["sd"])
                                scl = (128.0 ** -0.5) if kind == 2 else 1.0
                                P.op("dve", lambda e, kind=kind, j=j, scl=scl, acc=acc: e.scalar_tensor_tensor(
                                    out=QKV[:, kind, j, :], in0=acc[:], scalar=scl, in1=sd[:], op0=ALU.mult, op1=ALU.mult),
                                    reads=[ak, "sd"], writes=[f"qkv{kind}_{j}"])
                            pending.append(part_b)
                            for fn in prev_b:
                                fn()
                        assert not pending
                        if dbg:
                            P.barrier()
                            P.dma_op("sp", "dbg1", D_["qkv"], QKV[:].rearrange("p a b c -> p (a b c)"))
                            P.dma_op("sp", "dbg2", D_["rowsraw"], rows[:])
                            P.dma_op("sp", "dbg3", D_["fT"], fT[:].rearrange("p a b -> p (a b)"))
                        P.barrier()
                        P.emit()
                    if upto < 2:
                        return
                    with ExitStack() as s1:
                        P = Prog(nc, s1, "p1z")
                        wz = _sb(nc, s1, "wz", [128, 16, 512], BF16)
                        pacc = [_ps(nc, s1, f"pzacc{i}", [128, 512], F32) for i in range(4)]
                        P.dma_op("pool", "wz", wz[:], wz_d, writes=["wz"])
                        for t in range(NTILE):
                            pb = pacc[t % 4]
                            for kc in range(16):
                                P.op("pe", lambda e, t=t, kc=kc, pb=pb: e.matmul(
                                    out=pb[:], lhsT=hT[:, kc, t * 128:(t + 1) * 128], rhs=wz[:, kc, :],
                                    start=(kc == 0), stop=(kc == 15)), reads=["wz"], banks=[f"pz{t%4}"])
                            P.op("act", lambda e, t=t, pb=pb: e.activation(out=szt[:, t, :], in_=pb[:], func=AF.Silu),
                                 writes=[f"szt{t}"], banks=[f"pz{t%4}"])
                        if dbg:
                            P.barrier()
                            P.dma_op("sp", "dbg4", D_["szt"], szt[:].rearrange("p a b -> p (a b)"))
                        P.barrier()
                        P.emit()
                if upto < 3:
                    return
                with ExitStack() as s2:
                    P = Prog(nc, s2, "pf")
                    Y = _sb(nc, s2, "Y", [128, NTILE, H, 256], BF16)
                    DT = [_sb(nc, s2, f"DT{i}", [128, 16, 2, 512], BF16) for i in range(2)]
                    stg = [_sb(nc, s2, f"fstg{i}", [128, H, 512], BF16) for i in range(2)]
                    pf = [_ps(nc, s2, f"pf{i}", [128, 512], F32) for i in range(8)]
                    P.dma_op("sp", "dt0", DT[0][:], dft_d[0], writes=["DT0"])
                    P.dma_op("sp", "dt1", DT[1][:], dft_d[1], writes=["DT1"])
                    n = 0
                    for t in range(NTILE):
                        for gp in range(2):
                            pb = pf[n % 4]
                            for gi in range(2):
                                g = gp * 2 + gi
                                P.op("pe", lambda e, t=t, g=g, gi=gi, pb=pb: e.matmul(
                                    out=pb[:, gi * 256:(gi + 1) * 256], lhsT=fT[:, g, t * 128:(t + 1) * 128],
                                    rhs=C.CS.rearrange("p a b -> p (a b)"), start=True, stop=True),
                                    reads=["const"], banks=[f"pf{n%4}"])
                            if n % 2 == 0:
                                P.op("act", lambda e, t=t, gp=gp, pb=pb: e.copy(
                                    out=Y[:, t, gp * 2:gp * 2 + 2, :].rearrange("p a b -> p (a b)"), in_=pb[:]),
                                    writes=[f"Y{t}_{gp}"], banks=[f"pf{n%4}"])
                            else:
                                P.op("dve", lambda e, t=t, gp=gp, pb=pb: e.tensor_copy(
                                    out=Y[:, t, gp * 2:gp * 2 + 2, :].rearrange("p a b -> p (a b)"), in_=pb[:]),
                                    writes=[f"Y{t}_{gp}"], banks=[f"pf{n%4}"])
                            n += 1
                    yk = [f"Y{t}_{gp}" for t in range(NTILE) for gp in range(2)]
                    for blk in range(4):
                        dtb = DT[blk % 2]
                        sg = stg[blk % 2]
                        for g in range(H):
                            bi = (blk % 2) * 4 + g
                            pb = pf[bi]
                            for t in range(NTILE):
                                for cs in range(2):
                                    P.op("pe", lambda e, t=t, g=g, cs=cs, pb=pb, dtb=dtb: e.matmul(
                                        out=pb[:], lhsT=Y[:, t, g, cs * 128:(cs + 1) * 128], rhs=dtb[:, t, cs, :],
                                        start=(t == 0 and cs == 0), stop=(t == NTILE - 1 and cs == 1)),
                                        reads=yk + [f"DT{blk%2}"], banks=[f"pf{bi}"])
                            if g % 2 == 0:
                                P.op("act", lambda e, pb=pb, sg=sg, g=g: e.copy(out=sg[:, g, :], in_=pb[:]),
                                     writes=[f"fstg{blk%2}_{g}"], banks=[f"pf{bi}"])
                            else:
                                P.op("dve", lambda e, pb=pb, sg=sg, g=g: e.tensor_copy(out=sg[:, g, :], in_=pb[:]),
                                     writes=[f"fstg{blk%2}_{g}"], banks=[f"pf{bi}"])
                        dst = own_fr if blk < 2 else snd_fr
                        c0 = (blk % 2) * 512
                        P.dma_op("sp", f"fo{blk%2}", dst[:, :, c0:c0 + 512].rearrange("g p n -> p g n"), sg[:],
                                 reads=[f"fstg{blk%2}_{g}" for g in range(H)])
                        if blk + 2 < 4:
                            P.dma_op("sp", f"dt{blk%2}", DT[blk % 2][:], dft_d[blk + 2], writes=[f"DT{blk%2}"])
                    P.barrier()
                    P.emit()
            if upto < 4:
                return
            with ExitStack() as s3:
                build_gdn(nc, s3, C, QKV, szt, rows, pc, ghead, own_og, snd_og, D_, upto, C_selb_d)
    return


def build_gdn(nc, st, C, QKV, szt, rows, pc, ghead, own_og, snd_og, D_, upto=9, selb_d=None):
    sb = lambda name, shape, dt: _sb(nc, st, name, shape, dt)
    ps = lambda name, shape, dt: _ps(nc, st, name, shape, dt)
    selbt = sb("selbt", [128, 16, 128], BF16)
    C.selb = lambda kind, d, h: selbt[:, kind * 8 + d * 4 + h, :]
    Rhl = sb("Rhl", [128, 2, T], BF16)
    NRhl = sb("NRhl", [128, 2, T], BF16)
    CC = sb("CC", [128, NTILE, 3, 2, H], F32)
    GLbc = sb("GLbc", [128, 8, 32], F32)
    st1 = ExitStack()
    sb1 = lambda name, shape, dt: _sb(nc, st1, name, shape, dt)
    P = Prog(nc, st1, "pgs")
    R = sb1("R", [128, T], F32)
    NR = sb1("NR", [128, T], F32)
    Cs = sb1("Cs", [128, T], F32)
    R2 = sb1("R2", [128, T], F32)
    COLR = sb1("COLR", [128, NTILE, 128], F32)
    COLR2 = sb1("COLR2", [128, NTILE, 128], F32)
    m01 = sb1("m01", [128, T], F32)
    cmul = sb1("cmul", [128, 1], F32)
    GLrow = sb1("GLrow", [128, 32], F32)
    BK = [[ps(f"gB{d}{i}", [128, 4, 128], F32) for i in range(3)] for d in range(2)]
    W = [ps(f"gW{d}", [128, 4, 128], F32) for d in range(2)]
    B1, B2, B3 = BK[0]

    P.op("act", lambda e: e.activation(out=rows[:], in_=rows[:], func=AF.Exp, bias=pc[:, 0:1], scale=pc[:, 1:2]),
         reads=["rows", "pc"], writes=["rows"])
    P.op("act", lambda e: e.activation(out=rows[:], in_=rows[:], func=AF.Ln, bias=1.0, scale=1.0),
         reads=["rows"], writes=["rows"])
    P.op("act", lambda e: e.activation(out=cmul[:], in_=pc[:, 2:3], func=AF.Exp), reads=["pc"], writes=["cmul"])
    P.op("dve", lambda e: e.tensor_scalar(out=cmul[:], in0=cmul[:], scalar1=-1.0, scalar2=None, op0=ALU.mult),
         reads=["cmul"], writes=["cmul"])
    P.op("dve", lambda e: e.tensor_scalar(out=R[:], in0=rows[:], scalar1=cmul[:, 0:1], scalar2=None, op0=ALU.mult),
         reads=["rows", "cmul"], writes=["R"])
    P.op("pool", lambda e: e.memset(m01[:], 1.0), writes=["m01"])
    P.op("pool", lambda e: e.memset(m01[:].rearrange("p (c k) -> p c k", k=64)[:, :, 0:1], 0.0), writes=["m01"])
    v3 = lambda t: t[:].rearrange("p (c k) -> p c k", k=64)
    P.op("dve", lambda e: e.tensor_tensor_scan(out=Cs[:], data0=m01[:], data1=R[:], initial=0.0, op0=ALU.mult, op1=ALU.add),
         reads=["R", "m01"], writes=["Cs"])
    P.op("act", lambda e: e.copy(out=GLrow[:], in_=v3(Cs)[:, :, 63]), reads=["Cs"], writes=["GLrow"])
    P.op("dve", lambda e: e.tensor_tensor(out=v3(R2)[0:32], in0=v3(Cs)[0:32, :, 63:64].to_broadcast([32, 32, 64]),
                                          in1=v3(Cs)[0:32], op=ALU.subtract), reads=["Cs"], writes=["R2"])
    P.op("dve", lambda e: e.tensor_tensor(out=R2[32:64, :], in0=Cs[32:64, :], in1=R[32:64, :], op=ALU.subtract),
         reads=["Cs", "R"], writes=["R2"])
    P.op("pool", lambda e: e.memset(R2[64:128, :], 0.0), writes=["R2"])
    P.op("dve", lambda e: e.tensor_tensor(out=v3(R)[32:64], in0=v3(Cs)[32:64, :, 63:64].to_broadcast([32, 32, 64]),
                                          in1=v3(R2)[32:64], op=ALU.subtract), reads=["Cs", "R2", "R"], writes=["R"])
    P.op("act", lambda e: e.copy(out=R[0:32, :], in_=Cs[0:32, :]), reads=["Cs", "R"], writes=["R"])
    P.op("dve", lambda e: e.tensor_scalar(out=NR[:], in0=R[:], scalar1=-1.0, scalar2=None, op0=ALU.mult),
         reads=["R"], writes=["NR"])
    for t in range(NTILE):
        bk = B1 if t % 2 == 0 else B2
        bkn = "B1" if t % 2 == 0 else "B2"
        P.op("pe", lambda e, t=t, bk=bk: e.matmul(out=bk[:, 0, :], lhsT=R[:, t * 128:(t + 1) * 128], rhs=C.identf,
                                                  start=True, stop=True), reads=["R", "const"], banks=[bkn])
        P.op("pe", lambda e, t=t, bk=bk: e.matmul(out=bk[:, 1, :], lhsT=R2[:, t * 128:(t + 1) * 128], rhs=C.identf,
                                                  start=True, stop=True), reads=["R2", "const"], banks=[bkn])
        P.op("act", lambda e, t=t, bk=bk: e.copy(out=COLR[:, t, :], in_=bk[:, 0, :]), writes=["COLR"], banks=[bkn])
        P.op("dve", lambda e, t=t, bk=bk: e.tensor_copy(out=COLR2[:, t, :], in_=bk[:, 1, :]), writes=["COLR2"], banks=[bkn])
    for d in range(2):
        P.op("act", lambda e, d=d: e.activation(out=CC[:, :, 0, d, :], in_=COLR[:, :, 64 + 32 * d:64 + 32 * d + 4],
                                                func=AF.Exp), reads=["COLR"], writes=["CC"])
        P.op("dve", lambda e, d=d: e.tensor_tensor(out=CC[:, :, 1, d, :], in0=COLR[:, :, 32 * d:32 * d + 4],
                                                   in1=COLR[:, :, 64 + 32 * d:64 + 32 * d + 4], op=ALU.add),
             reads=["COLR"], writes=["CC"])
        P.op("act", lambda e, d=d: e.activation(out=CC[:, :, 1, d, :], in_=CC[:, :, 1, d, :], func=AF.Exp),
             reads=["CC"], writes=["CC"])
        P.op("act", lambda e, d=d: e.activation(out=CC[:, :, 2, d, :], in_=COLR2[:, :, 32 * d:32 * d + 4], func=AF.Exp),
             reads=["COLR2"], writes=["CC"])
    for d in range(2):
        for h in range(H):
            P.op("pe", lambda e, d=d, h=h: e.matmul(
                out=B3[:].rearrange("p a b -> p (a b)")[:, (d * 4 + h) * 32:(d * 4 + h + 1) * 32],
                lhsT=C.sel(0, d, h), rhs=GLrow[:], start=True, stop=True),
                 reads=["GLrow", "const"], banks=["B3"])
    P.op("act", lambda e: e.activation(out=GLbc[:].rearrange("p a b -> p (a b)"),
                                       in_=B3[:].rearrange("p a b -> p (a b)")[:, 0:256], func=AF.Exp),
         writes=["GLbc"], banks=["B3"])
    if D_:
        P.dma_op("sp", "dbg_rows", D_["rows"], R[:], reads=["R"])
        P.dma_op("sp", "dbg_cc", D_["cc"], CC[:].rearrange("p a b c d -> p (a b c d)"), reads=["CC"])
        P.dma_op("sp", "dbg_gl", D_["glbc"], GLbc[:].rearrange("p a b -> p (a b)"), reads=["GLbc"])

    P.dma_op("sp", "selb", selbt[:], selb_d.rearrange("p (a b) -> p a b", b=128), writes=["const"])
    for t in range(NTILE):
        P.op("pool", lambda e, t=t: e.tensor_tensor(out=szt[:, t, :].rearrange("p (a b) -> p a b", b=128),
                                                    in0=szt[:, t, :].rearrange("p (a b) -> p a b", b=128),
                                                    in1=ghead[:].unsqueeze(1).to_broadcast([128, 4, 128]), op=ALU.mult),
             reads=["ghead"], writes=[f"gz{t}"])
    P.op("act", lambda e: e.copy(out=Rhl[:, 0, :], in_=R[:]), reads=["R"], writes=["Rhi"])
    P.op("dve", lambda e: e.tensor_tensor(out=NR[:], in0=R[:], in1=Rhl[:, 0, :], op=ALU.subtract),
         reads=["R", "Rhi", "NR"], writes=["NR"])
    P.op("act", lambda e: e.copy(out=Rhl[:, 1, :], in_=NR[:]), reads=["NR"], writes=["Rlo"])
    P.op("dve", lambda e: e.tensor_scalar(out=NRhl[:].rearrange("p a b -> p (a b)"), in0=Rhl[:].rearrange("p a b -> p (a b)"),
                                          scalar1=-1.0, scalar2=None, op0=ALU.mult), reads=["Rhi", "Rlo"], writes=["NRhl"])
    P.barrier()
    P.emit()
    st1.close()
    if upto < 5:
        return
    P = Prog(nc, st, "pg")
    tmp = {}
    for d in range(2):
        tmp[d] = dict(
            E2=sb(f"E2_{d}", [128, 4, 128], F32), E3=sb(f"E3_{d}", [128, 4, 128], F32),
            NT=sb(f"NTm{d}", [128, 4, 128], BF16), Nm=sb(f"Nm{d}", [128, 4, 128], BF16),
            Pm=[sb(f"Pm{d}{i}", [128, 4, 128], BF16) for i in range(2)],
            PTm=[sb(f"PTm{d}{i}", [128, 4, 128], BF16) for i in range(2)],
            Rm=[sb(f"Rm{d}{i}", [128, 4, 128], BF16) for i in range(2)],
            kbg=sb(f"kbg{d}", [128, 4, 128], BF16), vb=sb(f"vb{d}", [128, 4, 128], BF16),
            egc=sb(f"egc{d}", [128, 4, 128], F32))
    slot = {}
    for d in range(2):
        for s in range(2):
            slot[(d, s)] = dict(
                qkT=sb(f"qkT{d}{s}", [128, 4, 128], BF16), u=sb(f"u{d}{s}", [128, 4, 128], F32),
                wT=sb(f"wT{d}{s}", [128, 4, 128], BF16), kd=sb(f"kd{d}{s}", [128, 4, 128], BF16),
                qd=sb(f"qd{d}{s}", [128, 4, 128], BF16))
    S32 = sb("S32", [128, 2, 4, 128], F32)
    Sbf = sb("Sbf", [128, 2, 4, 128], BF16)
    vnew = sb("vnew", [128, 2, 4, 128], BF16)
    ost = sb("ost", [128, 16, 4, 128], BF16)
    ot = sb("ot", [128, 2, 4, 128], F32)
    osq = sb("osq", [128, 2, 4, 128], F32)
    oss = sb("oss", [128, 2, 4, 2], F32)
    ogt = [sb(f"ogt{d}", [128, 4, 128], BF16) for d in range(2)]
    ostg = [sb(f"ostg{i}", [128, 4, 128], BF16) for i in range(2)]
    P.op("pool", lambda e: e.memset(S32[:].rearrange("p a b c -> p (a b c)"), 0.0), writes=["S32_0", "S32_1"])
    P.op("pool", lambda e: e.memset(Sbf[:].rearrange("p a b c -> p (a b c)"), 0.0), writes=["Sbf0", "Sbf1"])
    P.op("pool", lambda e: e.memset(vnew[:].rearrange("p a b c -> p (a b c)"), 0.0), writes=["vnew0", "vnew1"])

    f4 = lambda ap: ap.rearrange("p a b -> p (a b)")

    def prep(tt, d, s):
        sl = slot[(d, s)]
        tm = tmp[d]
        E2, E3, NT, Nm, Pm, PTm, Rm, kbg, vb, egc = (tm[k_] for k_ in ("E2", "E3", "NT", "Nm", "Pm", "PTm", "Rm", "kbg", "vb", "egc"))
        k = f"{d}{s}"
        B1, B2, B3 = BK[d]
        b1k, b2k, b3k = f"B1_{d}", f"B2_{d}", f"B3_{d}"
        tsl = slice(tt * 128, (tt + 1) * 128)
        bc = lambda kind: CC[:, tt, kind, d, :].unsqueeze(2).to_broadcast([128, 4, 128])
        for (incl, Et, ek) in ((0, E2, f"E2_{d}"), (1, E3, f"E3_{d}")):
            for h in range(H):
                for hl in range(2):
                    P.op("pe", lambda e, h=h, incl=incl, hl=hl: e.matmul(out=B3[:, h, :], lhsT=C.selb(1 - incl, d, h),
                                                                         rhs=Rhl[:, hl, tsl], start=(hl == 0), stop=False),
                         reads=["const"], banks=[b3k])
                for hl in range(2):
                    P.op("pe", lambda e, h=h, hl=hl: e.matmul(out=B3[:, h, :], lhsT=NRhl[:, hl, tsl], rhs=C.selb(0, d, h),
                                                              start=False, stop=False), reads=["const"], banks=[b3k])
                P.op("pe", lambda e, h=h, incl=incl: e.matmul(out=B3[:, h, :], lhsT=C.negI, rhs=C.mask(d, incl),
                                                              start=False, stop=True), reads=["const"], banks=[b3k])
            P.op("act", lambda e, Et=Et: e.activation(out=f4(Et[:]), in_=f4(B3[:]), func=AF.Exp),
                 writes=[ek], banks=[b3k])
            yield
            if incl == 0:
                for h in range(H):
                    P.op("pe", lambda e, h=h: e.matmul(out=B1[:, h, :], lhsT=QKV[:, 0, h, tsl], rhs=QKV[:, 0, h, tsl],
                                                       start=True, stop=True), banks=[b1k])
                P.op("dve", lambda e: e.scalar_tensor_tensor(out=f4(NT[:]), in0=f4(B1[:]), scalar=-1.0, in1=f4(E2[:]),
                                                             op0=ALU.mult, op1=ALU.mult),
                     reads=[f"E2_{d}"], writes=[f"NT{d}"], banks=[b1k])
                yield
                for h in range(H):
                    P.op("pe", lambda e, h=h: e.matmul(out=B2[:, h, :], lhsT=NT[:, h, :], rhs=C.identb, start=True, stop=True),
                         reads=[f"NT{d}", "const"], banks=[b2k])
                P.op("act", lambda e: e.copy(out=f4(Nm[:]), in_=f4(B2[:])), writes=[f"Nm{d}"], banks=[b2k])
                P.op("pool", lambda e: e.tensor_tensor(out=Rm[0][:], in0=NT[:],
                                                       in1=C.identb.unsqueeze(1).to_broadcast([128, 4, 128]), op=ALU.add),
                     reads=[f"NT{d}", "const"], writes=[f"Rm{d}0"])
                yield
        Pc, PTc, pk, ptk = Nm, NT, f"Nm{d}", f"NT{d}"
        ri = 0
        for lvl in range(5):
            Pn, PTn = Pm[lvl % 2], PTm[lvl % 2]
            pnk, ptnk = f"Pm{d}{lvl%2}", f"PTm{d}{lvl%2}"
            for h in range(H):
                P.op("pe", lambda e, h=h, Pc=Pc, PTc=PTc: e.matmul(out=B1[:, h, :], lhsT=PTc[:, h, :], rhs=Pc[:, h, :],
                                                                   start=True, stop=True), reads=[pk, ptk], banks=[b1k])
            P.op("act", lambda e, Pn=Pn: e.copy(out=f4(Pn[:]), in_=f4(B1[:])), writes=[pnk], banks=[b1k])
            if lvl < 4:
                for h in range(H):
                    P.op("pe", lambda e, h=h, Pc=Pc, PTc=PTc: e.matmul(out=B2[:, h, :], lhsT=Pc[:, h, :], rhs=PTc[:, h, :],
                                                                       start=True, stop=True), reads=[pk, ptk], banks=[b2k])
                if lvl % 2 == 0:
                    P.op("dve", lambda e, PTn=PTn: e.tensor_copy(out=f4(PTn[:]), in_=f4(B2[:])), writes=[ptnk], banks=[b2k])
                else:
                    P.op("act", lambda e, PTn=PTn: e.copy(out=f4(PTn[:]), in_=f4(B2[:])), writes=[ptnk], banks=[b2k])
            yield
            for h in range(H):
                P.op("pe", lambda e, h=h, Pn=Pn, ri=ri: e.matmul(out=B3[:, h, :], lhsT=Pn[:, h, :], rhs=Rm[ri][:, h, :],
                                                                 start=True, stop=True),
                     reads=[pnk, f"Rm{d}{ri}"], banks=[b3k])
            P.op("dve", lambda e, ri=ri: e.tensor_tensor(out=f4(Rm[1 - ri][:]), in0=f4(B3[:]), in1=f4(Rm[ri][:]), op=ALU.add),
                 reads=[f"Rm{d}{ri}"], writes=[f"Rm{d}{1-ri}"], banks=[b3k])
            ri = 1 - ri
            Pc, PTc, pk, ptk = Pn, PTn, pnk, ptnk
            yield
            if lvl == 0:
                for h in range(H):
                    P.op("pe", lambda e, h=h: e.matmul(out=B3[:, h, :], lhsT=QKV[:, 0, h, tsl], rhs=C.identb, start=True, stop=True),
                         reads=["const"], banks=[b3k])
                P.op("dve", lambda e: e.tensor_tensor(out=kbg[:], in0=B3[:], in1=bc(1), op=ALU.mult),
                     reads=["CC"], writes=[f"kbg{d}"], banks=[b3k])
                P.op("act", lambda e: e.copy(out=f4(egc[:]), in_=f4(B3[:])), writes=[f"egc{d}"], banks=[b3k])
                P.op("pool", lambda e: e.tensor_tensor(out=sl["kd"][:], in0=egc[:], in1=bc(2), op=ALU.mult),
                     reads=["CC", f"egc{d}"], writes=["kd" + k])
                yield
                for h in range(H):
                    P.op("pe", lambda e, h=h: e.matmul(out=B3[:, h, :], lhsT=QKV[:, 1, h, tsl], rhs=C.identb, start=True, stop=True),
                         reads=["const"], banks=[b3k])
                P.op("dve", lambda e: e.tensor_tensor(out=vb[:], in0=B3[:], in1=bc(0), op=ALU.mult),
                     reads=["CC"], writes=[f"vb{d}"], banks=[b3k])
                yield
            elif lvl == 1:
                for h in range(H):
                    P.op("pe", lambda e, h=h: e.matmul(out=B2[:, h, :], lhsT=QKV[:, 0, h, tsl], rhs=QKV[:, 2, h, tsl],
                                                       start=True, stop=True), banks=[b2k])
                P.op("dve", lambda e: e.tensor_tensor(out=f4(sl["qkT"][:]), in0=f4(B2[:]), in1=f4(E3[:]), op=ALU.mult),
                     reads=[f"E3_{d}"], writes=["qkT" + k], banks=[b2k])
                yield
            elif lvl == 2:
                for h in range(H):
                    for hl in range(2):
                        P.op("pe", lambda e, h=h, hl=hl: e.matmul(out=B3[:, h, :], lhsT=C.selb(0, d, h), rhs=Rhl[:, hl, tsl],
                                                                  start=(hl == 0), stop=(hl == 1)),
                             reads=["const"], banks=[b3k])
                P.op("act", lambda e: e.activation(out=f4(egc[:]), in_=f4(B3[:]), func=AF.Exp),
                     reads=["kd" + k], writes=[f"egc{d}"], banks=[b3k])
                P.op("pool", lambda e: e.tensor_tensor(out=sl["qd"][:], in0=QKV[:, 2, :, tsl], in1=egc[:], op=ALU.mult),
                     reads=[f"egc{d}"], writes=["qd" + k])
                yield
        Rf, rk = Rm[ri], f"Rm{d}{ri}"
        for h in range(H):
            P.op("pe", lambda e, h=h: e.matmul(out=B1[:, h, :], lhsT=Rf[:, h, :], rhs=vb[:, h, :], start=True, stop=True),
                 reads=[rk, f"vb{d}"], banks=[b1k])
        P.op("act", lambda e: e.copy(out=f4(sl["u"][:]), in_=f4(B1[:])), writes=["u" + k], banks=[b1k])
        for h in range(H):
            P.op("pe", lambda e, h=h: e.matmul(out=B2[:, h, :], lhsT=kbg[:, h, :], rhs=Rf[:, h, :], start=True, stop=True),
                 reads=[rk, f"kbg{d}"], banks=[b2k])
        P.op("dve", lambda e: e.tensor_copy(out=f4(sl["wT"][:]), in_=f4(B2[:])), writes=["wT" + k], banks=[b2k])
        yield

    def step(s, info):
        for d in range(2):
            c = s if d == 0 else 31 - s
            tt, half = c // 2, c % 2
            sl = slot[(d, tt % 2)]
            k = f"{d}{tt%2}"
            rs = slice(half * 64, half * 64 + 64)
            info.append((c, tt, half, sl, k, rs))
            for h in range(H):
                P.op("pe", lambda e, h=h, sl=sl, d=d: e.matmul(out=W[d][:, h, :], lhsT=sl["wT"][:, h, :], rhs=Sbf[:, d, h, :],
                                                               start=True, stop=True),
                     reads=["wT" + k, f"Sbf{d}"], banks=[f"W{d}"])
            P.op("dve", lambda e, sl=sl, d=d, rs=rs: e.tensor_tensor(out=vnew[rs, d], in0=sl["u"][rs], in1=W[d][rs],
                                                                      op=ALU.subtract),
                 reads=["u" + k], writes=[f"vnew{d}"], banks=[f"W{d}"])
            P.op("pool", lambda e, d=d, c=c: e.tensor_tensor(
                out=S32[:, d], in0=S32[:, d], in1=GLbc[:, d * 4:d * 4 + 4, c:c + 1].to_broadcast([128, 4, 128]), op=ALU.mult),
                reads=["GLbc", f"S32_{d}"], writes=[f"S32_{d}"])
        yield
        for d in range(2):
            c, tt, half, sl, k, rs = info[d]
            for h in range(H):
                P.op("pe", lambda e, h=h, sl=sl, d=d, rs=rs: e.matmul(out=W[d][:, h, :], lhsT=sl["kd"][rs, h, :],
                                                                      rhs=vnew[rs, d, h, :], start=True, stop=True),
                     reads=["kd" + k, f"vnew{d}"], banks=[f"W{d}"])
            P.op("dve", lambda e, d=d: e.tensor_tensor(out=S32[:, d], in0=S32[:, d], in1=W[d][:], op=ALU.add),
                 reads=[f"S32_{d}"], writes=[f"S32_{d}"], banks=[f"W{d}"])
        yield
        for d in range(2):
            c, tt, half, sl, k, rs = info[d]
            for h in range(H):
                P.op("pe", lambda e, h=h, sl=sl, d=d: e.matmul(out=W[d][:, h, :], lhsT=sl["qd"][:, h, :], rhs=Sbf[:, d, h, :],
                                                               start=True, stop=False),
                     reads=["qd" + k, f"Sbf{d}"], banks=[f"W{d}"])
                P.op("pe", lambda e, h=h, sl=sl, d=d, rs=rs: e.matmul(out=W[d][:, h, :], lhsT=sl["qkT"][rs, h, :],
                                                                      rhs=vnew[rs, d, h, :], start=False, stop=True),
                     reads=["qkT" + k, f"vnew{d}"], banks=[f"W{d}"])
            if s < 16:
                P.op("act", lambda e, d=d, rs=rs, s=s: e.copy(out=ost[rs, s], in_=W[d][rs]),
                     writes=[f"ost{s}_{d}"], banks=[f"W{d}"])
            else:
                s0 = 31 - s
                P.op("dve", lambda e, d=d, rs=rs, s0=s0: e.tensor_tensor(out=ot[rs, d], in0=W[d][rs], in1=ost[rs, s0], op=ALU.add),
                     reads=[f"ost{s0}_{1-d}"], writes=[f"ot{d}_{half}"], banks=[f"W{d}"])
            P.op("act", lambda e, d=d: e.copy(out=Sbf[:, d], in_=S32[:, d]), reads=[f"S32_{d}"], writes=[f"Sbf{d}"])
        yield

    def tail(s, info):
        for d in range(2):
            c, tt, half, sl, k, rs = info[d]
            for h in range(H):
                P.op("act", lambda e, d=d, rs=rs, h=h: e.activation(out=osq[rs, d, h, :], in_=ot[rs, d, h, :], func=AF.Square,
                                                                    accum_out=oss[rs, d, h, 0:1]),
                     reads=[f"ot{d}_{half}"], writes=[f"osq{d}_{half}", f"oss{d}_{half}"])
            P.op("pool", lambda e, d=d, rs=rs: e.tensor_scalar(out=oss[rs, d, :, 1], in0=oss[rs, d, :, 0], scalar1=1.0 / 128.0,
                                                               scalar2=EPS, op0=ALU.mult, op1=ALU.add),
                 reads=[f"oss{d}_{half}"], writes=[f"oss{d}_{half}"])
            P.op("pool", lambda e, d=d, rs=rs: e.tensor_tensor(out=oss[rs, d, :, 1], in0=oss[rs, d, :, 1],
                                                               in1=C.mhalf[rs, 0:1].to_broadcast([64, 4]), op=ALU.pow),
                 reads=[f"oss{d}_{half}", "mhalf"], writes=[f"oss{d}_{half}"])
        yield
        for d in range(2):
            c, tt, half, sl, k, rs = info[d]
            P.op("dve", lambda e, d=d, rs=rs: e.tensor_tensor(out=ot[rs, d], in0=ot[rs, d],
                                                              in1=oss[rs, d, :, 1:2].to_broadcast([64, 4, 128]), op=ALU.mult),
                 reads=[f"oss{d}_{half}", f"ot{d}_{half}"], writes=[f"ot{d}_{half}"])
            P.op("dve", lambda e, d=d, rs=rs, tt=tt: e.tensor_tensor(
                out=ogt[d][rs], in0=ot[rs, d], in1=szt[rs, tt, :].rearrange("p (a b) -> p a b", b=128), op=ALU.mult),
                reads=[f"ot{d}_{half}"], writes=[f"ogt{d}_{half}"])
        yield
        for d in range(2):
            c, tt, half, sl, k, rs = info[d]
            done = (half == 1) if d == 0 else (half == 0)
            if done:
                ij = C.Jb if d == 0 else C.identb
                for h in range(H):
                    P.op("pe", lambda e, h=h, d=d, ij=ij: e.matmul(out=BK[d][2][:, h, :], lhsT=ogt[d][:, h, :], rhs=ij,
                                                                   start=True, stop=True),
                         reads=[f"ogt{d}_0", f"ogt{d}_1", "const"], banks=[f"B3_{d}"])
                P.op("act", lambda e, d=d: e.copy(out=f4(ostg[d][:]), in_=f4(BK[d][2][:])), writes=[f"ostg{d}"],
                     banks=[f"B3_{d}"])
                if d == 0:
                    dst, blk = snd_og, 15 - tt
                else:
                    dst, blk = own_og, tt
                P.dma_op("sp", f"og{d}", dst[:, :, blk * 128:(blk + 1) * 128].rearrange("g p n -> p g n"), ostg[d][:],
                         reads=[f"ostg{d}"])
        yield

    active = []

    def steps2(a):
        for s_ in (2 * a, 2 * a + 1):
            info = []
            yield from step(s_, info)
            if s_ >= 16:
                active.append(tail(s_, info))

    def run(gens):
        active[:] = list(gens)
        while active:
            for g in list(active):
                try:
                    next(g)
                except StopIteration:
                    active.remove(g)

    nround = 16 if upto >= 9 else 2
    run([prep(0, 0, 0), prep(15, 1, 1)])
    for a in range(nround):
        gl = [steps2(a)]
        if a + 1 < 16:
            gl += [prep(a + 1, 0, (a + 1) % 2), prep(15 - (a + 1), 1, (15 - (a + 1)) % 2)]
        run(gl)
    P.barrier()
    P.emit()


def _consts(c):
    cf = np.zeros((128, 17, 128), np.float32)
    cf[:, 0, :] = np.eye(128, dtype=np.float32)
    for d in range(2):
        for h in range(H):
            cf[d * 32 + h, 1 + d * 4 + h, :] = 1.0
            cf[d * 32 + h, 1 + 8 + d * 4 + h, :] = 1.0
            cf[64 + d * 32 + h, 1 + 8 + d * 4 + h, :] = 1.0
    cb = np.zeros((128, 10, 128), np.float32)
    cb[:, 0, :] = np.eye(128)
    cb[:, 1, :] = 1.0
    cb[:, 2, :] = np.eye(128)[::-1]
    cb[:, 3, :] = -BIG * np.eye(128)
    j = np.arange(128)[:, None]
    i = np.arange(128)[None, :]
    same = (j // 64) == (i // 64)
    cb[:, 4, :] = 1.0 - (same & (i > j))
    cb[:, 5, :] = 1.0 - (same & (i >= j))
    cb[:, 6, :] = 1.0 - (same & (i < j))
    cb[:, 7, :] = 1.0 - (same & (i <= j))
    cc = np.arange(128)
    ang = 2 * np.pi * ((cc[:, None] * cc[None, :]) % 128) / 128.0
    cb[:, 8, :] = np.cos(ang)
    cb[:, 9, :] = -np.sin(ang)
    tin = np.arange(T)
    tg_in = tin if c == 0 else (T - 1 - tin)
    n = np.arange(T)
    tl = np.where(n < OWN, n, 3071 - n)
    tg_out = tl if c == 0 else (T - 1 - tl)
    prod = (tg_in[:, None].astype(np.int64) * tg_out[None, :].astype(np.int64)) % T
    ang = 2 * np.pi * prod / T
    norm = 1.0 / np.sqrt(T * 128.0)
    dtc = (np.cos(ang) * norm).astype(np.float32)
    dts = (np.sin(ang) * norm).astype(np.float32)
    dft = np.stack([dtc, dts], 0)
    dft = dft.reshape(2, 16, 128, 4, 512).transpose(3, 2, 1, 0, 4)
    selb = np.ascontiguousarray(cf[:, 1:17, :]).reshape(128, -1).astype(NPBF)
    return (cf.reshape(128, -1), cb.reshape(128, -1).astype(NPBF), np.ascontiguousarray(dft).astype(NPBF), selb)


def _klayout(w):
    K, N = w.shape
    return np.ascontiguousarray(w.reshape(K // 128, 128, N).transpose(1, 0, 2))


def prep_A(inp, b, c, consts):
    flip = c == 1
    x = inp["x"][b]
    x = x[::-1] if flip else x
    w_in = inp["w_in"][0]
    hs = [4 * c + j for j in range(4)]
    cols = []
    for j in hs:
        cols += [np.arange(1024 + j * 128, 1024 + (j + 1) * 128), np.arange(2048 + j * 128, 2048 + (j + 1) * 128),
                 np.arange(j * 128, (j + 1) * 128)]
    for j in hs:
        cols.append(np.arange(4096 + j * 128, 4096 + (j + 1) * 128))
    cols = np.concatenate(cols)
    wmix = _klayout(w_in[:, cols])
    zc = np.concatenate([np.arange(3072 + j * 128, 3072 + (j + 1) * 128) for j in hs])
    wz = _klayout(w_in[:, zc])
    kf, kb = (1, 0) if flip else (0, 1)
    wsc = np.zeros((D, 128), np.float32)
    for jj, j in enumerate(hs):
        wsc[:, 0 + jj] = w_in[:, 5120 + (2 + kf) * 8 + j]
        wsc[:, 32 + jj] = w_in[:, 5120 + (2 + kb) * 8 + j]
        wsc[:, 64 + jj] = w_in[:, 5120 + kf * 8 + j]
        wsc[:, 96 + jj] = w_in[:, 5120 + kb * 8 + j]
    wsc = _klayout(wsc)
    conv = inp["conv_w"][0]
    conv = conv[::-1] if flip else conv
    cw = np.zeros((128, 12, 5), np.float32)
    for jj, j in enumerate(hs):
        for kind, base in ((0, 1024), (1, 2048), (2, 0)):
            cw[:, 3 * jj + kind, :] = conv[:, base + j * 128: base + (j + 1) * 128].T
    alog = (inp["a_log_fwd"][0], inp["a_log_bwd"][0])
    dtb = (inp["dt_bias_fwd"][0], inp["dt_bias_bwd"][0])
    pc = np.zeros((128, 8), np.float32)
    for jj, j in enumerate(hs):
        pc[0 + jj, 0] = dtb[kf][j]
        pc[32 + jj, 0] = dtb[kb][j]
        pc[0 + jj, 2] = alog[kf][j]
        pc[32 + jj, 2] = alog[kb][j]
    pc[0:64, 1] = 1.0
    pc[64:128, 1] = -1.0
    cf, cb, dft, selb = consts[c]
    return {
        "selb": selb,
        "x": np.ascontiguousarray(x), "gmix": np.ascontiguousarray(np.tile(inp["g_mix"][0][None], (128, 1))),
        "wmix": wmix, "wz": wz, "wscal": wsc, "cw": cw.reshape(128, 60), "pc": pc,
        "ghead": np.ascontiguousarray(np.tile(inp["g_dn_head"][0][None], (128, 1))),
        "cf": cf, "cb": cb, "dft": dft,
    }


def build_B(nrcv=4):
    nc = bass.Bass("TRN2", target_bir_lowering=False)
    di = lambda name, shape, dt: nc.dram_tensor(name, shape, dt, kind="ExternalInput").ap()
    do = lambda name, shape, dt: nc.dram_tensor(name, shape, dt, kind="ExternalOutput").ap()
    x_d = di("xo", [OWN, D], F32)
    ogfr = [(di("og_own", [4, 128, OWN], BF16), 4), (di("og_rcv", [nrcv, 128, OWN], BF16), nrcv),
            (di("fr_own", [4, 128, OWN], BF16), 4), (di("fr_rcv", [nrcv, 128, OWN], BF16), nrcv)]
    w = declare_B_weights(nc, nrcv)
    out_d = do("out", [OWN, D], F32)
    build_B_body(nc, x_d, w["g3"], w["idb"], ogfr, nrcv, w["wst"], w["wo"], w["wup"], w["wdn"], out_d)
    return nc


def declare_B_weights(nc, nrcv):
    di = lambda name, shape, dt: nc.dram_tensor(name, shape, dt, kind="ExternalInput").ap()
    NW = 8 + 2 * nrcv + 32
    return dict(g3=di("g3", [3, 128, D], F32), idb=di("idb", [128, 128], BF16),
                wst=di("wst", [16, 128, NW, 128], F32), wo=di("wo", [4, 128, 16, 512], F32),
                wup=di("wup", [64, 128, 16, 128], F32), wdn=di("wdn", [4, 4, 128, 16, 512], F32))


def build_B_body(nc, x_d, g3_d, idb_d, ogfr_d, nrcv, wst_d, wo_d, wup_d, wdn_d, out_d, cc=None):
    NK = 8 + 2 * nrcv
    NW = NK + 32
    NT8 = OWN // 128
    with ExitStack() as stP:
        C = Ctx()
        C.nc = nc
        idb = _sb(nc, stP, "b_idb", [128, 128], BF16)
        C.identb = idb
        C.epsb = _sb(nc, stP, "b_epsb", [128, 1], F32)
        C.mhalf = _sb(nc, stP, "b_mhalf", [128, 1], F32)
        C.st = _sb(nc, stP, "b_st", [128, 8, 4], F32)
        C.junk = _sb(nc, stP, "b_junk", [128, D], BF16)
        hT = _sb(nc, stP, "b_hT", [128, 16, OWN], BF16)
        mU = _sb(nc, stP, "b_mU", [128, 16, OWN], BF16)
        gbuf = _sb(nc, stP, "b_gbuf", [128, D], F32)
        hb = _sb(nc, stP, "b_hb", [128, D], BF16)
        hkeys = lambda blk: [f"hT{t}{s}" for t in range(4 * blk, 4 * blk + 4) for s in "ab"]
        with ExitStack() as s:
            P = Prog(nc, s, "b1")
            xt = [_sb(nc, s, f"b_xt{i}", [128, D], F32) for i in range(2)]
            ogfr = _sb(nc, s, "b_ogfr", [128, NK, OWN], BF16)
            wsb = [_sb(nc, s, f"b_wsb{i}", [128, NW, 128], BF16) for i in range(2)]
            sg = [_sb(nc, s, f"b_sg{i}", [128, 512], F32) for i in range(2)]
            tt_ = [_sb(nc, s, f"b_t{i}", [128, 512], F32) for i in range(2)]
            C.tr = None
            pb = [_ps(nc, s, f"b_pc{i}", [128, 512], F32) for i in range(6)]
            C.tr = [_ps(nc, s, f"b_tr{i}", [128, 8, 128], BF16) for i in range(2)]
            P.dma_op("sp", "c0", idb[:], idb_d, writes=["const"])
            P.dma_op("sp", "c1", gbuf[:], g3_d[0], writes=["gbuf"])
            P.op("pool", lambda e: e.memset(C.epsb[:], EPS), writes=["epsb"])
            P.op("pool", lambda e: e.memset(C.mhalf[:], -0.5), writes=["mhalf"])
            if cc is not None:
                P.cc_allgather(*cc)

            def load_ws(oc):
                P.dma_op("pool", f"ws{oc%2}", wsb[oc % 2][:], wst_d[oc], writes=[f"wsb{oc%2}"])
            load_ws(0)
            hba = [hb, _sb(nc, s, "b_hba1", [128, D], BF16), _sb(nc, s, "b_hba2", [128, D], BF16)]

            def tra(t):
                transpose_tile16(P, C, hba[t % 3], f"hba{t%3}",
                                 lambda half, t=t: hT[:, half * 8:(half + 1) * 8, t * 128:(t + 1) * 128],
                                 [f"hT{t}a", f"hT{t}b"], "tr")
            for t in range(NT8):
                xb = xt[t % 2]
                P.dma_op("sp", f"x{t%2}", xb[:], x_d[t * 128:(t + 1) * 128, :], writes=[f"xt{t%2}"])
                rms_tile(P, C, xb[:], f"xt{t%2}", gbuf[:], "gbuf", hba[t % 3][:], f"hba{t%3}", t)
                if t >= 1:
                    tra(t - 1)
            tra(NT8 - 1)
            o0 = 0
            for (src, n) in ogfr_d:
                P.dma_op("sp", f"ogfr{o0}", ogfr[:, o0:o0 + n, :], src.rearrange("g p n -> p g n"), reads=["rcv"],
                         writes=["ogfr"])
                o0 += n
            nog = 4 + nrcv
            for oc in range(16):
                if oc + 1 < 16:
                    load_ws(oc + 1)
                w = wsb[oc % 2]
                wk = f"wsb{oc%2}"
                for blk in range(2):
                    bs = slice(blk * 512, (blk + 1) * 512)
                    Ya, Yf = pb[0], pb[1]
                    Ga, Gf = pb[2 + 2 * blk], pb[3 + 2 * blk]
                    gak, gfk = f"pc{2+2*blk}", f"pc{3+2*blk}"
                    for kc in range(16):
                        P.op("pe", lambda e, kc=kc, w=w, Ga=Ga, bs=bs: e.matmul(out=Ga[:], lhsT=w[:, NK + kc, :], rhs=hT[:, kc, bs],
                                                                             start=(kc == 0), stop=(kc == 15)),
                             reads=[wk] + hkeys(blk), banks=[gak])
                    P.op("act", lambda e, Ga=Ga: e.activation(out=sg[0][:], in_=Ga[:], func=AF.Sigmoid),
                         writes=["sg0"], banks=[gak])
                    for kc in range(16):
                        P.op("pe", lambda e, kc=kc, w=w, Gf=Gf, bs=bs: e.matmul(out=Gf[:], lhsT=w[:, NK + 16 + kc, :], rhs=hT[:, kc, bs],
                                                                             start=(kc == 0), stop=(kc == 15)),
                             reads=[wk] + hkeys(blk), banks=[gfk])
                    P.op("act", lambda e, Gf=Gf: e.activation(out=sg[1][:], in_=Gf[:], func=AF.Sigmoid),
                         writes=["sg1"], banks=[gfk])
                    for kc in range(nog):
                        P.op("pe", lambda e, kc=kc, w=w, bs=bs: e.matmul(out=Ya[:], lhsT=w[:, kc, :], rhs=ogfr[:, kc, bs],
                                                                      start=(kc == 0), stop=(kc == nog - 1)),
                             reads=[wk, "ogfr"], banks=["pc0"])
                    P.op("dve", lambda e: e.tensor_tensor(out=tt_[0][:], in0=Ya[:], in1=sg[0][:], op=ALU.mult),
                         reads=["sg0"], writes=["t0"], banks=["pc0"])
                    for kc in range(nog):
                        P.op("pe", lambda e, kc=kc, w=w, bs=bs: e.matmul(out=Yf[:], lhsT=w[:, nog + kc, :], rhs=ogfr[:, nog + kc, bs],
                                                                      start=(kc == 0), stop=(kc == nog - 1)),
                             reads=[wk, "ogfr"], banks=["pc1"])
                    P.op("dve", lambda e: e.tensor_tensor(out=tt_[1][:], in0=Yf[:], in1=sg[1][:], op=ALU.mult),
                         reads=["sg1"], writes=["t1"], banks=["pc1"])
                    P.op("pool", lambda e, oc=oc, bs=bs: e.tensor_tensor(out=mU[:, oc, bs], in0=tt_[0][:], in1=tt_[1][:], op=ALU.add),
                         reads=["t0", "t1"], writes=[f"mU{oc}"])
            P.barrier()
            P.emit()
        with ExitStack() as s:
            x1 = _sb(nc, s, "b_x1", [128, NT8, D], F32)
            mkeys = [f"mU{oc}" for oc in range(16)]
            with ExitStack() as s2:
                P = Prog(nc, s2, "b2")
                wob = [_sb(nc, s2, f"b_wob{i}", [128, 16, 512], BF16) for i in range(2)]
                hbs = [hb] + [_sb(nc, s2, f"b_hb{i}", [128, D], BF16) for i in range(2)]
                pb = [_ps(nc, s2, f"b_pd{i}", [128, 512], F32) for i in range(4)]
                C.tr = [_ps(nc, s2, f"b_tr2{i}", [128, 8, 128], BF16) for i in range(2)]
                P.dma_op("sp", "gl", gbuf[:], g3_d[1], writes=["gbuf"])
                for t in range(NT8):
                    P.dma_op("sp", f"x1_{t%4}", x1[:, t, :], x_d[t * 128:(t + 1) * 128, :], writes=[f"x1_{t}_{c}" for c in range(4)])
                P.dma_op("pool", "wo0", wob[0][:], wo_d[0], writes=["wob0"])
                n = 0
                for cb in range(4):
                    if cb + 1 < 4:
                        P.dma_op("pool", f"wo{(cb+1)%2}", wob[(cb + 1) % 2][:], wo_d[cb + 1], writes=[f"wob{(cb+1)%2}"])
                    w = wob[cb % 2]
                    for t in range(NT8):
                        bk = pb[n % 4]
                        for kc in range(16):
                            P.op("pe", lambda e, kc=kc, t=t, w=w, bk=bk: e.matmul(out=bk[:], lhsT=mU[:, kc, t * 128:(t + 1) * 128],
                                                                               rhs=w[:, kc, :], start=(kc == 0), stop=(kc == 15)),
                                 reads=[f"wob{cb%2}"], banks=[f"pd{n%4}"])
                        P.op("dve", lambda e, t=t, cb=cb, bk=bk: e.tensor_tensor(out=x1[:, t, cb * 512:(cb + 1) * 512],
                                                                                  in0=bk[:], in1=x1[:, t, cb * 512:(cb + 1) * 512], op=ALU.add),
                             reads=[f"x1_{t}_{cb}"], writes=[f"x1_{t}_{cb}"], banks=[f"pd{n%4}"])
                        n += 1
                        if cb == 3:
                            rms_tile(P, C, x1[:, t, :], [f"x1_{t}_{c}" for c in range(4)], gbuf[:], "gbuf",
                                     hbs[t % 3][:], f"hbs{t%3}", t)
                            if t >= 2:
                                tq = t - 2
                                transpose_tile16(P, C, hbs[tq % 3], f"hbs{tq%3}",
                                                 lambda half, tq=tq: hT[:, half * 8:(half + 1) * 8, tq * 128:(tq + 1) * 128],
                                                 [f"hT{tq}a", f"hT{tq}b"], "tr")
                for tq in (NT8 - 2, NT8 - 1):
                    transpose_tile16(P, C, hbs[tq % 3], f"hbs{tq%3}",
                                     lambda half, tq=tq: hT[:, half * 8:(half + 1) * 8, tq * 128:(tq + 1) * 128],
                                     [f"hT{tq}a", f"hT{tq}b"], "tr")
                P.barrier()
                P.emit()
            with ExitStack() as s2:
                P = Prog(nc, s2, "b3")
                wub = [_sb(nc, s2, f"b_wub{i}", [128, 16, 128], BF16) for i in range(2)]
                wdb = [_sb(nc, s2, f"b_wdb{i}", [128, 16, 512], BF16) for i in range(2)]
                rl = [_sb(nc, s2, f"b_rl{i}", [128, 512], F32) for i in range(2)]
                ob = [_sb(nc, s2, f"b_ob{i}", [128, D], F32) for i in range(2)]
                P.dma_op("sp", "gl", gbuf[:], g3_d[2], writes=["gbuf"])
                pu = [_ps(nc, s2, f"b_pu{i}", [128, 512], F32) for i in range(4)]
                pd = [_ps(nc, s2, f"b_pdn{i}", [128, 512], F32) for i in range(4)]
                nu = 0
                nd = 0
                nwd = 0

                def load_up(i):
                    P.dma_op("pool", f"wu{i%2}", wub[i % 2][:], wup_d[i], writes=[f"wub{i%2}"])

                def load_dn(i):
                    P.dma_op("pool", f"wd{i%2}", wdb[i % 2][:], wdn_d[i // 4, i % 4], writes=[f"wdb{i%2}"])
                load_up(0)
                load_dn(0)
                for q in range(4):
                    for j in range(16):
                        fc = q * 16 + j
                        if fc + 1 < 64:
                            load_up(fc + 1)
                        w = wub[fc % 2]
                        for blk in range(2):
                            bs = slice(blk * 512, (blk + 1) * 512)
                            bk = pu[nu % 4]
                            for kc in range(16):
                                P.op("pe", lambda e, kc=kc, w=w, bk=bk, bs=bs: e.matmul(out=bk[:], lhsT=w[:, kc, :], rhs=hT[:, kc, bs],
                                                                                     start=(kc == 0), stop=(kc == 15)),
                                     reads=[f"wub{fc%2}"], banks=[f"pu{nu%4}"])
                            r = rl[nu % 2]
                            P.op("act", lambda e, bk=bk, r=r: e.activation(out=r[:], in_=bk[:], func=AF.Relu),
                                 writes=[f"rl{nu%2}"], banks=[f"pu{nu%4}"])
                            eng = "dve" if nu % 2 == 0 else "pool"
                            P.op(eng, lambda e, r=r, j=j, bs=bs: e.tensor_tensor(out=mU[:, j, bs], in0=r[:], in1=r[:], op=ALU.mult),
                                 reads=[f"rl{nu%2}"], writes=[f"up{j}"])
                            nu += 1
                    upk = [f"up{j}" for j in range(16)]
                    for cb in range(4):
                        if nwd + 1 < 16:
                            load_dn(nwd + 1)
                        w = wdb[nwd % 2]
                        for t in range(NT8):
                            bk = pd[nd % 4]
                            for kc in range(16):
                                P.op("pe", lambda e, kc=kc, t=t, w=w, bk=bk: e.matmul(out=bk[:], lhsT=mU[:, kc, t * 128:(t + 1) * 128],
                                                                                   rhs=w[:, kc, :], start=(kc == 0), stop=(kc == 15)),
                                     reads=[f"wdb{nwd%2}"] + upk, banks=[f"pdn{nd%4}"])
                            P.op("dve", lambda e, t=t, cb=cb, bk=bk: e.tensor_tensor(out=x1[:, t, cb * 512:(cb + 1) * 512],
                                                                                      in0=bk[:], in1=x1[:, t, cb * 512:(cb + 1) * 512], op=ALU.add),
                                 reads=[f"x1_{t}_{cb}"], writes=[f"x1_{t}_{cb}"], banks=[f"pdn{nd%4}"])
                            nd += 1
                            if q == 3 and cb == 3:
                                rms_tile(P, C, x1[:, t, :], [f"x1_{t}_{c}" for c in range(4)], gbuf[:], "gbuf",
                                         ob[t % 2][:], f"ob{t%2}", t)
                                P.dma_op("sp", f"o{t%2}", out_d[t * 128:(t + 1) * 128, :], ob[t % 2][:], reads=[f"ob{t%2}"])
                        nwd += 1
                P.barrier()
                P.emit()


def prep_B(inp, b, c, rcv, idb):
    flip = c == 1
    x = inp["x"][b]
    x = x[::-1] if flip else x
    own = [4 * c + j for j in range(4)]
    oth = [4 * (1 - c) + j for j in range(4)]
    w_in = inp["w_in"][0]
    wdu = inp["w_dn_up"][0]
    wfo = inp["w_fourier"][0]
    rows = np.concatenate([np.arange(h * 128, (h + 1) * 128) for h in own + oth])
    ga = w_in[:, 5152:5152 + 2048]
    gf = w_in[:, 5152 + 2048:5152 + 4096]
    wst = np.concatenate([wdu[rows], wfo[rows], ga, gf], axis=0)
    nk = wst.shape[0] // 128
    wst = wst.reshape(nk, 128, 16, 128).transpose(2, 1, 0, 3)
    wo = inp["w_o"][0].reshape(16, 128, 4, 512).transpose(2, 1, 0, 3)
    wup = inp["w_mlp_up"][0].reshape(16, 128, 64, 128).transpose(2, 1, 0, 3)
    wdn = inp["w_mlp_down"][0].reshape(4, 16, 128, 4, 512).transpose(0, 3, 2, 1, 4)
    g3 = np.stack([np.tile(inp[k].reshape(-1)[None], (128, 1)) for k in ("g_mix", "g_mlp", "g_final")], 0)
    d = {
        "xo": np.ascontiguousarray(x[:OWN]), "g3": np.ascontiguousarray(g3), "idb": idb,
        "wst": np.ascontiguousarray(wst), "wo": np.ascontiguousarray(wo), "wup": np.ascontiguousarray(wup),
        "wdn": np.ascontiguousarray(wdn),
    }
    d.update(rcv)
    return d


def build_F():
    nc = bass.Bass("TRN2", target_bir_lowering=False)
    ins = declare_A_inputs(nc)
    w = declare_B_weights(nc, 8)
    out_d = nc.dram_tensor("out", [OWN, D], F32, kind="ExternalOutput").ap()
    own_og = nc.dram_tensor("i_own_og", [4, 128, OWN], BF16, kind="Internal").ap()
    own_fr = nc.dram_tensor("i_own_fr", [4, 128, OWN], BF16, kind="Internal").ap()
    snd = nc.dram_tensor("i_snd", [8 * 128, OWN], BF16, kind="Internal").ap()
    rcv = nc.dram_tensor("i_rcv", [2 * 8 * 128, OWN], BF16, kind="Internal").ap()
    snd3 = snd.rearrange("(g p) n -> g p n", p=128)
    rcv4 = rcv.rearrange("(r g p) n -> r g p n", r=2, p=128)
    emit_A(nc, ins, (own_og, snd3[0:4], own_fr, snd3[4:8]), {}, False, 9)
    ogfr = [(own_og, 4), (rcv4[0, 0:4], 4), (rcv4[1, 0:4], 4), (own_fr, 4), (rcv4[0, 4:8], 4), (rcv4[1, 4:8], 4)]
    build_B_body(nc, ins["x"][0:OWN, :], w["g3"], w["idb"], ogfr, 8, w["wst"], w["wo"], w["wup"], w["wdn"], out_d,
                 cc=(snd, rcv, [[0, 1], [2, 3], [4, 5], [6, 7]]))
    return nc


def prep_F(inp, b, c, consts, idb, shared):
    d = prep_A(inp, b, c, consts)
    wdu = inp["w_dn_up"][0]
    wfo = inp["w_fourier"][0]
    w_in = inp["w_in"][0]
    own = np.concatenate([np.arange(h * 128, (h + 1) * 128) for h in range(4 * c, 4 * c + 4)])
    z512 = np.zeros((512, D), np.float32)
    parts = []
    for wmat in (wdu, wfo):
        parts.append(wmat[own])
        parts.append(z512 if c == 0 else wmat[0:512])
        parts.append(z512 if c == 1 else wmat[512:1024])
    parts.append(w_in[:, 5152:5152 + 2048])
    parts.append(w_in[:, 5152 + 2048:5152 + 4096])
    wst = np.concatenate(parts, axis=0)
    nk = wst.shape[0] // 128
    wst = wst.reshape(nk, 128, 16, 128).transpose(2, 1, 0, 3)
    d["wst"] = np.ascontiguousarray(wst)
    d["idb"] = idb
    d.update(shared)
    return d


def _shared_B(inp):
    wo = inp["w_o"][0].reshape(16, 128, 4, 512).transpose(2, 1, 0, 3)
    wup = inp["w_mlp_up"][0].reshape(16, 128, 64, 128).transpose(2, 1, 0, 3)
    wdn = inp["w_mlp_down"][0].reshape(4, 16, 128, 4, 512).transpose(0, 3, 2, 1, 4)
    g3 = np.stack([np.tile(inp[k].reshape(-1)[None], (128, 1)) for k in ("g_mix", "g_mlp", "g_final")], 0)
    return {"wo": np.ascontiguousarray(wo), "wup": np.ascontiguousarray(wup), "wdn": np.ascontiguousarray(wdn),
            "g3": np.ascontiguousarray(g3)}


_CACHE = {}


def kernel(**inputs):
    inp = {k: np.asarray(v) for k, v in inputs.items()}
    if "F" not in _CACHE:
        _CACHE["F"] = build_F()
        _CACHE["consts"] = [_consts(0), _consts(1)]
    consts = _CACHE["consts"]
    idb = np.eye(128, dtype=np.float32).astype(NPBF)
    shared = _shared_B(inp)
    maps = [prep_F(inp, i // 2, i % 2, consts, idb, shared) for i in range(8)]
    res = run_bass_kernel_spmd(_CACHE["F"], maps, core_ids=list(range(8))).results
    out = np.zeros((4, T, D), np.float32)
    for i in range(8):
        b, c = i // 2, i % 2
        o = np.asarray(res[i]["out"])
        if c == 0:
            out[b, :OWN] = o
        else:
            out[b, OWN:] = o[::-1]
    return out
```

```python
from contextlib import ExitStack
import numpy as np
import ml_dtypes
import concourse.bass as bass
import concourse.mybir as mybir
from concourse.bass_utils import run_bass_kernel_spmd

F32 = mybir.dt.float32
BF16 = mybir.dt.bfloat16
AF = mybir.ActivationFunctionType
ALU = mybir.AluOpType
AX = mybir.AxisListType
NPBF = ml_dtypes.bfloat16

D = 2048
T = 2048
OWN = 1024
NTILE = 16
H = 4
EPS = 1e-6
DFF = 8192
BIG = 30000.0
SELF_SYNC = True


class Prog:
    ENGS = ("pe", "act", "dve", "pool", "sp")

    def __init__(self, nc, stack, tag=""):
        self.nc = nc
        self.stack = stack
        self.tag = tag
        self.streams = {e: [] for e in self.ENGS}
        self.count = {e: 0 for e in self.ENGS}
        self.sems = {e: stack.enter_context(nc.semaphore(f"pg{tag}_{e}")) for e in self.ENGS}
        self.waited = {e: {} for e in self.ENGS}
        self.last_w = {}
        self.readers = {}
        self.bank_last = {}
        self.dma = {}

    def _deps(self, eng, reads, writes, banks):
        deps = []
        for r in reads:
            lw = self.last_w.get(r)
            if lw is not None:
                deps.append(lw)
        for w in writes:
            lw = self.last_w.get(w)
            if lw is not None:
                deps.append(lw)
            deps.extend(self.readers.get(w, ()))
        need = {}
        for (k, v) in deps:
            if k == eng and (eng == "pe" or not SELF_SYNC):
                continue
            if need.get(k, 0) < v:
                need[k] = v
        for bk in banks:
            for k, v in self.bank_last.get(bk, {}).items():
                if k != eng and need.get(k, 0) < v:
                    need[k] = v
        for k, v in need.items():
            if self.waited[eng].get(k, 0) >= v:
                continue
            self.waited[eng][k] = v
            sem = self.sems[k] if k in self.sems else self.dma[k][0]
            self.streams[eng].append(("wait", sem, v))

    def _commit(self, ev, reads, writes, banks):
        for r in reads:
            self.readers.setdefault(r, []).append(ev)
        for w in writes:
            self.last_w[w] = ev
            self.readers[w] = []
        for bk in banks:
            self.bank_last.setdefault(bk, {})[ev[0]] = ev[1]

    def op(self, eng, fn, reads=(), writes=(), banks=()):
        self._deps(eng, reads, writes, banks)
        self.count[eng] += 1
        ev = (eng, self.count[eng])
        self.streams[eng].append(("op", fn))
        self._commit(ev, reads, writes, banks)
        return ev

    def dma_op(self, queue, key, out, in_, reads=(), writes=()):
        if key not in self.dma:
            self.dma[key] = [self.stack.enter_context(self.nc.semaphore(f"dm{self.tag}_{key}")), 0]
        ent = self.dma[key]
        self._deps(queue, reads, writes, ())
        if ent[1] and self.waited[queue].get(key, 0) < 16 * ent[1]:
            self.waited[queue][key] = 16 * ent[1]
            self.streams[queue].append(("wait", ent[0], 16 * ent[1]))
        ent[1] += 1
        ev = (key, 16 * ent[1])
        self.streams[queue].append(("dma", out, in_, ent[0]))
        self._commit(ev, reads, writes, ())
        return ev

    def cc_allgather(self, src, dst, groups):
        key = "cc"
        self.dma[key] = [self.stack.enter_context(self.nc.semaphore(f"cc{self.tag}")), 0]
        self.streams["pool"].append(("cc", src, dst, groups, self.dma[key][0]))
        self.dma[key][1] = 1.0 / 16.0
        self.last_w["rcv"] = (key, 1)
        self.readers["rcv"] = []

    def barrier(self):
        for e in self.ENGS:
            for k in self.ENGS:
                if k != e and self.count[k] > self.waited[e].get(k, 0):
                    self.waited[e][k] = self.count[k]
                    self.streams[e].append(("wait", self.sems[k], self.count[k]))
            for key, (sem, cnt) in self.dma.items():
                if cnt and self.waited[e].get(key, 0) < 16 * cnt:
                    self.waited[e][key] = int(16 * cnt)
                    self.streams[e].append(("wait", sem, int(16 * cnt)))

    def emit(self):
        nc = self.nc
        handles = {"pe": "tensor", "act": "scalar", "dve": "vector", "pool": "gpsimd", "sp": "sync"}
        with nc.Block() as block:
            for e in self.ENGS:
                stream = self.streams[e]
                sem = self.sems[e]

                def body(eng, stream=stream, sem=sem):
                    for it in stream:
                        if it[0] == "wait":
                            eng.wait_ge(it[1], it[2])
                        elif it[0] == "op":
                            it[1](eng).then_inc(sem, 1)
                        elif it[0] == "cc":
                            eng.collective_compute("AllGather", ALU.bypass, replica_groups=it[3], ins=[it[1]],
                                                   outs=[it[2]]).then_inc(it[4], 1)
                        else:
                            eng.dma_start(out=it[1], in_=it[2]).then_inc(it[3], 16)

                getattr(block, handles[e])(body)


class Ctx:
    pass


def _sb(nc, st, name, shape, dt):
    return st.enter_context(nc.sbuf_tensor("s_" + name, shape, dt))


def _ps(nc, st, name, shape, dt):
    return st.enter_context(nc.psum_tensor("p_" + name, shape, dt))


def rms_tile(P, C, xb, xkey, gbc, gkey, hb, hkey, col):
    xkeys = list(xkey) if isinstance(xkey, (list, tuple)) else [xkey]
    P.op("act", lambda e: e.activation(out=C.junk[:], in_=xb, func=AF.Square, accum_out=C.st[:, col, 0:1]),
         reads=xkeys, writes=["junk", "st"])
    P.op("pool", lambda e: e.tensor_scalar(out=C.st[:, col, 1:2], in0=C.st[:, col, 0:1], scalar1=1.0 / D, scalar2=EPS,
                                           op0=ALU.mult, op1=ALU.add), reads=["st"], writes=["st"])
    P.op("pool", lambda e: e.tensor_tensor(out=C.st[:, col, 2:3], in0=C.st[:, col, 1:2], in1=C.mhalf[:], op=ALU.pow),
         reads=["st", "mhalf"], writes=["st"])
    P.op("dve", lambda e: e.scalar_tensor_tensor(out=hb, in0=xb, scalar=C.st[:, col, 2:3], in1=gbc,
                                                 op0=ALU.mult, op1=ALU.mult),
         reads=xkeys + ["st", gkey], writes=[hkey])


def transpose_tile16(P, C, hb, hkey, dst, dkeys, trk):
    for half in range(2):
        pt = C.tr[half]
        for j in range(8):
            kc = half * 8 + j
            P.op("pe", lambda e, pt=pt, j=j, kc=kc: e.transpose(out=pt[:, j, :], in_=hb[:, kc * 128:(kc + 1) * 128],
                                                               identity=C.identb[:]),
                 reads=[hkey, "const"], banks=[f"{trk}{half}"])
        if half == 0:
            P.op("act", lambda e, pt=pt: e.copy(out=dst(0), in_=pt[:]), writes=[dkeys[0]], banks=[f"{trk}0"])
        else:
            P.op("dve", lambda e, pt=pt: e.tensor_copy(out=dst(1), in_=pt[:]), writes=[dkeys[1]], banks=[f"{trk}1"])


def build_A(dbg=False, upto=9):
    nc = bass.Bass("TRN2", target_bir_lowering=False)
    do = lambda name, shape, dt: nc.dram_tensor(name, shape, dt, kind="ExternalOutput").ap()
    ins = declare_A_inputs(nc)
    outs = (do("own_og", [4, 128, OWN], BF16), do("snd_og", [4, 128, OWN], BF16),
            do("own_fr", [4, 128, OWN], BF16), do("snd_fr", [4, 128, OWN], BF16))
    D_ = {}
    if dbg:
        D_["qkv"] = do("d_qkv", [128, 12 * T], BF16)
        D_["rowsraw"] = do("d_rowsraw", [128, T], F32)
        D_["fT"] = do("d_fT", [128, 4 * T], BF16)
        D_["hT"] = do("d_hT", [128, 16 * T], BF16)
        D_["rows"] = do("d_rows", [128, T], F32)
        D_["cc"] = do("d_cc", [128, 16 * 24], F32)
        D_["glbc"] = do("d_glbc", [128, 256], F32)
        D_["szt"] = do("d_szt", [128, 16 * 512], BF16)
    emit_A(nc, ins, outs, D_, dbg, upto)
    return nc


def declare_A_inputs(nc):
    di = lambda name, shape, dt: nc.dram_tensor(name, shape, dt, kind="ExternalInput").ap()
    return dict(
        x=di("x", [T, D], F32), gmix=di("gmix", [128, D], F32), wmix=di("wmix", [128, 16, 16 * 128], F32),
        wz=di("wz", [128, 16, 512], F32), wscal=di("wscal", [128, 16, 128], F32), cw=di("cw", [128, 60], F32),
        pc=di("pc", [128, 8], F32), ghead=di("ghead", [128, 128], F32), cf=di("cf", [128, 17 * 128], F32),
        cb=di("cb", [128, 10 * 128], BF16), selb=di("selb", [128, 16 * 128], BF16), dft=di("dft", [4, 128, 16, 2, 512], BF16))


def emit_A(nc, ins, outs, D_, dbg=False, upto=9):
    x_d, gmix_d, wmix_d, wz_d, wscal_d, cw_d, pc_d, ghead_d, cf_d, cb_d, dft_d = (
        ins[k] for k in ("x", "gmix", "wmix", "wz", "wscal", "cw", "pc", "ghead", "cf", "cb", "dft"))
    C_selb_d = ins["selb"]
    own_og, snd_og, own_fr, snd_fr = outs
    with ExitStack() as stA:
        C = Ctx()
        C.nc = nc
        cf = _sb(nc, stA, "cf", [128, 17, 128], F32)
        cb = _sb(nc, stA, "cb", [128, 10, 128], BF16)
        C.identf = cf[:, 0, :]
        C.sel = lambda kind, d, h: cf[:, 1 + kind * 8 + d * 4 + h, :]
        C.identb = cb[:, 0, :]
        C.onesb = cb[:, 1, :]
        C.Jb = cb[:, 2, :]
        C.negI = cb[:, 3, :]
        C.mask = lambda d, incl: cb[:, 4 + d * 2 + incl, :]
        C.CS = cb[:, 8:10, :]
        C.epsb = _sb(nc, stA, "epsb", [128, 1], F32)
        C.mhalf = _sb(nc, stA, "mhalf", [128, 1], F32)
        C.st = _sb(nc, stA, "st", [128, 16, 4], F32)
        pc = _sb(nc, stA, "pc", [128, 8], F32)
        cw = _sb(nc, stA, "cw", [128, 12, 5], F32)
        ghead = _sb(nc, stA, "ghead", [128, 128], F32)

        with ExitStack() as stB:
            QKV = _sb(nc, stB, "QKV", [128, 3, H, T], BF16)
            szt = _sb(nc, stB, "szt", [128, NTILE, 512], BF16)
            rows = _sb(nc, stB, "rows", [128, T], F32)
            with ExitStack() as stF:
                fT = _sb(nc, stF, "fT", [128, H, T], BF16)
                with ExitStack() as stH:
                    hT = _sb(nc, stH, "hT", [128, 16, T], BF16)
                    with ExitStack() as s0:
                        P = Prog(nc, s0, "p0")
                        xt = [_sb(nc, s0, f"xt{i}", [128, D], F32) for i in range(3)]
                        gm = _sb(nc, s0, "gm", [128, D], F32)
                        C.junk = _ps(nc, s0, "junkp", [128, D], F32)
                        hb = [_sb(nc, s0, f"hb{i}", [128, D], BF16) for i in range(2)]
                        C.tr = [_ps(nc, s0, f"tr{i}", [128, 8, 128], BF16) for i in range(2)]
                        P.dma_op("sp", "c0", cf[:], cf_d.rearrange("p (a b) -> p a b", b=128), writes=["const"])
                        P.dma_op("sp", "c1", cb[:], cb_d.rearrange("p (a b) -> p a b", b=128), writes=["const"])
                        P.dma_op("sp", "c2", gm[:], gmix_d, writes=["gm"])
                        P.dma_op("sp", "c3", pc[:], pc_d, writes=["pc"])
                        P.dma_op("sp", "c4", cw[:], cw_d.rearrange("p (a b) -> p a b", b=5), writes=["cw"])
                        P.dma_op("sp", "c5", ghead[:], ghead_d, writes=["ghead"])
                        P.op("pool", lambda e: e.memset(C.epsb[:], EPS), writes=["epsb"])
                        P.op("pool", lambda e: e.memset(C.mhalf[:], -0.5), writes=["mhalf"])

                        def tr0(t):
                            transpose_tile16(P, C, hb[t % 2], f"hb{t%2}",
                                             lambda half, t=t: hT[:, half * 8:(half + 1) * 8, t * 128:(t + 1) * 128],
                                             [f"hT{t}a", f"hT{t}b"], "tr")
                        for t in range(NTILE):
                            xb = xt[t % 3]
                            P.dma_op("sp", f"x{t%3}", xb[:], x_d[t * 128:(t + 1) * 128, :], writes=[f"xt{t%3}"])
                            rms_tile(P, C, xb[:], f"xt{t%3}", gm[:], "gm", hb[t % 2][:], f"hb{t%2}", t)
                            if t >= 1:
                                tr0(t - 1)
                        tr0(NTILE - 1)
                        if dbg:
                            P.barrier()
                            P.dma_op("sp", "dbg0", D_["hT"], hT[:].rearrange("p a b -> p (a b)"))
                        P.barrier()
                        P.emit()
                    if upto < 1:
                        return
                    with ExitStack() as s1:
                        P = Prog(nc, s1, "p1")
                        wb = [_sb(nc, s1, f"wb{i}", [128, 16, 128], BF16) for i in range(2)]
                        wsc = wb[1]
                        pre = _sb(nc, s1, "pre", [128, T + 4], F32)
                        accs = [_sb(nc, s1, f"acc{i}", [128, T], F32) for i in range(2)]
                        sd = _sb(nc, s1, "sd", [128, T], F32)
                        pacc = [_ps(nc, s1, f"pacc{i}", [128, 512], F32) for i in range(4)]
                        pss = [_ps(nc, s1, f"pss{i}", [128, 512], F32) for i in range(4)]
                        hkeys = lambda blk: [f"hT{t}{s}" for t in range(4 * blk, 4 * blk + 4) for s in "ab"]
                        allh = [f"hT{t}{s}" for t in range(16) for s in "ab"]
                        P.op("pool", lambda e: e.memset(pre[:, 0:2], 0.0), writes=["pre"])
                        P.op("pool", lambda e: e.memset(pre[:, T + 2:T + 4], 0.0), writes=["pre"])
                        P.dma_op("pool", "wb1", wsc[:], wscal_d, writes=["wb1"])

                        def load_w(ci):
                            P.dma_op("pool", f"wb{ci%2}", wb[ci % 2][:], wmix_d[:, :, ci * 128:(ci + 1) * 128],
                                     writes=[f"wb{ci%2}"])
                        load_w(0)
                        for blk in range(4):
                            for kc in range(16):
                                P.op("pe", lambda e, blk=blk, kc=kc: e.matmul(
                                    out=pacc[blk][:], lhsT=wsc[:, kc, :], rhs=hT[:, kc, blk * 512:(blk + 1) * 512],
                                    start=(kc == 0), stop=(kc == 15)), reads=["wb1"] + hkeys(blk), banks=[f"pacc{blk}"])
                            P.op("act", lambda e, blk=blk: e.copy(out=rows[:, blk * 512:(blk + 1) * 512], in_=pacc[blk][:]),
                                 writes=["rows"], banks=[f"pacc{blk}"])
                        pending = []
                        for ci in range(16):
                            if ci + 1 < 16:
                                load_w(ci + 1)
                            w = wb[ci % 2]
                            isf = ci >= 12
                            acc = accs[ci % 2]
                            ak = f"acc{ci % 2}"
                            j = (ci - 12) if isf else ci // 3
                            kind = None if isf else ci % 3
                            for blk in range(4):
                                for kc in range(16):
                                    P.op("pe", lambda e, blk=blk, kc=kc, w=w: e.matmul(
                                        out=pacc[blk][:], lhsT=w[:, kc, :], rhs=hT[:, kc, blk * 512:(blk + 1) * 512],
                                        start=(kc == 0), stop=(kc == 15)),
                                        reads=[f"wb{ci%2}"] + hkeys(blk), banks=[f"pacc{blk}"])
                                if isf:
                                    P.op("dve", lambda e, blk=blk, j=j: e.tensor_copy(
                                        out=fT[:, j, blk * 512:(blk + 1) * 512], in_=pacc[blk][:]),
                                        writes=[f"fT{j}"], banks=[f"pacc{blk}"])
                                else:
                                    P.op("act", lambda e, blk=blk: e.copy(out=pre[:, 2 + blk * 512:2 + (blk + 1) * 512],
                                                                          in_=pacc[blk][:]),
                                         writes=["pre"], banks=[f"pacc{blk}"])
                            prev_b = pending
                            pending = []
                            if isf:
                                for fn in prev_b:
                                    fn()
                                continue
                            P.op("dve", lambda e, ci=ci, acc=acc: e.tensor_scalar(out=acc[:], in0=pre[:, 0:T], scalar1=cw[:, ci, 0:1],
                                                                        scalar2=None, op0=ALU.mult),
                                 reads=["pre", "cw"], writes=[ak])
                            for tap in range(1, 5):
                                P.op("dve", lambda e, ci=ci, tap=tap, acc=acc: e.scalar_tensor_tensor(
                                    out=acc[:], in0=pre[:, tap:tap + T], scalar=cw[:, ci, tap:tap + 1], in1=acc[:],
                                    op0=ALU.mult, op1=ALU.add), reads=["pre", "cw", ak], writes=[ak])
                            if kind == 1:
                                P.op("act", lambda e, j=j, acc=acc: e.activation(out=QKV[:, 1, j, :], in_=acc[:], func=AF.Silu),
                                     reads=[ak], writes=[f"qkv1_{j}"])
                                for fn in prev_b:
                                    fn()
                                continue
                            P.op("act", lambda e, acc=acc: e.activation(out=acc[:], in_=acc[:], func=AF.Silu),
                                 reads=[ak], writes=[ak])
                            sqv = QKV[:, kind, j, :]
                            sqk = f"qkv{kind}_{j}"
                            P.op("act", lambda e, acc=acc, sqv=sqv: e.activation(out=sqv, in_=acc[:], func=AF.Square),
                                 reads=[ak], writes=[sqk])
                            def part_b(acc=acc, ak=ak, sqv=sqv, sqk=sqk, kind=kind, j=j):
                                for blk in range(4):
                                    pb = pss[blk]
                                    P.op("pe", lambda e, blk=blk, pb=pb, sqv=sqv: e.matmul(out=pb[:], lhsT=C.onesb,
                                                                                 rhs=sqv[:, blk * 512:(blk + 1) * 512],
                                                                                 start=True, stop=True),
                                         reads=[sqk, "const"], banks=[f"pss{blk}"])
                                    P.op("act", lambda e, blk=blk, pb=pb: e.activation(
                                        out=sd[:, blk * 512:(blk + 1) * 512], in_=pb[:], func=AF.Ln, bias=C.epsb[:], scale=1.0),
                                        reads=["epsb"], writes=["sd"], banks=[f"pss{blk}"])
                                P.op("act", lambda e: e.activation(out=sd[:], in_=sd[:], func=AF.Exp, scale=-0.5),
                                     reads=["sd"], writes=["sd"])
                                scl = (128.0 ** -0.5) if kind == 2 else 1.0
                                P.op("dve", lambda e, kind=kind, j=j, scl=scl, acc=acc: e.scalar_tensor_tensor(
                                    out=QKV[:, kind, j, :], in0=acc[:], scalar=scl, in1=sd[:], op0=ALU.mult, op1=ALU.mult),
                                    reads=[ak, "sd"], writes=[f"qkv{kind}_{j}"])
                            pending.append(part_b)
                            for fn in prev_b:
                                fn()
                        assert not pending
                        if dbg:
                            P.barrier()
                            P.dma_op("sp", "dbg1", D_["qkv"], QKV[:].rearrange("p a b c -> p (a b c)"))
                            P.dma_op("sp", "dbg2", D_["rowsraw"], rows[:])
                            P.dma_op("sp", "dbg3", D_["fT"], fT[:].rearrange("p a b -> p (a b)"))
                        P.barrier()
                        P.emit()
                    if upto < 2:
                        return
                    with ExitStack() as s1:
                        P = Prog(nc, s1, "p1z")
                        wz = _sb(nc, s1, "wz", [128, 16, 512], BF16)
                        pacc = [_ps(nc, s1, f"pzacc{i}", [128, 512], F32) for i in range(4)]
                        P.dma_op("pool", "wz", wz[:], wz_d, writes=["wz"])
                        for t in range(NTILE):
                            pb = pacc[t % 4]
                            for kc in range(16):
                                P.op("pe", lambda e, t=t, kc=kc, pb=pb: e.matmul(
                                    out=pb[:], lhsT=hT[:, kc, t * 128:(t + 1) * 128], rhs=wz[:, kc, :],
                                    start=(kc == 0), stop=(kc == 15)), reads=["wz"], banks=[f"pz{t%4}"])
                            P.op("act", lambda e, t=t, pb=pb: e.activation(out=szt[:, t, :], in_=pb[:], func=AF.Silu),
                                 writes=[f"szt{t}"], banks=[f"pz{t%4}"])
                        if dbg:
                            P.barrier()
                            P.dma_op("sp", "dbg4", D_["szt"], szt[:].rearrange("p a b -> p (a b)"))
                        P.barrier()
                        P.emit()
                if upto < 3:
                    return
                with ExitStack() as s2:
                    P = Prog(nc, s2, "pf")
                    Y = _sb(nc, s2, "Y", [128, NTILE, H, 256], BF16)
                    DT = [_sb(nc, s2, f"DT{i}", [128, 16, 2, 512], BF16) for i in range(2)]
                    stg = [_sb(nc, s2, f"fstg{i}", [128, H, 512], BF16) for i in range(2)]
                    pf = [_ps(nc, s2, f"pf{i}", [128, 512], F32) for i in range(8)]
                    P.dma_op("sp", "dt0", DT[0][:], dft_d[0], writes=["DT0"])
                    P.dma_op("sp", "dt1", DT[1][:], dft_d[1], writes=["DT1"])
                    n = 0
                    for t in range(NTILE):
                        for gp in range(2):
                            pb = pf[n % 4]
                            for gi in range(2):
                                g = gp * 2 + gi
                                P.op("pe", lambda e, t=t, g=g, gi=gi, pb=pb: e.matmul(
                                    out=pb[:, gi * 256:(gi + 1) * 256], lhsT=fT[:, g, t * 128:(t + 1) * 128],
                                    rhs=C.CS.rearrange("p a b -> p (a b)"), start=True, stop=True),
                                    reads=["const"], banks=[f"pf{n%4}"])
                            if n % 2 == 0:
                                P.op("act", lambda e, t=t, gp=gp, pb=pb: e.copy(
                                    out=Y[:, t, gp * 2:gp * 2 + 2, :].rearrange("p a b -> p (a b)"), in_=pb[:]),
                                    writes=[f"Y{t}_{gp}"], banks=[f"pf{n%4}"])
                            else:
                                P.op("dve", lambda e, t=t, gp=gp, pb=pb: e.tensor_copy(
                                    out=Y[:, t, gp * 2:gp * 2 + 2, :].rearrange("p a b -> p (a b)"), in_=pb[:]),
                                    writes=[f"Y{t}_{gp}"], banks=[f"pf{n%4}"])
                            n += 1
                    yk = [f"Y{t}_{gp}" for t in range(NTILE) for gp in range(2)]
                    for blk in range(4):
                        dtb = DT[blk % 2]
                        sg = stg[blk % 2]
                        for g in range(H):
                            bi = (blk % 2) * 4 + g
                            pb = pf[bi]
                            for t in range(NTILE):
                                for cs in range(2):
                                    P.op("pe", lambda e, t=t, g=g, cs=cs, pb=pb, dtb=dtb: e.matmul(
                                        out=pb[:], lhsT=Y[:, t, g, cs * 128:(cs + 1) * 128], rhs=dtb[:, t, cs, :],
                                        start=(t == 0 and cs == 0), stop=(t == NTILE - 1 and cs == 1)),
                                        reads=yk + [f"DT{blk%2}"], banks=[f"pf{bi}"])
                            if g % 2 == 0:
                                P.op("act", lambda e, pb=pb, sg=sg, g=g: e.copy(out=sg[:, g, :], in_=pb[:]),
                                     writes=[f"fstg{blk%2}_{g}"], banks=[f"pf{bi}"])
                            else:
                                P.op("dve", lambda e, pb=pb, sg=sg, g=g: e.tensor_copy(out=sg[:, g, :], in_=pb[:]),
                                     writes=[f"fstg{blk%2}_{g}"], banks=[f"pf{bi}"])
                        dst = own_fr if blk < 2 else snd_fr
                        c0 = (blk % 2) * 512
                        P.dma_op("sp", f"fo{blk%2}", dst[:, :, c0:c0 + 512].rearrange("g p n -> p g n"), sg[:],
                                 reads=[f"fstg{blk%2}_{g}" for g in range(H)])
                        if blk + 2 < 4:
                            P.dma_op("sp", f"dt{blk%2}", DT[blk % 2][:], dft_d[blk + 2], writes=[f"DT{blk%2}"])
                    P.barrier()
                    P.emit()
            if upto < 4:
                return
            with ExitStack() as s3:
                build_gdn(nc, s3, C, QKV, szt, rows, pc, ghead, own_og, snd_og, D_, upto, C_selb_d)
    return


def build_gdn(nc, st, C, QKV, szt, rows, pc, ghead, own_og, snd_og, D_, upto=9, selb_d=None):
    sb = lambda name, shape, dt: _sb(nc, st, name, shape, dt)
    ps = lambda name, shape, dt: _ps(nc, st, name, shape, dt)
    selbt = sb("selbt", [128, 16, 128], BF16)
    C.selb = lambda kind, d, h: selbt[:, kind * 8 + d * 4 + h, :]
    Rhl = sb("Rhl", [128, 2, T], BF16)
    NRhl = sb("NRhl", [128, 2, T], BF16)
    CC = sb("CC", [128, NTILE, 3, 2, H], F32)
    GLbc = sb("GLbc", [128, 8, 32], F32)
    st1 = ExitStack()
    sb1 = lambda name, shape, dt: _sb(nc, st1, name, shape, dt)
    P = Prog(nc, st1, "pgs")
    R = sb1("R", [128, T], F32)
    NR = sb1("NR", [128, T], F32)
    Cs = sb1("Cs", [128, T], F32)
    R2 = sb1("R2", [128, T], F32)
    COLR = sb1("COLR", [128, NTILE, 128], F32)
    COLR2 = sb1("COLR2", [128, NTILE, 128], F32)
    m01 = sb1("m01", [128, T], F32)
    cmul = sb1("cmul", [128, 1], F32)
    GLrow = sb1("GLrow", [128, 32], F32)
    BK = [[ps(f"gB{d}{i}", [128, 4, 128], F32) for i in range(3)] for d in range(2)]
    W = [ps(f"gW{d}", [128, 4, 128], F32) for d in range(2)]
    B1, B2, B3 = BK[0]

    P.op("act", lambda e: e.activation(out=rows[:], in_=rows[:], func=AF.Exp, bias=pc[:, 0:1], scale=pc[:, 1:2]),
         reads=["rows", "pc"], writes=["rows"])
    P.op("act", lambda e: e.activation(out=rows[:], in_=rows[:], func=AF.Ln, bias=1.0, scale=1.0),
         reads=["rows"], writes=["rows"])
    P.op("act", lambda e: e.activation(out=cmul[:], in_=pc[:, 2:3], func=AF.Exp), reads=["pc"], writes=["cmul"])
    P.op("dve", lambda e: e.tensor_scalar(out=cmul[:], in0=cmul[:], scalar1=-1.0, scalar2=None, op0=ALU.mult),
         reads=["cmul"], writes=["cmul"])
    P.op("dve", lambda e: e.tensor_scalar(out=R[:], in0=rows[:], scalar1=cmul[:, 0:1], scalar2=None, op0=ALU.mult),
         reads=["rows", "cmul"], writes=["R"])
    P.op("pool", lambda e: e.memset(m01[:], 1.0), writes=["m01"])
    P.op("pool", lambda e: e.memset(m01[:].rearrange("p (c k) -> p c k", k=64)[:, :, 0:1], 0.0), writes=["m01"])
    v3 = lambda t: t[:].rearrange("p (c k) -> p c k", k=64)
    P.op("dve", lambda e: e.tensor_tensor_scan(out=Cs[:], data0=m01[:], data1=R[:], initial=0.0, op0=ALU.mult, op1=ALU.add),
         reads=["R", "m01"], writes=["Cs"])
    P.op("act", lambda e: e.copy(out=GLrow[:], in_=v3(Cs)[:, :, 63]), reads=["Cs"], writes=["GLrow"])
    P.op("dve", lambda e: e.tensor_tensor(out=v3(R2)[0:32], in0=v3(Cs)[0:32, :, 63:64].to_broadcast([32, 32, 64]),
                                          in1=v3(Cs)[0:32], op=ALU.subtract), reads=["Cs"], writes=["R2"])
    P.op("dve", lambda e: e.tensor_tensor(out=R2[32:64, :], in0=Cs[32:64, :], in1=R[32:64, :], op=ALU.subtract),
         reads=["Cs", "R"], writes=["R2"])
    P.op("pool", lambda e: e.memset(R2[64:128, :], 0.0), writes=["R2"])
    P.op("dve", lambda e: e.tensor_tensor(out=v3(R)[32:64], in0=v3(Cs)[32:64, :, 63:64].to_broadcast([32, 32, 64]),
                                          in1=v3(R2)[32:64], op=ALU.subtract), reads=["Cs", "R2", "R"], writes=["R"])
    P.op("act", lambda e: e.copy(out=R[0:32, :], in_=Cs[0:32, :]), reads=["Cs", "R"], writes=["R"])
    P.op("dve", lambda e: e.tensor_scalar(out=NR[:], in0=R[:], scalar1=-1.0, scalar2=None, op0=ALU.mult),
         reads=["R"], writes=["NR"])
    for t in range(NTILE):
        bk = B1 if t % 2 == 0 else B2
        bkn = "B1" if t % 2 == 0 else "B2"
        P.op("pe", lambda e, t=t, bk=bk: e.matmul(out=bk[:, 0, :], lhsT=R[:, t * 128:(t + 1) * 128], rhs=C.identf,
                                                  start=True, stop=True), reads=["R", "const"], banks=[bkn])
        P.op("pe", lambda e, t=t, bk=bk: e.matmul(out=bk[:, 1, :], lhsT=R2[:, t * 128:(t + 1) * 128], rhs=C.identf,
                                                  start=True, stop=True), reads=["R2", "const"], banks=[bkn])
        P.op("act", lambda e, t=t, bk=bk: e.copy(out=COLR[:, t, :], in_=bk[:, 0, :]), writes=["COLR"], banks=[bkn])
        P.op("dve", lambda e, t=t, bk=bk: e.tensor_copy(out=COLR2[:, t, :], in_=bk[:, 1, :]), writes=["COLR2"], banks=[bkn])
    for d in range(2):
        P.op("act", lambda e, d=d: e.activation(out=CC[:, :, 0, d, :], in_=COLR[:, :, 64 + 32 * d:64 + 32 * d + 4],
                                                func=AF.Exp), reads=["COLR"], writes=["CC"])
        P.op("dve", lambda e, d=d: e.tensor_tensor(out=CC[:, :, 1, d, :], in0=COLR[:, :, 32 * d:32 * d + 4],
                                                   in1=COLR[:, :, 64 + 32 * d:64 + 32 * d + 4], op=ALU.add),
             reads=["COLR"], writes=["CC"])
        P.op("act", lambda e, d=d: e.activation(out=CC[:, :, 1, d, :], in_=CC[:, :, 1, d, :], func=AF.Exp),
             reads=["CC"], writes=["CC"])
        P.op("act", lambda e, d=d: e.activation(out=CC[:, :, 2, d, :], in_=COLR2[:, :, 32 * d:32 * d + 4], func=AF.Exp),
             reads=["COLR2"], writes=["CC"])
    for d in range(2):
        for h in range(H):
            P.op("pe", lambda e, d=d, h=h: e.matmul(
                out=B3[:].rearrange("p a b -> p (a b)")[:, (d * 4 + h) * 32:(d * 4 + h + 1) * 32],
                lhsT=C.sel(0, d, h), rhs=GLrow[:], start=True, stop=True),
                 reads=["GLrow", "const"], banks=["B3"])
    P.op("act", lambda e: e.activation(out=GLbc[:].rearrange("p a b -> p (a b)"),
                                       in_=B3[:].rearrange("p a b -> p (a b)")[:, 0:256], func=AF.Exp),
         writes=["GLbc"], banks=["B3"])
    if D_:
        P.dma_op("sp", "dbg_rows", D_["rows"], R[:], reads=["R"])
        P.dma_op("sp", "dbg_cc", D_["cc"], CC[:].rearrange("p a b c d -> p (a b c d)"), reads=["CC"])
        P.dma_op("sp", "dbg_gl", D_["glbc"], GLbc[:].rearrange("p a b -> p (a b)"), reads=["GLbc"])

    P.dma_op("sp", "selb", selbt[:], selb_d.rearrange("p (a b) -> p a b", b=128), writes=["const"])
    for t in range(NTILE):
        P.op("pool", lambda e, t=t: e.tensor_tensor(out=szt[:, t, :].rearrange("p (a b) -> p a b", b=128),
                                                    in0=szt[:, t, :].rearrange("p (a b) -> p a b", b=128),
                                                    in1=ghead[:].unsqueeze(1).to_broadcast([128, 4, 128]), op=ALU.mult),
             reads=["ghead"], writes=[f"gz{t}"])
    P.op("act", lambda e: e.copy(out=Rhl[:, 0, :], in_=R[:]), reads=["R"], writes=["Rhi"])
    P.op("dve", lambda e: e.tensor_tensor(out=NR[:], in0=R[:], in1=Rhl[:, 0, :], op=ALU.subtract),
         reads=["R", "Rhi", "NR"], writes=["NR"])
    P.op("act", lambda e: e.copy(out=Rhl[:, 1, :], in_=NR[:]), reads=["NR"], writes=["Rlo"])
    P.op("dve", lambda e: e.tensor_scalar(out=NRhl[:].rearrange("p a b -> p (a b)"), in0=Rhl[:].rearrange("p a b -> p (a b)"),
                                          scalar1=-1.0, scalar2=None, op0=ALU.mult), reads=["Rhi", "Rlo"], writes=["NRhl"])
    P.barrier()
    P.emit()
    st1.close()
    if upto < 5:
        return
    P = Prog(nc, st, "pg")
    tmp = {}
    for d in range(2):
        tmp[d] = dict(
            E2=sb(f"E2_{d}", [128, 4, 128], F32), E3=sb(f"E3_{d}", [128, 4, 128], F32),
            NT=sb(f"NTm{d}", [128, 4, 128], BF16), Nm=sb(f"Nm{d}", [128, 4, 128], BF16),
            Pm=[sb(f"Pm{d}{i}", [128, 4, 128], BF16) for i in range(2)],
            PTm=[sb(f"PTm{d}{i}", [128, 4, 128], BF16) for i in range(2)],
            Rm=[sb(f"Rm{d}{i}", [128, 4, 128], BF16) for i in range(2)],
            kbg=sb(f"kbg{d}", [128, 4, 128], BF16), vb=sb(f"vb{d}", [128, 4, 128], BF16),
            egc=sb(f"egc{d}", [128, 4, 128], F32))
    slot = {}
    for d in range(2):
        for s in range(2):
            slot[(d, s)] = dict(
                qkT=sb(f"qkT{d}{s}", [128, 4, 128], BF16), u=sb(f"u{d}{s}", [128, 4, 128], F32),
                wT=sb(f"wT{d}{s}", [128, 4, 128], BF16), kd=sb(f"kd{d}{s}", [128, 4, 128], BF16),
                qd=sb(f"qd{d}{s}", [128, 4, 128], BF16))
    S32 = sb("S32", [128, 2, 4, 128], F32)
    Sbf = sb("Sbf", [128, 2, 4, 128], BF16)
    vnew = sb("vnew", [128, 2, 4, 128], BF16)
    ost = sb("ost", [128, 16, 4, 128], BF16)
    ot = sb("ot", [128, 2, 4, 128], F32)
    osq = sb("osq", [128, 2, 4, 128], F32)
    oss = sb("oss", [128, 2, 4, 2], F32)
    ogt = [sb(f"ogt{d}", [128, 4, 128], BF16) for d in range(2)]
    ostg = [sb(f"ostg{i}", [128, 4, 128], BF16) for i in range(2)]
    P.op("pool", lambda e: e.memset(S32[:].rearrange("p a b c -> p (a b c)"), 0.0), writes=["S32_0", "S32_1"])
    P.op("pool", lambda e: e.memset(Sbf[:].rearrange("p a b c -> p (a b c)"), 0.0), writes=["Sbf0", "Sbf1"])
    P.op("pool", lambda e: e.memset(vnew[:].rearrange("p a b c -> p (a b c)"), 0.0), writes=["vnew0", "vnew1"])

    f4 = lambda ap: ap.rearrange("p a b -> p (a b)")

    def prep(tt, d, s):
        sl = slot[(d, s)]
        tm = tmp[d]
        E2, E3, NT, Nm, Pm, PTm, Rm, kbg, vb, egc = (tm[k_] for k_ in ("E2", "E3", "NT", "Nm", "Pm", "PTm", "Rm", "kbg", "vb", "egc"))
        k = f"{d}{s}"
        B1, B2, B3 = BK[d]
        b1k, b2k, b3k = f"B1_{d}", f"B2_{d}", f"B3_{d}"
        tsl = slice(tt * 128, (tt + 1) * 128)
        bc = lambda kind: CC[:, tt, kind, d, :].unsqueeze(2).to_broadcast([128, 4, 128])
        for (incl, Et, ek) in ((0, E2, f"E2_{d}"), (1, E3, f"E3_{d}")):
            for h in range(H):
                for hl in range(2):
                    P.op("pe", lambda e, h=h, incl=incl, hl=hl: e.matmul(out=B3[:, h, :], lhsT=C.selb(1 - incl, d, h),
                                                                         rhs=Rhl[:, hl, tsl], start=(hl == 0), stop=False),
                         reads=["const"], banks=[b3k])
                for hl in range(2):
                    P.op("pe", lambda e, h=h, hl=hl: e.matmul(out=B3[:, h, :], lhsT=NRhl[:, hl, tsl], rhs=C.selb(0, d, h),
                                                              start=False, stop=False), reads=["const"], banks=[b3k])
                P.op("pe", lambda e, h=h, incl=incl: e.matmul(out=B3[:, h, :], lhsT=C.negI, rhs=C.mask(d, incl),
                                                              start=False, stop=True), reads=["const"], banks=[b3k])
            P.op("act", lambda e, Et=Et: e.activation(out=f4(Et[:]), in_=f4(B3[:]), func=AF.Exp),
                 writes=[ek], banks=[b3k])
            yield
            if incl == 0:
                for h in range(H):
                    P.op("pe", lambda e, h=h: e.matmul(out=B1[:, h, :], lhsT=QKV[:, 0, h, tsl], rhs=QKV[:, 0, h, tsl],
                                                       start=True, stop=True), banks=[b1k])
                P.op("dve", lambda e: e.scalar_tensor_tensor(out=f4(NT[:]), in0=f4(B1[:]), scalar=-1.0, in1=f4(E2[:]),
                                                             op0=ALU.mult, op1=ALU.mult),
                     reads=[f"E2_{d}"], writes=[f"NT{d}"], banks=[b1k])
                yield
                for h in range(H):
                    P.op("pe", lambda e, h=h: e.matmul(out=B2[:, h, :], lhsT=NT[:, h, :], rhs=C.identb, start=True, stop=True),
                         reads=[f"NT{d}", "const"], banks=[b2k])
                P.op("act", lambda e: e.copy(out=f4(Nm[:]), in_=f4(B2[:])), writes=[f"Nm{d}"], banks=[b2k])
                P.op("pool", lambda e: e.tensor_tensor(out=Rm[0][:], in0=NT[:],
                                                       in1=C.identb.unsqueeze(1).to_broadcast([128, 4, 128]), op=ALU.add),
                     reads=[f"NT{d}", "const"], writes=[f"Rm{d}0"])
                yield
        Pc, PTc, pk, ptk = Nm, NT, f"Nm{d}", f"NT{d}"
        ri = 0
        for lvl in range(5):
            Pn, PTn = Pm[lvl % 2], PTm[lvl % 2]
            pnk, ptnk = f"Pm{d}{lvl%2}", f"PTm{d}{lvl%2}"
            for h in range(H):
                P.op("pe", lambda e, h=h, Pc=Pc, PTc=PTc: e.matmul(out=B1[:, h, :], lhsT=PTc[:, h, :], rhs=Pc[:, h, :],
                                                                   start=True, stop=True), reads=[pk, ptk], banks=[b1k])
            P.op("act", lambda e, Pn=Pn: e.copy(out=f4(Pn[:]), in_=f4(B1[:])), writes=[pnk], banks=[b1k])
            if lvl < 4:
                for h in range(H):
                    P.op("pe", lambda e, h=h, Pc=Pc, PTc=PTc: e.matmul(out=B2[:, h, :], lhsT=Pc[:, h, :], rhs=PTc[:, h, :],
                                                                       start=True, stop=True), reads=[pk, ptk], banks=[b2k])
                if lvl % 2 == 0:
                    P.op("dve", lambda e, PTn=PTn: e.tensor_copy(out=f4(PTn[:]), in_=f4(B2[:])), writes=[ptnk], banks=[b2k])
                else:
                    P.op("act", lambda e, PTn=PTn: e.copy(out=f4(PTn[:]), in_=f4(B2[:])), writes=[ptnk], banks=[b2k])
            yield
            for h in range(H):
                P.op("pe", lambda e, h=h, Pn=Pn, ri=ri: e.matmul(out=B3[:, h, :], lhsT=Pn[:, h, :], rhs=Rm[ri][:, h, :],
                                                                 start=True, stop=True),
                     reads=[pnk, f"Rm{d}{ri}"], banks=[b3k])
            P.op("dve", lambda e, ri=ri: e.tensor_tensor(out=f4(Rm[1 - ri][:]), in0=f4(B3[:]), in1=f4(Rm[ri][:]), op=ALU.add),
                 reads=[f"Rm{d}{ri}"], writes=[f"Rm{d}{1-ri}"], banks=[b3k])
            ri = 1 - ri
            Pc, PTc, pk, ptk = Pn, PTn, pnk, ptnk
            yield
            if lvl == 0:
                for h in range(H):
                    P.op("pe", lambda e, h=h: e.matmul(out=B3[:, h, :], lhsT=QKV[:, 0, h, tsl], rhs=C.identb, start=True, stop=True),
                         reads=["const"], banks=[b3k])
                P.op("dve", lambda e: e.tensor_tensor(out=kbg[:], in0=B3[:], in1=bc(1), op=ALU.mult),
                     reads=["CC"], writes=[f"kbg{d}"], banks=[b3k])
                P.op("act", lambda e: e.copy(out=f4(egc[:]), in_=f4(B3[:])), writes=[f"egc{d}"], banks=[b3k])
                P.op("pool", lambda e: e.tensor_tensor(out=sl["kd"][:], in0=egc[:], in1=bc(2), op=ALU.mult),
                     reads=["CC", f"egc{d}"], writes=["kd" + k])
                yield
                for h in range(H):
                    P.op("pe", lambda e, h=h: e.matmul(out=B3[:, h, :], lhsT=QKV[:, 1, h, tsl], rhs=C.identb, start=True, stop=True),
                         reads=["const"], banks=[b3k])
                P.op("dve", lambda e: e.tensor_tensor(out=vb[:], in0=B3[:], in1=bc(0), op=ALU.mult),
                     reads=["CC"], writes=[f"vb{d}"], banks=[b3k])
                yield
            elif lvl == 1:
                for h in range(H):
                    P.op("pe", lambda e, h=h: e.matmul(out=B2[:, h, :], lhsT=QKV[:, 0, h, tsl], rhs=QKV[:, 2, h, tsl],
                                                       start=True, stop=True), banks=[b2k])
                P.op("dve", lambda e: e.tensor_tensor(out=f4(sl["qkT"][:]), in0=f4(B2[:]), in1=f4(E3[:]), op=ALU.mult),
                     reads=[f"E3_{d}"], writes=["qkT" + k], banks=[b2k])
                yield
            elif lvl == 2:
                for h in range(H):
                    for hl in range(2):
                        P.op("pe", lambda e, h=h, hl=hl: e.matmul(out=B3[:, h, :], lhsT=C.selb(0, d, h), rhs=Rhl[:, hl, tsl],
                                                                  start=(hl == 0), stop=(hl == 1)),
                             reads=["const"], banks=[b3k])
                P.op("act", lambda e: e.activation(out=f4(egc[:]), in_=f4(B3[:]), func=AF.Exp),
                     reads=["kd" + k], writes=[f"egc{d}"], banks=[b3k])
                P.op("pool", lambda e: e.tensor_tensor(out=sl["qd"][:], in0=QKV[:, 2, :, tsl], in1=egc[:], op=ALU.mult),
                     reads=[f"egc{d}"], writes=["qd" + k])
                yield
        Rf, rk = Rm[ri], f"Rm{d}{ri}"
        for h in range(H):
            P.op("pe", lambda e, h=h: e.matmul(out=B1[:, h, :], lhsT=Rf[:, h, :], rhs=vb[:, h, :], start=True, stop=True),
                 reads=[rk, f"vb{d}"], banks=[b1k])
        P.op("act", lambda e: e.copy(out=f4(sl["u"][:]), in_=f4(B1[:])), writes=["u" + k], banks=[b1k])
        for h in range(H):
            P.op("pe", lambda e, h=h: e.matmul(out=B2[:, h, :], lhsT=kbg[:, h, :], rhs=Rf[:, h, :], start=True, stop=True),
                 reads=[rk, f"kbg{d}"], banks=[b2k])
        P.op("dve", lambda e: e.tensor_copy(out=f4(sl["wT"][:]), in_=f4(B2[:])), writes=["wT" + k], banks=[b2k])
        yield

    def step(s, info):
        for d in range(2):
            c = s if d == 0 else 31 - s
            tt, half = c // 2, c % 2
            sl = slot[(d, tt % 2)]
            k = f"{d}{tt%2}"
            rs = slice(half * 64, half * 64 + 64)
            info.append((c, tt, half, sl, k, rs))
            for h in range(H):
                P.op("pe", lambda e, h=h, sl=sl, d=d: e.matmul(out=W[d][:, h, :], lhsT=sl["wT"][:, h, :], rhs=Sbf[:, d, h, :],
                                                               start=True, stop=True),
                     reads=["wT" + k, f"Sbf{d}"], banks=[f"W{d}"])
            P.op("dve", lambda e, sl=sl, d=d, rs=rs: e.tensor_tensor(out=vnew[rs, d], in0=sl["u"][rs], in1=W[d][rs],
                                                                      op=ALU.subtract),
                 reads=["u" + k], writes=[f"vnew{d}"], banks=[f"W{d}"])
            P.op("pool", lambda e, d=d, c=c: e.tensor_tensor(
                out=S32[:, d], in0=S32[:, d], in1=GLbc[:, d * 4:d * 4 + 4, c:c + 1].to_broadcast([128, 4, 128]), op=ALU.mult),
                reads=["GLbc", f"S32_{d}"], writes=[f"S32_{d}"])
        yield
        for d in range(2):
            c, tt, half, sl, k, rs = info[d]
            for h in range(H):
                P.op("pe", lambda e, h=h, sl=sl, d=d, rs=rs: e.matmul(out=W[d][:, h, :], lhsT=sl["kd"][rs, h, :],
                                                                      rhs=vnew[rs, d, h, :], start=True, stop=True),
                     reads=["kd" + k, f"vnew{d}"], banks=[f"W{d}"])
            P.op("dve", lambda e, d=d: e.tensor_tensor(out=S32[:, d], in0=S32[:, d], in1=W[d][:], op=ALU.add),
                 reads=[f"S32_{d}"], writes=[f"S32_{d}"], banks=[f"W{d}"])
        yield
        for d in range(2):
            c, tt, half, sl, k, rs = info[d]
            for h in range(H):
                P.op("pe", lambda e, h=h, sl=sl, d=d: e.matmul(out=W[d][:, h, :], lhsT=sl["qd"][:, h, :], rhs=Sbf[:, d, h, :],
                                                               start=True, stop=False),
                     reads=["qd" + k, f"Sbf{d}"], banks=[f"W{d}"])
                P.op("pe", lambda e, h=h, sl=sl, d=d, rs=rs: e.matmul(out=W[d][:, h, :], lhsT=sl["qkT"][rs, h, :],
                                                                      rhs=vnew[rs, d, h, :], start=False, stop=True),
                     reads=["qkT" + k, f"vnew{d}"], banks=[f"W{d}"])
            if s < 16:
                P.op("act", lambda e, d=d, rs=rs, s=s: e.copy(out=ost[rs, s], in_=W[d][rs]),
                     writes=[f"ost{s}_{d}"], banks=[f"W{d}"])
            else:
                s0 = 31 - s
                P.op("dve", lambda e, d=d, rs=rs, s0=s0: e.tensor_tensor(out=ot[rs, d], in0=W[d][rs], in1=ost[rs, s0], op=ALU.add),
                     reads=[f"ost{s0}_{1-d}"], writes=[f"ot{d}_{half}"], banks=[f"W{d}"])
            P.op("act", lambda e, d=d: e.copy(out=Sbf[:, d], in_=S32[:, d]), reads=[f"S32_{d}"], writes=[f"Sbf{d}"])
        yield

    def tail(s, info):
        for d in range(2):
            c, tt, half, sl, k, rs = info[d]
            for h in range(H):
                P.op("act", lambda e, d=d, rs=rs, h=h: e.activation(out=osq[rs, d, h, :], in_=ot[rs, d, h, :], func=AF.Square,
                                                                    accum_out=oss[rs, d, h, 0:1]),
                     reads=[f"ot{d}_{half}"], writes=[f"osq{d}_{half}", f"oss{d}_{half}"])
            P.op("pool", lambda e, d=d, rs=rs: e.tensor_scalar(out=oss[rs, d, :, 1], in0=oss[rs, d, :, 0], scalar1=1.0 / 128.0,
                                                               scalar2=EPS, op0=ALU.mult, op1=ALU.add),
                 reads=[f"oss{d}_{half}"], writes=[f"oss{d}_{half}"])
            P.op("pool", lambda e, d=d, rs=rs: e.tensor_tensor(out=oss[rs, d, :, 1], in0=oss[rs, d, :, 1],
                                                               in1=C.mhalf[rs, 0:1].to_broadcast([64, 4]), op=ALU.pow),
                 reads=[f"oss{d}_{half}", "mhalf"], writes=[f"oss{d}_{half}"])
        yield
        for d in range(2):
            c, tt, half, sl, k, rs = info[d]
            P.op("dve", lambda e, d=d, rs=rs: e.tensor_tensor(out=ot[rs, d], in0=ot[rs, d],
                                                              in1=oss[rs, d, :, 1:2].to_broadcast([64, 4, 128]), op=ALU.mult),
                 reads=[f"oss{d}_{half}", f"ot{d}_{half}"], writes=[f"ot{d}_{half}"])
            P.op("dve", lambda e, d=d, rs=rs, tt=tt: e.tensor_tensor(
                out=ogt[d][rs], in0=ot[rs, d], in1=szt[rs, tt, :].rearrange("p (a b) -> p a b", b=128), op=ALU.mult),
                reads=[f"ot{d}_{half}"], writes=[f"ogt{d}_{half}"])
        yield
        for d in range(2):
            c, tt, half, sl, k, rs = info[d]
            done = (half == 1) if d == 0 else (half == 0)
            if done:
                ij = C.Jb if d == 0 else C.identb
                for h in range(H):
                    P.op("pe", lambda e, h=h, d=d, ij=ij: e.matmul(out=BK[d][2][:, h, :], lhsT=ogt[d][:, h, :], rhs=ij,
                                                                   start=True, stop=True),
                         reads=[f"ogt{d}_0", f"ogt{d}_1", "const"], banks=[f"B3_{d}"])
                P.op("act", lambda e, d=d: e.copy(out=f4(ostg[d][:]), in_=f4(BK[d][2][:])), writes=[f"ostg{d}"],
                     banks=[f"B3_{d}"])
                if d == 0:
                    dst, blk = snd_og, 15 - tt
                else:
                    dst, blk = own_og, tt
                P.dma_op("sp", f"og{d}", dst[:, :, blk * 128:(blk + 1) * 128].rearrange("g p n -> p g n"), ostg[d][:],
                         reads=[f"ostg{d}"])
        yield

    active = []

    def steps2(a):
        for s_ in (2 * a, 2 * a + 1):
            info = []
            yield from step(s_, info)
            if s_ >= 16:
                active.append(tail(s_, info))

    def run(gens):
        active[:] = list(gens)
        while active:
            for g in list(active):
                try:
                    next(g)
                except StopIteration:
                    active.remove(g)

    nround = 16 if upto >= 9 else 2
    run([prep(0, 0, 0), prep(15, 1, 1)])
    for a in range(nround):
        gl = [steps2(a)]
        if a + 1 < 16:
            gl += [prep(a + 1, 0, (a + 1) % 2), prep(15 - (a + 1), 1, (15 - (a + 1)) % 2)]
        run(gl)
    P.barrier()
    P.emit()


def _consts(c):
    cf = np.zeros((128, 17, 128), np.float32)
    cf[:, 0, :] = np.eye(128, dtype=np.float32)
    for d in range(2):
        for h in range(H):
            cf[d * 32 + h, 1 + d * 4 + h, :] = 1.0
            cf[d * 32 + h, 1 + 8 + d * 4 + h, :] = 1.0
            cf[64 + d * 32 + h, 1 + 8 + d * 4 + h, :] = 1.0
    cb = np.zeros((128, 10, 128), np.float32)
    cb[:, 0, :] = np.eye(128)
    cb[:, 1, :] = 1.0
    cb[:, 2, :] = np.eye(128)[::-1]
    cb[:, 3, :] = -BIG * np.eye(128)
    j = np.arange(128)[:, None]
    i = np.arange(128)[None, :]
    same = (j // 64) == (i // 64)
    cb[:, 4, :] = 1.0 - (same & (i > j))
    cb[:, 5, :] = 1.0 - (same & (i >= j))
    cb[:, 6, :] = 1.0 - (same & (i < j))
    cb[:, 7, :] = 1.0 - (same & (i <= j))
    cc = np.arange(128)
    ang = 2 * np.pi * ((cc[:, None] * cc[None, :]) % 128) / 128.0
    cb[:, 8, :] = np.cos(ang)
    cb[:, 9, :] = -np.sin(ang)
    tin = np.arange(T)
    tg_in = tin if c == 0 else (T - 1 - tin)
    n = np.arange(T)
    tl = np.where(n < OWN, n, 3071 - n)
    tg_out = tl if c == 0 else (T - 1 - tl)
    prod = (tg_in[:, None].astype(np.int64) * tg_out[None, :].astype(np.int64)) % T
    ang = 2 * np.pi * prod / T
    norm = 1.0 / np.sqrt(T * 128.0)
    dtc = (np.cos(ang) * norm).astype(np.float32)
    dts = (np.sin(ang) * norm).astype(np.float32)
    dft = np.stack([dtc, dts], 0)
    dft = dft.reshape(2, 16, 128, 4, 512).transpose(3, 2, 1, 0, 4)
    selb = np.ascontiguousarray(cf[:, 1:17, :]).reshape(128, -1).astype(NPBF)
    return (cf.reshape(128, -1), cb.reshape(128, -1).astype(NPBF), np.ascontiguousarray(dft).astype(NPBF), selb)


def _klayout(w):
    K, N = w.shape
    return np.ascontiguousarray(w.reshape(K // 128, 128, N).transpose(1, 0, 2))


def prep_A(inp, b, c, consts):
    flip = c == 1
    x = inp["x"][b]
    x = x[::-1] if flip else x
    w_in = inp["w_in"][0]
    hs = [4 * c + j for j in range(4)]
    cols = []
    for j in hs:
        cols += [np.arange(1024 + j * 128, 1024 + (j + 1) * 128), np.arange(2048 + j * 128, 2048 + (j + 1) * 128),
                 np.arange(j * 128, (j + 1) * 128)]
    for j in hs:
        cols.append(np.arange(4096 + j * 128, 4096 + (j + 1) * 128))
    cols = np.concatenate(cols)
    wmix = _klayout(w_in[:, cols])
    zc = np.concatenate([np.arange(3072 + j * 128, 3072 + (j + 1) * 128) for j in hs])
    wz = _klayout(w_in[:, zc])
    kf, kb = (1, 0) if flip else (0, 1)
    wsc = np.zeros((D, 128), np.float32)
    for jj, j in enumerate(hs):
        wsc[:, 0 + jj] = w_in[:, 5120 + (2 + kf) * 8 + j]
        wsc[:, 32 + jj] = w_in[:, 5120 + (2 + kb) * 8 + j]
        wsc[:, 64 + jj] = w_in[:, 5120 + kf * 8 + j]
        wsc[:, 96 + jj] = w_in[:, 5120 + kb * 8 + j]
    wsc = _klayout(wsc)
    conv = inp["conv_w"][0]
    conv = conv[::-1] if flip else conv
    cw = np.zeros((128, 12, 5), np.float32)
    for jj, j in enumerate(hs):
        for kind, base in ((0, 1024), (1, 2048), (2, 0)):
            cw[:, 3 * jj + kind, :] = conv[:, base + j * 128: base + (j + 1) * 128].T
    alog = (inp["a_log_fwd"][0], inp["a_log_bwd"][0])
    dtb = (inp["dt_bias_fwd"][0], inp["dt_bias_bwd"][0])
    pc = np.zeros((128, 8), np.float32)
    for jj, j in enumerate(hs):
        pc[0 + jj, 0] = dtb[kf][j]
        pc[32 + jj, 0] = dtb[kb][j]
        pc[0 + jj, 2] = alog[kf][j]
        pc[32 + jj, 2] = alog[kb][j]
    pc[0:64, 1] = 1.0
    pc[64:128, 1] = -1.0
    cf, cb, dft, selb = consts[c]
    return {
        "selb": selb,
        "x": np.ascontiguousarray(x), "gmix": np.ascontiguousarray(np.tile(inp["g_mix"][0][None], (128, 1))),
        "wmix": wmix, "wz": wz, "wscal": wsc, "cw": cw.reshape(128, 60), "pc": pc,
        "ghead": np.ascontiguousarray(np.tile(inp["g_dn_head"][0][None], (128, 1))),
        "cf": cf, "cb": cb, "dft": dft,
    }


def build_B(nrcv=4):
    nc = bass.Bass("TRN2", target_bir_lowering=False)
    di = lambda name, shape, dt: nc.dram_tensor(name, shape, dt, kind="ExternalInput").ap()
    do = lambda name, shape, dt: nc.dram_tensor(name, shape, dt, kind="ExternalOutput").ap()
    x_d = di("xo", [OWN, D], F32)
    ogfr = [(di("og_own", [4, 128, OWN], BF16), 4), (di("og_rcv", [nrcv, 128, OWN], BF16), nrcv),
            (di("fr_own", [4, 128, OWN], BF16), 4), (di("fr_rcv", [nrcv, 128, OWN], BF16), nrcv)]
    w = declare_B_weights(nc, nrcv)
    out_d = do("out", [OWN, D], F32)
    build_B_body(nc, x_d, w["g3"], w["idb"], ogfr, nrcv, w["wst"], w["wo"], w["wup"], w["wdn"], out_d)
    return nc


def declare_B_weights(nc, nrcv):
    di = lambda name, shape, dt: nc.dram_tensor(name, shape, dt, kind="ExternalInput").ap()
    NW = 8 + 2 * nrcv + 32
    return dict(g3=di("g3", [3, 128, D], F32), idb=di("idb", [128, 128], BF16),
                wst=di("wst", [16, 128, NW, 128], F32), wo=di("wo", [4, 128, 16, 512], F32),
                wup=di("wup", [64, 128, 16, 128], F32), wdn=di("wdn", [4, 4, 128, 16, 512], F32))


def build_B_body(nc, x_d, g3_d, idb_d, ogfr_d, nrcv, wst_d, wo_d, wup_d, wdn_d, out_d, cc=None):
    NK = 8 + 2 * nrcv
    NW = NK + 32
    NT8 = OWN // 128
    with ExitStack() as stP:
        C = Ctx()
        C.nc = nc
        idb = _sb(nc, stP, "b_idb", [128, 128], BF16)
        C.identb = idb
        C.epsb = _sb(nc, stP, "b_epsb", [128, 1], F32)
        C.mhalf = _sb(nc, stP, "b_mhalf", [128, 1], F32)
        C.st = _sb(nc, stP, "b_st", [128, 8, 4], F32)
        C.junk = _sb(nc, stP, "b_junk", [128, D], BF16)
        hT = _sb(nc, stP, "b_hT", [128, 16, OWN], BF16)
        mU = _sb(nc, stP, "b_mU", [128, 16, OWN], BF16)
        gbuf = _sb(nc, stP, "b_gbuf", [128, D], F32)
        hb = _sb(nc, stP, "b_hb", [128, D], BF16)
        hkeys = lambda blk: [f"hT{t}{s}" for t in range(4 * blk, 4 * blk + 4) for s in "ab"]
        with ExitStack() as s:
            P = Prog(nc, s, "b1")
            xt = [_sb(nc, s, f"b_xt{i}", [128, D], F32) for i in range(3)]
            ogfr = _sb(nc, s, "b_ogfr", [128, NK, OWN], BF16)
            wsb = [_sb(nc, s, f"b_wsb{i}", [128, NW, 128], BF16) for i in range(2)]
            sg = [_sb(nc, s, f"b_sg{i}", [128, 512], F32) for i in range(2)]
            tt_ = [_sb(nc, s, f"b_t{i}", [128, 512], F32) for i in range(2)]
            C.tr = None
            pb = [_ps(nc, s, f"b_pc{i}", [128, 512], F32) for i in range(6)]
            C.tr = [_ps(nc, s, f"b_tr{i}", [128, 8, 128], BF16) for i in range(2)]
            P.dma_op("sp", "c0", idb[:], idb_d, writes=["const"])
            P.dma_op("sp", "c1", gbuf[:], g3_d[0], writes=["gbuf"])
            P.op("pool", lambda e: e.memset(C.epsb[:], EPS), writes=["epsb"])
            P.op("pool", lambda e: e.memset(C.mhalf[:], -0.5), writes=["mhalf"])
            if cc is not None:
                P.cc_allgather(*cc)

            def load_ws(oc):
                P.dma_op("pool", f"ws{oc%2}", wsb[oc % 2][:], wst_d[oc], writes=[f"wsb{oc%2}"])
            load_ws(0)
            hba = [hb, _sb(nc, s, "b_hba1", [128, D], BF16), _sb(nc, s, "b_hba2", [128, D], BF16)]

            def tra(t):
                transpose_tile16(P, C, hba[t % 3], f"hba{t%3}",
                                 lambda half, t=t: hT[:, half * 8:(half + 1) * 8, t * 128:(t + 1) * 128],
                                 [f"hT{t}a", f"hT{t}b"], "tr")
            for t in range(NT8):
                xb = xt[t % 3]
                P.dma_op("sp", f"x{t%3}", xb[:], x_d[t * 128:(t + 1) * 128, :], writes=[f"xt{t%3}"])
                rms_tile(P, C, xb[:], f"xt{t%3}", gbuf[:], "gbuf", hba[t % 3][:], f"hba{t%3}", t)
                if t >= 1:
                    tra(t - 1)
            tra(NT8 - 1)
            o0 = 0
            for (src, n) in ogfr_d:
                P.dma_op("sp", f"ogfr{o0}", ogfr[:, o0:o0 + n, :], src.rearrange("g p n -> p g n"), reads=["rcv"],
                         writes=["ogfr"])
                o0 += n
            nog = 4 + nrcv
            for oc in range(16):
                if oc + 1 < 16:
                    load_ws(oc + 1)
                w = wsb[oc % 2]
                wk = f"wsb{oc%2}"
                for blk in range(2):
                    bs = slice(blk * 512, (blk + 1) * 512)
                    Ya, Yf = pb[0], pb[1]
                    Ga, Gf = pb[2 + 2 * blk], pb[3 + 2 * blk]
                    gak, gfk = f"pc{2+2*blk}", f"pc{3+2*blk}"
                    for kc in range(16):
                        P.op("pe", lambda e, kc=kc, w=w, Ga=Ga, bs=bs: e.matmul(out=Ga[:], lhsT=w[:, NK + kc, :], rhs=hT[:, kc, bs],
                                                                             start=(kc == 0), stop=(kc == 15)),
                             reads=[wk] + hkeys(blk), banks=[gak])
                    P.op("act", lambda e, Ga=Ga: e.activation(out=sg[0][:], in_=Ga[:], func=AF.Sigmoid),
                         writes=["sg0"], banks=[gak])
                    for kc in range(16):
                        P.op("pe", lambda e, kc=kc, w=w, Gf=Gf, bs=bs: e.matmul(out=Gf[:], lhsT=w[:, NK + 16 + kc, :], rhs=hT[:, kc, bs],
                                                                             start=(kc == 0), stop=(kc == 15)),
                             reads=[wk] + hkeys(blk), banks=[gfk])
                    P.op("act", lambda e, Gf=Gf: e.activation(out=sg[1][:], in_=Gf[:], func=AF.Sigmoid),
                         writes=["sg1"], banks=[gfk])
                    for kc in range(nog):
                        P.op("pe", lambda e, kc=kc, w=w, bs=bs: e.matmul(out=Ya[:], lhsT=w[:, kc, :], rhs=ogfr[:, kc, bs],
                                                                      start=(kc == 0), stop=(kc == nog - 1)),
                             reads=[wk, "ogfr"], banks=["pc0"])
                    P.op("dve", lambda e: e.tensor_tensor(out=tt_[0][:], in0=Ya[:], in1=sg[0][:], op=ALU.mult),
                         reads=["sg0"], writes=["t0"], banks=["pc0"])
                    for kc in range(nog):
                        P.op("pe", lambda e, kc=kc, w=w, bs=bs: e.matmul(out=Yf[:], lhsT=w[:, nog + kc, :], rhs=ogfr[:, nog + kc, bs],
                                                                      start=(kc == 0), stop=(kc == nog - 1)),
                             reads=[wk, "ogfr"], banks=["pc1"])
                    P.op("dve", lambda e: e.tensor_tensor(out=tt_[1][:], in0=Yf[:], in1=sg[1][:], op=ALU.mult),
                         reads=["sg1"], writes=["t1"], banks=["pc1"])
                    P.op("pool", lambda e, oc=oc, bs=bs: e.tensor_tensor(out=mU[:, oc, bs], in0=tt_[0][:], in1=tt_[1][:], op=ALU.add),
                         reads=["t0", "t1"], writes=[f"mU{oc}"])
            P.barrier()
            P.emit()
        with ExitStack() as s:
            x1 = _sb(nc, s, "b_x1", [128, NT8, D], F32)
            mkeys = [f"mU{oc}" for oc in range(16)]
            with ExitStack() as s2:
                P = Prog(nc, s2, "b2")
                wob = [_sb(nc, s2, f"b_wob{i}", [128, 16, 512], BF16) for i in range(2)]
                hbs = [hb] + [_sb(nc, s2, f"b_hb{i}", [128, D], BF16) for i in range(2)]
                pb = [_ps(nc, s2, f"b_pd{i}", [128, 512], F32) for i in range(4)]
                C.tr = [_ps(nc, s2, f"b_tr2{i}", [128, 8, 128], BF16) for i in range(2)]
                P.dma_op("sp", "gl", gbuf[:], g3_d[1], writes=["gbuf"])
                for t in range(NT8):
                    P.dma_op("sp", f"x1_{t%4}", x1[:, t, :], x_d[t * 128:(t + 1) * 128, :], writes=[f"x1_{t}_{c}" for c in range(4)])
                P.dma_op("pool", "wo0", wob[0][:], wo_d[0], writes=["wob0"])
                n = 0
                for cb in range(4):
                    if cb + 1 < 4:
                        P.dma_op("pool", f"wo{(cb+1)%2}", wob[(cb + 1) % 2][:], wo_d[cb + 1], writes=[f"wob{(cb+1)%2}"])
                    w = wob[cb % 2]
                    for t in range(NT8):
                        bk = pb[n % 4]
                        for kc in range(16):
                            P.op("pe", lambda e, kc=kc, t=t, w=w, bk=bk: e.matmul(out=bk[:], lhsT=mU[:, kc, t * 128:(t + 1) * 128],
                                                                               rhs=w[:, kc, :], start=(kc == 0), stop=(kc == 15)),
                                 reads=[f"wob{cb%2}"], banks=[f"pd{n%4}"])
                        P.op("dve", lambda e, t=t, cb=cb, bk=bk: e.tensor_tensor(out=x1[:, t, cb * 512:(cb + 1) * 512],
                                                                                  in0=bk[:], in1=x1[:, t, cb * 512:(cb + 1) * 512], op=ALU.add),
                             reads=[f"x1_{t}_{cb}"], writes=[f"x1_{t}_{cb}"], banks=[f"pd{n%4}"])
                        n += 1
                        if cb == 3:
                            rms_tile(P, C, x1[:, t, :], [f"x1_{t}_{c}" for c in range(4)], gbuf[:], "gbuf",
                                     hbs[t % 3][:], f"hbs{t%3}", t)
                            if t >= 2:
                                tq = t - 2
                                transpose_tile16(P, C, hbs[tq % 3], f"hbs{tq%3}",
                                                 lambda half, tq=tq: hT[:, half * 8:(half + 1) * 8, tq * 128:(tq + 1) * 128],
                                                 [f"hT{tq}a", f"hT{tq}b"], "tr")
                for tq in (NT8 - 2, NT8 - 1):
                    transpose_tile16(P, C, hbs[tq % 3], f"hbs{tq%3}",
                                     lambda half, tq=tq: hT[:, half * 8:(half + 1) * 8, tq * 128:(tq + 1) * 128],
                                     [f"hT{tq}a", f"hT{tq}b"], "tr")
                P.barrier()
                P.emit()
            with ExitStack() as s2:
                P = Prog(nc, s2, "b3")
                wub = [_sb(nc, s2, f"b_wub{i}", [128, 16, 128], BF16) for i in range(2)]
                wdb = [_sb(nc, s2, f"b_wdb{i}", [128, 16, 512], BF16) for i in range(2)]
                rl = [_sb(nc, s2, f"b_rl{i}", [128, 512], F32) for i in range(2)]
                ob = [_sb(nc, s2, f"b_ob{i}", [128, D], F32) for i in range(2)]
                P.dma_op("sp", "gl", gbuf[:], g3_d[2], writes=["gbuf"])
                pu = [_ps(nc, s2, f"b_pu{i}", [128, 512], F32) for i in range(4)]
                pd = [_ps(nc, s2, f"b_pdn{i}", [128, 512], F32) for i in range(4)]
                nu = 0
                nd = 0
                nwd = 0

                def load_up(i):
                    P.dma_op("pool", f"wu{i%2}", wub[i % 2][:], wup_d[i], writes=[f"wub{i%2}"])

                def load_dn(i):
                    P.dma_op("pool", f"wd{i%2}", wdb[i % 2][:], wdn_d[i // 4, i % 4], writes=[f"wdb{i%2}"])
                load_up(0)
                load_dn(0)
                for q in range(4):
                    for j in range(16):
                        fc = q * 16 + j
                        if fc + 1 < 64:
                            load_up(fc + 1)
                        w = wub[fc % 2]
                        for blk in range(2):
                            bs = slice(blk * 512, (blk + 1) * 512)
                            bk = pu[nu % 4]
                            for kc in range(16):
                                P.op("pe", lambda e, kc=kc, w=w, bk=bk, bs=bs: e.matmul(out=bk[:], lhsT=w[:, kc, :], rhs=hT[:, kc, bs],
                                                                                     start=(kc == 0), stop=(kc == 15)),
                                     reads=[f"wub{fc%2}"], banks=[f"pu{nu%4}"])
                            r = rl[nu % 2]
                            P.op("act", lambda e, bk=bk, r=r: e.activation(out=r[:], in_=bk[:], func=AF.Relu),
                                 writes=[f"rl{nu%2}"], banks=[f"pu{nu%4}"])
                            eng = "dve" if nu % 2 == 0 else "pool"
                            P.op(eng, lambda e, r=r, j=j, bs=bs: e.tensor_tensor(out=mU[:, j, bs], in0=r[:], in1=r[:], op=ALU.mult),
                                 reads=[f"rl{nu%2}"], writes=[f"up{j}"])
                            nu += 1
                    upk = [f"up{j}" for j in range(16)]
                    for cb in range(4):
                        if nwd + 1 < 16:
                            load_dn(nwd + 1)
                        w = wdb[nwd % 2]
                        for t in range(NT8):
                            bk = pd[nd % 4]
                            for kc in range(16):
                                P.op("pe", lambda e, kc=kc, t=t, w=w, bk=bk: e.matmul(out=bk[:], lhsT=mU[:, kc, t * 128:(t + 1) * 128],
                                                                                   rhs=w[:, kc, :], start=(kc == 0), stop=(kc == 15)),
                                     reads=[f"wdb{nwd%2}"] + upk, banks=[f"pdn{nd%4}"])
                            P.op("dve", lambda e, t=t, cb=cb, bk=bk: e.tensor_tensor(out=x1[:, t, cb * 512:(cb + 1) * 512],
                                                                                      in0=bk[:], in1=x1[:, t, cb * 512:(cb + 1) * 512], op=ALU.add),
                                 reads=[f"x1_{t}_{cb}"], writes=[f"x1_{t}_{cb}"], banks=[f"pdn{nd%4}"])
                            nd += 1
                            if q == 3 and cb == 3:
                                rms_tile(P, C, x1[:, t, :], [f"x1_{t}_{c}" for c in range(4)], gbuf[:], "gbuf",
                                         ob[t % 2][:], f"ob{t%2}", t)
                                P.dma_op("sp", f"o{t%2}", out_d[t * 128:(t + 1) * 128, :], ob[t % 2][:], reads=[f"ob{t%2}"])
                        nwd += 1
                P.barrier()
                P.emit()


def prep_B(inp, b, c, rcv, idb):
    flip = c == 1
    x = inp["x"][b]
    x = x[::-1] if flip else x
    own = [4 * c + j for j in range(4)]
    oth = [4 * (1 - c) + j for j in range(4)]
    w_in = inp["w_in"][0]
    wdu = inp["w_dn_up"][0]
    wfo = inp["w_fourier"][0]
    rows = np.concatenate([np.arange(h * 128, (h + 1) * 128) for h in own + oth])
    ga = w_in[:, 5152:5152 + 2048]
    gf = w_in[:, 5152 + 2048:5152 + 4096]
    wst = np.concatenate([wdu[rows], wfo[rows], ga, gf], axis=0)
    nk = wst.shape[0] // 128
    wst = wst.reshape(nk, 128, 16, 128).transpose(2, 1, 0, 3)
    wo = inp["w_o"][0].reshape(16, 128, 4, 512).transpose(2, 1, 0, 3)
    wup = inp["w_mlp_up"][0].reshape(16, 128, 64, 128).transpose(2, 1, 0, 3)
    wdn = inp["w_mlp_down"][0].reshape(4, 16, 128, 4, 512).transpose(0, 3, 2, 1, 4)
    g3 = np.stack([np.tile(inp[k].reshape(-1)[None], (128, 1)) for k in ("g_mix", "g_mlp", "g_final")], 0)
    d = {
        "xo": np.ascontiguousarray(x[:OWN]), "g3": np.ascontiguousarray(g3), "idb": idb,
        "wst": np.ascontiguousarray(wst), "wo": np.ascontiguousarray(wo), "wup": np.ascontiguousarray(wup),
        "wdn": np.ascontiguousarray(wdn),
    }
    d.update(rcv)
    return d


def build_F():
    nc = bass.Bass("TRN2", target_bir_lowering=False)
    ins = declare_A_inputs(nc)
    w = declare_B_weights(nc, 8)
    out_d = nc.dram_tensor("out", [OWN, D], F32, kind="ExternalOutput").ap()
    own_og = nc.dram_tensor("i_own_og", [4, 128, OWN], BF16, kind="Internal").ap()
    own_fr = nc.dram_tensor("i_own_fr", [4, 128, OWN], BF16, kind="Internal").ap()
    snd = nc.dram_tensor("i_snd", [8 * 128, OWN], BF16, kind="Internal").ap()
    rcv = nc.dram_tensor("i_rcv", [2 * 8 * 128, OWN], BF16, kind="Internal").ap()
    snd3 = snd.rearrange("(g p) n -> g p n", p=128)
    rcv4 = rcv.rearrange("(r g p) n -> r g p n", r=2, p=128)
    emit_A(nc, ins, (own_og, snd3[0:4], own_fr, snd3[4:8]), {}, False, 9)
    ogfr = [(own_og, 4), (rcv4[0, 0:4], 4), (rcv4[1, 0:4], 4), (own_fr, 4), (rcv4[0, 4:8], 4), (rcv4[1, 4:8], 4)]
    build_B_body(nc, ins["x"][0:OWN, :], w["g3"], w["idb"], ogfr, 8, w["wst"], w["wo"], w["wup"], w["wdn"], out_d,
                 cc=(snd, rcv, [[0, 1], [2, 3], [4, 5], [6, 7]]))
    return nc


def prep_F(inp, b, c, consts, idb, shared):
    d = prep_A(inp, b, c, consts)
    wdu = inp["w_dn_up"][0]
    wfo = inp["w_fourier"][0]
    w_in = inp["w_in"][0]
    own = np.concatenate([np.arange(h * 128, (h + 1) * 128) for h in range(4 * c, 4 * c + 4)])
    z512 = np.zeros((512, D), np.float32)
    parts = []
    for wmat in (wdu, wfo):
        parts.append(wmat[own])
        parts.append(z512 if c == 0 else wmat[0:512])
        parts.append(z512 if c == 1 else wmat[512:1024])
    parts.append(w_in[:, 5152:5152 + 2048])
    parts.append(w_in[:, 5152 + 2048:5152 + 4096])
    wst = np.concatenate(parts, axis=0)
    nk = wst.shape[0] // 128
    wst = wst.reshape(nk, 128, 16, 128).transpose(2, 1, 0, 3)
    d["wst"] = np.ascontiguousarray(wst)
    d["idb"] = idb
    d.update(shared)
    return d


def _shared_B(inp):
    wo = inp["w_o"][0].reshape(16, 128, 4, 512).transpose(2, 1, 0, 3)
    wup = inp["w_mlp_up"][0].reshape(16, 128, 64, 128).transpose(2, 1, 0, 3)
    wdn = inp["w_mlp_down"][0].reshape(4, 16, 128, 4, 512).transpose(0, 3, 2, 1, 4)
    g3 = np.stack([np.tile(inp[k].reshape(-1)[None], (128, 1)) for k in ("g_mix", "g_mlp", "g_final")], 0)
    return {"wo": np.ascontiguousarray(wo), "wup": np.ascontiguousarray(wup), "wdn": np.ascontiguousarray(wdn),
            "g3": np.ascontiguousarray(g3)}


_CACHE = {}


def kernel(**inputs):
    inp = {k: np.asarray(v) for k, v in inputs.items()}
    if "F" not in _CACHE:
        _CACHE["F"] = build_F()
        _CACHE["consts"] = [_consts(0), _consts(1)]
    consts = _CACHE["consts"]
    idb = np.eye(128, dtype=np.float32).astype(NPBF)
    shared = _shared_B(inp)
    maps = [prep_F(inp, i // 2, i % 2, consts, idb, shared) for i in range(8)]
    res = run_bass_kernel_spmd(_CACHE["F"], maps, core_ids=list(range(8))).results
    out = np.zeros((4, T, D), np.float32)
    for i in range(8):
        b, c = i // 2, i % 2
        o = np.asarray(res[i]["out"])
        if c == 0:
            out[b, :OWN] = o
        else:
            out[b, OWN:] = o[::-1]
    return out
```
